# Optimizing a Trainium2 kernel written in Bass

```python
import math
import jax, jax.numpy as jnp
from jax import lax
import numpy as np


D_MODEL = 2048
BATCH = 4
SEQ = 2048
DEPTH = 4

NSA_HEADS = 8
NSA_KV_GROUPS = 2
NSA_HEAD_DIM = 128
CMP_LEN = 32
CMP_STRIDE = 16
SLC_BLOCK = 64
SLC_TOPK = 8
WINDOW = 512
WIN_QBLOCK = 128
SLC_QBLOCK = 64
SSD_INNER = D_MODEL
SSD_HEAD_DIM = 64
SSD_HEADS = SSD_INNER // SSD_HEAD_DIM
SSD_GROUPS = 4
SSD_STATE = 128
SSD_CONV = 4
SSD_CHUNK = 128
RET_HEADS = 4
RET_QK_DIM = 128
RET_V_DIM = 256
RET_CHUNK = 128
D_FF = -(-8 * D_MODEL // (3 * 256)) * 256

NSA_Q = NSA_HEADS * NSA_HEAD_DIM
NSA_KV = NSA_KV_GROUPS * NSA_HEAD_DIM
SSD_BC = SSD_GROUPS * SSD_STATE
SSD_CONV_DIM = SSD_INNER + 2 * SSD_BC
RET_QK = RET_HEADS * RET_QK_DIM
RET_V = RET_HEADS * RET_V_DIM
IN_SPLITS = (NSA_Q, 6 * NSA_KV, 3 * NSA_HEADS, SSD_INNER, SSD_CONV_DIM, SSD_HEADS, 2 * RET_QK, RET_V, RET_V, 3 * D_MODEL)
D_IN = sum(IN_SPLITS)
NEG_INF = -1e30
FORCE_SCORE = 1e9
EPS = 1e-6

kernel_name = 'hybrid_nsa_ssd_retention_block'


def rms_norm(x, w):
    xf = x.astype(jnp.float32)
    y = xf * lax.rsqrt(jnp.mean(xf * xf, axis=-1, keepdims=True) + EPS)
    return (y * w.astype(jnp.float32)).astype(x.dtype)


def alibi_slopes(n):
    return jnp.exp2(-8.0 * (jnp.arange(n, dtype=jnp.float32) + 1.0) / n)


def masked_softmax(s, mask):
    return jax.nn.softmax(jnp.where(mask, s, NEG_INF), axis=-1)


def nsa_mixer(q, kv, gate_logits, k_pe, k_w1, k_w2, v_pe, v_w1, v_w2):
    f32 = jnp.float32
    B, S, _ = q.shape
    G, Dh = NSA_KV_GROUPS, NSA_HEAD_DIM
    J = NSA_HEADS // G
    q = q.reshape(B, S, G, J, Dh) * (Dh ** -0.5)
    kc, vc, ks, vs, kw, vw = [t.reshape(B, S, G, Dh) for t in jnp.split(kv, 6, axis=-1)]
    slopes = alibi_slopes(NSA_HEADS).reshape(G, J)
    pos = jnp.arange(S)

    n_cmp = (S - CMP_LEN) // CMP_STRIDE + 1
    cmp_start = jnp.arange(n_cmp) * CMP_STRIDE
    cmp_end = cmp_start + CMP_LEN - 1
    blk_idx = cmp_start[:, None] + jnp.arange(CMP_LEN)[None, :]

    def compress(t, pe, w1, w2):
        blocks = t[:, blk_idx] + pe[None, None, :, None, :]
        blocks = blocks.transpose(0, 1, 3, 2, 4).reshape(B, n_cmp, G, CMP_LEN * Dh)
        return jax.nn.gelu(blocks @ w1) @ w2

    k_cmp = compress(kc, k_pe, k_w1, k_w2)
    v_cmp = compress(vc, v_pe, v_w1, v_w2)
    d_cmp = (pos[:, None] - cmp_end[None, :]).astype(f32)
    ok_cmp = d_cmp >= 0
    s_cmp = jnp.einsum('bsgjd,bngd->bgjsn', q, k_cmp).astype(f32) - slopes[:, :, None, None] * d_cmp
    p_cmp = masked_softmax(s_cmp, ok_cmp) * jnp.any(ok_cmp, axis=-1, keepdims=True)
    o_cmp = jnp.einsum('bgjsn,bngd->bsgjd', p_cmp.astype(v_cmp.dtype), v_cmp)

    n_slc = S // SLC_BLOCK
    top = min(SLC_TOPK, n_slc)
    slc_start = jnp.arange(n_slc) * SLC_BLOCK
    overlap = ((cmp_start[:, None] < slc_start[None, :] + SLC_BLOCK) & (cmp_end[:, None] >= slc_start[None, :])).astype(f32)
    imp = jnp.einsum('bgjsn,nk->bgsk', p_cmp, overlap)
    blk = jnp.arange(n_slc)[None, :]
    cur = (pos // SLC_BLOCK)[:, None]
    imp = jnp.where(blk > cur, NEG_INF, imp)
    imp = jnp.where((blk == cur) | (blk == 0), FORCE_SCORE, imp)
    sel = lax.top_k(imp, top)[1]

    ks_blk = ks.reshape(B, n_slc, SLC_BLOCK, G, Dh).transpose(0, 3, 1, 2, 4)
    vs_blk = vs.reshape(B, n_slc, SLC_BLOCK, G, Dh).transpose(0, 3, 1, 2, 4)
    nq = S // SLC_QBLOCK
    q_blocks = q.reshape(B, nq, SLC_QBLOCK, G, J, Dh).transpose(1, 0, 2, 3, 4, 5)
    sel_blocks = sel.reshape(B, G, nq, SLC_QBLOCK, top).transpose(2, 0, 1, 3, 4)
    bi = jnp.arange(B)[:, None, None, None]
    gi = jnp.arange(G)[None, :, None, None]

    def slc_block(args):
        qb, sb, c = args
        kb = ks_blk[bi, gi, sb]
        vb = vs_blk[bi, gi, sb]
        tq = c * SLC_QBLOCK + jnp.arange(SLC_QBLOCK)
        tk = sb[..., None] * SLC_BLOCK + jnp.arange(SLC_BLOCK)
        d = (tq[None, None, :, None, None] - tk).astype(f32)[:, :, None]
        sc = jnp.einsum('bqgjd,bgqnld->bgjqnl', qb, kb).astype(f32) - slopes[None, :, :, None, None, None] * d
        sc = jnp.where(d >= 0, sc, NEG_INF)
        Bq, Gq, Jq, Qq = sc.shape[:4]
        p = jax.nn.softmax(sc.reshape(Bq, Gq, Jq, Qq, -1), axis=-1).reshape(sc.shape)
        return jnp.einsum('bgjqnl,bgqnld->bqgjd', p.astype(vb.dtype), vb)

    o_slc = lax.map(slc_block, (q_blocks, sel_blocks, jnp.arange(nq)))
    o_slc = o_slc.transpose(1, 0, 2, 3, 4, 5).reshape(B, S, G, J, Dh)

    nw = S // WIN_QBLOCK
    span = WINDOW + WIN_QBLOCK
    kidx = jnp.arange(nw)[:, None] * WIN_QBLOCK + jnp.arange(span)[None, :] - WINDOW
    kidx_c = jnp.clip(kidx, 0, S - 1)
    kwb = kw[:, kidx_c]
    vwb = vw[:, kidx_c]
    qw = q.reshape(B, nw, WIN_QBLOCK, G, J, Dh)
    tq = pos.reshape(nw, WIN_QBLOCK)
    dw = tq[:, :, None] - kidx[:, None, :]
    ok_w = ((dw >= 0) & (dw < WINDOW) & (kidx[:, None, :] >= 0))[None, :, None, None]
    dwf = dw.astype(f32)[None, :, None, None]
    s_w = jnp.einsum('bcqgjd,bckgd->bcgjqk', qw, kwb).astype(f32) - slopes[None, None, :, :, None, None] * dwf
    p_w = masked_softmax(s_w, ok_w)
    o_win = jnp.einsum('bcgjqk,bckgd->bcqgjd', p_w.astype(vwb.dtype), vwb).reshape(B, S, G, J, Dh)

    gates = jax.nn.sigmoid(gate_logits.astype(f32)).reshape(B, S, G, J, 3).astype(o_cmp.dtype)
    out = gates[..., 0:1] * o_cmp + gates[..., 1:2] * o_slc + gates[..., 2:3] * o_win
    return out.reshape(B, S, NSA_Q)


def ssd_mixer(z, xbc, dt, conv_w, conv_b, dt_bias, a_log, d_skip, norm_w):
    f32 = jnp.float32
    B, S, _ = z.shape
    G, Hg, P, N, Lc = SSD_GROUPS, SSD_HEADS // SSD_GROUPS, SSD_HEAD_DIM, SSD_STATE, SSD_CHUNK
    nc = S // Lc
    xbc = lax.conv_general_dilated(xbc, conv_w[:, None, :].astype(xbc.dtype), window_strides=(1,), padding=[(SSD_CONV - 1, 0)], dimension_numbers=('NWC', 'WIO', 'NWC'), feature_group_count=SSD_CONV_DIM)
    xbc = jax.nn.silu((xbc + conv_b).astype(f32))
    xs, bm, cm = jnp.split(xbc, [SSD_INNER, SSD_INNER + SSD_BC], axis=-1)
    dt = jax.nn.softplus(dt.astype(f32) + dt_bias.astype(f32))
    a = (dt * -jnp.exp(a_log.astype(f32))).reshape(B, nc, Lc, G, Hg).transpose(0, 3, 4, 1, 2)
    x = xs.reshape(B, nc, Lc, G, Hg, P)
    xdt = x * dt.reshape(B, nc, Lc, G, Hg)[..., None]
    bmat = bm.reshape(B, nc, Lc, G, N)
    cmat = cm.reshape(B, nc, Lc, G, N)
    a_cum = jnp.cumsum(a, axis=-1)
    causal = jnp.tril(jnp.ones((Lc, Lc), dtype=bool))
    seg = a_cum[..., :, None] - a_cum[..., None, :]
    decay_in = jnp.exp(jnp.where(causal, seg, -jnp.inf))
    cb = jnp.einsum('bclgn,bcsgn->bgcls', cmat, bmat)
    y_diag = jnp.einsum('bghcls,bcsghp->bclghp', cb[:, :, None] * decay_in, xdt)
    decay_to_end = jnp.exp(a_cum[..., -1:] - a_cum)
    chunk_states = jnp.einsum('bclgn,bghcl,bclghp->bcghpn', bmat, decay_to_end, xdt)
    chunk_decay = jnp.exp(a_cum[..., -1])

    def step(h, inp):
        st, dec = inp
        return h * dec[..., None, None] + st, h

    h0 = jnp.zeros((B, G, Hg, P, N), f32)
    _, prev = lax.scan(step, h0, (chunk_states.transpose(1, 0, 2, 3, 4, 5), chunk_decay.transpose(3, 0, 1, 2)))
    prev = prev.transpose(1, 0, 2, 3, 4, 5)
    y_off = jnp.einsum('bclgn,bcghpn,bghcl->bclghp', cmat, prev, jnp.exp(a_cum))
    y = y_diag + y_off + x * d_skip.astype(f32).reshape(G, Hg, 1)
    y = y.reshape(B, S, SSD_INNER) * jax.nn.silu(z.astype(f32))
    y = y.reshape(B, S, G, SSD_INNER // G)
    y = y * lax.rsqrt(jnp.mean(y * y, axis=-1, keepdims=True) + EPS)
    return y.reshape(B, S, SSD_INNER) * norm_w.astype(f32)


def retention_mixer(q, k, v, g, norm_w):
    f32 = jnp.float32
    B, S, _ = q.shape
    H, Dk, Dv, Lc = RET_HEADS, RET_QK_DIM, RET_V_DIM, RET_CHUNK
    nc = S // Lc
    q = q.astype(f32).reshape(B, nc, Lc, H, Dk) * (Dk ** -0.5)
    k = k.astype(f32).reshape(B, nc, Lc, H, Dk)
    v = v.astype(f32).reshape(B, nc, Lc, H, Dv)
    log_g = jnp.log1p(-jnp.exp2(-5.0 - jnp.arange(H, dtype=f32)))
    idx = jnp.arange(Lc, dtype=f32)
    diff = idx[:, None] - idx[None, :]
    dmat = jnp.where(diff >= 0, jnp.exp(diff * log_g[:, None, None]), 0.0)
    scores = jnp.einsum('bclhd,bcshd->bchls', q, k) * dmat
    y_in = jnp.einsum('bchls,bcshv->bclhv', scores, v)
    k_dec = jnp.exp((Lc - 1.0 - idx)[None, :] * log_g[:, None])
    q_dec = jnp.exp((idx + 1.0)[None, :] * log_g[:, None])
    chunk_kv = jnp.einsum('bclhd,hl,bclhv->bchdv', k, k_dec, v)
    chunk_decay = jnp.exp(Lc * log_g)

    def step(r, kv_c):
        return r * chunk_decay[:, None, None] + kv_c, r

    r0 = jnp.zeros((B, H, Dk, Dv), f32)
    _, prev = lax.scan(step, r0, chunk_kv.transpose(1, 0, 2, 3, 4))
    prev = prev.transpose(1, 0, 2, 3, 4)
    y = y_in + jnp.einsum('bclhd,bchdv,hl->bclhv', q, prev, q_dec)
    mu = jnp.mean(y, axis=-1, keepdims=True)
    var = jnp.mean(jnp.square(y - mu), axis=-1, keepdims=True)
    y = (y - mu) * lax.rsqrt(var + EPS) * norm_w.astype(f32)
    return jax.nn.silu(g.astype(f32)) * y.reshape(B, S, RET_V)


def hybrid_mixer(u, w_in, k_pe, k_w1, k_w2, v_pe, v_w1, v_w2, conv_w, conv_b, dt_bias, a_log, d_skip, ssd_norm, ret_norm, p_nsa, p_ssd, p_ret, w_out):
    proj = u @ w_in
    offs = np.cumsum(IN_SPLITS)[:-1].tolist()
    nsa_q, nsa_kv, nsa_g, ssd_z, ssd_xbc, ssd_dt, ret_qk, ret_v, ret_g, merge_g = jnp.split(proj, offs, axis=-1)
    y_nsa = nsa_mixer(nsa_q, nsa_kv, nsa_g, k_pe, k_w1, k_w2, v_pe, v_w1, v_w2).astype(u.dtype)
    y_ssd = ssd_mixer(ssd_z, ssd_xbc, ssd_dt, conv_w, conv_b, dt_bias, a_log, d_skip, ssd_norm).astype(u.dtype)
    ret_q, ret_k = jnp.split(ret_qk, 2, axis=-1)
    y_ret = retention_mixer(ret_q, ret_k, ret_v, ret_g, ret_norm).astype(u.dtype)
    g_nsa, g_ssd, g_ret = jnp.split(jax.nn.sigmoid(merge_g), 3, axis=-1)
    merged = g_nsa * (y_nsa @ p_nsa) + g_ssd * (y_ssd @ p_ssd) + g_ret * (y_ret @ p_ret)
    return merged @ w_out


def setup_inputs(seed: int = 0):
    key = jax.random.key(seed)
    ks = jax.random.split(key, 32)
    f32 = jnp.float32
    L = DEPTH
    Dh = NSA_HEAD_DIM

    def nrm(k, shape, fan_in):
        return jax.random.normal(k, shape, f32) * (fan_in ** -0.5)

    def gain(k, shape):
        return 1.0 + 0.02 * jax.random.normal(k, shape, f32)

    dt0 = jnp.exp(jax.random.uniform(ks[12], (L, SSD_HEADS), f32, math.log(1e-3), math.log(1e-1)))
    return {
        'x': jax.random.normal(ks[0], (BATCH, SEQ, D_MODEL), f32),
        'norm_mix': gain(ks[1], (L, D_MODEL)),
        'w_in': nrm(ks[2], (L, D_MODEL, D_IN), D_MODEL),
        'cmp_k_pe': 0.02 * jax.random.normal(ks[3], (L, CMP_LEN, Dh), f32),
        'cmp_k_w1': nrm(ks[4], (L, CMP_LEN * Dh, Dh), CMP_LEN * Dh),
        'cmp_k_w2': nrm(ks[5], (L, Dh, Dh), Dh),
        'cmp_v_pe': 0.02 * jax.random.normal(ks[6], (L, CMP_LEN, Dh), f32),
        'cmp_v_w1': nrm(ks[7], (L, CMP_LEN * Dh, Dh), CMP_LEN * Dh),
        'cmp_v_w2': nrm(ks[8], (L, Dh, Dh), Dh),
        'conv_w': nrm(ks[9], (L, SSD_CONV, SSD_CONV_DIM), SSD_CONV),
        'conv_b': 0.02 * jax.random.normal(ks[10], (L, SSD_CONV_DIM), f32),
        'dt_bias': dt0 + jnp.log(-jnp.expm1(-dt0)),
        'a_log': jnp.log(jax.random.uniform(ks[13], (L, SSD_HEADS), f32, 1.0, 16.0)),
        'd_skip': 1.0 + 0.1 * jax.random.normal(ks[14], (L, SSD_HEADS), f32),
        'ssd_norm': gain(ks[15], (L, SSD_INNER)),
        'ret_norm': gain(ks[16], (L, RET_HEADS, RET_V_DIM)),
        'p_nsa': nrm(ks[17], (L, NSA_Q, D_MODEL), NSA_Q),
        'p_ssd': nrm(ks[18], (L, SSD_INNER, D_MODEL), SSD_INNER),
        'p_ret': nrm(ks[19], (L, RET_V, D_MODEL), RET_V),
        'w_out': nrm(ks[20], (L, D_MODEL, D_MODEL), D_MODEL),
        'norm_ffn': gain(ks[21], (L, D_MODEL)),
        'w_gate': nrm(ks[22], (L, D_MODEL, D_FF), D_MODEL),
        'w_up': nrm(ks[23], (L, D_MODEL, D_FF), D_MODEL),
        'w_down': nrm(ks[24], (L, D_FF, D_MODEL), D_FF),
        'norm_final': gain(ks[25], (D_MODEL,)),
    }


def reference(x, norm_mix, w_in, cmp_k_pe, cmp_k_w1, cmp_k_w2, cmp_v_pe, cmp_v_w1, cmp_v_w2, conv_w, conv_b, dt_bias, a_log, d_skip, ssd_norm, ret_norm, p_nsa, p_ssd, p_ret, w_out, norm_ffn, w_gate, w_up, w_down, norm_final):
    for l in range(DEPTH):
        u = rms_norm(x, norm_mix[l])
        mix = hybrid_mixer(u, w_in[l], cmp_k_pe[l], cmp_k_w1[l], cmp_k_w2[l], cmp_v_pe[l], cmp_v_w1[l], cmp_v_w2[l], conv_w[l], conv_b[l], dt_bias[l], a_log[l], d_skip[l], ssd_norm[l], ret_norm[l], p_nsa[l], p_ssd[l], p_ret[l], w_out[l])
        x = x + mix.astype(x.dtype)
        f = rms_norm(x, norm_ffn[l])
        x = x + ((jax.nn.silu(f @ w_gate[l]) * (f @ w_up[l])) @ w_down[l]).astype(x.dtype)
    return rms_norm(x, norm_final)
```

```python
import numpy as np
import concourse.bass as bass
import concourse.mybir as mybir
from concourse.bass_utils import run_bass_kernel_spmd
from contextlib import ExitStack

F32 = mybir.dt.float32
BF16 = mybir.dt.bfloat16
ALU = mybir.AluOpType
AF = mybir.ActivationFunctionType
AX = mybir.AxisListType

COMPUTE = ("pe", "dve", "act", "pool")
NSLOT = {"sp": 16, "pool": 16, "act": 8}
ARENA = 53000

NT = 2048
D = 2048
DIN = 16952
DFF = 5632
NL = 4
C_Q, C_KV, C_NG, C_Z, C_XBC, C_DT, C_RQ, C_RK, C_RV, C_RG, C_MG = 0, 1024, 2560, 2584, 4632, 7704, 7736, 8248, 8760, 9784, 10808
NEG = -30000.0
EPS = 1e-6


class St:
    __slots__ = ("lw", "rd")

    def __init__(self, o=None):
        self.lw = o.lw if o else None
        self.rd = list(o.rd) if o else []


class T:
    def __init__(self, ap, name, excl=False):
        self.ap = ap
        self.name = name
        self.excl = excl
        self.st = {None: St()}

    def __getitem__(self, k):
        return self.ap[k]

    def states(self, key):
        if key is None:
            return list(self.st.values())
        if key not in self.st:
            self.st[key] = St(self.st[None])
        return [self.st[key]]


class _Rec:
    def __getattr__(self, name):
        return lambda *a, **k: (name, a, k)


_REC = _Rec()


class FW:
    def __init__(self, nc):
        self.nc = nc
        self.stream = {e: [] for e in ("pe", "dve", "act", "pool", "sp")}
        self.nops = {e: 0 for e in COMPUTE}
        self.sig = {e: set() for e in COMPUTE}
        self.ndma = {q: 0 for q in NSLOT}
        self.known = {e: {} for e in self.stream}
        self.es = ExitStack()
        self.sems = {}
        self.dsems = {}
        self.arena = self.es.enter_context(nc.sbuf_tensor("arena", [128, ARENA], F32))
        self.aoff = 0
        self.PS = [T(self.es.enter_context(nc.psum_tensor("ps%d" % i, [128, 512], F32))[:, :], "ps%d" % i, excl=True)
                   for i in range(8)]

    def sb(self, name, shape, dtype=F32):
        p = shape[0]
        n = int(np.prod(shape[1:]))
        nb = n * (4 if dtype == F32 else 2)
        nf = ((nb + 63) // 64) * 16
        assert self.aoff + nf <= ARENA, ("SBUF arena overflow", name, self.aoff, nf)
        ap = self.arena[0:p, self.aoff:self.aoff + nf]
        self.aoff += nf
        if dtype != F32:
            ap = ap.bitcast(dtype)
        ap = ap[:, 0:n]
        if len(shape) == 3:
            ap = ap.rearrange("p (a b) -> p a b", a=shape[1], b=shape[2])
        elif len(shape) == 4:
            ap = ap.rearrange("p (a b c) -> p a b c", a=shape[1], b=shape[2], c=shape[3])
        return T(ap, name)

    def mark(self):
        return self.aoff

    def release(self, m):
        self.barrier()
        self.aoff = m

    def dram(self, name, shape, dtype, kind="Internal"):
        h = self.nc.dram_tensor(name, list(shape), dtype, kind=kind)
        return T(h.ap(), name)

    def _need(self, eng, ev, waits):
        if ev is None:
            return
        if ev[0] == "c":
            _, f, idx = ev
            if f == eng and eng == "pe":
                return
            k = ("c", f)
            if self.known[eng].get(k, 0) >= idx:
                return
            self.known[eng][k] = idx
            self.sig[f].add(idx)
            waits.append(ev)
        else:
            _, q, j = ev
            k = ("d", q, j % NSLOT[q])
            if self.known[eng].get(k, -1) >= j:
                return
            self.known[eng][k] = j
            waits.append(ev)

    def _deps(self, eng, r, w, is_dma):
        waits = []
        rs, ws = [], []
        for x in r:
            t, key = x if isinstance(x, tuple) else (x, None)
            (ws if t.excl else rs).extend(t.states(key))
        for x in w:
            t, key = x if isinstance(x, tuple) else (x, None)
            ws.extend(t.states(key))
        for s in rs:
            self._need(eng, s.lw, waits)
        for s in ws:
            lw = s.lw
            if lw is not None and not ((not is_dma) and lw[0] == "c" and lw[1] == eng):
                self._need(eng, lw, waits)
            for ev in s.rd:
                if (not is_dma) and ev[0] == "c" and ev[1] == eng:
                    continue
                self._need(eng, ev, waits)
        return waits, rs, ws

    def _commit(self, ev, rs, ws):
        for s in rs:
            if ev[0] == "c":
                s.rd = [e for e in s.rd if not (e[0] == "c" and e[1] == ev[1])]
            s.rd.append(ev)
        for s in ws:
            s.lw = ev
            s.rd = []

    def op(self, eng, fn, r=(), w=()):
        waits, rs, ws = self._deps(eng, r, w, False)
        for ev in waits:
            self.stream[eng].append(("wait", ev))
        self.nops[eng] += 1
        idx = self.nops[eng]
        self.stream[eng].append(("op", fn(_REC), idx))
        self._commit(("c", eng, idx), rs, ws)

    def dma(self, q, out_ap, in_ap, r=(), w=()):
        waits, rs, ws = self._deps(q, r, w, True)
        j = self.ndma[q]
        K = NSLOT[q]
        if j >= K:
            self._need(q, ("d", q, j - K), waits)
        for ev in waits:
            self.stream[q].append(("wait", ev))
        self.ndma[q] += 1
        self.stream[q].append(("dma", out_ap, in_ap, j))
        self._commit(("d", q, j), rs, ws)

    def barrier(self):
        for e in self.stream:
            waits = []
            for f in COMPUTE:
                if self.nops[f] > 0:
                    self._need(e, ("c", f, self.nops[f]), waits)
            for q in NSLOT:
                for j in range(max(0, self.ndma[q] - NSLOT[q]), self.ndma[q]):
                    self._need(e, ("d", q, j), waits)
            for ev in waits:
                self.stream[e].append(("wait", ev))

    def emit(self):
        nc = self.nc
        self.barrier()
        es = self.es
        for e in COMPUTE:
            self.sems[e] = es.enter_context(nc.semaphore("s_" + e))
        for q in NSLOT:
            self.dsems[q] = [es.enter_context(nc.semaphore("d_%s_%d" % (q, i))) for i in range(NSLOT[q])]
        cum = {}
        for e in COMPUTE:
            c = 0
            m = {}
            for i in range(1, self.nops[e] + 1):
                if i in self.sig[e]:
                    c += 1
                    m[i] = c
            cum[e] = m

        def replay(ename, eng):
            for rec in self.stream[ename]:
                if rec[0] == "wait":
                    ev = rec[1]
                    if ev[0] == "c":
                        eng.wait_ge(self.sems[ev[1]], cum[ev[1]][ev[2]])
                    else:
                        _, q, j = ev
                        eng.wait_ge(self.dsems[q][j % NSLOT[q]], 16 * (j // NSLOT[q] + 1))
                elif rec[0] == "op":
                    nm, ar, kw = rec[1]
                    ins = getattr(eng, nm)(*ar, **kw)
                    if rec[2] in self.sig[ename]:
                        ins.then_inc(self.sems[ename], 1)
                else:
                    _, o, i, j = rec
                    eng.dma_start(out=o, in_=i).then_inc(self.dsems[ename][j % NSLOT[ename]], 16)

        with nc.Block() as block:
            @block.tensor
            def _(e):
                replay("pe", e)

            @block.vector
            def _(e):
                replay("dve", e)

            @block.scalar
            def _(e):
                replay("act", e)

            @block.gpsimd
            def _(e):
                replay("pool", e)

            @block.sync
            def _(e):
                replay("sp", e)
        es.close()


CO = {}
_off = 0
for _n, _w in (("IDENT", 128), ("CAUS", 128), ("TGT", 128), ("ONESM", 128), ("ONES1", 128), ("MSK", 1536),
               ("OVL", 33), ("DMT", 512), ("QD", 512), ("KD", 4)):
    CO[_n] = (_off, _w)
    _off += _w
NCONST = _off
TABW = 896 + 512 + 1408 + 2048
TO_U, TO_W, TO_UW, TO_C = 0, 896, 1408, 2816


def host_consts():
    c = np.zeros((128, NCONST), np.float64)
    p = np.arange(128)[:, None]
    f = np.arange(128)[None, :]
    c[:, CO["IDENT"][0]:][:, :128] = (p == f)
    c[:, CO["CAUS"][0]:][:, :128] = (f >= p)
    c[:, CO["TGT"][0]:][:, :128] = (p > f)
    c[:, CO["ONESM"][0]:][:, :128] = 1.0 / D
    c[:, CO["ONES1"][0]:][:, :128] = 1.0
    msk = np.zeros((128, 3, 16, 32))
    for Q in range(16):
        tq = Q * 128 + np.arange(128)
        cur = (tq // 64)[:, None]
        blk = np.arange(32)[None, :]
        msk[:, 0, Q, :] = ((blk > 0) & (blk < cur))
        msk[:, 1, Q, :] = np.where((blk == cur) | (blk == 0), 1e9, np.where(blk > cur, -1e30, 0.0))
        msk[:, 2, Q, :] = (blk <= cur)
    c[:, CO["MSK"][0]:][:, :1536] = msk.reshape(128, -1)
    n = np.arange(128)[:, None]
    k = np.arange(32)[None, :]
    ovl = ((16 * n < 64 * k + 64) & (16 * n + 31 >= 64 * k) & (n < 127)).astype(np.float64)
    c[:, CO["OVL"][0]] = 1.0
    c[:, CO["OVL"][0] + 1:][:, :32] = ovl
    h = np.arange(4, dtype=np.float64)
    log_g = np.log1p(-np.exp2(-5.0 - h))
    s_ = np.arange(128)[:, None]
    l_ = np.arange(128)[None, :]
    for hh in range(4):
        dm = np.where(l_ >= s_, np.exp((l_ - s_) * log_g[hh]), 0.0)
        c[:, CO["DMT"][0] + hh * 128:][:, :128] = dm
        c[:, CO["QD"][0] + hh * 128:][:, :128] = np.exp((np.arange(128) + 1.0) * log_g[hh])[None, :]
        c[:, CO["KD"][0] + hh] = np.exp((127.0 - np.arange(128)) * log_g[hh])
    cdec = [float(np.exp(128 * log_g[hh])) for hh in range(4)]
    tab = np.zeros((8, 128, TABW), np.float64)
    ki = np.arange(128)[:, None]
    for hd in range(8):
        s = 2.0 ** (-(hd + 1))
        cc = np.arange(896)[None, :]
        rel = cc - 384 - ki
        tab[hd, :, TO_U:TO_U + 896] = np.where(rel >= 0, -s * rel, NEG)
        qi = np.arange(512)[None, :]
        tab[hd, :, TO_W:TO_W + 512] = -s * (qi - ki)
        cc = np.arange(1408)[None, :]
        rel = cc - 384 - ki
        tab[hd, :, TO_UW:TO_UW + 1408] = np.where((rel >= 0) & (rel < 512), -s * rel, NEG)
        tq = np.arange(2048)[None, :]
        rel = tq - (16 * ki + 31)
        tab[hd, :, TO_C:TO_C + 2048] = np.where((rel >= 0) & (ki < 127), -s * rel, NEG)
    et = (np.arange(2048)[None, :] // 64 == np.arange(32)[:, None]).astype(np.float32)
    return c.astype(np.float32), tab.astype(np.float32), et, cdec


class Prog:
    def __init__(self, n_layers=NL, dbg=False, stop_after=None):
        self.nl = n_layers
        self.dbg = dbg
        self.stop_after = stop_after
        self.nc = bass.Bass("TRN2", target_bir_lowering=False)
        self.fw = FW(self.nc)
        self.cdec = host_consts()[3]
        self.rot = 0
        self.build()

    def bank(self):
        b = self.fw.PS[self.rot % 8]
        self.rot += 1
        return b

    def din(self, name, shape, dtype=F32):
        return self.fw.dram(name, shape, dtype, kind="ExternalInput")

    def scr(self, name, shape, dtype):
        return self.fw.dram(name, shape, dtype, kind="ExternalOutput" if self.dbg else "Internal")

    def wload(self, Wt, w_ap, KC, n):
        fw = self.fw
        buf = self.WB[self.wrot % len(self.WB)]
        self.wrot += 1
        view = buf.ap[:, 0:KC * n].rearrange("p (k c) -> p k c", k=KC, c=n)
        src = w_ap.rearrange("(k p) c -> p k c", p=128)
        step = 4
        for i, k0 in enumerate(range(0, KC, step)):
            k1 = min(KC, k0 + step)
            fw.dma("pool", view[:, k0:k1, :], src[:, k0:k1, :], r=[Wt], w=[(buf, i)])
        return buf, view

    def rmsnorm(self, xg, ntok, wcol, out_fn, sq, rs):
        fw = self.fw
        C = self.C
        ps = self.bank()
        for kc in range(16):
            s = sq[kc % 2]
            fw.op("act", lambda e, s=s, kc=kc: e.activation(out=s[:, 0:ntok], in_=xg[:, kc, :], func=AF.Square), r=[xg], w=[s])
            fw.op("pe", lambda e, s=s, kc=kc: e.matmul(ps[:, 0:ntok], lhsT=self.cv("ONESM"), rhs=s[:, 0:ntok], start=(kc == 0), stop=(kc == 15)),
                  r=[s, C], w=[ps])
        fw.op("act", lambda e: e.activation(out=rs[:, 0:ntok], in_=ps[:, 0:ntok], func=AF.Sqrt, bias=self.EPSC[:, 0:1], scale=1.0), r=[ps, self.EPSC], w=[rs])
        fw.op("dve", lambda e: e.reciprocal(out=rs[:, 0:ntok], in_=rs[:, 0:ntok]), r=[rs], w=[rs])
        for kc in range(16):
            ot, oap = out_fn(kc)
            fw.op("dve", lambda e, kc=kc, oap=oap: e.scalar_tensor_tensor(out=oap, in0=xg[:, kc, :], scalar=wcol[:, kc:kc + 1], in1=rs[:, 0:ntok],
                                                                         op0=ALU.mult, op1=ALU.mult), r=[xg, rs, self.NRM], w=[ot])

    def cv(self, name, width=None):
        o, w = CO[name]
        return self.C[:, o:o + (width or w)]

    def cvt(self, name):
        if name == "EPSC":
            return self.EPSC
        raise KeyError(name)

    def build(self):
        fw = self.fw
        nl = self.nl
        self.xT_in = self.din("xT", [D, NT])
        self.d_consts = self.din("consts", [128, NCONST])
        self.d_tab = self.din("tab", [8, 128, TABW])
        self.d_et = self.din("et", [32, NT])
        self.d_nrm = self.din("nrm", [128, (2 * NL + 1) * 16])
        self.d_cw = self.din("cw", [NL, 128, 24 * 5])
        self.d_ssdv = self.din("ssdv", [NL, 128, 96])
        self.d_snw = self.din("snw", [NL, 128, 2048])
        self.d_rnw = self.din("rnw", [NL, 128, 1024])
        self.d_pet = self.din("pet", [NL, 2, 128, 32])
        self.w_in = self.din("w_in", [NL, D, DIN])
        self.cw1 = [self.din("cmp_k_w1", [NL, 4096, 128]), self.din("cmp_v_w1", [NL, 4096, 128])]
        self.cw2 = [self.din("cmp_k_w2", [NL, 128, 128]), self.din("cmp_v_w2", [NL, 128, 128])]
        self.p_nsa = self.din("p_nsa", [NL, 1024, D])
        self.p_ssd = self.din("p_ssd", [NL, 2048, D])
        self.p_ret = self.din("p_ret", [NL, 1024, D])
        self.w_out = self.din("w_out", [NL, D, D])
        self.w_gate = self.din("w_gate", [NL, D, DFF])
        self.w_up = self.din("w_up", [NL, D, DFF])
        self.w_down = self.din("w_down", [NL, DFF, D])
        self.outT = fw.dram("outT", [D, NT], F32, kind="ExternalOutput")
        self.XT = self.scr("XT", [D, NT], F32)
        self.QT = self.scr("QT", [1024, NT], BF16)
        self.KVT = self.scr("KVT", [1024, NT], BF16)
        self.VTOK = self.scr("VTOK", [NT, 512], BF16)
        self.NG = self.scr("NG", [NT, 24], F32)
        self.ZS = self.scr("ZS", [NT, 2048], F32)
        self.XBCT = self.scr("XBCT", [3072, NT], F32)
        self.DTR = self.scr("DTR", [NT, 32], F32)
        self.RQT = self.scr("RQT", [512, NT], BF16)
        self.RKT = self.scr("RKT", [512, NT], BF16)
        self.RKTOK = self.scr("RKTOK", [NT, 512], BF16)
        self.RV = self.scr("RV", [NT, 1024], BF16)
        self.RGS = self.scr("RGS", [NT, 1024], F32)
        self.MGT = self.scr("MGT", [6144, NT], BF16)
        self.XTOK = self.scr("XTOK", [NT, 2048], F32)
        self.BCT = self.scr("BCT", [1024, NT], BF16)
        self.BTOK = self.scr("BTOK", [NT, 512], BF16)
        self.YT = self.scr("YT", [4096, NT], BF16)

        self.C = fw.sb("C", [128, NCONST])
        self.NRM = fw.sb("NRM", [128, (2 * NL + 1) * 16])
        self.EPSC = fw.sb("EPSC", [128, 2])
        fw.dma("sp", self.C[:, :], self.d_consts[:, :], r=[self.d_consts], w=[self.C])
        fw.dma("sp", self.NRM[:, :], self.d_nrm[:, :], r=[self.d_nrm], w=[self.NRM])
        fw.op("dve", lambda e: e.memset(self.EPSC[:, 0:1], EPS), w=[self.EPSC])
        fw.op("dve", lambda e: e.memset(self.EPSC[:, 1:2], 1.0), w=[self.EPSC])
        self.WB = [fw.sb("WB%d" % i, [128, 11264], BF16) for i in range(3)]
        self.wrot = 0
        self.base = fw.mark()

        src = self.xT_in
        for l in range(nl):
            self.layer(l, src)
            src = self.XT
            if self.stop_after is not None:
                break
        if self.stop_after is None:
            self.final_norm(src)
        fw.emit()

    def layer(self, l, src):
        sa = self.stop_after
        self.p1_inproj(l, src)
        if sa == "p1":
            return
        self.p2_ssdprep(l)
        if sa == "p2":
            return
        self.p34_nsa(l)
        if sa == "p4":
            return
        self.p5_ssd(l)
        if sa == "p5":
            return
        self.p6_ret(l)
        if sa == "p6":
            return
        self.p7_merge(l, src)
        if sa == "p7":
            return
        self.p8_ffn(l)

    def p1_inproj(self, l, src):
        fw = self.fw
        m = fw.mark()
        uT = fw.sb("uT", [128, 16, NT], BF16)
        stF = [fw.sb("stF%d" % i, [128, NT], F32) for i in range(2)]
        stFb = [fw.sb("stFb%d" % i, [128, NT], BF16) for i in range(2)]
        stT = [fw.sb("stT%d" % i, [128, 512], F32) for i in range(2)]
        stTb = [fw.sb("stTb%d" % i, [128, 512], BF16) for i in range(2)]
        m2 = fw.mark()
        xg = [fw.sb("xg%d" % i, [128, 16, 256], F32) for i in range(1)]
        sq = [fw.sb("sq%d" % i, [128, 512], F32) for i in range(2)]
        rs = fw.sb("rs", [128, 512], F32)
        wcol = self.NRM[:, (2 * l) * 16:(2 * l + 1) * 16]
        srcv = src.ap.rearrange("(k p) t -> p k t", p=128)
        for tg in range(8):
            x = xg[0]
            for k0 in range(0, 16, 4):
                fw.dma("sp", x[:, k0:k0 + 4, :], srcv[:, k0:k0 + 4, tg * 256:(tg + 1) * 256], r=[src], w=[(x, k0)])
            self.rmsnorm(x, 256, wcol, lambda kc, tg=tg: (uT, uT[:, kc, tg * 256:(tg + 1) * 256]), sq, rs)
        fw.aoff = m2
        fw.barrier()

        W = self.w_in
        wl = W.ap[l]
        cnt = [0]

        def fm(c0, c1, dst, r0, func, scale, bf):
            c = c0
            while c < c1:
                nblk = min(512, c1 - c)
                buf, view = self.wload(W, wl[:, c:c + nblk], 16, nblk)
                for j0 in range(0, nblk, 128):
                    n = min(128, nblk - j0)
                    banks = [self.bank() for _ in range(4)]
                    for kc in range(16):
                        for tg in range(4):
                            fw.op("pe", lambda e, kc=kc, tg=tg, b=banks[tg], n=n, j0=j0, view=view: e.matmul(
                                b[0:n, :], lhsT=view[:, kc, j0:j0 + n], rhs=uT[:, kc, tg * 512:(tg + 1) * 512],
                                start=(kc == 0), stop=(kc == 15)), r=[buf, uT], w=[banks[tg]])
                    i = cnt[0] % 2
                    cnt[0] += 1
                    st = stFb[i] if bf else stF[i]
                    for tg in range(4):
                        fw.op("act", lambda e, tg=tg, b=banks[tg], n=n, st=st: e.activation(
                            out=st[0:n, tg * 512:(tg + 1) * 512], in_=b[0:n, :], func=func, scale=scale), r=[banks[tg]], w=[(st, tg)])
                    rr = r0 + (c - c0) + j0
                    fw.dma("sp", dst[rr:rr + n, :], st[0:n, :], r=[st], w=[(dst, rr)])
                c += nblk

        def tm(c0, c1, dst, d0, func, bf):
            c = c0
            while c < c1:
                nblk = min(512, c1 - c)
                buf, view = self.wload(W, wl[:, c:c + nblk], 16, nblk)
                for tt in range(16):
                    b = self.bank()
                    for kc in range(16):
                        fw.op("pe", lambda e, kc=kc, tt=tt, b=b, nblk=nblk, view=view: e.matmul(
                            b[:, 0:nblk], lhsT=uT[:, kc, tt * 128:(tt + 1) * 128], rhs=view[:, kc, 0:nblk],
                            start=(kc == 0), stop=(kc == 15)), r=[buf, uT], w=[b])
                    i = cnt[0] % 2
                    cnt[0] += 1
                    st = stTb[i] if bf else stT[i]
                    fw.op("act", lambda e, b=b, nblk=nblk, st=st: e.activation(out=st[:, 0:nblk], in_=b[:, 0:nblk], func=func), r=[b], w=[st])
                    dd = d0 + (c - c0)
                    fw.dma("sp", dst[tt * 128:(tt + 1) * 128, dd:dd + nblk], st[:, 0:nblk], r=[st], w=[(dst, (tt, dd))])
                c += nblk

        ID, SIG, SILU = AF.Identity, AF.Sigmoid, AF.Silu
        sc = 128.0 ** -0.5
        fm(C_Q, C_Q + 1024, self.QT, 0, ID, sc, True)
        fm(C_KV, C_KV + 768, self.KVT, 0, ID, 1.0, True)
        tm(C_KV + 768, C_KV + 1024, self.VTOK, 0, ID, True)
        fm(C_KV + 1024, C_KV + 1280, self.KVT, 768, ID, 1.0, True)
        tm(C_KV + 1280, C_KV + 1536, self.VTOK, 256, ID, True)
        tm(C_NG, C_NG + 24, self.NG, 0, SIG, False)
        tm(C_Z, C_Z + 2048, self.ZS, 0, SILU, False)
        fm(C_XBC, C_XBC + 3072, self.XBCT, 0, ID, 1.0, False)
        tm(C_DT, C_DT + 32, self.DTR, 0, ID, False)
        fm(C_RQ, C_RQ + 512, self.RQT, 0, ID, sc, True)
        fm(C_RK, C_RK + 512, self.RKT, 0, ID, 1.0, True)
        tm(C_RK, C_RK + 512, self.RKTOK, 0, ID, True)
        tm(C_RV, C_RV + 1024, self.RV, 0, ID, True)
        tm(C_RG, C_RG + 1024, self.RGS, 0, SILU, False)
        fm(C_MG, C_MG + 6144, self.MGT, 0, SIG, 1.0, True)
        fw.release(m)

    def p2_ssdprep(self, l):
        fw = self.fw
        m = fw.mark()
        cw = fw.sb("cw", [128, 24, 5])
        fw.dma("sp", cw[:, :, :], self.d_cw.ap[l].rearrange("p (t k) -> p t k", k=5), r=[self.d_cw], w=[cw])
        xp = [fw.sb("xp%d" % i, [128, 3 + NT]) for i in range(2)]
        acc = [fw.sb("acc%d" % i, [128, NT]) for i in range(2)]
        xo = [fw.sb("xo%d" % i, [128, NT]) for i in range(2)]
        xb = [fw.sb("xob%d" % i, [128, NT], BF16) for i in range(2)]
        stt = [fw.sb("sttok%d" % i, [128, 4, 128]) for i in range(2)]
        sttb = [fw.sb("sttokb%d" % i, [128, 4, 128], BF16) for i in range(2)]
        for i in range(2):
            fw.op("dve", lambda e, i=i: e.memset(xp[i][:, 0:3], 0.0), w=[xp[i]])
        k = 0
        for ct in range(24):
            i = ct % 2
            fw.dma("sp", xp[i][:, 3:3 + NT], self.XBCT[ct * 128:(ct + 1) * 128, :], r=[self.XBCT], w=[xp[i]])
            a = acc[i]
            fw.op("dve", lambda e, i=i, ct=ct, a=a: e.tensor_scalar(out=a[:, :], in0=xp[i][:, 0:NT], scalar1=cw[:, ct, 0:1], scalar2=None, op0=ALU.mult),
                  r=[xp[i], cw], w=[a])
            for kk in range(1, 4):
                fw.op("dve", lambda e, i=i, ct=ct, a=a, kk=kk: e.scalar_tensor_tensor(out=a[:, :], in0=xp[i][:, kk:kk + NT], scalar=cw[:, ct, kk:kk + 1],
                                                                                      in1=a[:, :], op0=ALU.mult, op1=ALU.add), r=[xp[i], cw, a], w=[a])
            o = xo[i]
            fw.op("act", lambda e, a=a, o=o, ct=ct: e.activation(out=o[:, :], in_=a[:, :], func=AF.Silu, bias=cw[:, ct, 4:5], scale=1.0), r=[a, cw], w=[o])
            if ct >= 16:
                ob = xb[i]
                fw.op("pool", lambda e, o=o, ob=ob: e.tensor_copy(out=ob[:, :], in_=o[:, :]), r=[o], w=[ob])
                r0 = (ct - 16) * 128
                fw.dma("sp", self.BCT[r0:r0 + 128, :], ob[:, :], r=[ob], w=[(self.BCT, r0)])
            if ct < 20:
                for t4 in range(4):
                    b = self.bank()
                    for j in range(4):
                        tt = t4 * 4 + j
                        fw.op("pe", lambda e, b=b, j=j, tt=tt, o=o: e.transpose(b[:, j * 128:(j + 1) * 128], o[:, tt * 128:(tt + 1) * 128], self.cv("IDENT")),
                              r=[o, self.C], w=[b])
                    if ct < 16:
                        s = stt[k % 2]
                        dst = self.XTOK.ap[t4 * 512:(t4 + 1) * 512, ct * 128:(ct + 1) * 128]
                        dt_ = self.XTOK
                    else:
                        s = sttb[k % 2]
                        cc = (ct - 16) * 128
                        dst = self.BTOK.ap[t4 * 512:(t4 + 1) * 512, cc:cc + 128]
                        dt_ = self.BTOK
                    k += 1
                    fw.op("act", lambda e, b=b, s=s: e.copy(out=s.ap.rearrange("p a b -> p (a b)"), in_=b[:, :]), r=[b], w=[s])
                    fw.dma("sp", dst.rearrange("(a p) c -> p a c", p=128), s[:, :, :], r=[s], w=[(dt_, (t4, ct))])
        fw.release(m)

    def p34_nsa(self, l):
        fw = self.fw
        C = self.C
        m = fw.mark()
        KCT = [fw.sb("KCT%d" % g, [128, 128], BF16) for g in range(2)]
        VCX = [fw.sb("VCX%d" % g, [128, 161]) for g in range(2)]
        m3 = fw.mark()
        w1s = fw.sb("w1s", [128, 32, 128], BF16)
        w2s = fw.sb("w2s", [128, 128], BF16)
        pet = fw.sb("pet", [128, 32], BF16)
        kcT = [fw.sb("kcT%d" % i, [128, NT], BF16) for i in range(2)]
        pb = fw.sb("pb", [128, 1])
        hx = fw.sb("hx", [128, 128])
        h2 = fw.sb("h2", [128, 128])
        G = fw.sb("G", [128, 128], BF16)
        for t in range(2):
            W1 = self.cw1[t]
            w1v = W1.ap[l].rearrange("(l d) o -> d l o", d=128)
            for l0 in range(0, 32, 8):
                fw.dma("pool", w1s[:, l0:l0 + 8, :], w1v[:, l0:l0 + 8, :], r=[W1], w=[(w1s, l0)])
            fw.dma("pool", w2s[:, :], self.cw2[t].ap[l], r=[self.cw2[t]], w=[w2s])
            fw.dma("pool", pet[:, :], self.d_pet.ap[l, t], r=[self.d_pet], w=[pet])
            for g in range(2):
                src = kcT[g]
                r0 = t * 256 + g * 128
                fw.dma("sp", src[:, :], self.KVT[r0:r0 + 128, :], r=[self.KVT], w=[src])
                sv = src.ap.rearrange("p (n s) -> p n s", s=16)
                ps = self.bank()
                for li in range(32):
                    rhs = sv[:, 0:127, li] if li < 16 else sv[:, 1:128, li - 16]
                    fw.op("pe", lambda e, li=li, rhs=rhs, ps=ps: e.matmul(ps[:, 0:127], lhsT=w1s[:, li, :], rhs=rhs, start=(li == 0), stop=(li == 31)),
                          r=[w1s, src], w=[ps])
                ps2 = self.bank()
                for li in range(32):
                    fw.op("pe", lambda e, li=li, ps2=ps2: e.matmul(ps2[:, 0:1], lhsT=w1s[:, li, :], rhs=pet[:, li:li + 1], start=(li == 0), stop=(li == 31)),
                          r=[w1s, pet], w=[ps2])
                fw.op("act", lambda e, ps2=ps2: e.copy(out=pb[:, :], in_=ps2[:, 0:1]), r=[ps2], w=[pb])
                fw.op("dve", lambda e: e.memset(hx[:, 127:128], 0.0), w=[hx])
                fw.op("act", lambda e, ps=ps: e.activation(out=hx[:, 0:127], in_=ps[:, 0:127], func=AF.Identity, bias=pb[:, 0:1], scale=1.0), r=[ps, pb], w=[hx])
                fw.op("dve", lambda e: e.tensor_tensor(out=h2[:, :], in0=hx[:, :], in1=hx[:, :], op=ALU.mult), r=[hx], w=[h2])
                fw.op("dve", lambda e: e.tensor_scalar(out=h2[:, :], in0=h2[:, :], scalar1=0.044715, scalar2=1.0, op0=ALU.mult, op1=ALU.add), r=[h2], w=[h2])
                fw.op("dve", lambda e: e.tensor_tensor(out=h2[:, :], in0=h2[:, :], in1=hx[:, :], op=ALU.mult), r=[h2, hx], w=[h2])
                fw.op("act", lambda e: e.activation(out=h2[:, :], in_=h2[:, :], func=AF.Sigmoid, scale=1.5957691216057308), r=[h2], w=[h2])
                fw.op("dve", lambda e: e.tensor_tensor(out=G[:, :], in0=h2[:, :], in1=hx[:, :], op=ALU.mult), r=[h2, hx], w=[G])
                ps3 = self.bank()
                if t == 0:
                    fw.op("pe", lambda e, ps3=ps3: e.matmul(ps3[:, 0:128], lhsT=w2s[:, :], rhs=G[:, :], start=True, stop=True), r=[w2s, G], w=[ps3])
                    fw.op("act", lambda e, ps3=ps3, g=g: e.copy(out=KCT[g][:, :], in_=ps3[:, 0:128]), r=[ps3], w=[KCT[g]])
                else:
                    fw.op("pe", lambda e, ps3=ps3: e.matmul(ps3[:, 0:128], lhsT=G[:, :], rhs=w2s[:, :], start=True, stop=True), r=[w2s, G], w=[ps3])
                    fw.op("act", lambda e, ps3=ps3, g=g: e.copy(out=VCX[g][:, 0:128], in_=ps3[:, 0:128]), r=[ps3], w=[VCX[g]])
                    fw.op("dve", lambda e, g=g: e.tensor_copy(out=VCX[g][:, 128:161], in_=self.cv("OVL")), r=[C], w=[VCX[g]])
        fw.release(m3)

        ET = fw.sb("ET", [32, NT], BF16)
        fw.dma("pool", ET[:, :], self.d_et[:, :], r=[self.d_et], w=[ET])
        NGs = fw.sb("NGs", [128, 16, 24])
        fw.dma("sp", NGs[:, :, :], self.NG.ap.rearrange("(t p) c -> p t c", p=128), r=[self.NG], w=[NGs])
        ksT = fw.sb("ksT", [128, NT], BF16)
        kwT = fw.sb("kwT", [128, NT], BF16)
        VSX = fw.sb("VSX", [128, 16, 129], BF16)
        VWX = fw.sb("VWX", [128, 16, 129], BF16)
        QTg = [fw.sb("QTg%d" % j, [128, NT], BF16) for j in range(4)]
        TABC = [fw.sb("TABC%d" % i, [128, 2048]) for i in range(2)]
        TABS = fw.sb("TABS", [128, TO_C])
        IMP = fw.sb("IMP", [128, 16, 32])
        NEGT = fw.sb("NEGT", [32, NT], BF16)
        Yh = [fw.sb("Yh%d" % j, [128, 16, 128]) for j in range(4)]
        scb = [fw.sb("scb%d" % i, [128, 512]) for i in range(2)]
        p32 = [fw.sb("p32_%d" % i, [128, 512]) for i in range(2)]
        pbf = [fw.sb("pbf%d" % i, [128, 512], BF16) for i in range(3)]
        sm = [fw.sb("sm%d" % i, [128, 48]) for i in range(4)]
        ytb = [fw.sb("ytb%d" % i, [128, 512], BF16) for i in range(2)]
        MSK = self.cv("MSK").rearrange("p (a q k) -> p a q k", a=3, q=16, k=32)
        cnt = {"s": 0, "p": 0, "sm": 0, "y": 0}

        def evac_acc(acc, Yt, Q, h, br, first, zcol, imp):
            s = sm[cnt["sm"] % 4]
            cnt["sm"] += 1
            fw.op("dve", lambda e: e.tensor_scalar(out=s[:, 0:1], in0=acc[:, zcol:zcol + 1], scalar1=1e-30, scalar2=None, op0=ALU.max), r=[acc], w=[s])
            fw.op("dve", lambda e: e.reciprocal(out=s[:, 0:1], in_=s[:, 0:1]), r=[s], w=[s])
            fw.op("dve", lambda e: e.tensor_tensor(out=s[:, 1:2], in0=s[:, 0:1], in1=NGs[:, Q, h * 3 + br:h * 3 + br + 1], op=ALU.mult), r=[s, NGs], w=[s])
            if first:
                fw.op("dve", lambda e: e.tensor_scalar(out=Yt[:, Q, :], in0=acc[:, 0:128], scalar1=s[:, 1:2], scalar2=None, op0=ALU.mult), r=[acc, s], w=[(Yt, Q)])
            else:
                fw.op("dve", lambda e: e.scalar_tensor_tensor(out=Yt[:, Q, :], in0=acc[:, 0:128], scalar=s[:, 1:2], in1=Yt[:, Q, :], op0=ALU.mult, op1=ALU.add),
                      r=[acc, s, (Yt, Q)], w=[(Yt, Q)])
            if imp:
                fw.op("dve", lambda e: e.scalar_tensor_tensor(out=IMP[:, Q, :], in0=acc[:, 129:161], scalar=s[:, 0:1], in1=IMP[:, Q, :], op0=ALU.mult, op1=ALU.add),
                      r=[acc, s, (IMP, Q)], w=[(IMP, Q)])

        for g in range(2):
            fw.dma("sp", ksT[:, :], self.KVT[512 + g * 128:512 + (g + 1) * 128, :], r=[self.KVT], w=[ksT])
            fw.dma("sp", kwT[:, :], self.KVT[768 + g * 128:768 + (g + 1) * 128, :], r=[self.KVT], w=[kwT])
            fw.dma("sp", VSX[:, :, 0:128], self.VTOK.ap[:, g * 128:(g + 1) * 128].rearrange("(t p) d -> p t d", p=128), r=[self.VTOK], w=[VSX])
            fw.dma("sp", VWX[:, :, 0:128], self.VTOK.ap[:, 256 + g * 128:256 + (g + 1) * 128].rearrange("(t p) d -> p t d", p=128), r=[self.VTOK], w=[VWX])
            fw.op("dve", lambda e: e.memset(VSX[:, :, 128:129], 1.0), w=[VSX])
            fw.op("dve", lambda e: e.memset(VWX[:, :, 128:129], 1.0), w=[VWX])
            fw.op("dve", lambda e: e.memset(IMP[:, :, :], 0.0), w=[IMP])
            tabs = []
            for j in range(4):
                h = g * 4 + j
                fw.dma("sp", QTg[j][:, :], self.QT[h * 128:(h + 1) * 128, :], r=[self.QT], w=[QTg[j]])
            for j in range(4):
                h = g * 4 + j
                tb = TABC[h % 2]
                fw.dma("sp", tb[:, :], self.d_tab.ap[h][:, TO_C:TO_C + 2048], r=[self.d_tab], w=[tb])
                for qg in range(4):
                    S = self.bank()
                    fw.op("pe", lambda e, S=S, j=j, qg=qg: e.matmul(S[:, :], lhsT=KCT[g][:, :], rhs=QTg[j][:, qg * 512:(qg + 1) * 512], start=True, stop=True),
                          r=[KCT[g], QTg[j]], w=[S])
                    sc_ = scb[cnt["s"] % 2]
                    pp = p32[cnt["s"] % 2]
                    cnt["s"] += 1
                    fw.op("dve", lambda e, S=S, sc_=sc_, tb=tb, qg=qg: e.tensor_tensor(out=sc_[:, :], in0=S[:, :], in1=tb[:, qg * 512:(qg + 1) * 512], op=ALU.add),
                          r=[S, tb], w=[sc_])
                    fw.op("act", lambda e, sc_=sc_, pp=pp: e.activation(out=pp[:, :], in_=sc_[:, :], func=AF.Exp), r=[sc_], w=[pp])
                    for qt in range(4):
                        Q = qg * 4 + qt
                        acc = self.bank()
                        fw.op("pe", lambda e, acc=acc, pp=pp, qt=qt: e.matmul(acc[:, 0:161], lhsT=pp[:, qt * 128:(qt + 1) * 128], rhs=VCX[g][:, :], start=True, stop=True),
                              r=[pp, VCX[g]], w=[acc])
                        evac_acc(acc, Yh[j], Q, h, 0, True, 128, True)
            for Q in range(16):
                s = sm[cnt["sm"] % 4]
                cnt["sm"] += 1
                im = s[:, 8:40]
                fw.op("dve", lambda e, Q=Q, im=im: e.tensor_tensor(out=im, in0=IMP[:, Q, :], in1=MSK[:, 0, Q, :], op=ALU.mult), r=[(IMP, Q), C], w=[s])
                fw.op("dve", lambda e, Q=Q, im=im: e.tensor_tensor(out=im, in0=im, in1=MSK[:, 1, Q, :], op=ALU.add), r=[s, C], w=[s])
                fw.op("dve", lambda e, s=s, im=im: e.max(out=s[:, 0:8], in_=im), r=[s], w=[s])
                fw.op("dve", lambda e, s=s, im=im: e.tensor_scalar(out=im, in0=im, scalar1=s[:, 7:8], scalar2=None, op0=ALU.is_ge), r=[s], w=[s])
                fw.op("dve", lambda e, Q=Q, im=im: e.tensor_tensor(out=im, in0=im, in1=MSK[:, 2, Q, :], op=ALU.mult), r=[s, C], w=[s])
                fw.op("dve", lambda e, im=im: e.tensor_scalar(out=im, in0=im, scalar1=-1.0, scalar2=-NEG, op0=ALU.add, op1=ALU.mult), r=[s], w=[s])
                b = self.bank()
                fw.op("pe", lambda e, b=b, im=im: e.transpose(b[0:32, 0:128], im, self.cv("IDENT")), r=[s, C], w=[b])
                fw.op("act", lambda e, b=b, Q=Q: e.copy(out=NEGT[:, Q * 128:(Q + 1) * 128], in_=b[0:32, 0:128]), r=[b], w=[(NEGT, Q)])
            for j in range(4):
                h = g * 4 + j
                slope = 2.0 ** (-(h + 1))
                tb = TABS
                fw.dma("sp", tb[:, :], self.d_tab.ap[h][:, 0:TO_C], r=[self.d_tab], w=[tb])
                for br in (1, 2):
                    KT = ksT if br == 1 else kwT
                    VX = VSX if br == 1 else VWX
                    for qg in range(4):
                        accs = fw.PS[0:4]
                        kt_lo = 0 if br == 1 else max(0, 4 * qg - 4)
                        for kt in range(kt_lo, 4 * qg + 4):
                            D0 = qg * 512 - kt * 128
                            S = fw.PS[4 + self.rot % 4]
                            self.rot += 1
                            fw.op("pe", lambda e, S=S, kt=kt, qg=qg, KT=KT: e.matmul(S[:, :], lhsT=KT[:, kt * 128:(kt + 1) * 128], rhs=QTg[j][:, qg * 512:(qg + 1) * 512],
                                                                                start=True, stop=(br == 2)), r=[KT, QTg[j]], w=[S])
                            if br == 1:
                                fw.op("pe", lambda e, S=S, kt=kt, qg=qg: e.matmul(S[:, :], lhsT=ET[:, kt * 128:(kt + 1) * 128], rhs=NEGT[:, qg * 512:(qg + 1) * 512],
                                                                              start=False, stop=True), r=[ET, NEGT], w=[S])
                            sc_ = scb[cnt["s"] % 2]
                            cnt["s"] += 1
                            pp = pbf[cnt["p"] % 3]
                            cnt["p"] += 1
                            bias = 0.0
                            if br == 1:
                                if D0 <= 0:
                                    tsl = tb[:, TO_U + D0 + 384:TO_U + D0 + 384 + 512]
                                else:
                                    tsl = tb[:, TO_W:TO_W + 512]
                                    bias = -slope * D0
                            else:
                                tsl = tb[:, TO_UW + D0 + 384:TO_UW + D0 + 384 + 512]
                            fw.op("dve", lambda e, S=S, sc_=sc_, tsl=tsl: e.tensor_tensor(out=sc_[:, :], in0=S[:, :], in1=tsl, op=ALU.add), r=[S, tb], w=[sc_])
                            if bias == 0.0:
                                fw.op("act", lambda e, sc_=sc_, pp=pp: e.activation(out=pp[:, :], in_=sc_[:, :], func=AF.Exp), r=[sc_], w=[pp])
                            else:
                                fw.op("dve", lambda e, sc_=sc_, bias=bias: e.tensor_scalar(out=sc_[:, :], in0=sc_[:, :], scalar1=bias, scalar2=None, op0=ALU.add), r=[sc_], w=[sc_])
                                fw.op("act", lambda e, sc_=sc_, pp=pp: e.activation(out=pp[:, :], in_=sc_[:, :], func=AF.Exp), r=[sc_], w=[pp])
                            for qt in range(4):
                                Q = qg * 4 + qt
                                lo = 0 if br == 1 else max(0, Q - 4)
                                if kt < lo or kt > Q:
                                    continue
                                fw.op("pe", lambda e, a=accs[qt], pp=pp, qt=qt, kt=kt, lo=lo, Q=Q, VX=VX: e.matmul(
                                    a[:, 0:129], lhsT=pp[:, qt * 128:(qt + 1) * 128], rhs=VX[:, kt, :], start=(kt == lo), stop=(kt == Q)), r=[pp, VX], w=[accs[qt]])
                        for qt in range(4):
                            evac_acc(accs[qt], Yh[j], qg * 4 + qt, h, br, False, 128, False)
                for t4 in range(4):
                    b = fw.PS[4 + self.rot % 4]
                    self.rot += 1
                    for jj in range(4):
                        Q = t4 * 4 + jj
                        fw.op("pe", lambda e, b=b, jj=jj, Q=Q: e.transpose(b[:, jj * 128:(jj + 1) * 128], Yh[j][:, Q, :], self.cv("IDENT")), r=[(Yh[j], Q), C], w=[b])
                    y = ytb[cnt["y"] % 2]
                    cnt["y"] += 1
                    fw.op("act", lambda e, b=b, y=y: e.copy(out=y[:, :], in_=b[:, :]), r=[b], w=[y])
                    fw.dma("sp", self.YT[h * 128:(h + 1) * 128, t4 * 512:(t4 + 1) * 512], y[:, :], r=[y], w=[(self.YT, (h, t4))])
        fw.release(m)

    def p5_ssd(self, l):
        fw = self.fw
        C = self.C
        m = fw.mark()
        sv = fw.sb("ssdv", [128, 96])
        fw.dma("sp", sv[:, :], self.d_ssdv.ap[l], r=[self.d_ssdv], w=[sv])
        snw = fw.sb("snw", [128, 2048])
        fw.dma("sp", snw[:, :], self.d_snw.ap[l], r=[self.d_snw], w=[snw])
        negA = fw.sb("negA", [128, 32])
        fw.op("act", lambda e: e.activation(out=negA[:, :], in_=sv[:, 32:64], func=AF.Exp), r=[sv], w=[negA])
        fw.op("dve", lambda e: e.tensor_scalar(out=negA[:, :], in0=negA[:, :], scalar1=-1.0, scalar2=None, op0=ALU.mult), r=[negA], w=[negA])
        BT = fw.sb("BT", [128, 4, NT], BF16)
        CT = fw.sb("CT", [128, 4, NT], BF16)
        fw.dma("sp", BT[:, :, :], self.BCT.ap[0:512, :].rearrange("(g p) t -> p g t", p=128), r=[self.BCT], w=[BT])
        fw.dma("sp", CT[:, :, :], self.BCT.ap[512:1024, :].rearrange("(g p) t -> p g t", p=128), r=[self.BCT], w=[CT])
        H = fw.sb("H", [128, 4, 512])
        Hb = fw.sb("Hb", [128, 4, 512], BF16)
        fw.op("dve", lambda e: e.memset(H[:, :, :], 0.0), w=[H])
        fw.op("dve", lambda e: e.memset(Hb[:, :, :], 0.0), w=[Hb])
        xc = [fw.sb("xc%d" % i, [128, 2048]) for i in range(2)]
        zc = [fw.sb("zc%d" % i, [128, 2048]) for i in range(1)]
        acs = fw.sb("acs", [128, 64])
        bk = [fw.sb("bk%d" % i, [128, 512], BF16) for i in range(2)]
        dtr = [fw.sb("dtr%d" % i, [128, 32]) for i in range(2)]
        dt = fw.sb("dt", [128, 32])
        tmp = fw.sb("tmp32", [128, 32])
        a = fw.sb("a", [128, 32])
        ea = fw.sb("ea", [128, 32])
        dte = fw.sb("dte", [128, 32])
        cd = fw.sb("cd", [128, 32])
        xdt = fw.sb("xdt", [128, 2048], BF16)
        xdte = fw.sb("xdte", [128, 2048], BF16)
        cbm = fw.sb("cbm", [128, 128])
        seg = [fw.sb("seg%d" % i, [128, 4, 128]) for i in range(2)]
        dec = [fw.sb("dec%d" % i, [128, 4, 128]) for i in range(2)]
        MT = [fw.sb("MT%d" % i, [128, 4, 128], BF16) for i in range(2)]
        y = fw.sb("y", [128, 2048])
        ysq = fw.sb("ysq", [128, 512])
        st4 = fw.sb("st4", [128, 8])
        ytb = [fw.sb("ytb%d" % i, [128, 512], BF16) for i in range(2)]
        yT = fw.sb("yT", [128, 16, 128], BF16)
        CAUS = self.cv("CAUS")
        k = 0
        for c in range(16):
            i = c % 2
            x, z, bt, dr = xc[i], zc[0], bk[i], dtr[i]
            rows = slice(c * 128, (c + 1) * 128)
            fw.dma("sp", x[:, :], self.XTOK[rows, :], r=[self.XTOK], w=[x])
            fw.dma("sp", z[:, :], self.ZS[rows, :], r=[self.ZS], w=[z])
            fw.dma("sp", bt[:, :], self.BTOK[rows, :], r=[self.BTOK], w=[bt])
            fw.dma("sp", dr[:, :], self.DTR[rows, :], r=[self.DTR], w=[dr])
            fw.op("dve", lambda e, dr=dr: e.tensor_tensor(out=dt[:, :], in0=dr[:, :], in1=sv[:, 0:32], op=ALU.add), r=[dr, sv], w=[dt])
            fw.op("dve", lambda e: e.tensor_scalar(out=tmp[:, :], in0=dt[:, :], scalar1=-1.0, scalar2=None, op0=ALU.mult), r=[dt], w=[tmp])
            fw.op("dve", lambda e: e.tensor_tensor(out=tmp[:, :], in0=tmp[:, :], in1=dt[:, :], op=ALU.max), r=[dt, tmp], w=[tmp])
            fw.op("act", lambda e: e.activation(out=tmp[:, :], in_=tmp[:, :], func=AF.Exp, scale=-1.0), r=[tmp], w=[tmp])
            fw.op("act", lambda e: e.activation(out=tmp[:, :], in_=tmp[:, :], func=AF.Ln, bias=self.EPSC[:, 1:2], scale=1.0), r=[tmp, self.EPSC], w=[tmp])
            fw.op("dve", lambda e: e.scalar_tensor_tensor(out=dt[:, :], in0=dt[:, :], scalar=0.0, in1=tmp[:, :], op0=ALU.max, op1=ALU.add), r=[dt, tmp], w=[dt])
            fw.op("dve", lambda e: e.tensor_tensor(out=a[:, :], in0=dt[:, :], in1=negA[:, :], op=ALU.mult), r=[dt, negA], w=[a])
            pc = self.bank()
            fw.op("pe", lambda e, pc=pc: e.matmul(pc[:, 0:32], lhsT=CAUS, rhs=a[:, :], start=True, stop=True), r=[a, C], w=[pc])
            fw.op("pe", lambda e, pc=pc: e.matmul(pc[:, 32:64], lhsT=self.cv("ONES1"), rhs=a[:, :], start=True, stop=True), r=[a, C], w=[pc])
            fw.op("act", lambda e, pc=pc: e.copy(out=acs[:, :], in_=pc[:, 0:64]), r=[pc], w=[acs])
            fw.op("act", lambda e: e.activation(out=ea[:, :], in_=acs[:, 0:32], func=AF.Exp), r=[acs], w=[ea])
            fw.op("act", lambda e: e.activation(out=cd[:, :], in_=acs[:, 32:64], func=AF.Exp), r=[acs], w=[cd])
            fw.op("dve", lambda e: e.tensor_tensor(out=dte[:, :], in0=acs[:, 32:64], in1=acs[:, 0:32], op=ALU.subtract), r=[acs], w=[dte])
            fw.op("act", lambda e: e.activation(out=dte[:, :], in_=dte[:, :], func=AF.Exp), r=[dte], w=[dte])
            x3 = x.ap.rearrange("p (h q) -> p h q", q=64)
            fw.op("dve", lambda e, x3=x3: e.tensor_tensor(out=xdt.ap.rearrange("p (h q) -> p h q", q=64), in0=x3, in1=dt[:, :].unsqueeze(2).broadcast_to([128, 32, 64]), op=ALU.mult),
                  r=[x, dt], w=[xdt])
            fw.op("pool", lambda e: e.tensor_tensor(out=xdte.ap.rearrange("p (h q) -> p h q", q=64), in0=xdt.ap.rearrange("p (h q) -> p h q", q=64),
                                                    in1=dte[:, :].unsqueeze(2).broadcast_to([128, 32, 64]), op=ALU.mult), r=[xdt, dte], w=[xdte])
            for g in range(4):
                pcb = self.bank()
                fw.op("pe", lambda e, pcb=pcb, g=g, c=c: e.matmul(pcb[:, 0:128], lhsT=BT[:, g, c * 128:(c + 1) * 128], rhs=CT[:, g, c * 128:(c + 1) * 128], start=True, stop=True),
                      r=[BT, CT], w=[pcb])
                fw.op("dve", lambda e, pcb=pcb: e.tensor_tensor(out=cbm[:, :], in0=pcb[:, 0:128], in1=CAUS, op=ALU.mult), r=[pcb, C], w=[cbm])
                yd = self.bank()
                for hh in range(2):
                    h0 = g * 8 + hh * 4
                    sg, dc, mt = seg[k % 2], dec[k % 2], MT[k % 2]
                    k += 1
                    fw.op("dve", lambda e, sg=sg, h0=h0: e.tensor_tensor(out=sg[:, :, :], in0=CAUS.unsqueeze(1).broadcast_to([128, 4, 128]),
                                                                       in1=a[:, h0:h0 + 4].unsqueeze(2).broadcast_to([128, 4, 128]), op=ALU.mult), r=[a, C], w=[sg])
                    pseg = self.bank()
                    fw.op("pe", lambda e, pseg=pseg, sg=sg: e.matmul(pseg[:, :], lhsT=self.cv("TGT"), rhs=sg.ap.rearrange("p a b -> p (a b)"), start=True, stop=True),
                          r=[sg, C], w=[pseg])
                    fw.op("act", lambda e, pseg=pseg, dc=dc: e.activation(out=dc.ap.rearrange("p a b -> p (a b)"), in_=pseg[:, :], func=AF.Exp), r=[pseg], w=[dc])
                    fw.op("dve", lambda e, dc=dc, mt=mt: e.tensor_tensor(out=mt[:, :, :], in0=dc[:, :, :], in1=cbm[:, :].unsqueeze(1).broadcast_to([128, 4, 128]), op=ALU.mult),
                          r=[dc, cbm], w=[mt])
                    for q in range(4):
                        hd = h0 + q
                        cc = (hh * 4 + q) * 64
                        fw.op("pe", lambda e, yd=yd, mt=mt, q=q, hd=hd, cc=cc: e.matmul(yd[:, cc:cc + 64], lhsT=mt[:, q, :], rhs=xdt[:, hd * 64:(hd + 1) * 64], start=True, stop=True),
                              r=[mt, xdt], w=[yd])
                yo = self.bank()
                fw.op("pe", lambda e, yo=yo, g=g, c=c: e.matmul(yo[:, :], lhsT=CT[:, g, c * 128:(c + 1) * 128], rhs=Hb[:, g, :], start=True, stop=True), r=[CT, (Hb, g)], w=[yo])
                ysl = y.ap[:, g * 512:(g + 1) * 512].rearrange("p (h q) -> p h q", q=64)
                fw.op("dve", lambda e, yo=yo, ysl=ysl, g=g: e.tensor_tensor(out=ysl, in0=yo.ap.rearrange("p (h q) -> p h q", q=64),
                                                                          in1=ea[:, g * 8:(g + 1) * 8].unsqueeze(2).broadcast_to([128, 8, 64]), op=ALU.mult), r=[yo, ea], w=[(y, g)])
                fw.op("dve", lambda e, yd=yd, g=g: e.tensor_tensor(out=y[:, g * 512:(g + 1) * 512], in0=yd[:, :], in1=y[:, g * 512:(g + 1) * 512], op=ALU.add), r=[yd, (y, g)], w=[(y, g)])
                pst = self.bank()
                fw.op("pe", lambda e, pst=pst, g=g, bt=bt: e.matmul(pst[:, :], lhsT=bt[:, g * 128:(g + 1) * 128], rhs=xdte[:, g * 512:(g + 1) * 512], start=True, stop=True),
                      r=[bt, xdte], w=[pst])
                Hg = H.ap[:, g, :].rearrange("p (h q) -> p h q", q=64)
                fw.op("dve", lambda e, Hg=Hg, g=g: e.tensor_tensor(out=Hg, in0=Hg, in1=cd[:, g * 8:(g + 1) * 8].unsqueeze(2).broadcast_to([128, 8, 64]), op=ALU.mult),
                      r=[(H, g), cd], w=[(H, g)])
                fw.op("dve", lambda e, pst=pst, g=g: e.tensor_tensor(out=H[:, g, :], in0=pst[:, :], in1=H[:, g, :], op=ALU.add), r=[pst, (H, g)], w=[(H, g)])
                fw.op("pool", lambda e, g=g: e.tensor_copy(out=Hb[:, g, :], in_=H[:, g, :]), r=[(H, g)], w=[(Hb, g)])
            fw.op("pool", lambda e, x3=x3: e.tensor_tensor(out=x3, in0=x3, in1=sv[:, 64:96].unsqueeze(2).broadcast_to([128, 32, 64]), op=ALU.mult), r=[x, sv], w=[x])
            fw.op("dve", lambda e, x=x: e.tensor_tensor(out=y[:, :], in0=y[:, :], in1=x[:, :], op=ALU.add), r=[y, x], w=[y])
            fw.op("dve", lambda e, z=z: e.tensor_tensor(out=y[:, :], in0=y[:, :], in1=z[:, :], op=ALU.mult), r=[y, z], w=[y])
            for g in range(4):
                fw.op("act", lambda e, g=g: e.activation(out=ysq[:, :], in_=y[:, g * 512:(g + 1) * 512], func=AF.Square), r=[y], w=[ysq])
                fw.op("dve", lambda e, g=g: e.tensor_reduce(out=st4[:, g:g + 1], in_=ysq[:, :], axis=AX.X, op=ALU.add), r=[ysq], w=[st4])
            fw.op("act", lambda e: e.activation(out=st4[:, 4:8], in_=st4[:, 0:4], func=AF.Sqrt, bias=self.EPSC[:, 0:1], scale=1.0 / 512), r=[st4, self.EPSC], w=[st4])
            fw.op("dve", lambda e: e.reciprocal(out=st4[:, 4:8], in_=st4[:, 4:8]), r=[st4], w=[st4])
            for g in range(4):
                fw.op("dve", lambda e, g=g: e.scalar_tensor_tensor(out=y[:, g * 512:(g + 1) * 512], in0=y[:, g * 512:(g + 1) * 512], scalar=st4[:, 4 + g:5 + g],
                                                                   in1=snw[:, g * 512:(g + 1) * 512], op0=ALU.mult, op1=ALU.mult), r=[y, st4, snw], w=[y])
            for t4 in range(4):
                b = self.bank()
                for jj in range(4):
                    f = t4 * 4 + jj
                    fw.op("pe", lambda e, b=b, jj=jj, f=f: e.transpose(b[:, jj * 128:(jj + 1) * 128], y[:, f * 128:(f + 1) * 128], self.cv("IDENT")), r=[y, C], w=[b])
                fw.op("act", lambda e, b=b, t4=t4: e.copy(out=yT.ap[:, t4 * 4:(t4 + 1) * 4, :].rearrange("p a b -> p (a b)"), in_=b[:, :]), r=[b], w=[yT])
            fw.dma("sp", self.YT.ap[1024:3072, c * 128:(c + 1) * 128].rearrange("(f p) t -> p f t", p=128), yT[:, :, :], r=[yT], w=[(self.YT, ("s", c))])
        fw.release(m)

    def p6_ret(self, l):
        fw = self.fw
        C = self.C
        m = fw.mark()
        rnw = fw.sb("rnw", [128, 1024])
        fw.dma("sp", rnw[:, :], self.d_rnw.ap[l], r=[self.d_rnw], w=[rnw])
        QT = fw.sb("rQT", [128, 4, NT], BF16)
        KT = fw.sb("rKT", [128, 4, NT], BF16)
        fw.dma("sp", QT[:, :, :], self.RQT.ap.rearrange("(h p) t -> p h t", p=128), r=[self.RQT], w=[QT])
        fw.dma("sp", KT[:, :, :], self.RKT.ap.rearrange("(h p) t -> p h t", p=128), r=[self.RKT], w=[KT])
        QS = fw.sb("rQS", [128, 4, NT], BF16)
        QD = self.cv("QD").rearrange("p (h l) -> p h l", l=128)
        for h in range(4):
            fw.op("dve", lambda e, h=h: e.tensor_tensor(out=QS.ap[:, h, :].rearrange("p (c l) -> p c l", l=128), in0=QT.ap[:, h, :].rearrange("p (c l) -> p c l", l=128),
                                                        in1=QD[:, h, :].unsqueeze(1).broadcast_to([128, 16, 128]), op=ALU.mult), r=[QT, C], w=[QS])
        R = fw.sb("R", [128, 4, 256])
        Rb = fw.sb("Rb", [128, 4, 256], BF16)
        fw.op("dve", lambda e: e.memset(R[:, :, :], 0.0), w=[R])
        fw.op("dve", lambda e: e.memset(Rb[:, :, :], 0.0), w=[Rb])
        vc = [fw.sb("vc%d" % i, [128, 1024], BF16) for i in range(2)]
        kc_ = [fw.sb("kc%d" % i, [128, 512], BF16) for i in range(2)]
        gc = [fw.sb("gc%d" % i, [128, 1024]) for i in range(2)]
        kd = fw.sb("kd", [128, 512], BF16)
        sT = [fw.sb("sT%d" % i, [128, 128], BF16) for i in range(2)]
        ys = [fw.sb("ys%d" % i, [128, 256]) for i in range(2)]
        y2 = fw.sb("y2", [128, 256])
        st = [fw.sb("rst%d" % i, [128, 8]) for i in range(2)]
        yr = fw.sb("yr", [128, 1024])
        yT = fw.sb("ryT", [128, 8, 128], BF16)
        DMT = self.cv("DMT").rearrange("p (h l) -> p h l", l=128)
        KD = self.cv("KD")
        k = 0
        for c in range(16):
            i = c % 2
            v, kk, gg = vc[i], kc_[i], gc[i]
            rows = slice(c * 128, (c + 1) * 128)
            fw.dma("sp", v[:, :], self.RV[rows, :], r=[self.RV], w=[v])
            fw.dma("sp", kk[:, :], self.RKTOK[rows, :], r=[self.RKTOK], w=[kk])
            fw.dma("sp", gg[:, :], self.RGS[rows, :], r=[self.RGS], w=[gg])
            fw.op("pool", lambda e, kk=kk: e.tensor_tensor(out=kd.ap.rearrange("p (h d) -> p h d", d=128), in0=kk.ap.rearrange("p (h d) -> p h d", d=128),
                                                          in1=KD.unsqueeze(2).broadcast_to([128, 4, 128]), op=ALU.mult), r=[kk, C], w=[kd])
            for h in range(4):
                cs = slice(c * 128, (c + 1) * 128)
                ps = self.bank()
                fw.op("pe", lambda e, ps=ps, h=h, cs=cs: e.matmul(ps[:, 0:128], lhsT=KT[:, h, cs], rhs=QT[:, h, cs], start=True, stop=True), r=[KT, QT], w=[ps])
                s_ = sT[k % 2]
                yy = ys[k % 2]
                s8 = st[k % 2]
                k += 1
                fw.op("dve", lambda e, ps=ps, s_=s_, h=h: e.tensor_tensor(out=s_[:, :], in0=ps[:, 0:128], in1=DMT[:, h, :], op=ALU.mult), r=[ps, C], w=[s_])
                py = self.bank()
                fw.op("pe", lambda e, py=py, s_=s_, v=v, h=h: e.matmul(py[:, 0:256], lhsT=s_[:, :], rhs=v[:, h * 256:(h + 1) * 256], start=True, stop=False), r=[s_, v], w=[py])
                fw.op("pe", lambda e, py=py, h=h, cs=cs: e.matmul(py[:, 0:256], lhsT=QS[:, h, cs], rhs=Rb[:, h, :], start=False, stop=True), r=[QS, (Rb, h)], w=[py])
                pk = self.bank()
                fw.op("pe", lambda e, pk=pk, h=h, v=v: e.matmul(pk[:, 0:256], lhsT=kd[:, h * 128:(h + 1) * 128], rhs=v[:, h * 256:(h + 1) * 256], start=True, stop=True), r=[kd, v], w=[pk])
                fw.op("dve", lambda e, pk=pk, h=h: e.scalar_tensor_tensor(out=R[:, h, :], in0=R[:, h, :], scalar=self.cdec[h], in1=pk[:, 0:256], op0=ALU.mult, op1=ALU.add),
                      r=[pk, (R, h)], w=[(R, h)])
                fw.op("pool", lambda e, h=h: e.tensor_copy(out=Rb[:, h, :], in_=R[:, h, :]), r=[(R, h)], w=[(Rb, h)])
                fw.op("act", lambda e, py=py, yy=yy: e.copy(out=yy[:, :], in_=py[:, 0:256]), r=[py], w=[yy])
                fw.op("dve", lambda e, yy=yy, s8=s8: e.tensor_reduce(out=s8[:, 0:1], in_=yy[:, :], axis=AX.X, op=ALU.add), r=[yy], w=[s8])
                fw.op("dve", lambda e, s8=s8: e.tensor_scalar(out=s8[:, 1:2], in0=s8[:, 0:1], scalar1=1.0 / 256, scalar2=None, op0=ALU.mult), r=[s8], w=[s8])
                fw.op("dve", lambda e, yy=yy, s8=s8: e.tensor_scalar(out=yy[:, :], in0=yy[:, :], scalar1=s8[:, 1:2], scalar2=None, op0=ALU.subtract), r=[yy, s8], w=[yy])
                fw.op("act", lambda e, yy=yy: e.activation(out=y2[:, :], in_=yy[:, :], func=AF.Square), r=[yy], w=[y2])
                fw.op("dve", lambda e, s8=s8: e.tensor_reduce(out=s8[:, 2:3], in_=y2[:, :], axis=AX.X, op=ALU.add), r=[y2], w=[s8])
                fw.op("act", lambda e, s8=s8: e.activation(out=s8[:, 3:4], in_=s8[:, 2:3], func=AF.Sqrt, bias=self.EPSC[:, 0:1], scale=1.0 / 256), r=[s8, self.EPSC], w=[s8])
                fw.op("dve", lambda e, s8=s8: e.reciprocal(out=s8[:, 3:4], in_=s8[:, 3:4]), r=[s8], w=[s8])
                fw.op("dve", lambda e, yy=yy, s8=s8, h=h: e.scalar_tensor_tensor(out=yy[:, :], in0=yy[:, :], scalar=s8[:, 3:4], in1=rnw[:, h * 256:(h + 1) * 256], op0=ALU.mult, op1=ALU.mult),
                      r=[yy, s8, rnw], w=[yy])
                fw.op("dve", lambda e, yy=yy, gg=gg, h=h: e.tensor_tensor(out=yr[:, h * 256:(h + 1) * 256], in0=yy[:, :], in1=gg[:, h * 256:(h + 1) * 256], op=ALU.mult), r=[yy, gg], w=[(yr, h)])
            for t4 in range(2):
                b = self.bank()
                for jj in range(4):
                    f = t4 * 4 + jj
                    fw.op("pe", lambda e, b=b, jj=jj, f=f: e.transpose(b[:, jj * 128:(jj + 1) * 128], yr[:, f * 128:(f + 1) * 128], self.cv("IDENT")), r=[yr, C], w=[b])
                fw.op("act", lambda e, b=b, t4=t4: e.copy(out=yT.ap[:, t4 * 4:(t4 + 1) * 4, :].rearrange("p a b -> p (a b)"), in_=b[:, :]), r=[b], w=[yT])
            fw.dma("sp", self.YT.ap[3072:4096, c * 128:(c + 1) * 128].rearrange("(f p) t -> p f t", p=128), yT[:, :, :], r=[yT], w=[(self.YT, ("r", c))])
        fw.release(m)

    def p7_merge(self, l, src):
        fw = self.fw
        m = fw.mark()
        yt = fw.sb("ytall", [128, 32, 512], BF16)
        mg = [fw.sb("mg%d" % i, [128, 512], BF16) for i in range(3)]
        mer = fw.sb("mer", [128, 16, 512])
        merb = fw.sb("merb", [128, 16, 512], BF16)
        tmp = [fw.sb("mtmp%d" % i, [128, 512]) for i in range(2)]
        xt = [fw.sb("mxt%d" % i, [128, 512]) for i in range(2)]
        branches = ((self.p_nsa, 8, 0, 0), (self.p_ssd, 16, 8, 1), (self.p_ret, 8, 24, 2))
        k = 0
        for tg in range(4):
            ts = slice(tg * 512, (tg + 1) * 512)
            for k0 in range(0, 32, 8):
                fw.dma("sp", yt[:, k0:k0 + 8, :], self.YT.ap[k0 * 128:(k0 + 8) * 128, ts].rearrange("(k p) t -> p k t", p=128), r=[self.YT], w=[(yt, k0)])
            for (W, KC, koff, bi) in branches:
                for cb in range(4):
                    buf, view = self.wload(W, W.ap[l][:, cb * 512:(cb + 1) * 512], KC, 512)
                    for j in range(4):
                        ct = cb * 4 + j
                        g = mg[k % 3]
                        t_ = tmp[k % 2]
                        k += 1
                        r0 = bi * 2048 + ct * 128
                        fw.dma("sp", g[:, :], self.MGT[r0:r0 + 128, ts], r=[self.MGT], w=[g])
                        ps = self.bank()
                        for kc in range(KC):
                            fw.op("pe", lambda e, ps=ps, kc=kc, j=j, view=view, koff=koff, KC=KC: e.matmul(ps[:, :], lhsT=view[:, kc, j * 128:(j + 1) * 128], rhs=yt[:, koff + kc, :],
                                                                                                   start=(kc == 0), stop=(kc == KC - 1)), r=[buf, yt], w=[ps])
                        if bi == 0:
                            fw.op("dve", lambda e, ps=ps, g=g, ct=ct: e.tensor_tensor(out=mer[:, ct, :], in0=ps[:, :], in1=g[:, :], op=ALU.mult), r=[ps, g], w=[(mer, ct)])
                        else:
                            fw.op("dve", lambda e, ps=ps, g=g, t_=t_: e.tensor_tensor(out=t_[:, :], in0=ps[:, :], in1=g[:, :], op=ALU.mult), r=[ps, g], w=[t_])
                            fw.op("dve", lambda e, t_=t_, ct=ct: e.tensor_tensor(out=mer[:, ct, :], in0=mer[:, ct, :], in1=t_[:, :], op=ALU.add), r=[t_, (mer, ct)], w=[(mer, ct)])
            for ct in range(16):
                fw.op("act", lambda e, ct=ct: e.copy(out=merb[:, ct, :], in_=mer[:, ct, :]), r=[(mer, ct)], w=[(merb, ct)])
            for cb in range(4):
                buf, view = self.wload(self.w_out, self.w_out.ap[l][:, cb * 512:(cb + 1) * 512], 16, 512)
                for j in range(4):
                    ct = cb * 4 + j
                    x = xt[k % 2]
                    k += 1
                    fw.dma("sp", x[:, :], src[ct * 128:(ct + 1) * 128, ts], r=[(src, (ct, tg))], w=[x])
                    ps = self.bank()
                    for kc in range(16):
                        fw.op("pe", lambda e, ps=ps, kc=kc, j=j, view=view: e.matmul(ps[:, :], lhsT=view[:, kc, j * 128:(j + 1) * 128], rhs=merb[:, kc, :], start=(kc == 0), stop=(kc == 15)),
                              r=[buf, merb], w=[ps])
                    fw.op("dve", lambda e, ps=ps, x=x: e.tensor_tensor(out=x[:, :], in0=ps[:, :], in1=x[:, :], op=ALU.add), r=[ps, x], w=[x])
                    fw.dma("sp", self.XT[ct * 128:(ct + 1) * 128, ts], x[:, :], r=[x], w=[(self.XT, (ct, tg))])
        fw.release(m)

    def p8_ffn(self, l):
        fw = self.fw
        m = fw.mark()
        xg = fw.sb("fxg", [128, 16, 512])
        fT = fw.sb("fT", [128, 16, 512], BF16)
        hT = fw.sb("hT", [128, 44, 512], BF16)
        sq = [fw.sb("fsq%d" % i, [128, 512]) for i in range(2)]
        rs = fw.sb("frs", [128, 512])
        sg = [fw.sb("fsg%d" % i, [128, 512]) for i in range(2)]
        xo = [fw.sb("fxo%d" % i, [128, 512]) for i in range(2)]
        wcol = self.NRM[:, (2 * l + 1) * 16:(2 * l + 2) * 16]
        xv = self.XT.ap.rearrange("(k p) t -> p k t", p=128)
        k = 0
        for tg in range(4):
            ts = slice(tg * 512, (tg + 1) * 512)
            for k0 in range(0, 16, 4):
                fw.dma("sp", xg[:, k0:k0 + 4, :], xv[:, k0:k0 + 4, ts], r=[self.XT], w=[(xg, k0)])
            self.rmsnorm(xg, 512, wcol, lambda kc: (fT, fT[:, kc, :]), sq, rs)
            for cb in range(11):
                bg, vg = self.wload(self.w_gate, self.w_gate.ap[l][:, cb * 512:(cb + 1) * 512], 16, 512)
                bu, vu = self.wload(self.w_up, self.w_up.ap[l][:, cb * 512:(cb + 1) * 512], 16, 512)
                for j in range(4):
                    f = cb * 4 + j
                    pg = self.bank()
                    pu = self.bank()
                    for kc in range(16):
                        fw.op("pe", lambda e, pg=pg, kc=kc, j=j, vg=vg: e.matmul(pg[:, :], lhsT=vg[:, kc, j * 128:(j + 1) * 128], rhs=fT[:, kc, :], start=(kc == 0), stop=(kc == 15)),
                              r=[bg, fT], w=[pg])
                    for kc in range(16):
                        fw.op("pe", lambda e, pu=pu, kc=kc, j=j, vu=vu: e.matmul(pu[:, :], lhsT=vu[:, kc, j * 128:(j + 1) * 128], rhs=fT[:, kc, :], start=(kc == 0), stop=(kc == 15)),
                              r=[bu, fT], w=[pu])
                    s = sg[k % 2]
                    k += 1
                    fw.op("act", lambda e, pg=pg, s=s: e.activation(out=s[:, :], in_=pg[:, :], func=AF.Silu), r=[pg], w=[s])
                    fw.op("dve", lambda e, pu=pu, s=s, f=f: e.tensor_tensor(out=hT[:, f, :], in0=pu[:, :], in1=s[:, :], op=ALU.mult), r=[pu, s], w=[(hT, f)])
            for cb in range(8):
                bd, vd = self.wload(self.w_down, self.w_down.ap[l][:, cb * 256:(cb + 1) * 256], 44, 256)
                for j in range(2):
                    ct = cb * 2 + j
                    ps = self.bank()
                    for kc in range(44):
                        fw.op("pe", lambda e, ps=ps, kc=kc, j=j, vd=vd: e.matmul(ps[:, :], lhsT=vd[:, kc, j * 128:(j + 1) * 128], rhs=hT[:, kc, :], start=(kc == 0), stop=(kc == 43)),
                              r=[bd, hT], w=[ps])
                    x = xo[k % 2]
                    k += 1
                    fw.op("dve", lambda e, ps=ps, x=x, ct=ct: e.tensor_tensor(out=x[:, :], in0=ps[:, :], in1=xg[:, ct, :], op=ALU.add), r=[ps, xg], w=[x])
                    fw.dma("sp", self.XT[ct * 128:(ct + 1) * 128, ts], x[:, :], r=[x], w=[(self.XT, (ct, tg))])
        fw.release(m)

    def final_norm(self, src):
        fw = self.fw
        m = fw.mark()
        xg = fw.sb("nxg", [128, 16, 512])
        og = fw.sb("nog", [128, 16, 512])
        sq = [fw.sb("nsq%d" % i, [128, 512]) for i in range(2)]
        rs = fw.sb("nrs", [128, 512])
        wcol = self.NRM[:, (2 * NL) * 16:(2 * NL + 1) * 16]
        xv = src.ap.rearrange("(k p) t -> p k t", p=128)
        ov = self.outT.ap.rearrange("(k p) t -> p k t", p=128)
        for tg in range(4):
            ts = slice(tg * 512, (tg + 1) * 512)
            for k0 in range(0, 16, 4):
                fw.dma("sp", xg[:, k0:k0 + 4, :], xv[:, k0:k0 + 4, ts], r=[src], w=[(xg, k0)])
            self.rmsnorm(xg, 512, wcol, lambda kc: (og, og[:, kc, :]), sq, rs)
            for k0 in range(0, 16, 4):
                fw.dma("sp", ov[:, k0:k0 + 4, ts], og[:, k0:k0 + 4, :], r=[og], w=[(self.outT, (tg, k0))])
        fw.release(m)


def host_inputs(inp, b):
    consts, tab, et, _ = host_consts()
    f = np.float32
    nrm = np.zeros((128, (2 * NL + 1) * 16), f)
    for l in range(NL):
        nrm[:, (2 * l) * 16:(2 * l + 1) * 16] = np.asarray(inp["norm_mix"][l], f).reshape(16, 128).T
        nrm[:, (2 * l + 1) * 16:(2 * l + 2) * 16] = np.asarray(inp["norm_ffn"][l], f).reshape(16, 128).T
    nrm[:, 2 * NL * 16:] = np.asarray(inp["norm_final"], f).reshape(16, 128).T
    cw = np.zeros((NL, 128, 24, 5), f)
    for l in range(NL):
        cw[l, :, :, 0:4] = np.asarray(inp["conv_w"][l], f).T.reshape(24, 128, 4).transpose(1, 0, 2)
        cw[l, :, :, 4] = np.asarray(inp["conv_b"][l], f).reshape(24, 128).T
    ssdv = np.zeros((NL, 128, 96), f)
    ssdv[:, :, 0:32] = np.asarray(inp["dt_bias"], f)[:, None, :]
    ssdv[:, :, 32:64] = np.asarray(inp["a_log"], f)[:, None, :]
    ssdv[:, :, 64:96] = np.asarray(inp["d_skip"], f)[:, None, :]
    snw = np.ascontiguousarray(np.broadcast_to(np.asarray(inp["ssd_norm"], f)[:, None, :], (NL, 128, 2048)))
    rnw = np.ascontiguousarray(np.broadcast_to(np.asarray(inp["ret_norm"], f).reshape(NL, 1, 1024), (NL, 128, 1024)))
    pet = np.stack([np.asarray(inp["cmp_k_pe"], f).transpose(0, 2, 1), np.asarray(inp["cmp_v_pe"], f).transpose(0, 2, 1)], axis=1)
    d = {
        "xT": np.ascontiguousarray(np.asarray(inp["x"][b], f).T),
        "consts": consts, "tab": tab, "et": et, "nrm": nrm, "cw": cw.reshape(NL, 128, 120), "ssdv": ssdv,
        "snw": snw, "rnw": rnw, "pet": np.ascontiguousarray(pet),
    }
    for k in ("w_in", "cmp_k_w1", "cmp_v_w1", "cmp_k_w2", "cmp_v_w2", "p_nsa", "p_ssd", "p_ret", "w_out", "w_gate", "w_up", "w_down"):
        d[k] = np.ascontiguousarray(np.asarray(inp[k], f))
    return d


_PROG = {}


def kernel(**inputs):
    if "p" not in _PROG:
        _PROG["p"] = Prog()
    prog = _PROG["p"]
    ncores = 4
    base = host_inputs(inputs, 0)
    in_maps = []
    for b in range(ncores):
        d = dict(base)
        d["xT"] = np.ascontiguousarray(np.asarray(inputs["x"][b], np.float32).T)
        in_maps.append(d)
    res = run_bass_kernel_spmd(prog.nc, in_maps, core_ids=list(range(ncores)))
    out = np.stack([np.asarray(res.results[b]["outT"], np.float32).T for b in range(4)], axis=0)
    return np.ascontiguousarray(out)
```

```python
import numpy as np
import concourse.bass as bass
import concourse.mybir as mybir
from concourse.bass_utils import run_bass_kernel_spmd
from contextlib import ExitStack

F32 = mybir.dt.float32
BF16 = mybir.dt.bfloat16
ALU = mybir.AluOpType
AF = mybir.ActivationFunctionType
AX = mybir.AxisListType

COMPUTE = ("pe", "dve", "act", "pool")
NSLOT = {"sp": 16, "pool": 16, "act": 8}
ARENA = 53000

NT = 2048
D = 2048
DIN = 16952
DFF = 5632
NL = 4
C_Q, C_KV, C_NG, C_Z, C_XBC, C_DT, C_RQ, C_RK, C_RV, C_RG, C_MG = 0, 1024, 2560, 2584, 4632, 7704, 7736, 8248, 8760, 9784, 10808
NEG = -30000.0
EPS = 1e-6


class St:
    __slots__ = ("lw", "rd")

    def __init__(self, o=None):
        self.lw = o.lw if o else None
        self.rd = list(o.rd) if o else []


class T:
    def __init__(self, ap, name, excl=False):
        self.ap = ap
        self.name = name
        self.excl = excl
        self.st = {None: St()}

    def __getitem__(self, k):
        return self.ap[k]

    def states(self, key):
        if key is None:
            return list(self.st.values())
        if key not in self.st:
            self.st[key] = St(self.st[None])
        return [self.st[key]]


class _Rec:
    def __getattr__(self, name):
        return lambda *a, **k: (name, a, k)


_REC = _Rec()


class FW:
    def __init__(self, nc):
        self.nc = nc
        self.stream = {e: [] for e in ("pe", "dve", "act", "pool", "sp")}
        self.nops = {e: 0 for e in COMPUTE}
        self.sig = {e: set() for e in COMPUTE}
        self.ndma = {q: 0 for q in NSLOT}
        self.known = {e: {} for e in self.stream}
        self.es = ExitStack()
        self.sems = {}
        self.dsems = {}
        self.arena = self.es.enter_context(nc.sbuf_tensor("arena", [128, ARENA], F32))
        self.aoff = 0
        self.PS = [T(self.es.enter_context(nc.psum_tensor("ps%d" % i, [128, 512], F32))[:, :], "ps%d" % i, excl=True)
                   for i in range(8)]

    def sb(self, name, shape, dtype=F32):
        p = shape[0]
        n = int(np.prod(shape[1:]))
        nb = n * (4 if dtype == F32 else 2)
        nf = ((nb + 63) // 64) * 16
        assert self.aoff + nf <= ARENA, ("SBUF arena overflow", name, self.aoff, nf)
        ap = self.arena[0:p, self.aoff:self.aoff + nf]
        self.aoff += nf
        if dtype != F32:
            ap = ap.bitcast(dtype)
        ap = ap[:, 0:n]
        if len(shape) == 3:
            ap = ap.rearrange("p (a b) -> p a b", a=shape[1], b=shape[2])
        elif len(shape) == 4:
            ap = ap.rearrange("p (a b c) -> p a b c", a=shape[1], b=shape[2], c=shape[3])
        return T(ap, name)

    def mark(self):
        return self.aoff

    def release(self, m):
        self.barrier()
        self.aoff = m

    def dram(self, name, shape, dtype, kind="Internal"):
        h = self.nc.dram_tensor(name, list(shape), dtype, kind=kind)
        return T(h.ap(), name)

    def _need(self, eng, ev, waits):
        if ev is None:
            return
        if ev[0] == "c":
            _, f, idx = ev
            if f == eng and eng == "pe":
                return
            k = ("c", f)
            if self.known[eng].get(k, 0) >= idx:
                return
            self.known[eng][k] = idx
            self.sig[f].add(idx)
            waits.append(ev)
        else:
            _, q, j = ev
            k = ("d", q, j % NSLOT[q])
            if self.known[eng].get(k, -1) >= j:
                return
            self.known[eng][k] = j
            waits.append(ev)

    def _deps(self, eng, r, w, is_dma):
        waits = []
        rs, ws = [], []
        for x in r:
            t, key = x if isinstance(x, tuple) else (x, None)
            (ws if t.excl else rs).extend(t.states(key))
        for x in w:
            t, key = x if isinstance(x, tuple) else (x, None)
            ws.extend(t.states(key))
        for s in rs:
            self._need(eng, s.lw, waits)
        for s in ws:
            lw = s.lw
            if lw is not None and not ((not is_dma) and lw[0] == "c" and lw[1] == eng):
                self._need(eng, lw, waits)
            for ev in s.rd:
                if (not is_dma) and ev[0] == "c" and ev[1] == eng:
                    continue
                self._need(eng, ev, waits)
        return waits, rs, ws

    def _commit(self, ev, rs, ws):
        for s in rs:
            if ev[0] == "c":
                s.rd = [e for e in s.rd if not (e[0] == "c" and e[1] == ev[1])]
            s.rd.append(ev)
        for s in ws:
            s.lw = ev
            s.rd = []

    def op(self, eng, fn, r=(), w=()):
        waits, rs, ws = self._deps(eng, r, w, False)
        for ev in waits:
            self.stream[eng].append(("wait", ev))
        self.nops[eng] += 1
        idx = self.nops[eng]
        self.stream[eng].append(("op", fn(_REC), idx))
        self._commit(("c", eng, idx), rs, ws)

    def dma(self, q, out_ap, in_ap, r=(), w=()):
        waits, rs, ws = self._deps(q, r, w, True)
        j = self.ndma[q]
        K = NSLOT[q]
        if j >= K:
            self._need(q, ("d", q, j - K), waits)
        for ev in waits:
            self.stream[q].append(("wait", ev))
        self.ndma[q] += 1
        self.stream[q].append(("dma", out_ap, in_ap, j))
        self._commit(("d", q, j), rs, ws)

    def barrier(self):
        for e in self.stream:
            waits = []
            for f in COMPUTE:
                if self.nops[f] > 0:
                    self._need(e, ("c", f, self.nops[f]), waits)
            for q in NSLOT:
                for j in range(max(0, self.ndma[q] - NSLOT[q]), self.ndma[q]):
                    self._need(e, ("d", q, j), waits)
            for ev in waits:
                self.stream[e].append(("wait", ev))

    def emit(self):
        nc = self.nc
        self.barrier()
        es = self.es
        for e in COMPUTE:
            self.sems[e] = es.enter_context(nc.semaphore("s_" + e))
        for q in NSLOT:
            self.dsems[q] = [es.enter_context(nc.semaphore("d_%s_%d" % (q, i))) for i in range(NSLOT[q])]
        cum = {}
        for e in COMPUTE:
            c = 0
            m = {}
            for i in range(1, self.nops[e] + 1):
                if i in self.sig[e]:
                    c += 1
                    m[i] = c
            cum[e] = m

        def replay(ename, eng):
            for rec in self.stream[ename]:
                if rec[0] == "wait":
                    ev = rec[1]
                    if ev[0] == "c":
                        eng.wait_ge(self.sems[ev[1]], cum[ev[1]][ev[2]])
                    else:
                        _, q, j = ev
                        eng.wait_ge(self.dsems[q][j % NSLOT[q]], 16 * (j // NSLOT[q] + 1))
                elif rec[0] == "op":
                    nm, ar, kw = rec[1]
                    ins = getattr(eng, nm)(*ar, **kw)
                    if rec[2] in self.sig[ename]:
                        ins.then_inc(self.sems[ename], 1)
                elif rec[0] == "cc":
                    _, kind, i, o, rg, j = rec
                    eng.collective_compute(kind, ALU.bypass, replica_groups=rg, ins=[i], outs=[o]).then_inc(self.dsems[ename][j % NSLOT[ename]], 16)
                else:
                    _, o, i, j = rec
                    eng.dma_start(out=o, in_=i).then_inc(self.dsems[ename][j % NSLOT[ename]], 16)

        with nc.Block() as block:
            @block.tensor
            def _(e):
                replay("pe", e)

            @block.vector
            def _(e):
                replay("dve", e)

            @block.scalar
            def _(e):
                replay("act", e)

            @block.gpsimd
            def _(e):
                replay("pool", e)

            @block.sync
            def _(e):
                replay("sp", e)
        es.close()


CO = {}
_off = 0
for _n, _w in (("IDENT", 128), ("CAUS", 128), ("TGT", 128), ("ONESM", 128), ("ONES1", 128), ("MSK", 1536),
               ("OVL", 33), ("DMT", 512), ("QD", 512), ("KD", 4)):
    CO[_n] = (_off, _w)
    _off += _w
NCONST = _off
TABW = 896 + 512 + 1408 + 2048
TO_U, TO_W, TO_UW, TO_C = 0, 896, 1408, 2816


def host_consts():
    c = np.zeros((128, NCONST), np.float64)
    p = np.arange(128)[:, None]
    f = np.arange(128)[None, :]
    c[:, CO["IDENT"][0]:][:, :128] = (p == f)
    c[:, CO["CAUS"][0]:][:, :128] = (f >= p)
    c[:, CO["TGT"][0]:][:, :128] = (p > f)
    c[:, CO["ONESM"][0]:][:, :128] = 1.0 / D
    c[:, CO["ONES1"][0]:][:, :128] = 1.0
    msk = np.zeros((128, 3, 16, 32))
    for Q in range(16):
        tq = Q * 128 + np.arange(128)
        cur = (tq // 64)[:, None]
        blk = np.arange(32)[None, :]
        msk[:, 0, Q, :] = ((blk > 0) & (blk < cur))
        msk[:, 1, Q, :] = np.where((blk == cur) | (blk == 0), 1e9, np.where(blk > cur, -1e30, 0.0))
        msk[:, 2, Q, :] = (blk <= cur)
    c[:, CO["MSK"][0]:][:, :1536] = msk.reshape(128, -1)
    n = np.arange(128)[:, None]
    k = np.arange(32)[None, :]
    ovl = ((16 * n < 64 * k + 64) & (16 * n + 31 >= 64 * k) & (n < 127)).astype(np.float64)
    c[:, CO["OVL"][0]] = 1.0
    c[:, CO["OVL"][0] + 1:][:, :32] = ovl
    h = np.arange(4, dtype=np.float64)
    log_g = np.log1p(-np.exp2(-5.0 - h))
    s_ = np.arange(128)[:, None]
    l_ = np.arange(128)[None, :]
    for hh in range(4):
        dm = np.where(l_ >= s_, np.exp((l_ - s_) * log_g[hh]), 0.0)
        c[:, CO["DMT"][0] + hh * 128:][:, :128] = dm
        c[:, CO["QD"][0] + hh * 128:][:, :128] = np.exp((np.arange(128) + 1.0) * log_g[hh])[None, :]
        c[:, CO["KD"][0] + hh] = np.exp((127.0 - np.arange(128)) * log_g[hh])
    cdec = [float(np.exp(128 * log_g[hh])) for hh in range(4)]
    tab = np.zeros((8, 128, TABW), np.float64)
    ki = np.arange(128)[:, None]
    for hd in range(8):
        s = 2.0 ** (-(hd + 1))
        cc = np.arange(896)[None, :]
        rel = cc - 384 - ki
        tab[hd, :, TO_U:TO_U + 896] = np.where(rel >= 0, -s * rel, NEG)
        qi = np.arange(512)[None, :]
        tab[hd, :, TO_W:TO_W + 512] = -s * (qi - ki)
        cc = np.arange(1408)[None, :]
        rel = cc - 384 - ki
        tab[hd, :, TO_UW:TO_UW + 1408] = np.where((rel >= 0) & (rel < 512), -s * rel, NEG)
        tq = np.arange(2048)[None, :]
        rel = tq - (16 * ki + 31)
        tab[hd, :, TO_C:TO_C + 2048] = np.where((rel >= 0) & (ki < 127), -s * rel, NEG)
    et = (np.arange(2048)[None, :] // 64 == np.arange(32)[:, None]).astype(np.float32)
    return c.astype(np.float32), tab.astype(np.float32), et, cdec


class Prog:
    def __init__(self, n_layers=NL, dbg=False, stop_after=None):
        self.nl = n_layers
        self.dbg = dbg
        self.stop_after = stop_after
        self.nc = bass.Bass("TRN2", target_bir_lowering=False)
        self.fw = FW(self.nc)
        self.cdec = host_consts()[3]
        self.rot = 0
        self.build()

    def bank(self):
        b = self.fw.PS[self.rot % 8]
        self.rot += 1
        return b

    def din(self, name, shape, dtype=F32):
        return self.fw.dram(name, shape, dtype, kind="ExternalInput")

    def scr(self, name, shape, dtype):
        return self.fw.dram(name, shape, dtype, kind="ExternalOutput" if self.dbg else "Internal")

    def wload(self, Wt, w_ap, KC, n):
        fw = self.fw
        buf = self.WB[self.wrot % len(self.WB)]
        self.wrot += 1
        view = buf.ap[:, 0:KC * n].rearrange("p (k c) -> p k c", k=KC, c=n)
        src = w_ap.rearrange("(k p) c -> p k c", p=128)
        step = 4
        for i, k0 in enumerate(range(0, KC, step)):
            k1 = min(KC, k0 + step)
            fw.dma("pool", view[:, k0:k1, :], src[:, k0:k1, :], r=[Wt], w=[(buf, i)])
        return buf, view

    def rmsnorm(self, xg, ntok, wcol, out_fn, sq, rs):
        fw = self.fw
        C = self.C
        ps = self.bank()
        for kc in range(16):
            s = sq[kc % 2]
            fw.op("act", lambda e, s=s, kc=kc: e.activation(out=s[:, 0:ntok], in_=xg[:, kc, :], func=AF.Square), r=[xg], w=[s])
            fw.op("pe", lambda e, s=s, kc=kc: e.matmul(ps[:, 0:ntok], lhsT=self.cv("ONESM"), rhs=s[:, 0:ntok], start=(kc == 0), stop=(kc == 15)),
                  r=[s, C], w=[ps])
        fw.op("act", lambda e: e.activation(out=rs[:, 0:ntok], in_=ps[:, 0:ntok], func=AF.Sqrt, bias=self.EPSC[:, 0:1], scale=1.0), r=[ps, self.EPSC], w=[rs])
        fw.op("dve", lambda e: e.reciprocal(out=rs[:, 0:ntok], in_=rs[:, 0:ntok]), r=[rs], w=[rs])
        for kc in range(16):
            ot, oap = out_fn(kc)
            fw.op("dve", lambda e, kc=kc, oap=oap: e.scalar_tensor_tensor(out=oap, in0=xg[:, kc, :], scalar=wcol[:, kc:kc + 1], in1=rs[:, 0:ntok],
                                                                         op0=ALU.mult, op1=ALU.mult), r=[xg, rs, self.NRM], w=[ot])

    def cv(self, name, width=None):
        o, w = CO[name]
        return self.C[:, o:o + (width or w)]

    def cvt(self, name):
        if name == "EPSC":
            return self.EPSC
        raise KeyError(name)

    def build(self):
        fw = self.fw
        nl = self.nl
        self.xT_in = self.din("xT", [D, NT])
        self.d_consts = self.din("consts", [128, NCONST])
        self.d_tab = self.din("tab", [8, 128, TABW])
        self.d_et = self.din("et", [32, NT])
        self.d_nrm = self.din("nrm", [128, (2 * NL + 1) * 16])
        self.d_cw = self.din("cw", [NL, 128, 24 * 5])
        self.d_ssdv = self.din("ssdv", [NL, 128, 96])
        self.d_snw = self.din("snw", [NL, 128, 2048])
        self.d_rnw = self.din("rnw", [NL, 128, 1024])
        self.d_pet = self.din("pet", [NL, 2, 128, 32])
        self.w_in = self.din("w_in", [NL, D, DIN])
        self.cw1 = [self.din("cmp_k_w1", [NL, 4096, 128]), self.din("cmp_v_w1", [NL, 4096, 128])]
        self.cw2 = [self.din("cmp_k_w2", [NL, 128, 128]), self.din("cmp_v_w2", [NL, 128, 128])]
        self.p_nsa = self.din("p_nsa", [NL, 1024, D])
        self.p_ssd = self.din("p_ssd", [NL, 2048, D])
        self.p_ret = self.din("p_ret", [NL, 1024, D])
        self.w_out = self.din("w_out", [NL, D, D])
        self.w_gate = self.din("w_gate", [NL, D, DFF])
        self.w_up = self.din("w_up", [NL, D, DFF])
        self.w_down = self.din("w_down", [NL, DFF, D])
        self.outT = fw.dram("outT", [D, NT], F32, kind="ExternalOutput")
        self.XT = self.scr("XT", [D, NT], F32)
        self.QT = self.scr("QT", [1024, NT], BF16)
        self.KVT = self.scr("KVT", [1024, NT], BF16)
        self.VTOK = self.scr("VTOK", [NT, 512], BF16)
        self.NG = self.scr("NG", [NT, 24], F32)
        self.ZS = self.scr("ZS", [NT, 2048], F32)
        self.XBCT = self.scr("XBCT", [3072, NT], F32)
        self.DTR = self.scr("DTR", [NT, 32], F32)
        self.RQT = self.scr("RQT", [512, NT], BF16)
        self.RKT = self.scr("RKT", [512, NT], BF16)
        self.RKTOK = self.scr("RKTOK", [NT, 512], BF16)
        self.RV = self.scr("RV", [NT, 1024], BF16)
        self.RGS = self.scr("RGS", [NT, 1024], F32)
        self.MGT = self.scr("MGT", [6144, NT], BF16)
        self.XTOK = self.scr("XTOK", [NT, 2048], F32)
        self.BCT = self.scr("BCT", [1024, NT], BF16)
        self.BTOK = self.scr("BTOK", [NT, 512], BF16)
        self.YT = self.scr("YT", [4096, NT], BF16)

        self.C = fw.sb("C", [128, NCONST])
        self.NRM = fw.sb("NRM", [128, (2 * NL + 1) * 16])
        self.EPSC = fw.sb("EPSC", [128, 2])
        fw.dma("sp", self.C[:, :], self.d_consts[:, :], r=[self.d_consts], w=[self.C])
        fw.dma("sp", self.NRM[:, :], self.d_nrm[:, :], r=[self.d_nrm], w=[self.NRM])
        fw.op("dve", lambda e: e.memset(self.EPSC[:, 0:1], EPS), w=[self.EPSC])
        fw.op("dve", lambda e: e.memset(self.EPSC[:, 1:2], 1.0), w=[self.EPSC])
        self.wrot = 0
        self.base = fw.mark()

        src = self.xT_in
        for l in range(nl):
            self.layer(l, src)
            src = self.XT
            if self.stop_after is not None:
                break
        if self.stop_after is None:
            self.final_norm(src)
        fw.emit()

    def layer(self, l, src):
        sa = self.stop_after
        self.p1_inproj(l, src)
        if sa == "p1":
            return
        self.p2_ssdprep(l)
        if sa == "p2":
            return
        self.p34_nsa(l)
        if sa == "p4":
            return
        self.p5_ssd(l)
        if sa == "p5":
            return
        self.p6_ret(l)
        if sa == "p6":
            return
        self.p7_merge(l, src)
        if sa == "p7":
            return
        self.p8_ffn(l)

    def p1_inproj(self, l, src):
        fw = self.fw
        m = fw.mark()
        self.WB = [fw.sb("WB%d" % i, [128, 11264], BF16) for i in range(3)]
        uT = fw.sb("uT", [128, 16, NT], BF16)
        stF = [fw.sb("stF%d" % i, [128, NT], F32) for i in range(2)]
        stFb = [fw.sb("stFb%d" % i, [128, NT], BF16) for i in range(2)]
        stT = [fw.sb("stT%d" % i, [128, 512], F32) for i in range(2)]
        stTb = [fw.sb("stTb%d" % i, [128, 512], BF16) for i in range(2)]
        m2 = fw.mark()
        xg = [fw.sb("xg%d" % i, [128, 16, 256], F32) for i in range(1)]
        sq = [fw.sb("sq%d" % i, [128, 512], F32) for i in range(2)]
        rs = fw.sb("rs", [128, 512], F32)
        wcol = self.NRM[:, (2 * l) * 16:(2 * l + 1) * 16]
        srcv = src.ap.rearrange("(k p) t -> p k t", p=128)
        for tg in range(8):
            x = xg[0]
            for k0 in range(0, 16, 4):
                fw.dma("sp", x[:, k0:k0 + 4, :], srcv[:, k0:k0 + 4, tg * 256:(tg + 1) * 256], r=[src], w=[(x, k0)])
            self.rmsnorm(x, 256, wcol, lambda kc, tg=tg: (uT, uT[:, kc, tg * 256:(tg + 1) * 256]), sq, rs)
        fw.aoff = m2
        fw.barrier()

        W = self.w_in
        wl = W.ap[l]
        cnt = [0]

        def fm(c0, c1, dst, r0, func, scale, bf):
            c = c0
            while c < c1:
                nblk = min(512, c1 - c)
                buf, view = self.wload(W, wl[:, c:c + nblk], 16, nblk)
                for j0 in range(0, nblk, 128):
                    n = min(128, nblk - j0)
                    banks = [self.bank() for _ in range(4)]
                    for kc in range(16):
                        for tg in range(4):
                            fw.op("pe", lambda e, kc=kc, tg=tg, b=banks[tg], n=n, j0=j0, view=view: e.matmul(
                                b[0:n, :], lhsT=view[:, kc, j0:j0 + n], rhs=uT[:, kc, tg * 512:(tg + 1) * 512],
                                start=(kc == 0), stop=(kc == 15)), r=[buf, uT], w=[banks[tg]])
                    i = cnt[0] % 2
                    cnt[0] += 1
                    st = stFb[i] if bf else stF[i]
                    for tg in range(4):
                        fw.op("act", lambda e, tg=tg, b=banks[tg], n=n, st=st: e.activation(
                            out=st[0:n, tg * 512:(tg + 1) * 512], in_=b[0:n, :], func=func, scale=scale), r=[banks[tg]], w=[(st, tg)])
                    rr = r0 + (c - c0) + j0
                    fw.dma("sp", dst[rr:rr + n, :], st[0:n, :], r=[st], w=[(dst, rr)])
                c += nblk

        def tm(c0, c1, dst, d0, func, bf):
            c = c0
            while c < c1:
                nblk = min(512, c1 - c)
                buf, view = self.wload(W, wl[:, c:c + nblk], 16, nblk)
                for tt in range(16):
                    b = self.bank()
                    for kc in range(16):
                        fw.op("pe", lambda e, kc=kc, tt=tt, b=b, nblk=nblk, view=view: e.matmul(
                            b[:, 0:nblk], lhsT=uT[:, kc, tt * 128:(tt + 1) * 128], rhs=view[:, kc, 0:nblk],
                            start=(kc == 0), stop=(kc == 15)), r=[buf, uT], w=[b])
                    i = cnt[0] % 2
                    cnt[0] += 1
                    st = stTb[i] if bf else stT[i]
                    fw.op("act", lambda e, b=b, nblk=nblk, st=st: e.activation(out=st[:, 0:nblk], in_=b[:, 0:nblk], func=func), r=[b], w=[st])
                    dd = d0 + (c - c0)
                    fw.dma("sp", dst[tt * 128:(tt + 1) * 128, dd:dd + nblk], st[:, 0:nblk], r=[st], w=[(dst, (tt, dd))])
                c += nblk

        ID, SIG, SILU = AF.Identity, AF.Sigmoid, AF.Silu
        sc = 128.0 ** -0.5
        fm(C_Q, C_Q + 1024, self.QT, 0, ID, sc, True)
        fm(C_KV, C_KV + 768, self.KVT, 0, ID, 1.0, True)
        tm(C_KV + 768, C_KV + 1024, self.VTOK, 0, ID, True)
        fm(C_KV + 1024, C_KV + 1280, self.KVT, 768, ID, 1.0, True)
        tm(C_KV + 1280, C_KV + 1536, self.VTOK, 256, ID, True)
        tm(C_NG, C_NG + 24, self.NG, 0, SIG, False)
        tm(C_Z, C_Z + 2048, self.ZS, 0, SILU, False)
        fm(C_XBC, C_XBC + 3072, self.XBCT, 0, ID, 1.0, False)
        tm(C_DT, C_DT + 32, self.DTR, 0, ID, False)
        fm(C_RQ, C_RQ + 512, self.RQT, 0, ID, sc, True)
        fm(C_RK, C_RK + 512, self.RKT, 0, ID, 1.0, True)
        tm(C_RK, C_RK + 512, self.RKTOK, 0, ID, True)
        tm(C_RV, C_RV + 1024, self.RV, 0, ID, True)
        tm(C_RG, C_RG + 1024, self.RGS, 0, SILU, False)
        fm(C_MG, C_MG + 6144, self.MGT, 0, SIG, 1.0, True)
        fw.release(m)

    def p2_ssdprep(self, l):
        fw = self.fw
        m = fw.mark()
        cw = fw.sb("cw", [128, 24, 5])
        fw.dma("sp", cw[:, :, :], self.d_cw.ap[l].rearrange("p (t k) -> p t k", k=5), r=[self.d_cw], w=[cw])
        xp = [fw.sb("xp%d" % i, [128, 3 + NT]) for i in range(2)]
        acc = [fw.sb("acc%d" % i, [128, NT]) for i in range(2)]
        xo = [fw.sb("xo%d" % i, [128, NT]) for i in range(2)]
        xb = [fw.sb("xob%d" % i, [128, NT], BF16) for i in range(2)]
        stt = [fw.sb("sttok%d" % i, [128, 4, 128]) for i in range(2)]
        sttb = [fw.sb("sttokb%d" % i, [128, 4, 128], BF16) for i in range(2)]
        for i in range(2):
            fw.op("dve", lambda e, i=i: e.memset(xp[i][:, 0:3], 0.0), w=[xp[i]])
        k = 0
        for ct in range(24):
            i = ct % 2
            fw.dma("sp", xp[i][:, 3:3 + NT], self.XBCT[ct * 128:(ct + 1) * 128, :], r=[self.XBCT], w=[xp[i]])
            a = acc[i]
            fw.op("dve", lambda e, i=i, ct=ct, a=a: e.tensor_scalar(out=a[:, :], in0=xp[i][:, 0:NT], scalar1=cw[:, ct, 0:1], scalar2=None, op0=ALU.mult),
                  r=[xp[i], cw], w=[a])
            for kk in range(1, 4):
                fw.op("dve", lambda e, i=i, ct=ct, a=a, kk=kk: e.scalar_tensor_tensor(out=a[:, :], in0=xp[i][:, kk:kk + NT], scalar=cw[:, ct, kk:kk + 1],
                                                                                      in1=a[:, :], op0=ALU.mult, op1=ALU.add), r=[xp[i], cw, a], w=[a])
            o = xo[i]
            fw.op("act", lambda e, a=a, o=o, ct=ct: e.activation(out=o[:, :], in_=a[:, :], func=AF.Silu, bias=cw[:, ct, 4:5], scale=1.0), r=[a, cw], w=[o])
            if ct >= 16:
                ob = xb[i]
                fw.op("pool", lambda e, o=o, ob=ob: e.tensor_copy(out=ob[:, :], in_=o[:, :]), r=[o], w=[ob])
                r0 = (ct - 16) * 128
                fw.dma("sp", self.BCT[r0:r0 + 128, :], ob[:, :], r=[ob], w=[(self.BCT, r0)])
            if ct < 20:
                for t4 in range(4):
                    b = self.bank()
                    for j in range(4):
                        tt = t4 * 4 + j
                        fw.op("pe", lambda e, b=b, j=j, tt=tt, o=o: e.transpose(b[:, j * 128:(j + 1) * 128], o[:, tt * 128:(tt + 1) * 128], self.cv("IDENT")),
                              r=[o, self.C], w=[b])
                    if ct < 16:
                        s = stt[k % 2]
                        dst = self.XTOK.ap[t4 * 512:(t4 + 1) * 512, ct * 128:(ct + 1) * 128]
                        dt_ = self.XTOK
                    else:
                        s = sttb[k % 2]
                        cc = (ct - 16) * 128
                        dst = self.BTOK.ap[t4 * 512:(t4 + 1) * 512, cc:cc + 128]
                        dt_ = self.BTOK
                    k += 1
                    fw.op("act", lambda e, b=b, s=s: e.copy(out=s.ap.rearrange("p a b -> p (a b)"), in_=b[:, :]), r=[b], w=[s])
                    fw.dma("sp", dst.rearrange("(a p) c -> p a c", p=128), s[:, :, :], r=[s], w=[(dt_, (t4, ct))])
        fw.release(m)

    def p34_nsa(self, l):
        fw = self.fw
        C = self.C
        m = fw.mark()
        KCT = [fw.sb("KCT%d" % g, [128, 128], BF16) for g in range(2)]
        VCX = [fw.sb("VCX%d" % g, [128, 161]) for g in range(2)]
        m3 = fw.mark()
        w1s = fw.sb("w1s", [128, 32, 128], BF16)
        w2s = fw.sb("w2s", [128, 128], BF16)
        pet = fw.sb("pet", [128, 32], BF16)
        kcT = [fw.sb("kcT%d" % i, [128, NT], BF16) for i in range(2)]
        pb = fw.sb("pb", [128, 1])
        hx = fw.sb("hx", [128, 128])
        h2 = fw.sb("h2", [128, 128])
        G = fw.sb("G", [128, 128], BF16)
        for t in range(2):
            W1 = self.cw1[t]
            w1v = W1.ap[l].rearrange("(l d) o -> d l o", d=128)
            for l0 in range(0, 32, 8):
                fw.dma("pool", w1s[:, l0:l0 + 8, :], w1v[:, l0:l0 + 8, :], r=[W1], w=[(w1s, l0)])
            fw.dma("pool", w2s[:, :], self.cw2[t].ap[l], r=[self.cw2[t]], w=[w2s])
            fw.dma("pool", pet[:, :], self.d_pet.ap[l, t], r=[self.d_pet], w=[pet])
            for g in range(2):
                src = kcT[g]
                r0 = t * 256 + g * 128
                fw.dma("sp", src[:, :], self.KVT[r0:r0 + 128, :], r=[self.KVT], w=[src])
                sv = src.ap.rearrange("p (n s) -> p n s", s=16)
                ps = self.bank()
                for li in range(32):
                    rhs = sv[:, 0:127, li] if li < 16 else sv[:, 1:128, li - 16]
                    fw.op("pe", lambda e, li=li, rhs=rhs, ps=ps: e.matmul(ps[:, 0:127], lhsT=w1s[:, li, :], rhs=rhs, start=(li == 0), stop=(li == 31)),
                          r=[w1s, src], w=[ps])
                ps2 = self.bank()
                for li in range(32):
                    fw.op("pe", lambda e, li=li, ps2=ps2: e.matmul(ps2[:, 0:1], lhsT=w1s[:, li, :], rhs=pet[:, li:li + 1], start=(li == 0), stop=(li == 31)),
                          r=[w1s, pet], w=[ps2])
                fw.op("act", lambda e, ps2=ps2: e.copy(out=pb[:, :], in_=ps2[:, 0:1]), r=[ps2], w=[pb])
                fw.op("dve", lambda e: e.memset(hx[:, 127:128], 0.0), w=[hx])
                fw.op("act", lambda e, ps=ps: e.activation(out=hx[:, 0:127], in_=ps[:, 0:127], func=AF.Identity, bias=pb[:, 0:1], scale=1.0), r=[ps, pb], w=[hx])
                fw.op("dve", lambda e: e.tensor_tensor(out=h2[:, :], in0=hx[:, :], in1=hx[:, :], op=ALU.mult), r=[hx], w=[h2])
                fw.op("dve", lambda e: e.tensor_scalar(out=h2[:, :], in0=h2[:, :], scalar1=0.044715, scalar2=1.0, op0=ALU.mult, op1=ALU.add), r=[h2], w=[h2])
                fw.op("dve", lambda e: e.tensor_tensor(out=h2[:, :], in0=h2[:, :], in1=hx[:, :], op=ALU.mult), r=[h2, hx], w=[h2])
                fw.op("act", lambda e: e.activation(out=h2[:, :], in_=h2[:, :], func=AF.Sigmoid, scale=1.5957691216057308), r=[h2], w=[h2])
                fw.op("dve", lambda e: e.tensor_tensor(out=G[:, :], in0=h2[:, :], in1=hx[:, :], op=ALU.mult), r=[h2, hx], w=[G])
                ps3 = self.bank()
                if t == 0:
                    fw.op("pe", lambda e, ps3=ps3: e.matmul(ps3[:, 0:128], lhsT=w2s[:, :], rhs=G[:, :], start=True, stop=True), r=[w2s, G], w=[ps3])
                    fw.op("act", lambda e, ps3=ps3, g=g: e.copy(out=KCT[g][:, :], in_=ps3[:, 0:128]), r=[ps3], w=[KCT[g]])
                else:
                    fw.op("pe", lambda e, ps3=ps3: e.matmul(ps3[:, 0:128], lhsT=G[:, :], rhs=w2s[:, :], start=True, stop=True), r=[w2s, G], w=[ps3])
                    fw.op("act", lambda e, ps3=ps3, g=g: e.copy(out=VCX[g][:, 0:128], in_=ps3[:, 0:128]), r=[ps3], w=[VCX[g]])
                    fw.op("dve", lambda e, g=g: e.tensor_copy(out=VCX[g][:, 128:161], in_=self.cv("OVL")), r=[C], w=[VCX[g]])
        fw.release(m3)

        ET = fw.sb("ET", [32, NT], BF16)
        fw.dma("pool", ET[:, :], self.d_et[:, :], r=[self.d_et], w=[ET])
        NGs = fw.sb("NGs", [128, 16, 24])
        fw.dma("sp", NGs[:, :, :], self.NG.ap.rearrange("(t p) c -> p t c", p=128), r=[self.NG], w=[NGs])
        ksT = fw.sb("ksT", [128, NT], BF16)
        kwT = fw.sb("kwT", [128, NT], BF16)
        VSX = fw.sb("VSX", [128, 16, 129], BF16)
        VWX = fw.sb("VWX", [128, 16, 129], BF16)
        QTg = [fw.sb("QTg%d" % j, [128, NT], BF16) for j in range(4)]
        TABC = [fw.sb("TABC%d" % i, [128, 2048]) for i in range(2)]
        TABS = [fw.sb("TABS%d" % i, [128, TO_C]) for i in range(2)]
        IMP = fw.sb("IMP", [128, 16, 32])
        NEGT = fw.sb("NEGT", [32, NT], BF16)
        Yh = [fw.sb("Yh%d" % j, [128, 16, 128]) for j in range(4)]
        scb = [fw.sb("scb%d" % i, [128, 512]) for i in range(4)]
        p32 = [fw.sb("p32_%d" % i, [128, 512]) for i in range(3)]
        pbf = [fw.sb("pbf%d" % i, [128, 512], BF16) for i in range(4)]
        sm = [fw.sb("sm%d" % i, [128, 48]) for i in range(4)]
        ytb = [fw.sb("ytb%d" % i, [128, 512], BF16) for i in range(2)]
        MSK = self.cv("MSK").rearrange("p (a q k) -> p a q k", a=3, q=16, k=32)
        cnt = {"s": 0, "p": 0, "sm": 0, "y": 0, "r": 0, "c": 0, "p32": 0}
        PS = fw.PS
        LA = 2

        def sbank():
            b = PS[4 + cnt["c"] % 4]
            cnt["c"] += 1
            return b

        def pipeline(items):
            n = len(items)
            for i in range(n + LA):
                if i < n:
                    items[i][0]()
                if i >= LA:
                    items[i - LA][1]()

        def evac_round(banks, W, Yt, Q0, h, br, first, imp):
            s = sm[cnt["sm"] % 4]
            cnt["sm"] += 1
            for bi in range(2):
                fw.op("dve", lambda e, bi=bi: e.tensor_scalar(out=s[:, 2 * bi:2 * bi + 2], in0=banks[bi][:, 128:128 + W + 1:W], scalar1=1e-30, scalar2=None, op0=ALU.max),
                      r=[banks[bi]], w=[s])
            fw.op("dve", lambda e: e.reciprocal(out=s[:, 0:4], in_=s[:, 0:4]), r=[s], w=[s])
            fw.op("dve", lambda e: e.tensor_tensor(out=s[:, 4:8], in0=s[:, 0:4], in1=NGs[:, Q0:Q0 + 4, h * 3 + br], op=ALU.mult), r=[s, NGs], w=[s])
            for qt in range(4):
                bk = banks[qt // 2]
                c0 = (qt % 2) * W
                Q = Q0 + qt
                if first:
                    fw.op("dve", lambda e, bk=bk, c0=c0, Q=Q, qt=qt: e.tensor_scalar(out=Yt[:, Q, :], in0=bk[:, c0:c0 + 128], scalar1=s[:, 4 + qt:5 + qt], scalar2=None, op0=ALU.mult),
                          r=[bk, s], w=[(Yt, Q)])
                else:
                    fw.op("dve", lambda e, bk=bk, c0=c0, Q=Q, qt=qt: e.scalar_tensor_tensor(out=Yt[:, Q, :], in0=bk[:, c0:c0 + 128], scalar=s[:, 4 + qt:5 + qt], in1=Yt[:, Q, :],
                                                                                           op0=ALU.mult, op1=ALU.add), r=[bk, s, (Yt, Q)], w=[(Yt, Q)])
            if imp:
                for qt in range(4):
                    bk = banks[qt // 2]
                    c0 = (qt % 2) * W
                    Q = Q0 + qt
                    fw.op("dve", lambda e, bk=bk, c0=c0, Q=Q, qt=qt: e.scalar_tensor_tensor(out=IMP[:, Q, :], in0=bk[:, c0 + 129:c0 + 161], scalar=s[:, qt:qt + 1], in1=IMP[:, Q, :],
                                                                                           op0=ALU.mult, op1=ALU.add), r=[bk, s, (IMP, Q)], w=[(IMP, Q)])

        for g in range(2):
            fw.dma("sp", ksT[:, :], self.KVT[512 + g * 128:512 + (g + 1) * 128, :], r=[self.KVT], w=[ksT])
            fw.dma("sp", kwT[:, :], self.KVT[768 + g * 128:768 + (g + 1) * 128, :], r=[self.KVT], w=[kwT])
            fw.dma("sp", VSX[:, :, 0:128], self.VTOK.ap[:, g * 128:(g + 1) * 128].rearrange("(t p) d -> p t d", p=128), r=[self.VTOK], w=[VSX])
            fw.dma("sp", VWX[:, :, 0:128], self.VTOK.ap[:, 256 + g * 128:256 + (g + 1) * 128].rearrange("(t p) d -> p t d", p=128), r=[self.VTOK], w=[VWX])
            fw.op("dve", lambda e: e.memset(VSX[:, :, 128:129], 1.0), w=[VSX])
            fw.op("dve", lambda e: e.memset(VWX[:, :, 128:129], 1.0), w=[VWX])
            fw.op("dve", lambda e: e.memset(IMP[:, :, :], 0.0), w=[IMP])
            for j in range(4):
                h = g * 4 + j
                fw.dma("sp", QTg[j][:, :], self.QT[h * 128:(h + 1) * 128, :], r=[self.QT], w=[QTg[j]])
            items = []
            for j in range(4):
                h = g * 4 + j
                for qg in range(4):
                    st = {}

                    def front(j=j, h=h, qg=qg, st=st):
                        tb = TABC[h % 2]
                        if qg == 0:
                            fw.dma("sp", tb[:, :], self.d_tab.ap[h][:, TO_C:TO_C + 2048], r=[self.d_tab], w=[tb])
                        S = sbank()
                        fw.op("pe", lambda e: e.matmul(S[:, :], lhsT=KCT[g][:, :], rhs=QTg[j][:, qg * 512:(qg + 1) * 512], start=True, stop=True), r=[KCT[g], QTg[j]], w=[S])
                        sc_ = scb[cnt["s"] % 4]
                        cnt["s"] += 1
                        pp = p32[cnt["p32"] % 3]
                        cnt["p32"] += 1
                        fw.op("dve", lambda e: e.tensor_tensor(out=sc_[:, :], in0=S[:, :], in1=tb[:, qg * 512:(qg + 1) * 512], op=ALU.add), r=[S, tb], w=[sc_])
                        fw.op("act", lambda e: e.activation(out=pp[:, :], in_=sc_[:, :], func=AF.Exp), r=[sc_], w=[pp])
                        st["pp"] = pp

                    def back(j=j, h=h, qg=qg, st=st):
                        pp = st["pp"]
                        r_ = cnt["r"] % 2
                        cnt["r"] += 1
                        banks = [PS[2 * r_], PS[2 * r_ + 1]]
                        for qt in range(4):
                            bk = banks[qt // 2]
                            c0 = (qt % 2) * 161
                            fw.op("pe", lambda e, bk=bk, c0=c0, qt=qt: e.matmul(bk[:, c0:c0 + 161], lhsT=pp[:, qt * 128:(qt + 1) * 128], rhs=VCX[g][:, :], start=True, stop=True),
                                  r=[pp, VCX[g]], w=[bk])
                        evac_round(banks, 161, Yh[j], qg * 4, h, 0, True, True)

                    items.append((front, back))
            pipeline(items)
            for Q in range(16):
                s = sm[cnt["sm"] % 4]
                cnt["sm"] += 1
                im = s[:, 8:40]
                fw.op("dve", lambda e, Q=Q, im=im: e.tensor_tensor(out=im, in0=IMP[:, Q, :], in1=MSK[:, 0, Q, :], op=ALU.mult), r=[(IMP, Q), C], w=[s])
                fw.op("dve", lambda e, Q=Q, im=im: e.tensor_tensor(out=im, in0=im, in1=MSK[:, 1, Q, :], op=ALU.add), r=[s, C], w=[s])
                fw.op("dve", lambda e, s=s, im=im: e.max(out=s[:, 0:8], in_=im), r=[s], w=[s])
                fw.op("dve", lambda e, s=s, im=im: e.tensor_scalar(out=im, in0=im, scalar1=s[:, 7:8], scalar2=None, op0=ALU.is_ge), r=[s], w=[s])
                fw.op("dve", lambda e, Q=Q, im=im: e.tensor_tensor(out=im, in0=im, in1=MSK[:, 2, Q, :], op=ALU.mult), r=[s, C], w=[s])
                fw.op("dve", lambda e, im=im: e.tensor_scalar(out=im, in0=im, scalar1=-1.0, scalar2=-NEG, op0=ALU.add, op1=ALU.mult), r=[s], w=[s])
                b = sbank()
                fw.op("pe", lambda e, b=b, im=im: e.transpose(b[0:32, 0:128], im, self.cv("IDENT")), r=[s, C], w=[b])
                fw.op("act", lambda e, b=b, Q=Q: e.copy(out=NEGT[:, Q * 128:(Q + 1) * 128], in_=b[0:32, 0:128]), r=[b], w=[(NEGT, Q)])
            items = []
            for j in range(4):
                h = g * 4 + j
                slope = 2.0 ** (-(h + 1))
                for br in (1, 2):
                    for qg in range(4):
                        kt_lo = 0 if br == 1 else max(0, 4 * qg - 4)
                        kts = list(range(kt_lo, 4 * qg + 4))
                        rst = {"used": [False, False]}
                        for kt in kts:
                            st = {}
                            first_of_head = (br == 1 and qg == 0 and kt == kts[0])
                            last_of_round = (kt == kts[-1])
                            last_of_head = (br == 2 and qg == 3 and last_of_round)

                            def front(j=j, h=h, slope=slope, br=br, qg=qg, kt=kt, st=st, first_of_head=first_of_head):
                                tb = TABS[j % 2]
                                if first_of_head:
                                    fw.dma("sp", tb[:, :], self.d_tab.ap[h][:, 0:TO_C], r=[self.d_tab], w=[tb])
                                KT = ksT if br == 1 else kwT
                                D0 = qg * 512 - kt * 128
                                S = sbank()
                                fw.op("pe", lambda e: e.matmul(S[:, :], lhsT=KT[:, kt * 128:(kt + 1) * 128], rhs=QTg[j][:, qg * 512:(qg + 1) * 512], start=True, stop=(br == 2)),
                                      r=[KT, QTg[j]], w=[S])
                                if br == 1:
                                    fw.op("pe", lambda e: e.matmul(S[:, :], lhsT=ET[:, kt * 128:(kt + 1) * 128], rhs=NEGT[:, qg * 512:(qg + 1) * 512], start=False, stop=True),
                                          r=[ET, NEGT], w=[S])
                                sc_ = scb[cnt["s"] % 4]
                                cnt["s"] += 1
                                pp = pbf[cnt["p"] % 4]
                                cnt["p"] += 1
                                bias = 0.0
                                if br == 1:
                                    if D0 <= 0:
                                        tsl = tb[:, TO_U + D0 + 384:TO_U + D0 + 384 + 512]
                                    else:
                                        tsl = tb[:, TO_W:TO_W + 512]
                                        bias = -slope * D0
                                else:
                                    tsl = tb[:, TO_UW + D0 + 384:TO_UW + D0 + 384 + 512]
                                if bias == 0.0:
                                    fw.op("dve", lambda e: e.tensor_tensor(out=sc_[:, :], in0=S[:, :], in1=tsl, op=ALU.add), r=[S, tb], w=[sc_])
                                else:
                                    fw.op("dve", lambda e: e.scalar_tensor_tensor(out=sc_[:, :], in0=S[:, :], scalar=bias, in1=tsl, op0=ALU.add, op1=ALU.add), r=[S, tb], w=[sc_])
                                fw.op("act", lambda e: e.activation(out=pp[:, :], in_=sc_[:, :], func=AF.Exp), r=[sc_], w=[pp])
                                st["pp"] = pp

                            def back(j=j, h=h, br=br, qg=qg, kt=kt, st=st, rst=rst, first=(kt == kts[0]), last_of_round=last_of_round, last_of_head=last_of_head):
                                pp = st["pp"]
                                VX = VSX if br == 1 else VWX
                                if first:
                                    r_ = cnt["r"] % 2
                                    cnt["r"] += 1
                                    rst["banks"] = [PS[2 * r_], PS[2 * r_ + 1]]
                                banks = rst["banks"]
                                for qt in range(4):
                                    Q = qg * 4 + qt
                                    lo = 0 if br == 1 else max(0, Q - 4)
                                    if kt < lo or kt > Q:
                                        continue
                                    bi = qt // 2
                                    bk = banks[bi]
                                    c0 = (qt % 2) * 129
                                    stt_ = not rst["used"][bi]
                                    rst["used"][bi] = True
                                    fw.op("pe", lambda e, bk=bk, c0=c0, qt=qt, stt_=stt_, Q=Q: e.matmul(bk[:, c0:c0 + 129], lhsT=pp[:, qt * 128:(qt + 1) * 128], rhs=VX[:, kt, :],
                                                                                                    start=stt_, stop=(kt == Q), skip_group_check=True), r=[pp, VX], w=[bk])
                                if last_of_round:
                                    evac_round(banks, 129, Yh[j], qg * 4, h, br, False, False)
                                if last_of_head:
                                    for t4 in range(4):
                                        b = sbank()
                                        for jj in range(4):
                                            Q = t4 * 4 + jj
                                            fw.op("pe", lambda e, b=b, jj=jj, Q=Q: e.transpose(b[:, jj * 128:(jj + 1) * 128], Yh[j][:, Q, :], self.cv("IDENT")), r=[(Yh[j], Q), C], w=[b])
                                        y = ytb[cnt["y"] % 2]
                                        cnt["y"] += 1
                                        fw.op("act", lambda e, b=b, y=y: e.copy(out=y[:, :], in_=b[:, :]), r=[b], w=[y])
                                        fw.dma("sp", self.YT[h * 128:(h + 1) * 128, t4 * 512:(t4 + 1) * 512], y[:, :], r=[y], w=[(self.YT, (h, t4))])

                            items.append((front, back))
            pipeline(items)
        fw.release(m)
        fw.release(m)

    def p5_ssd(self, l):
        fw = self.fw
        C = self.C
        PS = fw.PS
        m = fw.mark()
        sv = fw.sb("ssdv", [128, 96])
        fw.dma("sp", sv[:, :], self.d_ssdv.ap[l], r=[self.d_ssdv], w=[sv])
        snw = fw.sb("snw", [128, 2048])
        fw.dma("sp", snw[:, :], self.d_snw.ap[l], r=[self.d_snw], w=[snw])
        negA = fw.sb("negA", [128, 32])
        fw.op("act", lambda e: e.activation(out=negA[:, :], in_=sv[:, 32:64], func=AF.Exp), r=[sv], w=[negA])
        fw.op("dve", lambda e: e.tensor_scalar(out=negA[:, :], in0=negA[:, :], scalar1=-1.0, scalar2=None, op0=ALU.mult), r=[negA], w=[negA])
        BT = fw.sb("BT", [128, 4, NT], BF16)
        CT = fw.sb("CT", [128, 4, NT], BF16)
        fw.dma("sp", BT[:, :, :], self.BCT.ap[0:512, :].rearrange("(g p) t -> p g t", p=128), r=[self.BCT], w=[BT])
        fw.dma("sp", CT[:, :, :], self.BCT.ap[512:1024, :].rearrange("(g p) t -> p g t", p=128), r=[self.BCT], w=[CT])
        H = fw.sb("H", [128, 4, 512])
        Hb = fw.sb("Hb", [128, 4, 512], BF16)
        fw.op("dve", lambda e: e.memset(H[:, :, :], 0.0), w=[H])
        fw.op("dve", lambda e: e.memset(Hb[:, :, :], 0.0), w=[Hb])
        dtA = fw.sb("dtA", [128, 16, 32])
        tmpA = fw.sb("tmpA", [128, 16, 32])
        aA = fw.sb("aA", [128, 16, 32])
        acA = fw.sb("acA", [128, 16, 32])
        atA = fw.sb("atA", [128, 16, 32])
        eaA = fw.sb("eaA", [128, 16, 32])
        cdA = fw.sb("cdA", [128, 16, 32])
        dteA = fw.sb("dteA", [128, 16, 32])
        fl = lambda t: t.ap.rearrange("p c h -> p (c h)")
        bc = lambda ap: ap.unsqueeze(1).broadcast_to([128, 16, 32])
        fw.dma("sp", dtA[:, :, :], self.DTR.ap.rearrange("(c p) h -> p c h", p=128), r=[self.DTR], w=[dtA])
        fw.op("dve", lambda e: e.tensor_tensor(out=dtA[:, :, :], in0=dtA[:, :, :], in1=bc(sv[:, 0:32]), op=ALU.add), r=[dtA, sv], w=[dtA])
        fw.op("dve", lambda e: e.tensor_scalar(out=fl(tmpA), in0=fl(dtA), scalar1=-1.0, scalar2=None, op0=ALU.mult), r=[dtA], w=[tmpA])
        fw.op("dve", lambda e: e.tensor_tensor(out=fl(tmpA), in0=fl(tmpA), in1=fl(dtA), op=ALU.max), r=[dtA, tmpA], w=[tmpA])
        fw.op("act", lambda e: e.activation(out=fl(tmpA), in_=fl(tmpA), func=AF.Exp, scale=-1.0), r=[tmpA], w=[tmpA])
        fw.op("act", lambda e: e.activation(out=fl(tmpA), in_=fl(tmpA), func=AF.Ln, bias=self.EPSC[:, 1:2], scale=1.0), r=[tmpA, self.EPSC], w=[tmpA])
        fw.op("dve", lambda e: e.scalar_tensor_tensor(out=fl(dtA), in0=fl(dtA), scalar=0.0, in1=fl(tmpA), op0=ALU.max, op1=ALU.add), r=[dtA, tmpA], w=[dtA])
        fw.op("dve", lambda e: e.tensor_tensor(out=aA[:, :, :], in0=dtA[:, :, :], in1=bc(negA[:, :]), op=ALU.mult), r=[dtA, negA], w=[aA])
        fw.op("pe", lambda e: e.matmul(PS[1][:, :], lhsT=self.cv("CAUS"), rhs=fl(aA), start=True, stop=True), r=[aA, C], w=[PS[1]])
        fw.op("pe", lambda e: e.matmul(PS[2][:, :], lhsT=self.cv("ONES1"), rhs=fl(aA), start=True, stop=True), r=[aA, C], w=[PS[2]])
        fw.op("act", lambda e: e.copy(out=fl(acA), in_=PS[1][:, :]), r=[PS[1]], w=[acA])
        fw.op("act", lambda e: e.copy(out=fl(atA), in_=PS[2][:, :]), r=[PS[2]], w=[atA])
        fw.op("act", lambda e: e.activation(out=fl(eaA), in_=fl(acA), func=AF.Exp), r=[acA], w=[eaA])
        fw.op("act", lambda e: e.activation(out=fl(cdA), in_=fl(atA), func=AF.Exp), r=[atA], w=[cdA])
        fw.op("dve", lambda e: e.tensor_tensor(out=fl(dteA), in0=fl(atA), in1=fl(acA), op=ALU.subtract), r=[atA, acA], w=[dteA])
        fw.op("act", lambda e: e.activation(out=fl(dteA), in_=fl(dteA), func=AF.Exp), r=[dteA], w=[dteA])

        xc = [fw.sb("xc%d" % i, [128, 2048]) for i in range(3)]
        zc = [fw.sb("zc%d" % i, [128, 2048]) for i in range(3)]
        bk = [fw.sb("bk%d" % i, [128, 512], BF16) for i in range(3)]
        xdt = fw.sb("xdt", [128, 2048], BF16)
        xdte = fw.sb("xdte", [128, 2048], BF16)
        cbm = fw.sb("cbm", [128, 4, 128])
        seg = [fw.sb("seg%d" % i, [128, 4, 128]) for i in range(2)]
        dec = [fw.sb("dec%d" % i, [128, 4, 128]) for i in range(2)]
        MT = [fw.sb("MT%d" % i, [128, 4, 128], BF16) for i in range(3)]
        ys_ = [fw.sb("y%d" % i, [128, 2048]) for i in range(2)]
        ysq = fw.sb("ysq", [128, 512])
        st4 = fw.sb("st4", [128, 8])
        yT = fw.sb("yT", [128, 16, 128], BF16)
        CAUS = self.cv("CAUS")
        cnt = {"k": 0, "m": 0, "b": 0}
        LA = 2

        def sbank():
            b = PS[1 + cnt["b"] % 3]
            cnt["b"] += 1
            return b

        def loadc(c):
            i = c % 3
            rows = slice(c * 128, (c + 1) * 128)
            fw.dma("sp", xc[i][:, :], self.XTOK[rows, :], r=[self.XTOK], w=[xc[i]])
            fw.dma("sp", bk[i][:, :], self.BTOK[rows, :], r=[self.BTOK], w=[bk[i]])
            fw.dma("sp", zc[i][:, :], self.ZS[rows, :], r=[self.ZS], w=[zc[i]])

        def stageA(c):
            i = c % 3
            x, z, bt = xc[i], zc[i], bk[i]
            y = ys_[c % 2]
            cs = slice(c * 128, (c + 1) * 128)
            x3 = x.ap.rearrange("p (h q) -> p h q", q=64)
            fw.op("dve", lambda e: e.tensor_tensor(out=xdt.ap.rearrange("p (h q) -> p h q", q=64), in0=x3, in1=dtA[:, c, :].unsqueeze(2).broadcast_to([128, 32, 64]), op=ALU.mult),
                  r=[x, dtA], w=[xdt])
            fw.op("dve", lambda e: e.tensor_tensor(out=xdte.ap.rearrange("p (h q) -> p h q", q=64), in0=xdt.ap.rearrange("p (h q) -> p h q", q=64),
                                                   in1=dteA[:, c, :].unsqueeze(2).broadcast_to([128, 32, 64]), op=ALU.mult), r=[xdt, dteA], w=[xdte])
            for g in range(4):
                fw.op("pe", lambda e, g=g: e.matmul(PS[0][:, g * 128:(g + 1) * 128], lhsT=BT[:, g, cs], rhs=CT[:, g, cs], start=True, stop=True), r=[BT, CT], w=[PS[0]])
            fw.op("dve", lambda e: e.tensor_tensor(out=cbm[:, :, :], in0=PS[0].ap.rearrange("p (g l) -> p g l", l=128), in1=CAUS.unsqueeze(1).broadcast_to([128, 4, 128]), op=ALU.mult),
                  r=[PS[0], C], w=[cbm])
            items = []
            for g in range(4):
                for hh in range(2):
                    st = {}

                    def front(g=g, hh=hh, st=st):
                        h0 = g * 8 + hh * 4
                        sg, dc = seg[cnt["k"] % 2], dec[cnt["k"] % 2]
                        cnt["k"] += 1
                        mt = MT[cnt["m"] % 3]
                        cnt["m"] += 1
                        fw.op("dve", lambda e: e.tensor_tensor(out=sg[:, :, :], in0=CAUS.unsqueeze(1).broadcast_to([128, 4, 128]),
                                                               in1=aA[:, c, h0:h0 + 4].unsqueeze(2).broadcast_to([128, 4, 128]), op=ALU.mult), r=[aA, C], w=[sg])
                        pseg = sbank()
                        fw.op("pe", lambda e: e.matmul(pseg[:, :], lhsT=self.cv("TGT"), rhs=sg.ap.rearrange("p a b -> p (a b)"), start=True, stop=True), r=[sg, C], w=[pseg])
                        fw.op("act", lambda e: e.activation(out=dc.ap.rearrange("p a b -> p (a b)"), in_=pseg[:, :], func=AF.Exp), r=[pseg], w=[dc])
                        fw.op("dve", lambda e: e.tensor_tensor(out=mt[:, :, :], in0=dc[:, :, :], in1=cbm[:, g, :].unsqueeze(1).broadcast_to([128, 4, 128]), op=ALU.mult),
                              r=[dc, cbm], w=[mt])
                        st["mt"] = mt

                    def back(g=g, hh=hh, st=st):
                        mt = st["mt"]
                        h0 = g * 8 + hh * 4
                        yd = PS[4 + g % 2]
                        for q in range(4):
                            hd = h0 + q
                            cc = (hh * 4 + q) * 64
                            fw.op("pe", lambda e, q=q, hd=hd, cc=cc: e.matmul(yd[:, cc:cc + 64], lhsT=mt[:, q, :], rhs=xdt[:, hd * 64:(hd + 1) * 64], start=True, stop=True),
                                  r=[mt, xdt], w=[yd])
                        if hh == 1:
                            yo = PS[6]
                            fw.op("pe", lambda e: e.matmul(yo[:, :], lhsT=CT[:, g, cs], rhs=Hb[:, g, :], start=True, stop=True), r=[CT, (Hb, g)], w=[yo])
                            ysl = y.ap[:, g * 512:(g + 1) * 512].rearrange("p (h q) -> p h q", q=64)
                            fw.op("dve", lambda e: e.tensor_tensor(out=ysl, in0=yo.ap.rearrange("p (h q) -> p h q", q=64),
                                                                   in1=eaA[:, c, g * 8:(g + 1) * 8].unsqueeze(2).broadcast_to([128, 8, 64]), op=ALU.mult), r=[yo, eaA], w=[(y, g)])
                            fw.op("dve", lambda e: e.tensor_tensor(out=y[:, g * 512:(g + 1) * 512], in0=yd[:, :], in1=y[:, g * 512:(g + 1) * 512], op=ALU.add), r=[yd, (y, g)], w=[(y, g)])
                            pst = PS[7]
                            fw.op("pe", lambda e: e.matmul(pst[:, :], lhsT=bt[:, g * 128:(g + 1) * 128], rhs=xdte[:, g * 512:(g + 1) * 512], start=True, stop=True),
                                  r=[bt, xdte], w=[pst])
                            Hg = H.ap[:, g, :].rearrange("p (h q) -> p h q", q=64)
                            fw.op("dve", lambda e: e.tensor_tensor(out=Hg, in0=Hg, in1=cdA[:, c, g * 8:(g + 1) * 8].unsqueeze(2).broadcast_to([128, 8, 64]), op=ALU.mult),
                                  r=[(H, g), cdA], w=[(H, g)])
                            fw.op("dve", lambda e: e.tensor_tensor(out=H[:, g, :], in0=pst[:, :], in1=H[:, g, :], op=ALU.add), r=[pst, (H, g)], w=[(H, g)])
                            fw.op("act", lambda e: e.copy(out=Hb[:, g, :], in_=H[:, g, :]), r=[(H, g)], w=[(Hb, g)])

                    items.append((front, back))
            n = len(items)
            for ii in range(n + LA):
                if ii < n:
                    items[ii][0]()
                if ii >= LA:
                    items[ii - LA][1]()

        def stageB(c):
            i = c % 3
            x, z = xc[i], zc[i]
            y = ys_[c % 2]
            x3 = x.ap.rearrange("p (h q) -> p h q", q=64)
            fw.op("dve", lambda e: e.tensor_tensor(out=x3, in0=x3, in1=sv[:, 64:96].unsqueeze(2).broadcast_to([128, 32, 64]), op=ALU.mult), r=[x, sv], w=[x])
            fw.op("dve", lambda e: e.tensor_tensor(out=x[:, :], in0=x[:, :], in1=y[:, :], op=ALU.add), r=[y, x], w=[x])
            fw.op("dve", lambda e: e.tensor_tensor(out=y[:, :], in0=x[:, :], in1=z[:, :], op=ALU.mult), r=[x, z], w=[y])
            for g in range(4):
                fw.op("act", lambda e, g=g: e.activation(out=ysq[:, :], in_=y[:, g * 512:(g + 1) * 512], func=AF.Square), r=[y], w=[ysq])
                fw.op("dve", lambda e, g=g: e.tensor_reduce(out=st4[:, g:g + 1], in_=ysq[:, :], axis=AX.X, op=ALU.add), r=[ysq], w=[st4])
            fw.op("act", lambda e: e.activation(out=st4[:, 4:8], in_=st4[:, 0:4], func=AF.Sqrt, bias=self.EPSC[:, 0:1], scale=1.0 / 512), r=[st4, self.EPSC], w=[st4])
            fw.op("dve", lambda e: e.reciprocal(out=st4[:, 4:8], in_=st4[:, 4:8]), r=[st4], w=[st4])
            for g in range(4):
                fw.op("dve", lambda e, g=g: e.scalar_tensor_tensor(out=y[:, g * 512:(g + 1) * 512], in0=y[:, g * 512:(g + 1) * 512], scalar=st4[:, 4 + g:5 + g],
                                                                   in1=snw[:, g * 512:(g + 1) * 512], op0=ALU.mult, op1=ALU.mult), r=[y, st4, snw], w=[y])
            for t4 in range(4):
                b = sbank()
                for jj in range(4):
                    f = t4 * 4 + jj
                    fw.op("pe", lambda e, b=b, jj=jj, f=f: e.transpose(b[:, jj * 128:(jj + 1) * 128], y[:, f * 128:(f + 1) * 128], self.cv("IDENT")), r=[y, C], w=[b])
                fw.op("act", lambda e, b=b, t4=t4: e.copy(out=yT.ap[:, t4 * 4:(t4 + 1) * 4, :].rearrange("p a b -> p (a b)"), in_=b[:, :]), r=[b], w=[yT])
            fw.dma("sp", self.YT.ap[1024:3072, c * 128:(c + 1) * 128].rearrange("(f p) t -> p f t", p=128), yT[:, :, :], r=[yT], w=[(self.YT, ("s", c))])

        loadc(0)
        loadc(1)
        for c in range(16):
            stageA(c)
            if c >= 1:
                stageB(c - 1)
            if c + 2 < 16:
                loadc(c + 2)
        stageB(15)
        fw.release(m)

    def p6_ret(self, l):
        fw = self.fw
        C = self.C
        m = fw.mark()
        rnw = fw.sb("rnw", [128, 1024])
        fw.dma("sp", rnw[:, :], self.d_rnw.ap[l], r=[self.d_rnw], w=[rnw])
        QT = fw.sb("rQT", [128, 4, NT], BF16)
        KT = fw.sb("rKT", [128, 4, NT], BF16)
        fw.dma("sp", QT[:, :, :], self.RQT.ap.rearrange("(h p) t -> p h t", p=128), r=[self.RQT], w=[QT])
        fw.dma("sp", KT[:, :, :], self.RKT.ap.rearrange("(h p) t -> p h t", p=128), r=[self.RKT], w=[KT])
        QS = fw.sb("rQS", [128, 4, NT], BF16)
        QD = self.cv("QD").rearrange("p (h l) -> p h l", l=128)
        for h in range(4):
            fw.op("dve", lambda e, h=h: e.tensor_tensor(out=QS.ap[:, h, :].rearrange("p (c l) -> p c l", l=128), in0=QT.ap[:, h, :].rearrange("p (c l) -> p c l", l=128),
                                                        in1=QD[:, h, :].unsqueeze(1).broadcast_to([128, 16, 128]), op=ALU.mult), r=[QT, C], w=[QS])
        R = fw.sb("R", [128, 4, 256])
        Rb = fw.sb("Rb", [128, 4, 256], BF16)
        fw.op("dve", lambda e: e.memset(R[:, :, :], 0.0), w=[R])
        fw.op("dve", lambda e: e.memset(Rb[:, :, :], 0.0), w=[Rb])
        vc = [fw.sb("vc%d" % i, [128, 1024], BF16) for i in range(2)]
        kc_ = [fw.sb("kc%d" % i, [128, 512], BF16) for i in range(2)]
        gc = [fw.sb("gc%d" % i, [128, 1024]) for i in range(2)]
        kd = fw.sb("kd", [128, 512], BF16)
        sT = [fw.sb("sT%d" % i, [128, 128], BF16) for i in range(2)]
        ys = [fw.sb("ys%d" % i, [128, 256]) for i in range(2)]
        y2 = fw.sb("y2", [128, 256])
        st = [fw.sb("rst%d" % i, [128, 8]) for i in range(2)]
        yr = fw.sb("yr", [128, 1024])
        yT = fw.sb("ryT", [128, 8, 128], BF16)
        DMT = self.cv("DMT").rearrange("p (h l) -> p h l", l=128)
        KD = self.cv("KD")
        k = 0
        for c in range(16):
            i = c % 2
            v, kk, gg = vc[i], kc_[i], gc[i]
            rows = slice(c * 128, (c + 1) * 128)
            fw.dma("sp", v[:, :], self.RV[rows, :], r=[self.RV], w=[v])
            fw.dma("sp", kk[:, :], self.RKTOK[rows, :], r=[self.RKTOK], w=[kk])
            fw.dma("sp", gg[:, :], self.RGS[rows, :], r=[self.RGS], w=[gg])
            fw.op("dve", lambda e, kk=kk: e.tensor_tensor(out=kd.ap.rearrange("p (h d) -> p h d", d=128), in0=kk.ap.rearrange("p (h d) -> p h d", d=128),
                                                         in1=KD.unsqueeze(2).broadcast_to([128, 4, 128]), op=ALU.mult), r=[kk, C], w=[kd])
            for h in range(4):
                cs = slice(c * 128, (c + 1) * 128)
                ps = self.bank()
                fw.op("pe", lambda e, ps=ps, h=h, cs=cs: e.matmul(ps[:, 0:128], lhsT=KT[:, h, cs], rhs=QT[:, h, cs], start=True, stop=True), r=[KT, QT], w=[ps])
                s_ = sT[k % 2]
                yy = ys[k % 2]
                s8 = st[k % 2]
                k += 1
                fw.op("dve", lambda e, ps=ps, s_=s_, h=h: e.tensor_tensor(out=s_[:, :], in0=ps[:, 0:128], in1=DMT[:, h, :], op=ALU.mult), r=[ps, C], w=[s_])
                py = self.bank()
                fw.op("pe", lambda e, py=py, s_=s_, v=v, h=h: e.matmul(py[:, 0:256], lhsT=s_[:, :], rhs=v[:, h * 256:(h + 1) * 256], start=True, stop=False), r=[s_, v], w=[py])
                fw.op("pe", lambda e, py=py, h=h, cs=cs: e.matmul(py[:, 0:256], lhsT=QS[:, h, cs], rhs=Rb[:, h, :], start=False, stop=True), r=[QS, (Rb, h)], w=[py])
                pk = self.bank()
                fw.op("pe", lambda e, pk=pk, h=h, v=v: e.matmul(pk[:, 0:256], lhsT=kd[:, h * 128:(h + 1) * 128], rhs=v[:, h * 256:(h + 1) * 256], start=True, stop=True), r=[kd, v], w=[pk])
                fw.op("dve", lambda e, pk=pk, h=h: e.scalar_tensor_tensor(out=R[:, h, :], in0=R[:, h, :], scalar=self.cdec[h], in1=pk[:, 0:256], op0=ALU.mult, op1=ALU.add),
                      r=[pk, (R, h)], w=[(R, h)])
                fw.op("act", lambda e, h=h: e.copy(out=Rb[:, h, :], in_=R[:, h, :]), r=[(R, h)], w=[(Rb, h)])
                fw.op("act", lambda e, py=py, yy=yy: e.copy(out=yy[:, :], in_=py[:, 0:256]), r=[py], w=[yy])
                fw.op("dve", lambda e, yy=yy, s8=s8: e.tensor_reduce(out=s8[:, 0:1], in_=yy[:, :], axis=AX.X, op=ALU.add), r=[yy], w=[s8])
                fw.op("dve", lambda e, s8=s8: e.tensor_scalar(out=s8[:, 1:2], in0=s8[:, 0:1], scalar1=1.0 / 256, scalar2=None, op0=ALU.mult), r=[s8], w=[s8])
                fw.op("dve", lambda e, yy=yy, s8=s8: e.tensor_scalar(out=yy[:, :], in0=yy[:, :], scalar1=s8[:, 1:2], scalar2=None, op0=ALU.subtract), r=[yy, s8], w=[yy])
                fw.op("act", lambda e, yy=yy: e.activation(out=y2[:, :], in_=yy[:, :], func=AF.Square), r=[yy], w=[y2])
                fw.op("dve", lambda e, s8=s8: e.tensor_reduce(out=s8[:, 2:3], in_=y2[:, :], axis=AX.X, op=ALU.add), r=[y2], w=[s8])
                fw.op("act", lambda e, s8=s8: e.activation(out=s8[:, 3:4], in_=s8[:, 2:3], func=AF.Sqrt, bias=self.EPSC[:, 0:1], scale=1.0 / 256), r=[s8, self.EPSC], w=[s8])
                fw.op("dve", lambda e, s8=s8: e.reciprocal(out=s8[:, 3:4], in_=s8[:, 3:4]), r=[s8], w=[s8])
                fw.op("dve", lambda e, yy=yy, s8=s8, h=h: e.scalar_tensor_tensor(out=yy[:, :], in0=yy[:, :], scalar=s8[:, 3:4], in1=rnw[:, h * 256:(h + 1) * 256], op0=ALU.mult, op1=ALU.mult),
                      r=[yy, s8, rnw], w=[yy])
                fw.op("dve", lambda e, yy=yy, gg=gg, h=h: e.tensor_tensor(out=yr[:, h * 256:(h + 1) * 256], in0=yy[:, :], in1=gg[:, h * 256:(h + 1) * 256], op=ALU.mult), r=[yy, gg], w=[(yr, h)])
            for t4 in range(2):
                b = self.bank()
                for jj in range(4):
                    f = t4 * 4 + jj
                    fw.op("pe", lambda e, b=b, jj=jj, f=f: e.transpose(b[:, jj * 128:(jj + 1) * 128], yr[:, f * 128:(f + 1) * 128], self.cv("IDENT")), r=[yr, C], w=[b])
                fw.op("act", lambda e, b=b, t4=t4: e.copy(out=yT.ap[:, t4 * 4:(t4 + 1) * 4, :].rearrange("p a b -> p (a b)"), in_=b[:, :]), r=[b], w=[yT])
            fw.dma("sp", self.YT.ap[3072:4096, c * 128:(c + 1) * 128].rearrange("(f p) t -> p f t", p=128), yT[:, :, :], r=[yT], w=[(self.YT, ("r", c))])
        fw.release(m)

    def p7_merge(self, l, src):
        fw = self.fw
        m = fw.mark()
        self.WB = [fw.sb("WB%d" % i, [128, 11264], BF16) for i in range(3)]
        yt = fw.sb("ytall", [128, 32, 512], BF16)
        mg = [fw.sb("mg%d" % i, [128, 512], BF16) for i in range(3)]
        mer = fw.sb("mer", [128, 16, 512])
        merb = fw.sb("merb", [128, 16, 512], BF16)
        tmp = [fw.sb("mtmp%d" % i, [128, 512]) for i in range(2)]
        xt = [fw.sb("mxt%d" % i, [128, 512]) for i in range(2)]
        branches = ((self.p_nsa, 8, 0, 0), (self.p_ssd, 16, 8, 1), (self.p_ret, 8, 24, 2))
        k = 0
        for tg in range(4):
            ts = slice(tg * 512, (tg + 1) * 512)
            for k0 in range(0, 32, 8):
                fw.dma("sp", yt[:, k0:k0 + 8, :], self.YT.ap[k0 * 128:(k0 + 8) * 128, ts].rearrange("(k p) t -> p k t", p=128), r=[self.YT], w=[(yt, k0)])
            for (W, KC, koff, bi) in branches:
                for cb in range(4):
                    buf, view = self.wload(W, W.ap[l][:, cb * 512:(cb + 1) * 512], KC, 512)
                    for j in range(4):
                        ct = cb * 4 + j
                        g = mg[k % 3]
                        t_ = tmp[k % 2]
                        k += 1
                        r0 = bi * 2048 + ct * 128
                        fw.dma("sp", g[:, :], self.MGT[r0:r0 + 128, ts], r=[self.MGT], w=[g])
                        ps = self.bank()
                        for kc in range(KC):
                            fw.op("pe", lambda e, ps=ps, kc=kc, j=j, view=view, koff=koff, KC=KC: e.matmul(ps[:, :], lhsT=view[:, kc, j * 128:(j + 1) * 128], rhs=yt[:, koff + kc, :],
                                                                                                   start=(kc == 0), stop=(kc == KC - 1)), r=[buf, yt], w=[ps])
                        if bi == 0:
                            fw.op("dve", lambda e, ps=ps, g=g, ct=ct: e.tensor_tensor(out=mer[:, ct, :], in0=ps[:, :], in1=g[:, :], op=ALU.mult), r=[ps, g], w=[(mer, ct)])
                        else:
                            fw.op("dve", lambda e, ps=ps, g=g, t_=t_: e.tensor_tensor(out=t_[:, :], in0=ps[:, :], in1=g[:, :], op=ALU.mult), r=[ps, g], w=[t_])
                            fw.op("dve", lambda e, t_=t_, ct=ct: e.tensor_tensor(out=mer[:, ct, :], in0=mer[:, ct, :], in1=t_[:, :], op=ALU.add), r=[t_, (mer, ct)], w=[(mer, ct)])
            for ct in range(16):
                fw.op("act", lambda e, ct=ct: e.copy(out=merb[:, ct, :], in_=mer[:, ct, :]), r=[(mer, ct)], w=[(merb, ct)])
            for cb in range(4):
                buf, view = self.wload(self.w_out, self.w_out.ap[l][:, cb * 512:(cb + 1) * 512], 16, 512)
                for j in range(4):
                    ct = cb * 4 + j
                    x = xt[k % 2]
                    k += 1
                    fw.dma("sp", x[:, :], src[ct * 128:(ct + 1) * 128, ts], r=[(src, (ct, tg))], w=[x])
                    ps = self.bank()
                    for kc in range(16):
                        fw.op("pe", lambda e, ps=ps, kc=kc, j=j, view=view: e.matmul(ps[:, :], lhsT=view[:, kc, j * 128:(j + 1) * 128], rhs=merb[:, kc, :], start=(kc == 0), stop=(kc == 15)),
                              r=[buf, merb], w=[ps])
                    fw.op("dve", lambda e, ps=ps, x=x: e.tensor_tensor(out=x[:, :], in0=ps[:, :], in1=x[:, :], op=ALU.add), r=[ps, x], w=[x])
                    fw.dma("sp", self.XT[ct * 128:(ct + 1) * 128, ts], x[:, :], r=[x], w=[(self.XT, (ct, tg))])
        fw.release(m)

    def p8_ffn(self, l):
        fw = self.fw
        m = fw.mark()
        self.WB = [fw.sb("WB%d" % i, [128, 11264], BF16) for i in range(3)]
        xg = fw.sb("fxg", [128, 16, 512])
        fT = fw.sb("fT", [128, 16, 512], BF16)
        hT = fw.sb("hT", [128, 44, 512], BF16)
        sq = [fw.sb("fsq%d" % i, [128, 512]) for i in range(2)]
        rs = fw.sb("frs", [128, 512])
        sg = [fw.sb("fsg%d" % i, [128, 512]) for i in range(2)]
        xo = [fw.sb("fxo%d" % i, [128, 512]) for i in range(2)]
        wcol = self.NRM[:, (2 * l + 1) * 16:(2 * l + 2) * 16]
        xv = self.XT.ap.rearrange("(k p) t -> p k t", p=128)
        k = 0
        for tg in range(4):
            ts = slice(tg * 512, (tg + 1) * 512)
            for k0 in range(0, 16, 4):
                fw.dma("sp", xg[:, k0:k0 + 4, :], xv[:, k0:k0 + 4, ts], r=[self.XT], w=[(xg, k0)])
            self.rmsnorm(xg, 512, wcol, lambda kc: (fT, fT[:, kc, :]), sq, rs)
            for cb in range(11):
                bg, vg = self.wload(self.w_gate, self.w_gate.ap[l][:, cb * 512:(cb + 1) * 512], 16, 512)
                bu, vu = self.wload(self.w_up, self.w_up.ap[l][:, cb * 512:(cb + 1) * 512], 16, 512)
                for j in range(4):
                    f = cb * 4 + j
                    pg = self.bank()
                    pu = self.bank()
                    for kc in range(16):
                        fw.op("pe", lambda e, pg=pg, kc=kc, j=j, vg=vg: e.matmul(pg[:, :], lhsT=vg[:, kc, j * 128:(j + 1) * 128], rhs=fT[:, kc, :], start=(kc == 0), stop=(kc == 15)),
                              r=[bg, fT], w=[pg])
                    for kc in range(16):
                        fw.op("pe", lambda e, pu=pu, kc=kc, j=j, vu=vu: e.matmul(pu[:, :], lhsT=vu[:, kc, j * 128:(j + 1) * 128], rhs=fT[:, kc, :], start=(kc == 0), stop=(kc == 15)),
                              r=[bu, fT], w=[pu])
                    s = sg[k % 2]
                    k += 1
                    fw.op("act", lambda e, pg=pg, s=s: e.activation(out=s[:, :], in_=pg[:, :], func=AF.Silu), r=[pg], w=[s])
                    fw.op("dve", lambda e, pu=pu, s=s, f=f: e.tensor_tensor(out=hT[:, f, :], in0=pu[:, :], in1=s[:, :], op=ALU.mult), r=[pu, s], w=[(hT, f)])
            for cb in range(8):
                bd, vd = self.wload(self.w_down, self.w_down.ap[l][:, cb * 256:(cb + 1) * 256], 44, 256)
                for j in range(2):
                    ct = cb * 2 + j
                    ps = self.bank()
                    for kc in range(44):
                        fw.op("pe", lambda e, ps=ps, kc=kc, j=j, vd=vd: e.matmul(ps[:, :], lhsT=vd[:, kc, j * 128:(j + 1) * 128], rhs=hT[:, kc, :], start=(kc == 0), stop=(kc == 43)),
                              r=[bd, hT], w=[ps])
                    x = xo[k % 2]
                    k += 1
                    fw.op("dve", lambda e, ps=ps, x=x, ct=ct: e.tensor_tensor(out=x[:, :], in0=ps[:, :], in1=xg[:, ct, :], op=ALU.add), r=[ps, xg], w=[x])
                    fw.dma("sp", self.XT[ct * 128:(ct + 1) * 128, ts], x[:, :], r=[x], w=[(self.XT, (ct, tg))])
        fw.release(m)

    def final_norm(self, src):
        fw = self.fw
        m = fw.mark()
        xg = fw.sb("nxg", [128, 16, 512])
        og = fw.sb("nog", [128, 16, 512])
        sq = [fw.sb("nsq%d" % i, [128, 512]) for i in range(2)]
        rs = fw.sb("nrs", [128, 512])
        wcol = self.NRM[:, (2 * NL) * 16:(2 * NL + 1) * 16]
        xv = src.ap.rearrange("(k p) t -> p k t", p=128)
        ov = self.outT.ap.rearrange("(k p) t -> p k t", p=128)
        for tg in range(4):
            ts = slice(tg * 512, (tg + 1) * 512)
            for k0 in range(0, 16, 4):
                fw.dma("sp", xg[:, k0:k0 + 4, :], xv[:, k0:k0 + 4, ts], r=[src], w=[(xg, k0)])
            self.rmsnorm(xg, 512, wcol, lambda kc: (og, og[:, kc, :]), sq, rs)
            for k0 in range(0, 16, 4):
                fw.dma("sp", ov[:, k0:k0 + 4, ts], og[:, k0:k0 + 4, :], r=[og], w=[(self.outT, (tg, k0))])
        fw.release(m)


def host_inputs(inp, b):
    consts, tab, et, _ = host_consts()
    f = np.float32
    nrm = np.zeros((128, (2 * NL + 1) * 16), f)
    for l in range(NL):
        nrm[:, (2 * l) * 16:(2 * l + 1) * 16] = np.asarray(inp["norm_mix"][l], f).reshape(16, 128).T
        nrm[:, (2 * l + 1) * 16:(2 * l + 2) * 16] = np.asarray(inp["norm_ffn"][l], f).reshape(16, 128).T
    nrm[:, 2 * NL * 16:] = np.asarray(inp["norm_final"], f).reshape(16, 128).T
    cw = np.zeros((NL, 128, 24, 5), f)
    for l in range(NL):
        cw[l, :, :, 0:4] = np.asarray(inp["conv_w"][l], f).T.reshape(24, 128, 4).transpose(1, 0, 2)
        cw[l, :, :, 4] = np.asarray(inp["conv_b"][l], f).reshape(24, 128).T
    ssdv = np.zeros((NL, 128, 96), f)
    ssdv[:, :, 0:32] = np.asarray(inp["dt_bias"], f)[:, None, :]
    ssdv[:, :, 32:64] = np.asarray(inp["a_log"], f)[:, None, :]
    ssdv[:, :, 64:96] = np.asarray(inp["d_skip"], f)[:, None, :]
    snw = np.ascontiguousarray(np.broadcast_to(np.asarray(inp["ssd_norm"], f)[:, None, :], (NL, 128, 2048)))
    rnw = np.ascontiguousarray(np.broadcast_to(np.asarray(inp["ret_norm"], f).reshape(NL, 1, 1024), (NL, 128, 1024)))
    pet = np.stack([np.asarray(inp["cmp_k_pe"], f).transpose(0, 2, 1), np.asarray(inp["cmp_v_pe"], f).transpose(0, 2, 1)], axis=1)
    d = {
        "xT": np.ascontiguousarray(np.asarray(inp["x"][b], f).T),
        "consts": consts, "tab": tab, "et": et, "nrm": nrm, "cw": cw.reshape(NL, 128, 120), "ssdv": ssdv,
        "snw": snw, "rnw": rnw, "pet": np.ascontiguousarray(pet),
    }
    for k in ("w_in", "cmp_k_w1", "cmp_v_w1", "cmp_k_w2", "cmp_v_w2", "p_nsa", "p_ssd", "p_ret", "w_out", "w_gate", "w_up", "w_down"):
        d[k] = np.ascontiguousarray(np.asarray(inp[k], f))
    return d


_PROG = {}


def kernel(**inputs):
    if "p" not in _PROG:
        _PROG["p"] = Prog()
    prog = _PROG["p"]
    ncores = 4
    base = host_inputs(inputs, 0)
    in_maps = []
    for b in range(ncores):
        d = dict(base)
        d["xT"] = np.ascontiguousarray(np.asarray(inputs["x"][b], np.float32).T)
        in_maps.append(d)
    res = run_bass_kernel_spmd(prog.nc, in_maps, core_ids=list(range(ncores)))
    out = np.stack([np.asarray(res.results[b]["outT"], np.float32).T for b in range(4)], axis=0)
    return np.ascontiguousarray(out)
```

```python
import numpy as np
import concourse.bass as bass
import concourse.mybir as mybir
from concourse.bass_utils import run_bass_kernel_spmd
from contextlib import ExitStack

F32 = mybir.dt.float32
BF16 = mybir.dt.bfloat16
ALU = mybir.AluOpType
AF = mybir.ActivationFunctionType
AX = mybir.AxisListType

COMPUTE = ("pe", "dve", "act", "pool")
NSLOT = {"sp": 16, "pool": 16, "act": 8}
ARENA = 53000

NT = 2048
D = 2048
DIN = 16952
DFF = 5632
NL = 4
C_Q, C_KV, C_NG, C_Z, C_XBC, C_DT, C_RQ, C_RK, C_RV, C_RG, C_MG = 0, 1024, 2560, 2584, 4632, 7704, 7736, 8248, 8760, 9784, 10808
NEG = -30000.0
EPS = 1e-6


class St:
    __slots__ = ("lw", "rd")

    def __init__(self, o=None):
        self.lw = o.lw if o else None
        self.rd = list(o.rd) if o else []


class T:
    def __init__(self, ap, name, excl=False):
        self.ap = ap
        self.name = name
        self.excl = excl
        self.st = {None: St()}

    def __getitem__(self, k):
        return self.ap[k]

    def states(self, key):
        if key is None:
            return list(self.st.values())
        if key not in self.st:
            self.st[key] = St(self.st[None])
        return [self.st[key]]


class _Rec:
    def __getattr__(self, name):
        return lambda *a, **k: (name, a, k)


_REC = _Rec()


class FW:
    def __init__(self, nc):
        self.nc = nc
        self.stream = {e: [] for e in ("pe", "dve", "act", "pool", "sp")}
        self.nops = {e: 0 for e in COMPUTE}
        self.sig = {e: set() for e in COMPUTE}
        self.ndma = {q: 0 for q in NSLOT}
        self.known = {e: {} for e in self.stream}
        self.es = ExitStack()
        self.sems = {}
        self.dsems = {}
        self.arena = self.es.enter_context(nc.sbuf_tensor("arena", [128, ARENA], F32))
        self.aoff = 0
        self.PS = [T(self.es.enter_context(nc.psum_tensor("ps%d" % i, [128, 512], F32))[:, :], "ps%d" % i, excl=True)
                   for i in range(8)]

    def sb(self, name, shape, dtype=F32):
        p = shape[0]
        n = int(np.prod(shape[1:]))
        nb = n * (4 if dtype == F32 else 2)
        nf = ((nb + 63) // 64) * 16
        assert self.aoff + nf <= ARENA, ("SBUF arena overflow", name, self.aoff, nf)
        ap = self.arena[0:p, self.aoff:self.aoff + nf]
        self.aoff += nf
        if dtype != F32:
            ap = ap.bitcast(dtype)
        ap = ap[:, 0:n]
        if len(shape) == 3:
            ap = ap.rearrange("p (a b) -> p a b", a=shape[1], b=shape[2])
        elif len(shape) == 4:
            ap = ap.rearrange("p (a b c) -> p a b c", a=shape[1], b=shape[2], c=shape[3])
        return T(ap, name)

    def mark(self):
        return self.aoff

    def release(self, m):
        self.barrier()
        self.aoff = m

    def dram(self, name, shape, dtype, kind="Internal"):
        h = self.nc.dram_tensor(name, list(shape), dtype, kind=kind)
        return T(h.ap(), name)

    def _need(self, eng, ev, waits):
        if ev is None:
            return
        if ev[0] == "c":
            _, f, idx = ev
            if f == eng and eng == "pe":
                return
            k = ("c", f)
            if self.known[eng].get(k, 0) >= idx:
                return
            self.known[eng][k] = idx
            self.sig[f].add(idx)
            waits.append(ev)
        else:
            _, q, j = ev
            k = ("d", q, j % NSLOT[q])
            if self.known[eng].get(k, -1) >= j:
                return
            self.known[eng][k] = j
            waits.append(ev)

    def _deps(self, eng, r, w, is_dma):
        waits = []
        rs, ws = [], []
        for x in r:
            t, key = x if isinstance(x, tuple) else (x, None)
            (ws if t.excl else rs).extend(t.states(key))
        for x in w:
            t, key = x if isinstance(x, tuple) else (x, None)
            ws.extend(t.states(key))
        for s in rs:
            self._need(eng, s.lw, waits)
        for s in ws:
            lw = s.lw
            if lw is not None and not ((not is_dma) and lw[0] == "c" and lw[1] == eng):
                self._need(eng, lw, waits)
            for ev in s.rd:
                if (not is_dma) and ev[0] == "c" and ev[1] == eng:
                    continue
                self._need(eng, ev, waits)
        return waits, rs, ws

    def _commit(self, ev, rs, ws):
        for s in rs:
            if ev[0] == "c":
                s.rd = [e for e in s.rd if not (e[0] == "c" and e[1] == ev[1])]
            s.rd.append(ev)
        for s in ws:
            s.lw = ev
            s.rd = []

    def op(self, eng, fn, r=(), w=()):
        waits, rs, ws = self._deps(eng, r, w, False)
        for ev in waits:
            self.stream[eng].append(("wait", ev))
        self.nops[eng] += 1
        idx = self.nops[eng]
        self.stream[eng].append(("op", fn(_REC), idx))
        self._commit(("c", eng, idx), rs, ws)

    def dma(self, q, out_ap, in_ap, r=(), w=()):
        waits, rs, ws = self._deps(q, r, w, True)
        j = self.ndma[q]
        K = NSLOT[q]
        if j >= K:
            self._need(q, ("d", q, j - K), waits)
        for ev in waits:
            self.stream[q].append(("wait", ev))
        self.ndma[q] += 1
        self.stream[q].append(("dma", out_ap, in_ap, j))
        self._commit(("d", q, j), rs, ws)

    def barrier(self):
        for e in self.stream:
            waits = []
            for f in COMPUTE:
                if self.nops[f] > 0:
                    self._need(e, ("c", f, self.nops[f]), waits)
            for q in NSLOT:
                for j in range(max(0, self.ndma[q] - NSLOT[q]), self.ndma[q]):
                    self._need(e, ("d", q, j), waits)
            for ev in waits:
                self.stream[e].append(("wait", ev))

    def emit(self):
        nc = self.nc
        self.barrier()
        es = self.es
        for e in COMPUTE:
            self.sems[e] = es.enter_context(nc.semaphore("s_" + e))
        for q in NSLOT:
            self.dsems[q] = [es.enter_context(nc.semaphore("d_%s_%d" % (q, i))) for i in range(NSLOT[q])]
        cum = {}
        for e in COMPUTE:
            c = 0
            m = {}
            for i in range(1, self.nops[e] + 1):
                if i in self.sig[e]:
                    c += 1
                    m[i] = c
            cum[e] = m

        def replay(ename, eng):
            for rec in self.stream[ename]:
                if rec[0] == "wait":
                    ev = rec[1]
                    if ev[0] == "c":
                        eng.wait_ge(self.sems[ev[1]], cum[ev[1]][ev[2]])
                    else:
                        _, q, j = ev
                        eng.wait_ge(self.dsems[q][j % NSLOT[q]], 16 * (j // NSLOT[q] + 1))
                elif rec[0] == "op":
                    nm, ar, kw = rec[1]
                    ins = getattr(eng, nm)(*ar, **kw)
                    if rec[2] in self.sig[ename]:
                        ins.then_inc(self.sems[ename], 1)
                elif rec[0] == "cc":
                    _, kind, i, o, rg, j = rec
                    eng.collective_compute(kind, ALU.bypass, replica_groups=rg, ins=[i], outs=[o]).then_inc(self.dsems[ename][j % NSLOT[ename]], 16)
                else:
                    _, o, i, j = rec
                    eng.dma_start(out=o, in_=i).then_inc(self.dsems[ename][j % NSLOT[ename]], 16)

        with nc.Block() as block:
            @block.tensor
            def _(e):
                replay("pe", e)

            @block.vector
            def _(e):
                replay("dve", e)

            @block.scalar
            def _(e):
                replay("act", e)

            @block.gpsimd
            def _(e):
                replay("pool", e)

            @block.sync
            def _(e):
                replay("sp", e)
        es.close()


CO = {}
_off = 0
for _n, _w in (("IDENT", 128), ("CAUS", 128), ("TGT", 128), ("ONESM", 128), ("ONES1", 128), ("MSK", 1536),
               ("OVL", 33), ("DMT", 512), ("QD", 512), ("KD", 4)):
    CO[_n] = (_off, _w)
    _off += _w
NCONST = _off
TABW = 896 + 512 + 1408 + 2048
TO_U, TO_W, TO_UW, TO_C = 0, 896, 1408, 2816


def host_consts():
    c = np.zeros((128, NCONST), np.float64)
    p = np.arange(128)[:, None]
    f = np.arange(128)[None, :]
    c[:, CO["IDENT"][0]:][:, :128] = (p == f)
    c[:, CO["CAUS"][0]:][:, :128] = (f >= p)
    c[:, CO["TGT"][0]:][:, :128] = (p > f)
    c[:, CO["ONESM"][0]:][:, :128] = 1.0 / D
    c[:, CO["ONES1"][0]:][:, :128] = 1.0
    msk = np.zeros((128, 3, 16, 32))
    for Q in range(16):
        tq = Q * 128 + np.arange(128)
        cur = (tq // 64)[:, None]
        blk = np.arange(32)[None, :]
        msk[:, 0, Q, :] = ((blk > 0) & (blk < cur))
        msk[:, 1, Q, :] = np.where((blk == cur) | (blk == 0), 1e9, np.where(blk > cur, -1e30, 0.0))
        msk[:, 2, Q, :] = (blk <= cur)
    c[:, CO["MSK"][0]:][:, :1536] = msk.reshape(128, -1)
    n = np.arange(128)[:, None]
    k = np.arange(32)[None, :]
    ovl = ((16 * n < 64 * k + 64) & (16 * n + 31 >= 64 * k) & (n < 127)).astype(np.float64)
    c[:, CO["OVL"][0]] = 1.0
    c[:, CO["OVL"][0] + 1:][:, :32] = ovl
    h = np.arange(4, dtype=np.float64)
    log_g = np.log1p(-np.exp2(-5.0 - h))
    s_ = np.arange(128)[:, None]
    l_ = np.arange(128)[None, :]
    for hh in range(4):
        dm = np.where(l_ >= s_, np.exp((l_ - s_) * log_g[hh]), 0.0)
        c[:, CO["DMT"][0] + hh * 128:][:, :128] = dm
        c[:, CO["QD"][0] + hh * 128:][:, :128] = np.exp((np.arange(128) + 1.0) * log_g[hh])[None, :]
        c[:, CO["KD"][0] + hh] = np.exp((127.0 - np.arange(128)) * log_g[hh])
    cdec = [float(np.exp(128 * log_g[hh])) for hh in range(4)]
    tab = np.zeros((8, 128, TABW), np.float64)
    ki = np.arange(128)[:, None]
    for hd in range(8):
        s = 2.0 ** (-(hd + 1))
        cc = np.arange(896)[None, :]
        rel = cc - 384 - ki
        tab[hd, :, TO_U:TO_U + 896] = np.where(rel >= 0, -s * rel, NEG)
        qi = np.arange(512)[None, :]
        tab[hd, :, TO_W:TO_W + 512] = -s * (qi - ki)
        cc = np.arange(1408)[None, :]
        rel = cc - 384 - ki
        tab[hd, :, TO_UW:TO_UW + 1408] = np.where((rel >= 0) & (rel < 512), -s * rel, NEG)
        tq = np.arange(2048)[None, :]
        rel = tq - (16 * ki + 31)
        tab[hd, :, TO_C:TO_C + 2048] = np.where((rel >= 0) & (ki < 127), -s * rel, NEG)
    et = (np.arange(2048)[None, :] // 64 == np.arange(32)[:, None]).astype(np.float32)
    return c.astype(np.float32), tab.astype(np.float32), et, cdec


class Prog:
    def __init__(self, n_layers=NL, dbg=False, stop_after=None):
        self.nl = n_layers
        self.dbg = dbg
        self.stop_after = stop_after
        self.nc = bass.Bass("TRN2", target_bir_lowering=False)
        self.fw = FW(self.nc)
        self.cdec = host_consts()[3]
        self.rot = 0
        self.build()

    def bank(self):
        b = self.fw.PS[self.rot % 8]
        self.rot += 1
        return b

    def din(self, name, shape, dtype=F32):
        return self.fw.dram(name, shape, dtype, kind="ExternalInput")

    def scr(self, name, shape, dtype):
        return self.fw.dram(name, shape, dtype, kind="ExternalOutput" if self.dbg else "Internal")

    def wload(self, Wt, w_ap, KC, n):
        fw = self.fw
        buf = self.WB[self.wrot % len(self.WB)]
        self.wrot += 1
        view = buf.ap[:, 0:KC * n].rearrange("p (k c) -> p k c", k=KC, c=n)
        src = w_ap.rearrange("(k p) c -> p k c", p=128)
        step = 4
        for i, k0 in enumerate(range(0, KC, step)):
            k1 = min(KC, k0 + step)
            fw.dma("pool", view[:, k0:k1, :], src[:, k0:k1, :], r=[Wt], w=[(buf, i)])
        return buf, view

    def rmsnorm(self, xg, ntok, wcol, out_fn, sq, rs):
        fw = self.fw
        C = self.C
        ps = self.bank()
        for kc in range(16):
            s = sq[kc % 2]
            fw.op("act", lambda e, s=s, kc=kc: e.activation(out=s[:, 0:ntok], in_=xg[:, kc, :], func=AF.Square), r=[xg], w=[s])
            fw.op("pe", lambda e, s=s, kc=kc: e.matmul(ps[:, 0:ntok], lhsT=self.cv("ONESM"), rhs=s[:, 0:ntok], start=(kc == 0), stop=(kc == 15)),
                  r=[s, C], w=[ps])
        fw.op("act", lambda e: e.activation(out=rs[:, 0:ntok], in_=ps[:, 0:ntok], func=AF.Sqrt, bias=self.EPSC[:, 0:1], scale=1.0), r=[ps, self.EPSC], w=[rs])
        fw.op("dve", lambda e: e.reciprocal(out=rs[:, 0:ntok], in_=rs[:, 0:ntok]), r=[rs], w=[rs])
        for kc in range(16):
            ot, oap = out_fn(kc)
            fw.op("dve", lambda e, kc=kc, oap=oap: e.scalar_tensor_tensor(out=oap, in0=xg[:, kc, :], scalar=wcol[:, kc:kc + 1], in1=rs[:, 0:ntok],
                                                                         op0=ALU.mult, op1=ALU.mult), r=[xg, rs, self.NRM], w=[ot])

    def cv(self, name, width=None):
        o, w = CO[name]
        return self.C[:, o:o + (width or w)]

    def cvt(self, name):
        if name == "EPSC":
            return self.EPSC
        raise KeyError(name)

    def build(self):
        fw = self.fw
        nl = self.nl
        self.xT_in = self.din("xT", [D, NT])
        self.d_consts = self.din("consts", [128, NCONST])
        self.d_tab = self.din("tab", [8, 128, TABW])
        self.d_et = self.din("et", [32, NT])
        self.d_nrm = self.din("nrm", [128, (2 * NL + 1) * 16])
        self.d_cw = self.din("cw", [NL, 128, 24 * 5])
        self.d_ssdv = self.din("ssdv", [NL, 128, 96])
        self.d_snw = self.din("snw", [NL, 128, 2048])
        self.d_rnw = self.din("rnw", [NL, 128, 1024])
        self.d_pet = self.din("pet", [NL, 2, 128, 32])
        self.w_in = self.din("w_in", [NL, D, DIN])
        self.cw1 = [self.din("cmp_k_w1", [NL, 4096, 128]), self.din("cmp_v_w1", [NL, 4096, 128])]
        self.cw2 = [self.din("cmp_k_w2", [NL, 128, 128]), self.din("cmp_v_w2", [NL, 128, 128])]
        self.p_nsa = self.din("p_nsa", [NL, 1024, D])
        self.p_ssd = self.din("p_ssd", [NL, 2048, D])
        self.p_ret = self.din("p_ret", [NL, 1024, D])
        self.w_out = self.din("w_out", [NL, D, D])
        self.w_gate = self.din("w_gate", [NL, D, DFF])
        self.w_up = self.din("w_up", [NL, D, DFF])
        self.w_down = self.din("w_down", [NL, DFF, D])
        self.outT = fw.dram("outT", [D, NT], F32, kind="ExternalOutput")
        self.XT = self.scr("XT", [D, NT], F32)
        self.QT = self.scr("QT", [1024, NT], BF16)
        self.KVT = self.scr("KVT", [1024, NT], BF16)
        self.VTOK = self.scr("VTOK", [NT, 512], BF16)
        self.NG = self.scr("NG", [NT, 24], F32)
        self.ZS = self.scr("ZS", [NT, 2048], F32)
        self.XBCT = self.scr("XBCT", [3072, NT], F32)
        self.DTR = self.scr("DTR", [NT, 32], F32)
        self.RQT = self.scr("RQT", [512, NT], BF16)
        self.RKT = self.scr("RKT", [512, NT], BF16)
        self.RKTOK = self.scr("RKTOK", [NT, 512], BF16)
        self.RV = self.scr("RV", [NT, 1024], BF16)
        self.RGS = self.scr("RGS", [NT, 1024], F32)
        self.MGT = self.scr("MGT", [6144, NT], BF16)
        self.XTOK = self.scr("XTOK", [NT, 2048], F32)
        self.BCT = self.scr("BCT", [1024, NT], BF16)
        self.BTOK = self.scr("BTOK", [NT, 512], BF16)
        self.YT = self.scr("YT", [4096, NT], BF16)

        self.C = fw.sb("C", [128, NCONST])
        self.NRM = fw.sb("NRM", [128, (2 * NL + 1) * 16])
        self.EPSC = fw.sb("EPSC", [128, 2])
        fw.dma("sp", self.C[:, :], self.d_consts[:, :], r=[self.d_consts], w=[self.C])
        fw.dma("sp", self.NRM[:, :], self.d_nrm[:, :], r=[self.d_nrm], w=[self.NRM])
        fw.op("dve", lambda e: e.memset(self.EPSC[:, 0:1], EPS), w=[self.EPSC])
        fw.op("dve", lambda e: e.memset(self.EPSC[:, 1:2], 1.0), w=[self.EPSC])
        self.wrot = 0
        self.base = fw.mark()

        src = self.xT_in
        for l in range(nl):
            self.layer(l, src)
            src = self.XT
            if self.stop_after is not None:
                break
        if self.stop_after is None:
            self.final_norm(src)
        fw.emit()

    def layer(self, l, src):
        sa = self.stop_after
        self.p1_inproj(l, src)
        if sa == "p1":
            return
        self.p2_ssdprep(l)
        if sa == "p2":
            return
        self.p34_nsa(l)
        if sa == "p4":
            return
        self.p5_ssd(l)
        if sa == "p5":
            return
        self.p6_ret(l)
        if sa == "p6":
            return
        self.p7_merge(l, src)
        if sa == "p7":
            return
        self.p8_ffn(l)

    def p1_inproj(self, l, src):
        fw = self.fw
        m = fw.mark()
        self.WB = [fw.sb("WB%d" % i, [128, 11264], BF16) for i in range(3)]
        uT = fw.sb("uT", [128, 16, NT], BF16)
        stF = [fw.sb("stF%d" % i, [128, NT], F32) for i in range(2)]
        stFb = [fw.sb("stFb%d" % i, [128, NT], BF16) for i in range(2)]
        stT = [fw.sb("stT%d" % i, [128, 512], F32) for i in range(2)]
        stTb = [fw.sb("stTb%d" % i, [128, 512], BF16) for i in range(2)]
        m2 = fw.mark()
        xg = [fw.sb("xg%d" % i, [128, 16, 256], F32) for i in range(1)]
        sq = [fw.sb("sq%d" % i, [128, 512], F32) for i in range(2)]
        rs = fw.sb("rs", [128, 512], F32)
        wcol = self.NRM[:, (2 * l) * 16:(2 * l + 1) * 16]
        srcv = src.ap.rearrange("(k p) t -> p k t", p=128)
        for tg in range(8):
            x = xg[0]
            for k0 in range(0, 16, 4):
                fw.dma("sp", x[:, k0:k0 + 4, :], srcv[:, k0:k0 + 4, tg * 256:(tg + 1) * 256], r=[src], w=[(x, k0)])
            self.rmsnorm(x, 256, wcol, lambda kc, tg=tg: (uT, uT[:, kc, tg * 256:(tg + 1) * 256]), sq, rs)
        fw.aoff = m2
        fw.barrier()

        W = self.w_in
        wl = W.ap[l]
        cnt = [0]

        def fm(c0, c1, dst, r0, func, scale, bf):
            c = c0
            while c < c1:
                nblk = min(512, c1 - c)
                buf, view = self.wload(W, wl[:, c:c + nblk], 16, nblk)
                for j0 in range(0, nblk, 128):
                    n = min(128, nblk - j0)
                    banks = [self.bank() for _ in range(4)]
                    for kc in range(16):
                        for tg in range(4):
                            fw.op("pe", lambda e, kc=kc, tg=tg, b=banks[tg], n=n, j0=j0, view=view: e.matmul(
                                b[0:n, :], lhsT=view[:, kc, j0:j0 + n], rhs=uT[:, kc, tg * 512:(tg + 1) * 512],
                                start=(kc == 0), stop=(kc == 15)), r=[buf, uT], w=[banks[tg]])
                    i = cnt[0] % 2
                    cnt[0] += 1
                    st = stFb[i] if bf else stF[i]
                    for tg in range(4):
                        fw.op("act", lambda e, tg=tg, b=banks[tg], n=n, st=st: e.activation(
                            out=st[0:n, tg * 512:(tg + 1) * 512], in_=b[0:n, :], func=func, scale=scale), r=[banks[tg]], w=[(st, tg)])
                    rr = r0 + (c - c0) + j0
                    fw.dma("sp", dst[rr:rr + n, :], st[0:n, :], r=[st], w=[(dst, rr)])
                c += nblk

        def tm(c0, c1, dst, d0, func, bf):
            c = c0
            while c < c1:
                nblk = min(512, c1 - c)
                buf, view = self.wload(W, wl[:, c:c + nblk], 16, nblk)
                for tt in range(16):
                    b = self.bank()
                    for kc in range(16):
                        fw.op("pe", lambda e, kc=kc, tt=tt, b=b, nblk=nblk, view=view: e.matmul(
                            b[:, 0:nblk], lhsT=uT[:, kc, tt * 128:(tt + 1) * 128], rhs=view[:, kc, 0:nblk],
                            start=(kc == 0), stop=(kc == 15)), r=[buf, uT], w=[b])
                    i = cnt[0] % 2
                    cnt[0] += 1
                    st = stTb[i] if bf else stT[i]
                    fw.op("act", lambda e, b=b, nblk=nblk, st=st: e.activation(out=st[:, 0:nblk], in_=b[:, 0:nblk], func=func), r=[b], w=[st])
                    dd = d0 + (c - c0)
                    fw.dma("sp", dst[tt * 128:(tt + 1) * 128, dd:dd + nblk], st[:, 0:nblk], r=[st], w=[(dst, (tt, dd))])
                c += nblk

        ID, SIG, SILU = AF.Identity, AF.Sigmoid, AF.Silu
        sc = 128.0 ** -0.5
        fm(C_Q, C_Q + 1024, self.QT, 0, ID, sc, True)
        fm(C_KV, C_KV + 768, self.KVT, 0, ID, 1.0, True)
        tm(C_KV + 768, C_KV + 1024, self.VTOK, 0, ID, True)
        fm(C_KV + 1024, C_KV + 1280, self.KVT, 768, ID, 1.0, True)
        tm(C_KV + 1280, C_KV + 1536, self.VTOK, 256, ID, True)
        tm(C_NG, C_NG + 24, self.NG, 0, SIG, False)
        tm(C_Z, C_Z + 2048, self.ZS, 0, SILU, False)
        fm(C_XBC, C_XBC + 3072, self.XBCT, 0, ID, 1.0, False)
        tm(C_DT, C_DT + 32, self.DTR, 0, ID, False)
        fm(C_RQ, C_RQ + 512, self.RQT, 0, ID, sc, True)
        fm(C_RK, C_RK + 512, self.RKT, 0, ID, 1.0, True)
        tm(C_RK, C_RK + 512, self.RKTOK, 0, ID, True)
        tm(C_RV, C_RV + 1024, self.RV, 0, ID, True)
        tm(C_RG, C_RG + 1024, self.RGS, 0, SILU, False)
        fm(C_MG, C_MG + 6144, self.MGT, 0, SIG, 1.0, True)
        fw.release(m)

    def p2_ssdprep(self, l):
        fw = self.fw
        m = fw.mark()
        cw = fw.sb("cw", [128, 24, 5])
        fw.dma("sp", cw[:, :, :], self.d_cw.ap[l].rearrange("p (t k) -> p t k", k=5), r=[self.d_cw], w=[cw])
        xp = [fw.sb("xp%d" % i, [128, 3 + NT]) for i in range(2)]
        acc = [fw.sb("acc%d" % i, [128, NT]) for i in range(2)]
        xo = [fw.sb("xo%d" % i, [128, NT]) for i in range(2)]
        xb = [fw.sb("xob%d" % i, [128, NT], BF16) for i in range(2)]
        stt = [fw.sb("sttok%d" % i, [128, 4, 128]) for i in range(2)]
        sttb = [fw.sb("sttokb%d" % i, [128, 4, 128], BF16) for i in range(2)]
        for i in range(2):
            fw.op("dve", lambda e, i=i: e.memset(xp[i][:, 0:3], 0.0), w=[xp[i]])
        k = 0
        fw.dma("sp", xp[0][:, 3:3 + NT], self.XBCT[0:128, :], r=[self.XBCT], w=[xp[0]])
        for ct in range(24):
            i = ct % 2
            if ct + 1 < 24:
                fw.dma("sp", xp[1 - i][:, 3:3 + NT], self.XBCT[(ct + 1) * 128:(ct + 2) * 128, :], r=[self.XBCT], w=[xp[1 - i]])
            a = acc[i]
            fw.op("dve", lambda e, i=i, ct=ct, a=a: e.tensor_scalar(out=a[:, :], in0=xp[i][:, 0:NT], scalar1=cw[:, ct, 0:1], scalar2=None, op0=ALU.mult),
                  r=[xp[i], cw], w=[a])
            for kk in range(1, 4):
                fw.op("dve", lambda e, i=i, ct=ct, a=a, kk=kk: e.scalar_tensor_tensor(out=a[:, :], in0=xp[i][:, kk:kk + NT], scalar=cw[:, ct, kk:kk + 1],
                                                                                      in1=a[:, :], op0=ALU.mult, op1=ALU.add), r=[xp[i], cw, a], w=[a])
            o = xo[i]
            fw.op("act", lambda e, a=a, o=o, ct=ct: e.activation(out=o[:, :], in_=a[:, :], func=AF.Silu, bias=cw[:, ct, 4:5], scale=1.0), r=[a, cw], w=[o])
            if ct >= 16:
                ob = xb[i]
                fw.op("pool", lambda e, o=o, ob=ob: e.tensor_copy(out=ob[:, :], in_=o[:, :]), r=[o], w=[ob])
                r0 = (ct - 16) * 128
                fw.dma("sp", self.BCT[r0:r0 + 128, :], ob[:, :], r=[ob], w=[(self.BCT, r0)])
            if ct < 20:
                for t4 in range(4):
                    b = self.bank()
                    for j in range(4):
                        tt = t4 * 4 + j
                        fw.op("pe", lambda e, b=b, j=j, tt=tt, o=o: e.transpose(b[:, j * 128:(j + 1) * 128], o[:, tt * 128:(tt + 1) * 128], self.cv("IDENT")),
                              r=[o, self.C], w=[b])
                    if ct < 16:
                        s = stt[k % 2]
                        dst = self.XTOK.ap[t4 * 512:(t4 + 1) * 512, ct * 128:(ct + 1) * 128]
                        dt_ = self.XTOK
                    else:
                        s = sttb[k % 2]
                        cc = (ct - 16) * 128
                        dst = self.BTOK.ap[t4 * 512:(t4 + 1) * 512, cc:cc + 128]
                        dt_ = self.BTOK
                    k += 1
                    fw.op("act", lambda e, b=b, s=s: e.copy(out=s.ap.rearrange("p a b -> p (a b)"), in_=b[:, :]), r=[b], w=[s])
                    fw.dma("sp", dst.rearrange("(a p) c -> p a c", p=128), s[:, :, :], r=[s], w=[(dt_, (t4, ct))])
        fw.release(m)

    def p34_nsa(self, l):
        fw = self.fw
        C = self.C
        m = fw.mark()
        KCT = [fw.sb("KCT%d" % g, [128, 128], BF16) for g in range(2)]
        VCX = [fw.sb("VCX%d" % g, [128, 161]) for g in range(2)]
        m3 = fw.mark()
        w1s = fw.sb("w1s", [128, 32, 128], BF16)
        w2s = fw.sb("w2s", [128, 128], BF16)
        pet = fw.sb("pet", [128, 32], BF16)
        kcT = [fw.sb("kcT%d" % i, [128, NT], BF16) for i in range(2)]
        pb = fw.sb("pb", [128, 1])
        hx = fw.sb("hx", [128, 128])
        h2 = fw.sb("h2", [128, 128])
        G = fw.sb("G", [128, 128], BF16)
        for t in range(2):
            W1 = self.cw1[t]
            w1v = W1.ap[l].rearrange("(l d) o -> d l o", d=128)
            for l0 in range(0, 32, 8):
                fw.dma("pool", w1s[:, l0:l0 + 8, :], w1v[:, l0:l0 + 8, :], r=[W1], w=[(w1s, l0)])
            fw.dma("pool", w2s[:, :], self.cw2[t].ap[l], r=[self.cw2[t]], w=[w2s])
            fw.dma("pool", pet[:, :], self.d_pet.ap[l, t], r=[self.d_pet], w=[pet])
            for g in range(2):
                src = kcT[g]
                r0 = t * 256 + g * 128
                fw.dma("sp", src[:, :], self.KVT[r0:r0 + 128, :], r=[self.KVT], w=[src])
                sv = src.ap.rearrange("p (n s) -> p n s", s=16)
                ps = self.bank()
                for li in range(32):
                    rhs = sv[:, 0:127, li] if li < 16 else sv[:, 1:128, li - 16]
                    fw.op("pe", lambda e, li=li, rhs=rhs, ps=ps: e.matmul(ps[:, 0:127], lhsT=w1s[:, li, :], rhs=rhs, start=(li == 0), stop=(li == 31)),
                          r=[w1s, src], w=[ps])
                ps2 = self.bank()
                for li in range(32):
                    fw.op("pe", lambda e, li=li, ps2=ps2: e.matmul(ps2[:, 0:1], lhsT=w1s[:, li, :], rhs=pet[:, li:li + 1], start=(li == 0), stop=(li == 31)),
                          r=[w1s, pet], w=[ps2])
                fw.op("act", lambda e, ps2=ps2: e.copy(out=pb[:, :], in_=ps2[:, 0:1]), r=[ps2], w=[pb])
                fw.op("dve", lambda e: e.memset(hx[:, 127:128], 0.0), w=[hx])
                fw.op("act", lambda e, ps=ps: e.activation(out=hx[:, 0:127], in_=ps[:, 0:127], func=AF.Identity, bias=pb[:, 0:1], scale=1.0), r=[ps, pb], w=[hx])
                fw.op("dve", lambda e: e.tensor_tensor(out=h2[:, :], in0=hx[:, :], in1=hx[:, :], op=ALU.mult), r=[hx], w=[h2])
                fw.op("dve", lambda e: e.tensor_scalar(out=h2[:, :], in0=h2[:, :], scalar1=0.044715, scalar2=1.0, op0=ALU.mult, op1=ALU.add), r=[h2], w=[h2])
                fw.op("dve", lambda e: e.tensor_tensor(out=h2[:, :], in0=h2[:, :], in1=hx[:, :], op=ALU.mult), r=[h2, hx], w=[h2])
                fw.op("act", lambda e: e.activation(out=h2[:, :], in_=h2[:, :], func=AF.Sigmoid, scale=1.5957691216057308), r=[h2], w=[h2])
                fw.op("dve", lambda e: e.tensor_tensor(out=G[:, :], in0=h2[:, :], in1=hx[:, :], op=ALU.mult), r=[h2, hx], w=[G])
                ps3 = self.bank()
                if t == 0:
                    fw.op("pe", lambda e, ps3=ps3: e.matmul(ps3[:, 0:128], lhsT=w2s[:, :], rhs=G[:, :], start=True, stop=True), r=[w2s, G], w=[ps3])
                    fw.op("act", lambda e, ps3=ps3, g=g: e.copy(out=KCT[g][:, :], in_=ps3[:, 0:128]), r=[ps3], w=[KCT[g]])
                else:
                    fw.op("pe", lambda e, ps3=ps3: e.matmul(ps3[:, 0:128], lhsT=G[:, :], rhs=w2s[:, :], start=True, stop=True), r=[w2s, G], w=[ps3])
                    fw.op("act", lambda e, ps3=ps3, g=g: e.copy(out=VCX[g][:, 0:128], in_=ps3[:, 0:128]), r=[ps3], w=[VCX[g]])
                    fw.op("dve", lambda e, g=g: e.tensor_copy(out=VCX[g][:, 128:161], in_=self.cv("OVL")), r=[C], w=[VCX[g]])
        fw.release(m3)

        ET = fw.sb("ET", [32, NT], BF16)
        fw.dma("pool", ET[:, :], self.d_et[:, :], r=[self.d_et], w=[ET])
        NGs = fw.sb("NGs", [128, 16, 24])
        fw.dma("sp", NGs[:, :, :], self.NG.ap.rearrange("(t p) c -> p t c", p=128), r=[self.NG], w=[NGs])
        ksT = fw.sb("ksT", [128, NT], BF16)
        kwT = fw.sb("kwT", [128, NT], BF16)
        VSX = fw.sb("VSX", [128, 16, 129], BF16)
        VWX = fw.sb("VWX", [128, 16, 129], BF16)
        QTg = [fw.sb("QTg%d" % j, [128, NT], BF16) for j in range(4)]
        TABC = [fw.sb("TABC%d" % i, [128, 2048]) for i in range(2)]
        TABS = [fw.sb("TABS%d" % i, [128, TO_C]) for i in range(2)]
        IMP = fw.sb("IMP", [128, 16, 32])
        NEGT = fw.sb("NEGT", [32, NT], BF16)
        Yh = [fw.sb("Yh%d" % j, [128, 16, 128]) for j in range(4)]
        scb = [fw.sb("scb%d" % i, [128, 512]) for i in range(4)]
        p32 = [fw.sb("p32_%d" % i, [128, 512]) for i in range(3)]
        pbf = [fw.sb("pbf%d" % i, [128, 512], BF16) for i in range(4)]
        sm = [fw.sb("sm%d" % i, [128, 48]) for i in range(4)]
        ytb = [fw.sb("ytb%d" % i, [128, 512], BF16) for i in range(2)]
        MSK = self.cv("MSK").rearrange("p (a q k) -> p a q k", a=3, q=16, k=32)
        cnt = {"s": 0, "p": 0, "sm": 0, "y": 0, "r": 0, "c": 0, "p32": 0}
        PS = fw.PS
        LA = 2

        def sbank():
            b = PS[4 + cnt["c"] % 4]
            cnt["c"] += 1
            return b

        def pipeline(items):
            n = len(items)
            for i in range(n + LA):
                if i < n:
                    items[i][0]()
                if i >= LA:
                    items[i - LA][1]()

        def evac_round(banks, W, Yt, Q0, h, br, first, imp):
            s = sm[cnt["sm"] % 4]
            cnt["sm"] += 1
            for bi in range(2):
                fw.op("dve", lambda e, bi=bi: e.tensor_scalar(out=s[:, 2 * bi:2 * bi + 2], in0=banks[bi][:, 128:128 + W + 1:W], scalar1=1e-30, scalar2=None, op0=ALU.max),
                      r=[banks[bi]], w=[s])
            fw.op("dve", lambda e: e.reciprocal(out=s[:, 0:4], in_=s[:, 0:4]), r=[s], w=[s])
            fw.op("dve", lambda e: e.tensor_tensor(out=s[:, 4:8], in0=s[:, 0:4], in1=NGs[:, Q0:Q0 + 4, h * 3 + br], op=ALU.mult), r=[s, NGs], w=[s])
            for qt in range(4):
                bk = banks[qt // 2]
                c0 = (qt % 2) * W
                Q = Q0 + qt
                if first:
                    fw.op("dve", lambda e, bk=bk, c0=c0, Q=Q, qt=qt: e.tensor_scalar(out=Yt[:, Q, :], in0=bk[:, c0:c0 + 128], scalar1=s[:, 4 + qt:5 + qt], scalar2=None, op0=ALU.mult),
                          r=[bk, s], w=[(Yt, Q)])
                else:
                    fw.op("dve", lambda e, bk=bk, c0=c0, Q=Q, qt=qt: e.scalar_tensor_tensor(out=Yt[:, Q, :], in0=bk[:, c0:c0 + 128], scalar=s[:, 4 + qt:5 + qt], in1=Yt[:, Q, :],
                                                                                           op0=ALU.mult, op1=ALU.add), r=[bk, s, (Yt, Q)], w=[(Yt, Q)])
            if imp:
                for qt in range(4):
                    bk = banks[qt // 2]
                    c0 = (qt % 2) * W
                    Q = Q0 + qt
                    fw.op("dve", lambda e, bk=bk, c0=c0, Q=Q, qt=qt: e.scalar_tensor_tensor(out=IMP[:, Q, :], in0=bk[:, c0 + 129:c0 + 161], scalar=s[:, qt:qt + 1], in1=IMP[:, Q, :],
                                                                                           op0=ALU.mult, op1=ALU.add), r=[bk, s, (IMP, Q)], w=[(IMP, Q)])

        for g in range(2):
            fw.dma("sp", ksT[:, :], self.KVT[512 + g * 128:512 + (g + 1) * 128, :], r=[self.KVT], w=[ksT])
            fw.dma("sp", kwT[:, :], self.KVT[768 + g * 128:768 + (g + 1) * 128, :], r=[self.KVT], w=[kwT])
            fw.dma("sp", VSX[:, :, 0:128], self.VTOK.ap[:, g * 128:(g + 1) * 128].rearrange("(t p) d -> p t d", p=128), r=[self.VTOK], w=[VSX])
            fw.dma("sp", VWX[:, :, 0:128], self.VTOK.ap[:, 256 + g * 128:256 + (g + 1) * 128].rearrange("(t p) d -> p t d", p=128), r=[self.VTOK], w=[VWX])
            fw.op("dve", lambda e: e.memset(VSX[:, :, 128:129], 1.0), w=[VSX])
            fw.op("dve", lambda e: e.memset(VWX[:, :, 128:129], 1.0), w=[VWX])
            fw.op("dve", lambda e: e.memset(IMP[:, :, :], 0.0), w=[IMP])
            for j in range(4):
                h = g * 4 + j
                fw.dma("sp", QTg[j][:, :], self.QT[h * 128:(h + 1) * 128, :], r=[self.QT], w=[QTg[j]])
            items = []
            for j in range(4):
                h = g * 4 + j
                for qg in range(4):
                    st = {}

                    def front(j=j, h=h, qg=qg, st=st):
                        tb = TABC[h % 2]
                        if qg == 0:
                            fw.dma("sp", tb[:, :], self.d_tab.ap[h][:, TO_C:TO_C + 2048], r=[self.d_tab], w=[tb])
                        S = sbank()
                        fw.op("pe", lambda e: e.matmul(S[:, :], lhsT=KCT[g][:, :], rhs=QTg[j][:, qg * 512:(qg + 1) * 512], start=True, stop=True), r=[KCT[g], QTg[j]], w=[S])
                        sc_ = scb[cnt["s"] % 4]
                        cnt["s"] += 1
                        pp = p32[cnt["p32"] % 3]
                        cnt["p32"] += 1
                        fw.op("dve", lambda e: e.tensor_tensor(out=sc_[:, :], in0=S[:, :], in1=tb[:, qg * 512:(qg + 1) * 512], op=ALU.add), r=[S, tb], w=[sc_])
                        fw.op("act", lambda e: e.activation(out=pp[:, :], in_=sc_[:, :], func=AF.Exp), r=[sc_], w=[pp])
                        st["pp"] = pp

                    def back(j=j, h=h, qg=qg, st=st):
                        pp = st["pp"]
                        r_ = cnt["r"] % 2
                        cnt["r"] += 1
                        banks = [PS[2 * r_], PS[2 * r_ + 1]]
                        for qt in range(4):
                            bk = banks[qt // 2]
                            c0 = (qt % 2) * 161
                            fw.op("pe", lambda e, bk=bk, c0=c0, qt=qt: e.matmul(bk[:, c0:c0 + 161], lhsT=pp[:, qt * 128:(qt + 1) * 128], rhs=VCX[g][:, :], start=True, stop=True),
                                  r=[pp, VCX[g]], w=[bk])
                        evac_round(banks, 161, Yh[j], qg * 4, h, 0, True, True)

                    items.append((front, back))
            pipeline(items)
            for Q in range(16):
                s = sm[cnt["sm"] % 4]
                cnt["sm"] += 1
                im = s[:, 8:40]
                fw.op("dve", lambda e, Q=Q, im=im: e.tensor_tensor(out=im, in0=IMP[:, Q, :], in1=MSK[:, 0, Q, :], op=ALU.mult), r=[(IMP, Q), C], w=[s])
                fw.op("dve", lambda e, Q=Q, im=im: e.tensor_tensor(out=im, in0=im, in1=MSK[:, 1, Q, :], op=ALU.add), r=[s, C], w=[s])
                fw.op("dve", lambda e, s=s, im=im: e.max(out=s[:, 0:8], in_=im), r=[s], w=[s])
                fw.op("dve", lambda e, s=s, im=im: e.tensor_scalar(out=im, in0=im, scalar1=s[:, 7:8], scalar2=None, op0=ALU.is_ge), r=[s], w=[s])
                fw.op("dve", lambda e, Q=Q, im=im: e.tensor_tensor(out=im, in0=im, in1=MSK[:, 2, Q, :], op=ALU.mult), r=[s, C], w=[s])
                fw.op("dve", lambda e, im=im: e.tensor_scalar(out=im, in0=im, scalar1=-1.0, scalar2=-NEG, op0=ALU.add, op1=ALU.mult), r=[s], w=[s])
                b = sbank()
                fw.op("pe", lambda e, b=b, im=im: e.transpose(b[0:32, 0:128], im, self.cv("IDENT")), r=[s, C], w=[b])
                fw.op("act", lambda e, b=b, Q=Q: e.copy(out=NEGT[:, Q * 128:(Q + 1) * 128], in_=b[0:32, 0:128]), r=[b], w=[(NEGT, Q)])
            items = []
            for j in range(4):
                h = g * 4 + j
                slope = 2.0 ** (-(h + 1))
                for br in (1, 2):
                    for qg in range(4):
                        kt_lo = 0 if br == 1 else max(0, 4 * qg - 4)
                        kts = list(range(kt_lo, 4 * qg + 4))
                        rst = {"used": [False, False]}
                        for kt in kts:
                            st = {}
                            first_of_head = (br == 1 and qg == 0 and kt == kts[0])
                            last_of_round = (kt == kts[-1])
                            last_of_head = (br == 2 and qg == 3 and last_of_round)

                            def front(j=j, h=h, slope=slope, br=br, qg=qg, kt=kt, st=st, first_of_head=first_of_head):
                                tb = TABS[j % 2]
                                if first_of_head:
                                    fw.dma("sp", tb[:, :], self.d_tab.ap[h][:, 0:TO_C], r=[self.d_tab], w=[tb])
                                KT = ksT if br == 1 else kwT
                                D0 = qg * 512 - kt * 128
                                S = sbank()
                                fw.op("pe", lambda e: e.matmul(S[:, :], lhsT=KT[:, kt * 128:(kt + 1) * 128], rhs=QTg[j][:, qg * 512:(qg + 1) * 512], start=True, stop=(br == 2)),
                                      r=[KT, QTg[j]], w=[S])
                                if br == 1:
                                    fw.op("pe", lambda e: e.matmul(S[:, :], lhsT=ET[:, kt * 128:(kt + 1) * 128], rhs=NEGT[:, qg * 512:(qg + 1) * 512], start=False, stop=True),
                                          r=[ET, NEGT], w=[S])
                                sc_ = scb[cnt["s"] % 4]
                                cnt["s"] += 1
                                pp = pbf[cnt["p"] % 4]
                                cnt["p"] += 1
                                bias = 0.0
                                if br == 1:
                                    if D0 <= 0:
                                        tsl = tb[:, TO_U + D0 + 384:TO_U + D0 + 384 + 512]
                                    else:
                                        tsl = tb[:, TO_W:TO_W + 512]
                                        bias = -slope * D0
                                else:
                                    tsl = tb[:, TO_UW + D0 + 384:TO_UW + D0 + 384 + 512]
                                if bias == 0.0:
                                    fw.op("dve", lambda e: e.tensor_tensor(out=sc_[:, :], in0=S[:, :], in1=tsl, op=ALU.add), r=[S, tb], w=[sc_])
                                else:
                                    fw.op("dve", lambda e: e.scalar_tensor_tensor(out=sc_[:, :], in0=S[:, :], scalar=bias, in1=tsl, op0=ALU.add, op1=ALU.add), r=[S, tb], w=[sc_])
                                fw.op("act", lambda e: e.activation(out=pp[:, :], in_=sc_[:, :], func=AF.Exp), r=[sc_], w=[pp])
                                st["pp"] = pp

                            def back(j=j, h=h, br=br, qg=qg, kt=kt, st=st, rst=rst, first=(kt == kts[0]), last_of_round=last_of_round, last_of_head=last_of_head):
                                pp = st["pp"]
                                VX = VSX if br == 1 else VWX
                                if first:
                                    r_ = cnt["r"] % 2
                                    cnt["r"] += 1
                                    rst["banks"] = [PS[2 * r_], PS[2 * r_ + 1]]
                                banks = rst["banks"]
                                for qt in range(4):
                                    Q = qg * 4 + qt
                                    lo = 0 if br == 1 else max(0, Q - 4)
                                    if kt < lo or kt > Q:
                                        continue
                                    bi = qt // 2
                                    bk = banks[bi]
                                    c0 = (qt % 2) * 129
                                    stt_ = not rst["used"][bi]
                                    rst["used"][bi] = True
                                    fw.op("pe", lambda e, bk=bk, c0=c0, qt=qt, stt_=stt_, Q=Q: e.matmul(bk[:, c0:c0 + 129], lhsT=pp[:, qt * 128:(qt + 1) * 128], rhs=VX[:, kt, :],
                                                                                                    start=stt_, stop=(kt == Q), skip_group_check=True), r=[pp, VX], w=[bk])
                                if last_of_round:
                                    evac_round(banks, 129, Yh[j], qg * 4, h, br, False, False)
                                if last_of_head:
                                    for t4 in range(4):
                                        b = sbank()
                                        for jj in range(4):
                                            Q = t4 * 4 + jj
                                            fw.op("pe", lambda e, b=b, jj=jj, Q=Q: e.transpose(b[:, jj * 128:(jj + 1) * 128], Yh[j][:, Q, :], self.cv("IDENT")), r=[(Yh[j], Q), C], w=[b])
                                        y = ytb[cnt["y"] % 2]
                                        cnt["y"] += 1
                                        fw.op("act", lambda e, b=b, y=y: e.copy(out=y[:, :], in_=b[:, :]), r=[b], w=[y])
                                        fw.dma("sp", self.YT[h * 128:(h + 1) * 128, t4 * 512:(t4 + 1) * 512], y[:, :], r=[y], w=[(self.YT, (h, t4))])

                            items.append((front, back))
            pipeline(items)
        fw.release(m)
        fw.release(m)

    def p5_ssd(self, l):
        fw = self.fw
        C = self.C
        PS = fw.PS
        m = fw.mark()
        sv = fw.sb("ssdv", [128, 96])
        fw.dma("sp", sv[:, :], self.d_ssdv.ap[l], r=[self.d_ssdv], w=[sv])
        snw = fw.sb("snw", [128, 2048])
        fw.dma("sp", snw[:, :], self.d_snw.ap[l], r=[self.d_snw], w=[snw])
        negA = fw.sb("negA", [128, 32])
        fw.op("act", lambda e: e.activation(out=negA[:, :], in_=sv[:, 32:64], func=AF.Exp), r=[sv], w=[negA])
        fw.op("dve", lambda e: e.tensor_scalar(out=negA[:, :], in0=negA[:, :], scalar1=-1.0, scalar2=None, op0=ALU.mult), r=[negA], w=[negA])
        BT = fw.sb("BT", [128, 4, NT], BF16)
        CT = fw.sb("CT", [128, 4, NT], BF16)
        fw.dma("sp", BT[:, :, :], self.BCT.ap[0:512, :].rearrange("(g p) t -> p g t", p=128), r=[self.BCT], w=[BT])
        fw.dma("sp", CT[:, :, :], self.BCT.ap[512:1024, :].rearrange("(g p) t -> p g t", p=128), r=[self.BCT], w=[CT])
        H = fw.sb("H", [128, 4, 512])
        Hb = fw.sb("Hb", [128, 4, 512], BF16)
        fw.op("dve", lambda e: e.memset(H[:, :, :], 0.0), w=[H])
        fw.op("dve", lambda e: e.memset(Hb[:, :, :], 0.0), w=[Hb])
        dtA = fw.sb("dtA", [128, 16, 32])
        tmpA = fw.sb("tmpA", [128, 16, 32])
        aA = fw.sb("aA", [128, 16, 32])
        acA = fw.sb("acA", [128, 16, 32])
        atA = fw.sb("atA", [128, 16, 32])
        eaA = fw.sb("eaA", [128, 16, 32])
        cdA = fw.sb("cdA", [128, 16, 32])
        dteA = fw.sb("dteA", [128, 16, 32])
        fl = lambda t: t.ap.rearrange("p c h -> p (c h)")
        bc = lambda ap: ap.unsqueeze(1).broadcast_to([128, 16, 32])
        fw.dma("sp", dtA[:, :, :], self.DTR.ap.rearrange("(c p) h -> p c h", p=128), r=[self.DTR], w=[dtA])
        fw.op("dve", lambda e: e.tensor_tensor(out=dtA[:, :, :], in0=dtA[:, :, :], in1=bc(sv[:, 0:32]), op=ALU.add), r=[dtA, sv], w=[dtA])
        fw.op("dve", lambda e: e.tensor_scalar(out=fl(tmpA), in0=fl(dtA), scalar1=-1.0, scalar2=None, op0=ALU.mult), r=[dtA], w=[tmpA])
        fw.op("dve", lambda e: e.tensor_tensor(out=fl(tmpA), in0=fl(tmpA), in1=fl(dtA), op=ALU.max), r=[dtA, tmpA], w=[tmpA])
        fw.op("act", lambda e: e.activation(out=fl(tmpA), in_=fl(tmpA), func=AF.Exp, scale=-1.0), r=[tmpA], w=[tmpA])
        fw.op("act", lambda e: e.activation(out=fl(tmpA), in_=fl(tmpA), func=AF.Ln, bias=self.EPSC[:, 1:2], scale=1.0), r=[tmpA, self.EPSC], w=[tmpA])
        fw.op("dve", lambda e: e.scalar_tensor_tensor(out=fl(dtA), in0=fl(dtA), scalar=0.0, in1=fl(tmpA), op0=ALU.max, op1=ALU.add), r=[dtA, tmpA], w=[dtA])
        fw.op("dve", lambda e: e.tensor_tensor(out=aA[:, :, :], in0=dtA[:, :, :], in1=bc(negA[:, :]), op=ALU.mult), r=[dtA, negA], w=[aA])
        fw.op("pe", lambda e: e.matmul(PS[1][:, :], lhsT=self.cv("CAUS"), rhs=fl(aA), start=True, stop=True), r=[aA, C], w=[PS[1]])
        fw.op("pe", lambda e: e.matmul(PS[2][:, :], lhsT=self.cv("ONES1"), rhs=fl(aA), start=True, stop=True), r=[aA, C], w=[PS[2]])
        fw.op("act", lambda e: e.copy(out=fl(acA), in_=PS[1][:, :]), r=[PS[1]], w=[acA])
        fw.op("act", lambda e: e.copy(out=fl(atA), in_=PS[2][:, :]), r=[PS[2]], w=[atA])
        fw.op("act", lambda e: e.activation(out=fl(eaA), in_=fl(acA), func=AF.Exp), r=[acA], w=[eaA])
        fw.op("act", lambda e: e.activation(out=fl(cdA), in_=fl(atA), func=AF.Exp), r=[atA], w=[cdA])
        fw.op("dve", lambda e: e.tensor_tensor(out=fl(dteA), in0=fl(atA), in1=fl(acA), op=ALU.subtract), r=[atA, acA], w=[dteA])
        fw.op("act", lambda e: e.activation(out=fl(dteA), in_=fl(dteA), func=AF.Exp), r=[dteA], w=[dteA])

        xc = [fw.sb("xc%d" % i, [128, 2048]) for i in range(3)]
        zc = [fw.sb("zc%d" % i, [128, 2048]) for i in range(3)]
        bk = [fw.sb("bk%d" % i, [128, 512], BF16) for i in range(3)]
        xdt = fw.sb("xdt", [128, 2048], BF16)
        xdte = fw.sb("xdte", [128, 2048], BF16)
        cbm = fw.sb("cbm", [128, 4, 128])
        seg = [fw.sb("seg%d" % i, [128, 4, 128]) for i in range(2)]
        dec = [fw.sb("dec%d" % i, [128, 4, 128]) for i in range(3)]
        MT = [fw.sb("MT%d" % i, [128, 4, 128], BF16) for i in range(3)]
        ys_ = [fw.sb("y%d" % i, [128, 2048]) for i in range(2)]
        ysq = fw.sb("ysq", [128, 512])
        st4 = fw.sb("st4", [128, 8])
        yT = fw.sb("yT", [128, 16, 128], BF16)
        CAUS = self.cv("CAUS")
        cnt = {"k": 0, "m": 0, "b": 0}
        LA = 2

        def sbank():
            b = PS[1 + cnt["b"] % 3]
            cnt["b"] += 1
            return b

        def loadc(c):
            i = c % 3
            rows = slice(c * 128, (c + 1) * 128)
            fw.dma("sp", xc[i][:, :], self.XTOK[rows, :], r=[self.XTOK], w=[xc[i]])
            fw.dma("sp", bk[i][:, :], self.BTOK[rows, :], r=[self.BTOK], w=[bk[i]])
            fw.dma("sp", zc[i][:, :], self.ZS[rows, :], r=[self.ZS], w=[zc[i]])

        def stageA(c):
            i = c % 3
            x, z, bt = xc[i], zc[i], bk[i]
            y = ys_[c % 2]
            cs = slice(c * 128, (c + 1) * 128)
            x3 = x.ap.rearrange("p (h q) -> p h q", q=64)
            fw.op("dve", lambda e: e.tensor_tensor(out=xdt.ap.rearrange("p (h q) -> p h q", q=64), in0=x3, in1=dtA[:, c, :].unsqueeze(2).broadcast_to([128, 32, 64]), op=ALU.mult),
                  r=[x, dtA], w=[xdt])
            fw.op("dve", lambda e: e.tensor_tensor(out=xdte.ap.rearrange("p (h q) -> p h q", q=64), in0=xdt.ap.rearrange("p (h q) -> p h q", q=64),
                                                   in1=dteA[:, c, :].unsqueeze(2).broadcast_to([128, 32, 64]), op=ALU.mult), r=[xdt, dteA], w=[xdte])
            for g in range(4):
                fw.op("pe", lambda e, g=g: e.matmul(PS[0][:, g * 128:(g + 1) * 128], lhsT=BT[:, g, cs], rhs=CT[:, g, cs], start=True, stop=True), r=[BT, CT], w=[PS[0]])
            fw.op("dve", lambda e: e.tensor_tensor(out=cbm[:, :, :], in0=PS[0].ap.rearrange("p (g l) -> p g l", l=128), in1=CAUS.unsqueeze(1).broadcast_to([128, 4, 128]), op=ALU.mult),
                  r=[PS[0], C], w=[cbm])
            items = []
            for g in range(4):
                for hh in range(2):
                    st = {}

                    def front(g=g, hh=hh, st=st):
                        h0 = g * 8 + hh * 4
                        sg, dc = seg[cnt["k"] % 2], dec[cnt["k"] % 3]
                        cnt["k"] += 1
                        mt = MT[cnt["m"] % 3]
                        cnt["m"] += 1
                        fw.op("dve", lambda e: e.tensor_tensor(out=sg[:, :, :], in0=CAUS.unsqueeze(1).broadcast_to([128, 4, 128]),
                                                               in1=aA[:, c, h0:h0 + 4].unsqueeze(2).broadcast_to([128, 4, 128]), op=ALU.mult), r=[aA, C], w=[sg])
                        pseg = sbank()
                        fw.op("pe", lambda e: e.matmul(pseg[:, :], lhsT=self.cv("TGT"), rhs=sg.ap.rearrange("p a b -> p (a b)"), start=True, stop=True), r=[sg, C], w=[pseg])
                        fw.op("act", lambda e: e.activation(out=dc.ap.rearrange("p a b -> p (a b)"), in_=pseg[:, :], func=AF.Exp), r=[pseg], w=[dc])
                        st["mt"] = mt
                        st["dc"] = dc

                    def mid(g=g, hh=hh, st=st):
                        mt, dc = st["mt"], st["dc"]
                        fw.op("dve", lambda e: e.tensor_tensor(out=mt[:, :, :], in0=dc[:, :, :], in1=cbm[:, g, :].unsqueeze(1).broadcast_to([128, 4, 128]), op=ALU.mult),
                              r=[dc, cbm], w=[mt])

                    def back(g=g, hh=hh, st=st):
                        mt = st["mt"]
                        h0 = g * 8 + hh * 4
                        yd = PS[4 + g % 2]
                        for q in range(4):
                            hd = h0 + q
                            cc = (hh * 4 + q) * 64
                            fw.op("pe", lambda e, q=q, hd=hd, cc=cc: e.matmul(yd[:, cc:cc + 64], lhsT=mt[:, q, :], rhs=xdt[:, hd * 64:(hd + 1) * 64], start=True, stop=True),
                                  r=[mt, xdt], w=[yd])
                        if hh == 1:
                            yo = PS[6]
                            fw.op("pe", lambda e: e.matmul(yo[:, :], lhsT=CT[:, g, cs], rhs=Hb[:, g, :], start=True, stop=True), r=[CT, (Hb, g)], w=[yo])
                            ysl = y.ap[:, g * 512:(g + 1) * 512].rearrange("p (h q) -> p h q", q=64)
                            fw.op("dve", lambda e: e.tensor_tensor(out=ysl, in0=yo.ap.rearrange("p (h q) -> p h q", q=64),
                                                                   in1=eaA[:, c, g * 8:(g + 1) * 8].unsqueeze(2).broadcast_to([128, 8, 64]), op=ALU.mult), r=[yo, eaA], w=[(y, g)])
                            fw.op("dve", lambda e: e.tensor_tensor(out=y[:, g * 512:(g + 1) * 512], in0=yd[:, :], in1=y[:, g * 512:(g + 1) * 512], op=ALU.add), r=[yd, (y, g)], w=[(y, g)])
                            pst = PS[7]
                            fw.op("pe", lambda e: e.matmul(pst[:, :], lhsT=bt[:, g * 128:(g + 1) * 128], rhs=xdte[:, g * 512:(g + 1) * 512], start=True, stop=True),
                                  r=[bt, xdte], w=[pst])
                            Hg = H.ap[:, g, :].rearrange("p (h q) -> p h q", q=64)
                            fw.op("dve", lambda e: e.tensor_tensor(out=Hg, in0=Hg, in1=cdA[:, c, g * 8:(g + 1) * 8].unsqueeze(2).broadcast_to([128, 8, 64]), op=ALU.mult),
                                  r=[(H, g), cdA], w=[(H, g)])
                            fw.op("dve", lambda e: e.tensor_tensor(out=H[:, g, :], in0=pst[:, :], in1=H[:, g, :], op=ALU.add), r=[pst, (H, g)], w=[(H, g)])
                            fw.op("act", lambda e: e.copy(out=Hb[:, g, :], in_=H[:, g, :]), r=[(H, g)], w=[(Hb, g)])

                    items.append((front, mid, back))
            n = len(items)
            for ii in range(n + 2):
                if ii < n:
                    items[ii][0]()
                if 0 <= ii - 1 < n:
                    items[ii - 1][1]()
                if 0 <= ii - 2 < n:
                    items[ii - 2][2]()

        def stageB(c):
            i = c % 3
            x, z = xc[i], zc[i]
            y = ys_[c % 2]
            x3 = x.ap.rearrange("p (h q) -> p h q", q=64)
            fw.op("dve", lambda e: e.tensor_tensor(out=x3, in0=x3, in1=sv[:, 64:96].unsqueeze(2).broadcast_to([128, 32, 64]), op=ALU.mult), r=[x, sv], w=[x])
            fw.op("dve", lambda e: e.tensor_tensor(out=x[:, :], in0=x[:, :], in1=y[:, :], op=ALU.add), r=[y, x], w=[x])
            fw.op("dve", lambda e: e.tensor_tensor(out=y[:, :], in0=x[:, :], in1=z[:, :], op=ALU.mult), r=[x, z], w=[y])
            for g in range(4):
                fw.op("act", lambda e, g=g: e.activation(out=ysq[:, :], in_=y[:, g * 512:(g + 1) * 512], func=AF.Square), r=[y], w=[ysq])
                fw.op("dve", lambda e, g=g: e.tensor_reduce(out=st4[:, g:g + 1], in_=ysq[:, :], axis=AX.X, op=ALU.add), r=[ysq], w=[st4])
            fw.op("act", lambda e: e.activation(out=st4[:, 4:8], in_=st4[:, 0:4], func=AF.Sqrt, bias=self.EPSC[:, 0:1], scale=1.0 / 512), r=[st4, self.EPSC], w=[st4])
            fw.op("dve", lambda e: e.reciprocal(out=st4[:, 4:8], in_=st4[:, 4:8]), r=[st4], w=[st4])
            for g in range(4):
                fw.op("dve", lambda e, g=g: e.scalar_tensor_tensor(out=y[:, g * 512:(g + 1) * 512], in0=y[:, g * 512:(g + 1) * 512], scalar=st4[:, 4 + g:5 + g],
                                                                   in1=snw[:, g * 512:(g + 1) * 512], op0=ALU.mult, op1=ALU.mult), r=[y, st4, snw], w=[y])
            for t4 in range(4):
                b = sbank()
                for jj in range(4):
                    f = t4 * 4 + jj
                    fw.op("pe", lambda e, b=b, jj=jj, f=f: e.transpose(b[:, jj * 128:(jj + 1) * 128], y[:, f * 128:(f + 1) * 128], self.cv("IDENT")), r=[y, C], w=[b])
                fw.op("act", lambda e, b=b, t4=t4: e.copy(out=yT.ap[:, t4 * 4:(t4 + 1) * 4, :].rearrange("p a b -> p (a b)"), in_=b[:, :]), r=[b], w=[yT])
            fw.dma("sp", self.YT.ap[1024:3072, c * 128:(c + 1) * 128].rearrange("(f p) t -> p f t", p=128), yT[:, :, :], r=[yT], w=[(self.YT, ("s", c))])

        loadc(0)
        loadc(1)
        for c in range(16):
            stageA(c)
            if c >= 1:
                stageB(c - 1)
            if c + 2 < 16:
                loadc(c + 2)
        stageB(15)
        fw.release(m)

    def p6_ret(self, l):
        fw = self.fw
        C = self.C
        PS = fw.PS
        m = fw.mark()
        rnw = fw.sb("rnw", [128, 1024])
        fw.dma("sp", rnw[:, :], self.d_rnw.ap[l], r=[self.d_rnw], w=[rnw])
        QT = fw.sb("rQT", [128, 4, NT], BF16)
        KT = fw.sb("rKT", [128, 4, NT], BF16)
        fw.dma("sp", QT[:, :, :], self.RQT.ap.rearrange("(h p) t -> p h t", p=128), r=[self.RQT], w=[QT])
        fw.dma("sp", KT[:, :, :], self.RKT.ap.rearrange("(h p) t -> p h t", p=128), r=[self.RKT], w=[KT])
        QS = fw.sb("rQS", [128, 4, NT], BF16)
        QD = self.cv("QD").rearrange("p (h l) -> p h l", l=128)
        for h in range(4):
            fw.op("dve", lambda e, h=h: e.tensor_tensor(out=QS.ap[:, h, :].rearrange("p (c l) -> p c l", l=128), in0=QT.ap[:, h, :].rearrange("p (c l) -> p c l", l=128),
                                                        in1=QD[:, h, :].unsqueeze(1).broadcast_to([128, 16, 128]), op=ALU.mult), r=[QT, C], w=[QS])
        R = fw.sb("R", [128, 4, 256])
        Rb = fw.sb("Rb", [128, 4, 256], BF16)
        CD = fw.sb("CD", [128, 4])
        fw.op("dve", lambda e: e.memset(R[:, :, :], 0.0), w=[R])
        fw.op("dve", lambda e: e.memset(Rb[:, :, :], 0.0), w=[Rb])
        for h in range(4):
            fw.op("dve", lambda e, h=h: e.memset(CD[:, h:h + 1], self.cdec[h]), w=[CD])
        vc = [fw.sb("vc%d" % i, [128, 1024], BF16) for i in range(3)]
        kc_ = [fw.sb("kc%d" % i, [128, 512], BF16) for i in range(3)]
        gc = [fw.sb("gc%d" % i, [128, 1024]) for i in range(3)]
        kdb = [fw.sb("kd%d" % i, [128, 512], BF16) for i in range(2)]
        sTm = [fw.sb("sTm%d" % i, [128, 4, 128], BF16) for i in range(2)]
        ysb = [fw.sb("ys%d" % i, [128, 4, 256]) for i in range(2)]
        y2 = fw.sb("y2", [128, 4, 256])
        stb = [fw.sb("rst%d" % i, [128, 16]) for i in range(2)]
        yr = fw.sb("yr", [128, 1024])
        yTb = [fw.sb("ryT%d" % i, [128, 8, 128], BF16) for i in range(2)]
        DMT = self.cv("DMT").rearrange("p (h l) -> p h l", l=128)
        KD = self.cv("KD")
        f3 = lambda t: t.ap.rearrange("p a b -> p (a b)")

        def loadc(c):
            i = c % 3
            rows = slice(c * 128, (c + 1) * 128)
            fw.dma("sp", vc[i][:, :], self.RV[rows, :], r=[self.RV], w=[vc[i]])
            fw.dma("sp", kc_[i][:, :], self.RKTOK[rows, :], r=[self.RKTOK], w=[kc_[i]])
            fw.dma("sp", gc[i][:, :], self.RGS[rows, :], r=[self.RGS], w=[gc[i]])

        def stageA(c):
            v, kk = vc[c % 3], kc_[c % 3]
            kd, sm_, ys = kdb[c % 2], sTm[c % 2], ysb[c % 2]
            cs = slice(c * 128, (c + 1) * 128)
            fw.op("dve", lambda e: e.tensor_tensor(out=kd.ap.rearrange("p (h d) -> p h d", d=128), in0=kk.ap.rearrange("p (h d) -> p h d", d=128),
                                                   in1=KD.unsqueeze(2).broadcast_to([128, 4, 128]), op=ALU.mult), r=[kk, C], w=[kd])
            Sb = PS[c % 2]
            for h in range(4):
                fw.op("pe", lambda e, h=h: e.matmul(Sb[:, h * 128:(h + 1) * 128], lhsT=KT[:, h, cs], rhs=QT[:, h, cs], start=True, stop=True), r=[KT, QT], w=[Sb])
            fw.op("dve", lambda e: e.tensor_tensor(out=sm_[:, :, :], in0=Sb.ap.rearrange("p (h l) -> p h l", l=128), in1=DMT, op=ALU.mult), r=[Sb, C], w=[sm_])
            for h in range(4):
                pk = PS[2 + h // 2]
                c0 = (h % 2) * 256
                fw.op("pe", lambda e, h=h, pk=pk, c0=c0: e.matmul(pk[:, c0:c0 + 256], lhsT=kd[:, h * 128:(h + 1) * 128], rhs=v[:, h * 256:(h + 1) * 256], start=True, stop=True),
                      r=[kd, v], w=[pk])
            pyb = [PS[4 + 2 * (c % 2)], PS[5 + 2 * (c % 2)]]
            for h in range(4):
                py = pyb[h // 2]
                c0 = (h % 2) * 256
                fw.op("pe", lambda e, h=h, py=py, c0=c0: e.matmul(py[:, c0:c0 + 256], lhsT=sm_[:, h, :], rhs=v[:, h * 256:(h + 1) * 256], start=True, stop=False, skip_group_check=True),
                      r=[sm_, v], w=[py])
                fw.op("pe", lambda e, h=h, py=py, c0=c0: e.matmul(py[:, c0:c0 + 256], lhsT=QS[:, h, cs], rhs=Rb[:, h, :], start=False, stop=True, skip_group_check=True),
                      r=[QS, Rb], w=[py])
            fw.op("dve", lambda e: e.tensor_tensor(out=R[:, :, :], in0=R[:, :, :], in1=CD[:, :].unsqueeze(2).broadcast_to([128, 4, 256]), op=ALU.mult), r=[R, CD], w=[R])
            for b2 in range(2):
                fw.op("dve", lambda e, b2=b2: e.tensor_tensor(out=f3(R)[:, b2 * 512:(b2 + 1) * 512], in0=PS[2 + b2][:, :], in1=f3(R)[:, b2 * 512:(b2 + 1) * 512], op=ALU.add),
                      r=[PS[2 + b2], R], w=[R])
            fw.op("act", lambda e: e.copy(out=f3(Rb), in_=f3(R)), r=[R], w=[Rb])
            for b2 in range(2):
                fw.op("act", lambda e, b2=b2: e.copy(out=f3(ys)[:, b2 * 512:(b2 + 1) * 512], in_=pyb[b2][:, :]), r=[pyb[b2]], w=[ys])

        def stageB(c):
            ys, s8, gg, yT = ysb[c % 2], stb[c % 2], gc[c % 3], yTb[c % 2]
            bc4 = lambda ap: ap.unsqueeze(2).broadcast_to([128, 4, 256])
            fw.op("dve", lambda e: e.tensor_reduce(out=s8[:, 0:4], in_=ys[:, :, :], axis=AX.X, op=ALU.add), r=[ys], w=[s8])
            fw.op("dve", lambda e: e.tensor_scalar(out=s8[:, 4:8], in0=s8[:, 0:4], scalar1=1.0 / 256, scalar2=None, op0=ALU.mult), r=[s8], w=[s8])
            fw.op("dve", lambda e: e.tensor_tensor(out=ys[:, :, :], in0=ys[:, :, :], in1=bc4(s8[:, 4:8]), op=ALU.subtract), r=[ys, s8], w=[ys])
            fw.op("act", lambda e: e.activation(out=f3(y2), in_=f3(ys), func=AF.Square), r=[ys], w=[y2])
            fw.op("dve", lambda e: e.tensor_reduce(out=s8[:, 8:12], in_=y2[:, :, :], axis=AX.X, op=ALU.add), r=[y2], w=[s8])
            fw.op("act", lambda e: e.activation(out=s8[:, 12:16], in_=s8[:, 8:12], func=AF.Sqrt, bias=self.EPSC[:, 0:1], scale=1.0 / 256), r=[s8, self.EPSC], w=[s8])
            fw.op("dve", lambda e: e.reciprocal(out=s8[:, 12:16], in_=s8[:, 12:16]), r=[s8], w=[s8])
            fw.op("dve", lambda e: e.tensor_tensor(out=ys[:, :, :], in0=ys[:, :, :], in1=bc4(s8[:, 12:16]), op=ALU.mult), r=[ys, s8], w=[ys])
            fw.op("dve", lambda e: e.tensor_tensor(out=f3(ys), in0=f3(ys), in1=rnw[:, :], op=ALU.mult), r=[ys, rnw], w=[ys])
            fw.op("dve", lambda e: e.tensor_tensor(out=yr[:, :], in0=f3(ys), in1=gg[:, :], op=ALU.mult), r=[ys, gg], w=[yr])
            for t4 in range(2):
                b = PS[2 + t4]
                for jj in range(4):
                    f = t4 * 4 + jj
                    fw.op("pe", lambda e, b=b, jj=jj, f=f: e.transpose(b[:, jj * 128:(jj + 1) * 128], yr[:, f * 128:(f + 1) * 128], self.cv("IDENT")), r=[yr, C], w=[b])
                fw.op("act", lambda e, b=b, t4=t4: e.copy(out=yT.ap[:, t4 * 4:(t4 + 1) * 4, :].rearrange("p a b -> p (a b)"), in_=b[:, :]), r=[b], w=[yT])
            fw.dma("sp", self.YT.ap[3072:4096, c * 128:(c + 1) * 128].rearrange("(f p) t -> p f t", p=128), yT[:, :, :], r=[yT], w=[(self.YT, ("r", c))])

        loadc(0)
        loadc(1)
        for c in range(16):
            stageA(c)
            if c >= 1:
                stageB(c - 1)
            if c + 2 < 16:
                loadc(c + 2)
        stageB(15)
        fw.release(m)

    def p7_merge(self, l, src):
        fw = self.fw
        m = fw.mark()
        self.WB = [fw.sb("WB%d" % i, [128, 11264], BF16) for i in range(3)]
        yt = fw.sb("ytall", [128, 32, 512], BF16)
        mg = [fw.sb("mg%d" % i, [128, 512], BF16) for i in range(3)]
        mer = fw.sb("mer", [128, 16, 512])
        merb = fw.sb("merb", [128, 16, 512], BF16)
        tmp = [fw.sb("mtmp%d" % i, [128, 512]) for i in range(2)]
        xt = [fw.sb("mxt%d" % i, [128, 512]) for i in range(2)]
        branches = ((self.p_nsa, 8, 0, 0), (self.p_ssd, 16, 8, 1), (self.p_ret, 8, 24, 2))
        k = 0
        for tg in range(4):
            ts = slice(tg * 512, (tg + 1) * 512)
            for k0 in range(0, 32, 8):
                fw.dma("sp", yt[:, k0:k0 + 8, :], self.YT.ap[k0 * 128:(k0 + 8) * 128, ts].rearrange("(k p) t -> p k t", p=128), r=[self.YT], w=[(yt, k0)])
            for (W, KC, koff, bi) in branches:
                for cb in range(4):
                    buf, view = self.wload(W, W.ap[l][:, cb * 512:(cb + 1) * 512], KC, 512)
                    for j in range(4):
                        ct = cb * 4 + j
                        g = mg[k % 3]
                        t_ = tmp[k % 2]
                        k += 1
                        r0 = bi * 2048 + ct * 128
                        fw.dma("sp", g[:, :], self.MGT[r0:r0 + 128, ts], r=[self.MGT], w=[g])
                        ps = self.bank()
                        for kc in range(KC):
                            fw.op("pe", lambda e, ps=ps, kc=kc, j=j, view=view, koff=koff, KC=KC: e.matmul(ps[:, :], lhsT=view[:, kc, j * 128:(j + 1) * 128], rhs=yt[:, koff + kc, :],
                                                                                                   start=(kc == 0), stop=(kc == KC - 1)), r=[buf, yt], w=[ps])
                        if bi == 0:
                            fw.op("dve", lambda e, ps=ps, g=g, ct=ct: e.tensor_tensor(out=mer[:, ct, :], in0=ps[:, :], in1=g[:, :], op=ALU.mult), r=[ps, g], w=[(mer, ct)])
                        else:
                            fw.op("dve", lambda e, ps=ps, g=g, t_=t_: e.tensor_tensor(out=t_[:, :], in0=ps[:, :], in1=g[:, :], op=ALU.mult), r=[ps, g], w=[t_])
                            fw.op("dve", lambda e, t_=t_, ct=ct: e.tensor_tensor(out=mer[:, ct, :], in0=mer[:, ct, :], in1=t_[:, :], op=ALU.add), r=[t_, (mer, ct)], w=[(mer, ct)])
            for ct in range(16):
                fw.op("act", lambda e, ct=ct: e.copy(out=merb[:, ct, :], in_=mer[:, ct, :]), r=[(mer, ct)], w=[(merb, ct)])
            for cb in range(4):
                buf, view = self.wload(self.w_out, self.w_out.ap[l][:, cb * 512:(cb + 1) * 512], 16, 512)
                for j in range(4):
                    ct = cb * 4 + j
                    x = xt[k % 2]
                    k += 1
                    fw.dma("sp", x[:, :], src[ct * 128:(ct + 1) * 128, ts], r=[(src, (ct, tg))], w=[x])
                    ps = self.bank()
                    for kc in range(16):
                        fw.op("pe", lambda e, ps=ps, kc=kc, j=j, view=view: e.matmul(ps[:, :], lhsT=view[:, kc, j * 128:(j + 1) * 128], rhs=merb[:, kc, :], start=(kc == 0), stop=(kc == 15)),
                              r=[buf, merb], w=[ps])
                    fw.op("dve", lambda e, ps=ps, x=x: e.tensor_tensor(out=x[:, :], in0=ps[:, :], in1=x[:, :], op=ALU.add), r=[ps, x], w=[x])
                    fw.dma("sp", self.XT[ct * 128:(ct + 1) * 128, ts], x[:, :], r=[x], w=[(self.XT, (ct, tg))])
        fw.release(m)

    def p8_ffn(self, l):
        fw = self.fw
        m = fw.mark()
        self.WB = [fw.sb("WB%d" % i, [128, 11264], BF16) for i in range(3)]
        xg = fw.sb("fxg", [128, 16, 512])
        fT = fw.sb("fT", [128, 16, 512], BF16)
        hT = fw.sb("hT", [128, 44, 512], BF16)
        sq = [fw.sb("fsq%d" % i, [128, 512]) for i in range(2)]
        rs = fw.sb("frs", [128, 512])
        sg = [fw.sb("fsg%d" % i, [128, 512]) for i in range(2)]
        xo = [fw.sb("fxo%d" % i, [128, 512]) for i in range(2)]
        wcol = self.NRM[:, (2 * l + 1) * 16:(2 * l + 2) * 16]
        xv = self.XT.ap.rearrange("(k p) t -> p k t", p=128)
        k = 0
        for tg in range(4):
            ts = slice(tg * 512, (tg + 1) * 512)
            for k0 in range(0, 16, 4):
                fw.dma("sp", xg[:, k0:k0 + 4, :], xv[:, k0:k0 + 4, ts], r=[self.XT], w=[(xg, k0)])
            self.rmsnorm(xg, 512, wcol, lambda kc: (fT, fT[:, kc, :]), sq, rs)
            for cb in range(11):
                bg, vg = self.wload(self.w_gate, self.w_gate.ap[l][:, cb * 512:(cb + 1) * 512], 16, 512)
                bu, vu = self.wload(self.w_up, self.w_up.ap[l][:, cb * 512:(cb + 1) * 512], 16, 512)
                for j in range(4):
                    f = cb * 4 + j
                    pg = self.bank()
                    pu = self.bank()
                    for kc in range(16):
                        fw.op("pe", lambda e, pg=pg, kc=kc, j=j, vg=vg: e.matmul(pg[:, :], lhsT=vg[:, kc, j * 128:(j + 1) * 128], rhs=fT[:, kc, :], start=(kc == 0), stop=(kc == 15)),
                              r=[bg, fT], w=[pg])
                    for kc in range(16):
                        fw.op("pe", lambda e, pu=pu, kc=kc, j=j, vu=vu: e.matmul(pu[:, :], lhsT=vu[:, kc, j * 128:(j + 1) * 128], rhs=fT[:, kc, :], start=(kc == 0), stop=(kc == 15)),
                              r=[bu, fT], w=[pu])
                    s = sg[k % 2]
                    k += 1
                    fw.op("act", lambda e, pg=pg, s=s: e.activation(out=s[:, :], in_=pg[:, :], func=AF.Silu), r=[pg], w=[s])
                    fw.op("dve", lambda e, pu=pu, s=s, f=f: e.tensor_tensor(out=hT[:, f, :], in0=pu[:, :], in1=s[:, :], op=ALU.mult), r=[pu, s], w=[(hT, f)])
            for cb in range(8):
                bd, vd = self.wload(self.w_down, self.w_down.ap[l][:, cb * 256:(cb + 1) * 256], 44, 256)
                for j in range(2):
                    ct = cb * 2 + j
                    ps = self.bank()
                    for kc in range(44):
                        fw.op("pe", lambda e, ps=ps, kc=kc, j=j, vd=vd: e.matmul(ps[:, :], lhsT=vd[:, kc, j * 128:(j + 1) * 128], rhs=hT[:, kc, :], start=(kc == 0), stop=(kc == 43)),
                              r=[bd, hT], w=[ps])
                    x = xo[k % 2]
                    k += 1
                    fw.op("dve", lambda e, ps=ps, x=x, ct=ct: e.tensor_tensor(out=x[:, :], in0=ps[:, :], in1=xg[:, ct, :], op=ALU.add), r=[ps, xg], w=[x])
                    fw.dma("sp", self.XT[ct * 128:(ct + 1) * 128, ts], x[:, :], r=[x], w=[(self.XT, (ct, tg))])
        fw.release(m)

    def final_norm(self, src):
        fw = self.fw
        m = fw.mark()
        xg = fw.sb("nxg", [128, 16, 512])
        og = fw.sb("nog", [128, 16, 512])
        sq = [fw.sb("nsq%d" % i, [128, 512]) for i in range(2)]
        rs = fw.sb("nrs", [128, 512])
        wcol = self.NRM[:, (2 * NL) * 16:(2 * NL + 1) * 16]
        xv = src.ap.rearrange("(k p) t -> p k t", p=128)
        ov = self.outT.ap.rearrange("(k p) t -> p k t", p=128)
        for tg in range(4):
            ts = slice(tg * 512, (tg + 1) * 512)
            for k0 in range(0, 16, 4):
                fw.dma("sp", xg[:, k0:k0 + 4, :], xv[:, k0:k0 + 4, ts], r=[src], w=[(xg, k0)])
            self.rmsnorm(xg, 512, wcol, lambda kc: (og, og[:, kc, :]), sq, rs)
            for k0 in range(0, 16, 4):
                fw.dma("sp", ov[:, k0:k0 + 4, ts], og[:, k0:k0 + 4, :], r=[og], w=[(self.outT, (tg, k0))])
        fw.release(m)


def host_inputs(inp, b):
    consts, tab, et, _ = host_consts()
    f = np.float32
    nrm = np.zeros((128, (2 * NL + 1) * 16), f)
    for l in range(NL):
        nrm[:, (2 * l) * 16:(2 * l + 1) * 16] = np.asarray(inp["norm_mix"][l], f).reshape(16, 128).T
        nrm[:, (2 * l + 1) * 16:(2 * l + 2) * 16] = np.asarray(inp["norm_ffn"][l], f).reshape(16, 128).T
    nrm[:, 2 * NL * 16:] = np.asarray(inp["norm_final"], f).reshape(16, 128).T
    cw = np.zeros((NL, 128, 24, 5), f)
    for l in range(NL):
        cw[l, :, :, 0:4] = np.asarray(inp["conv_w"][l], f).T.reshape(24, 128, 4).transpose(1, 0, 2)
        cw[l, :, :, 4] = np.asarray(inp["conv_b"][l], f).reshape(24, 128).T
    ssdv = np.zeros((NL, 128, 96), f)
    ssdv[:, :, 0:32] = np.asarray(inp["dt_bias"], f)[:, None, :]
    ssdv[:, :, 32:64] = np.asarray(inp["a_log"], f)[:, None, :]
    ssdv[:, :, 64:96] = np.asarray(inp["d_skip"], f)[:, None, :]
    snw = np.ascontiguousarray(np.broadcast_to(np.asarray(inp["ssd_norm"], f)[:, None, :], (NL, 128, 2048)))
    rnw = np.ascontiguousarray(np.broadcast_to(np.asarray(inp["ret_norm"], f).reshape(NL, 1, 1024), (NL, 128, 1024)))
    pet = np.stack([np.asarray(inp["cmp_k_pe"], f).transpose(0, 2, 1), np.asarray(inp["cmp_v_pe"], f).transpose(0, 2, 1)], axis=1)
    d = {
        "xT": np.ascontiguousarray(np.asarray(inp["x"][b], f).T),
        "consts": consts, "tab": tab, "et": et, "nrm": nrm, "cw": cw.reshape(NL, 128, 120), "ssdv": ssdv,
        "snw": snw, "rnw": rnw, "pet": np.ascontiguousarray(pet),
    }
    for k in ("w_in", "cmp_k_w1", "cmp_v_w1", "cmp_k_w2", "cmp_v_w2", "p_nsa", "p_ssd", "p_ret", "w_out", "w_gate", "w_up", "w_down"):
        d[k] = np.ascontiguousarray(np.asarray(inp[k], f))
    return d


_PROG = {}


def kernel(**inputs):
    if "p" not in _PROG:
        _PROG["p"] = Prog()
    prog = _PROG["p"]
    ncores = 4
    base = host_inputs(inputs, 0)
    in_maps = []
    for b in range(ncores):
        d = dict(base)
        d["xT"] = np.ascontiguousarray(np.asarray(inputs["x"][b], np.float32).T)
        in_maps.append(d)
    res = run_bass_kernel_spmd(prog.nc, in_maps, core_ids=list(range(ncores)))
    out = np.stack([np.asarray(res.results[b]["outT"], np.float32).T for b in range(4)], axis=0)
    return np.ascontiguousarray(out)
```

```python
import numpy as np
import concourse.bass as bass
import concourse.mybir as mybir
from concourse.bass_utils import run_bass_kernel_spmd
from contextlib import ExitStack

F32 = mybir.dt.float32
BF16 = mybir.dt.bfloat16
ALU = mybir.AluOpType
AF = mybir.ActivationFunctionType
AX = mybir.AxisListType

COMPUTE = ("pe", "dve", "act", "pool")
NSLOT = {"sp": 16, "pool": 16, "act": 8}
ARENA = 53000

NT = 2048
D = 2048
DIN = 16952
DFF = 5632
NL = 4
C_Q, C_KV, C_NG, C_Z, C_XBC, C_DT, C_RQ, C_RK, C_RV, C_RG, C_MG = 0, 1024, 2560, 2584, 4632, 7704, 7736, 8248, 8760, 9784, 10808
NEG = -30000.0
EPS = 1e-6


class St:
    __slots__ = ("lw", "rd")

    def __init__(self, o=None):
        self.lw = o.lw if o else None
        self.rd = list(o.rd) if o else []


class T:
    def __init__(self, ap, name, excl=False):
        self.ap = ap
        self.name = name
        self.excl = excl
        self.st = {None: St()}

    def __getitem__(self, k):
        return self.ap[k]

    def states(self, key):
        if key is None:
            return list(self.st.values())
        if key not in self.st:
            self.st[key] = St(self.st[None])
        return [self.st[key]]


class _Rec:
    def __getattr__(self, name):
        return lambda *a, **k: (name, a, k)


_REC = _Rec()


class FW:
    def __init__(self, nc):
        self.nc = nc
        self.stream = {e: [] for e in ("pe", "dve", "act", "pool", "sp")}
        self.nops = {e: 0 for e in COMPUTE}
        self.sig = {e: set() for e in COMPUTE}
        self.ndma = {q: 0 for q in NSLOT}
        self.known = {e: {} for e in self.stream}
        self.es = ExitStack()
        self.sems = {}
        self.dsems = {}
        self.arena = self.es.enter_context(nc.sbuf_tensor("arena", [128, ARENA], F32))
        self.aoff = 0
        self.PS = [T(self.es.enter_context(nc.psum_tensor("ps%d" % i, [128, 512], F32))[:, :], "ps%d" % i, excl=True)
                   for i in range(8)]

    def sb(self, name, shape, dtype=F32):
        p = shape[0]
        n = int(np.prod(shape[1:]))
        nb = n * (4 if dtype == F32 else 2)
        nf = ((nb + 63) // 64) * 16
        assert self.aoff + nf <= ARENA, ("SBUF arena overflow", name, self.aoff, nf)
        ap = self.arena[0:p, self.aoff:self.aoff + nf]
        self.aoff += nf
        if dtype != F32:
            ap = ap.bitcast(dtype)
        ap = ap[:, 0:n]
        if len(shape) == 3:
            ap = ap.rearrange("p (a b) -> p a b", a=shape[1], b=shape[2])
        elif len(shape) == 4:
            ap = ap.rearrange("p (a b c) -> p a b c", a=shape[1], b=shape[2], c=shape[3])
        return T(ap, name)

    def mark(self):
        return self.aoff

    def release(self, m):
        self.barrier()
        self.aoff = m

    def dram(self, name, shape, dtype, kind="Internal"):
        h = self.nc.dram_tensor(name, list(shape), dtype, kind=kind)
        return T(h.ap(), name)

    def _need(self, eng, ev, waits):
        if ev is None:
            return
        if ev[0] == "c":
            _, f, idx = ev
            if f == eng and eng == "pe":
                return
            k = ("c", f)
            if self.known[eng].get(k, 0) >= idx:
                return
            self.known[eng][k] = idx
            self.sig[f].add(idx)
            waits.append(ev)
        else:
            _, q, j = ev
            k = ("d", q, j % NSLOT[q])
            if self.known[eng].get(k, -1) >= j:
                return
            self.known[eng][k] = j
            waits.append(ev)

    def _deps(self, eng, r, w, is_dma):
        waits = []
        rs, ws = [], []
        for x in r:
            t, key = x if isinstance(x, tuple) else (x, None)
            (ws if t.excl else rs).extend(t.states(key))
        for x in w:
            t, key = x if isinstance(x, tuple) else (x, None)
            ws.extend(t.states(key))
        for s in rs:
            self._need(eng, s.lw, waits)
        for s in ws:
            lw = s.lw
            if lw is not None and not ((not is_dma) and lw[0] == "c" and lw[1] == eng):
                self._need(eng, lw, waits)
            for ev in s.rd:
                if (not is_dma) and ev[0] == "c" and ev[1] == eng:
                    continue
                self._need(eng, ev, waits)
        return waits, rs, ws

    def _commit(self, ev, rs, ws):
        for s in rs:
            if ev[0] == "c":
                s.rd = [e for e in s.rd if not (e[0] == "c" and e[1] == ev[1])]
            s.rd.append(ev)
        for s in ws:
            s.lw = ev
            s.rd = []

    def op(self, eng, fn, r=(), w=()):
        waits, rs, ws = self._deps(eng, r, w, False)
        for ev in waits:
            self.stream[eng].append(("wait", ev))
        self.nops[eng] += 1
        idx = self.nops[eng]
        self.stream[eng].append(("op", fn(_REC), idx))
        self._commit(("c", eng, idx), rs, ws)

    def dma(self, q, out_ap, in_ap, r=(), w=()):
        waits, rs, ws = self._deps(q, r, w, True)
        j = self.ndma[q]
        K = NSLOT[q]
        if j >= K:
            self._need(q, ("d", q, j - K), waits)
        for ev in waits:
            self.stream[q].append(("wait", ev))
        self.ndma[q] += 1
        self.stream[q].append(("dma", out_ap, in_ap, j))
        self._commit(("d", q, j), rs, ws)

    def barrier(self):
        for e in self.stream:
            waits = []
            for f in COMPUTE:
                if self.nops[f] > 0:
                    self._need(e, ("c", f, self.nops[f]), waits)
            for q in NSLOT:
                for j in range(max(0, self.ndma[q] - NSLOT[q]), self.ndma[q]):
                    self._need(e, ("d", q, j), waits)
            for ev in waits:
                self.stream[e].append(("wait", ev))

    def emit(self):
        nc = self.nc
        self.barrier()
        es = self.es
        for e in COMPUTE:
            self.sems[e] = es.enter_context(nc.semaphore("s_" + e))
        for q in NSLOT:
            self.dsems[q] = [es.enter_context(nc.semaphore("d_%s_%d" % (q, i))) for i in range(NSLOT[q])]
        cum = {}
        for e in COMPUTE:
            c = 0
            m = {}
            for i in range(1, self.nops[e] + 1):
                if i in self.sig[e]:
                    c += 1
                    m[i] = c
            cum[e] = m

        def replay(ename, eng):
            for rec in self.stream[ename]:
                if rec[0] == "wait":
                    ev = rec[1]
                    if ev[0] == "c":
                        eng.wait_ge(self.sems[ev[1]], cum[ev[1]][ev[2]])
                    else:
                        _, q, j = ev
                        eng.wait_ge(self.dsems[q][j % NSLOT[q]], 16 * (j // NSLOT[q] + 1))
                elif rec[0] == "op":
                    nm, ar, kw = rec[1]
                    ins = getattr(eng, nm)(*ar, **kw)
                    if rec[2] in self.sig[ename]:
                        ins.then_inc(self.sems[ename], 1)
                elif rec[0] == "cc":
                    _, kind, i, o, rg, j = rec
                    eng.collective_compute(kind, ALU.bypass, replica_groups=rg, ins=[i], outs=[o]).then_inc(self.dsems[ename][j % NSLOT[ename]], 16)
                else:
                    _, o, i, j = rec
                    eng.dma_start(out=o, in_=i).then_inc(self.dsems[ename][j % NSLOT[ename]], 16)

        with nc.Block() as block:
            @block.tensor
            def _(e):
                replay("pe", e)

            @block.vector
            def _(e):
                replay("dve", e)

            @block.scalar
            def _(e):
                replay("act", e)

            @block.gpsimd
            def _(e):
                replay("pool", e)

            @block.sync
            def _(e):
                replay("sp", e)
        es.close()


CO = {}
_off = 0
for _n, _w in (("IDENT", 128), ("CAUS", 128), ("TGT", 128), ("ONESM", 128), ("ONES1", 128), ("MSK", 1536),
               ("OVL", 33), ("DMT", 512), ("QD", 512), ("KD", 4)):
    CO[_n] = (_off, _w)
    _off += _w
NCONST = _off
TABW = 896 + 512 + 1408 + 2048
TO_U, TO_W, TO_UW, TO_C = 0, 896, 1408, 2816


def host_consts():
    c = np.zeros((128, NCONST), np.float64)
    p = np.arange(128)[:, None]
    f = np.arange(128)[None, :]
    c[:, CO["IDENT"][0]:][:, :128] = (p == f)
    c[:, CO["CAUS"][0]:][:, :128] = (f >= p)
    c[:, CO["TGT"][0]:][:, :128] = (p > f)
    c[:, CO["ONESM"][0]:][:, :128] = 1.0 / D
    c[:, CO["ONES1"][0]:][:, :128] = 1.0
    msk = np.zeros((128, 3, 16, 32))
    for Q in range(16):
        tq = Q * 128 + np.arange(128)
        cur = (tq // 64)[:, None]
        blk = np.arange(32)[None, :]
        msk[:, 0, Q, :] = ((blk > 0) & (blk < cur))
        msk[:, 1, Q, :] = np.where((blk == cur) | (blk == 0), 1e9, np.where(blk > cur, -1e30, 0.0))
        msk[:, 2, Q, :] = (blk <= cur)
    c[:, CO["MSK"][0]:][:, :1536] = msk.reshape(128, -1)
    n = np.arange(128)[:, None]
    k = np.arange(32)[None, :]
    ovl = ((16 * n < 64 * k + 64) & (16 * n + 31 >= 64 * k) & (n < 127)).astype(np.float64)
    c[:, CO["OVL"][0]] = 1.0
    c[:, CO["OVL"][0] + 1:][:, :32] = ovl
    h = np.arange(4, dtype=np.float64)
    log_g = np.log1p(-np.exp2(-5.0 - h))
    s_ = np.arange(128)[:, None]
    l_ = np.arange(128)[None, :]
    for hh in range(4):
        dm = np.where(l_ >= s_, np.exp((l_ - s_) * log_g[hh]), 0.0)
        c[:, CO["DMT"][0] + hh * 128:][:, :128] = dm
        c[:, CO["QD"][0] + hh * 128:][:, :128] = np.exp((np.arange(128) + 1.0) * log_g[hh])[None, :]
        c[:, CO["KD"][0] + hh] = np.exp((127.0 - np.arange(128)) * log_g[hh])
    cdec = [float(np.exp(128 * log_g[hh])) for hh in range(4)]
    tab = np.zeros((8, 128, TABW), np.float64)
    ki = np.arange(128)[:, None]
    for hd in range(8):
        s = 2.0 ** (-(hd + 1))
        cc = np.arange(896)[None, :]
        rel = cc - 384 - ki
        tab[hd, :, TO_U:TO_U + 896] = np.where(rel >= 0, -s * rel, NEG)
        qi = np.arange(512)[None, :]
        tab[hd, :, TO_W:TO_W + 512] = -s * (qi - ki)
        cc = np.arange(1408)[None, :]
        rel = cc - 384 - ki
        tab[hd, :, TO_UW:TO_UW + 1408] = np.where((rel >= 0) & (rel < 512), -s * rel, NEG)
        tq = np.arange(2048)[None, :]
        rel = tq - (16 * ki + 31)
        tab[hd, :, TO_C:TO_C + 2048] = np.where((rel >= 0) & (ki < 127), -s * rel, NEG)
    et = (np.arange(2048)[None, :] // 64 == np.arange(32)[:, None]).astype(np.float32)
    return c.astype(np.float32), tab.astype(np.float32), et, cdec


class Prog:
    def __init__(self, n_layers=NL, dbg=False, stop_after=None):
        self.nl = n_layers
        self.dbg = dbg
        self.stop_after = stop_after
        self.nc = bass.Bass("TRN2", target_bir_lowering=False)
        self.fw = FW(self.nc)
        self.cdec = host_consts()[3]
        self.rot = 0
        self.build()

    def bank(self):
        b = self.fw.PS[self.rot % 8]
        self.rot += 1
        return b

    def din(self, name, shape, dtype=F32):
        return self.fw.dram(name, shape, dtype, kind="ExternalInput")

    def scr(self, name, shape, dtype):
        return self.fw.dram(name, shape, dtype, kind="ExternalOutput" if self.dbg else "Internal")

    def wload(self, Wt, w_ap, KC, n):
        fw = self.fw
        buf = self.WB[self.wrot % len(self.WB)]
        self.wrot += 1
        view = buf.ap[:, 0:KC * n].rearrange("p (k c) -> p k c", k=KC, c=n)
        src = w_ap.rearrange("(k p) c -> p k c", p=128)
        step = 4
        for i, k0 in enumerate(range(0, KC, step)):
            k1 = min(KC, k0 + step)
            fw.dma("pool", view[:, k0:k1, :], src[:, k0:k1, :], r=[Wt], w=[(buf, i)])
        return buf, view

    def rmsnorm(self, xg, ntok, wcol, out_fn, sq, rs):
        fw = self.fw
        C = self.C
        ps = self.bank()
        for kc in range(16):
            s = sq[kc % 2]
            fw.op("act", lambda e, s=s, kc=kc: e.activation(out=s[:, 0:ntok], in_=xg[:, kc, :], func=AF.Square), r=[xg], w=[s])
            fw.op("pe", lambda e, s=s, kc=kc: e.matmul(ps[:, 0:ntok], lhsT=self.cv("ONESM"), rhs=s[:, 0:ntok], start=(kc == 0), stop=(kc == 15)),
                  r=[s, C], w=[ps])
        fw.op("act", lambda e: e.activation(out=rs[:, 0:ntok], in_=ps[:, 0:ntok], func=AF.Sqrt, bias=self.EPSC[:, 0:1], scale=1.0), r=[ps, self.EPSC], w=[rs])
        fw.op("dve", lambda e: e.reciprocal(out=rs[:, 0:ntok], in_=rs[:, 0:ntok]), r=[rs], w=[rs])
        for kc in range(16):
            ot, oap = out_fn(kc)
            fw.op("dve", lambda e, kc=kc, oap=oap: e.scalar_tensor_tensor(out=oap, in0=xg[:, kc, :], scalar=wcol[:, kc:kc + 1], in1=rs[:, 0:ntok],
                                                                         op0=ALU.mult, op1=ALU.mult), r=[xg, rs, self.NRM], w=[ot])

    def cv(self, name, width=None):
        o, w = CO[name]
        return self.C[:, o:o + (width or w)]

    def cvt(self, name):
        if name == "EPSC":
            return self.EPSC
        raise KeyError(name)

    def build(self):
        fw = self.fw
        nl = self.nl
        self.xT_in = self.din("xT", [D, NT])
        self.d_consts = self.din("consts", [128, NCONST])
        self.d_tab = self.din("tab", [8, 128, TABW])
        self.d_et = self.din("et", [32, NT])
        self.d_nrm = self.din("nrm", [128, (2 * NL + 1) * 16])
        self.d_cw = self.din("cw", [NL, 128, 24 * 5])
        self.d_ssdv = self.din("ssdv", [NL, 128, 96])
        self.d_snw = self.din("snw", [NL, 128, 2048])
        self.d_rnw = self.din("rnw", [NL, 128, 1024])
        self.d_pet = self.din("pet", [NL, 2, 128, 32])
        self.w_in = self.din("w_in", [NL, D, DIN])
        self.cw1 = [self.din("cmp_k_w1", [NL, 4096, 128]), self.din("cmp_v_w1", [NL, 4096, 128])]
        self.cw2 = [self.din("cmp_k_w2", [NL, 128, 128]), self.din("cmp_v_w2", [NL, 128, 128])]
        self.p_nsa = self.din("p_nsa", [NL, 1024, D])
        self.p_ssd = self.din("p_ssd", [NL, 2048, D])
        self.p_ret = self.din("p_ret", [NL, 1024, D])
        self.w_out = self.din("w_out", [NL, D, D])
        self.w_gate = self.din("w_gate", [NL, D, DFF])
        self.w_up = self.din("w_up", [NL, D, DFF])
        self.w_down = self.din("w_down", [NL, DFF, D])
        self.outT = fw.dram("outT", [D, NT], F32, kind="ExternalOutput")
        self.XT = self.scr("XT", [D, NT], F32)
        self.QT = self.scr("QT", [1024, NT], BF16)
        self.KVT = self.scr("KVT", [1024, NT], BF16)
        self.VTOK = self.scr("VTOK", [NT, 512], BF16)
        self.NG = self.scr("NG", [NT, 24], F32)
        self.ZS = self.scr("ZS", [NT, 2048], F32)
        self.XBCT = self.scr("XBCT", [3072, NT], F32)
        self.DTR = self.scr("DTR", [NT, 32], F32)
        self.RQT = self.scr("RQT", [512, NT], BF16)
        self.RKT = self.scr("RKT", [512, NT], BF16)
        self.RKTOK = self.scr("RKTOK", [NT, 512], BF16)
        self.RV = self.scr("RV", [NT, 1024], BF16)
        self.RGS = self.scr("RGS", [NT, 1024], F32)
        self.MGT = self.scr("MGT", [6144, NT], BF16)
        self.XTOK = self.scr("XTOK", [NT, 2048], F32)
        self.BCT = self.scr("BCT", [1024, NT], BF16)
        self.BTOK = self.scr("BTOK", [NT, 512], BF16)
        self.YT = self.scr("YT", [4096, NT], BF16)

        self.C = fw.sb("C", [128, NCONST])
        self.NRM = fw.sb("NRM", [128, (2 * NL + 1) * 16])
        self.EPSC = fw.sb("EPSC", [128, 2])
        fw.dma("sp", self.C[:, :], self.d_consts[:, :], r=[self.d_consts], w=[self.C])
        fw.dma("sp", self.NRM[:, :], self.d_nrm[:, :], r=[self.d_nrm], w=[self.NRM])
        fw.op("dve", lambda e: e.memset(self.EPSC[:, 0:1], EPS), w=[self.EPSC])
        fw.op("dve", lambda e: e.memset(self.EPSC[:, 1:2], 1.0), w=[self.EPSC])
        self.wrot = 0
        self.base = fw.mark()

        src = self.xT_in
        for l in range(nl):
            self.layer(l, src)
            src = self.XT
            if self.stop_after is not None:
                break
        if self.stop_after is None:
            self.final_norm(src)
        fw.emit()

    def layer(self, l, src):
        sa = self.stop_after
        self.p1_inproj(l, src)
        if sa == "p1":
            return
        self.p2_ssdprep(l)
        if sa == "p2":
            return
        self.p34_nsa(l)
        if sa == "p4":
            return
        self.p5_ssd(l)
        if sa == "p5":
            return
        self.p6_ret(l)
        if sa == "p6":
            return
        self.p7_merge(l, src)
        if sa == "p7":
            return
        self.p8_ffn(l)

    def p1_inproj(self, l, src):
        fw = self.fw
        m = fw.mark()
        self.WB = [fw.sb("WB%d" % i, [128, 11264], BF16) for i in range(3)]
        uT = fw.sb("uT", [128, 16, NT], BF16)
        stF = [fw.sb("stF%d" % i, [128, NT], F32) for i in range(2)]
        stFb = [fw.sb("stFb%d" % i, [128, NT], BF16) for i in range(2)]
        stT = [fw.sb("stT%d" % i, [128, 512], F32) for i in range(2)]
        stTb = [fw.sb("stTb%d" % i, [128, 512], BF16) for i in range(2)]
        m2 = fw.mark()
        xg = [fw.sb("xg%d" % i, [128, 16, 256], F32) for i in range(1)]
        sq = [fw.sb("sq%d" % i, [128, 512], F32) for i in range(2)]
        rs = fw.sb("rs", [128, 512], F32)
        wcol = self.NRM[:, (2 * l) * 16:(2 * l + 1) * 16]
        srcv = src.ap.rearrange("(k p) t -> p k t", p=128)
        for tg in range(8):
            x = xg[0]
            for k0 in range(0, 16, 4):
                fw.dma("sp", x[:, k0:k0 + 4, :], srcv[:, k0:k0 + 4, tg * 256:(tg + 1) * 256], r=[src], w=[(x, k0)])
            self.rmsnorm(x, 256, wcol, lambda kc, tg=tg: (uT, uT[:, kc, tg * 256:(tg + 1) * 256]), sq, rs)
        fw.aoff = m2
        fw.barrier()

        W = self.w_in
        wl = W.ap[l]
        cnt = [0]

        def fm(c0, c1, dst, r0, func, scale, bf):
            c = c0
            while c < c1:
                nblk = min(512, c1 - c)
                buf, view = self.wload(W, wl[:, c:c + nblk], 16, nblk)
                for j0 in range(0, nblk, 128):
                    n = min(128, nblk - j0)
                    banks = [self.bank() for _ in range(4)]
                    for kc in range(16):
                        for tg in range(4):
                            fw.op("pe", lambda e, kc=kc, tg=tg, b=banks[tg], n=n, j0=j0, view=view: e.matmul(
                                b[0:n, :], lhsT=view[:, kc, j0:j0 + n], rhs=uT[:, kc, tg * 512:(tg + 1) * 512],
                                start=(kc == 0), stop=(kc == 15)), r=[buf, uT], w=[banks[tg]])
                    i = cnt[0] % 2
                    cnt[0] += 1
                    st = stFb[i] if bf else stF[i]
                    for tg in range(4):
                        fw.op("act", lambda e, tg=tg, b=banks[tg], n=n, st=st: e.activation(
                            out=st[0:n, tg * 512:(tg + 1) * 512], in_=b[0:n, :], func=func, scale=scale), r=[banks[tg]], w=[(st, tg)])
                    rr = r0 + (c - c0) + j0
                    fw.dma("sp", dst[rr:rr + n, :], st[0:n, :], r=[st], w=[(dst, rr)])
                c += nblk

        def tm(c0, c1, dst, d0, func, bf):
            c = c0
            while c < c1:
                nblk = min(512, c1 - c)
                buf, view = self.wload(W, wl[:, c:c + nblk], 16, nblk)
                for tt in range(16):
                    b = self.bank()
                    for kc in range(16):
                        fw.op("pe", lambda e, kc=kc, tt=tt, b=b, nblk=nblk, view=view: e.matmul(
                            b[:, 0:nblk], lhsT=uT[:, kc, tt * 128:(tt + 1) * 128], rhs=view[:, kc, 0:nblk],
                            start=(kc == 0), stop=(kc == 15)), r=[buf, uT], w=[b])
                    i = cnt[0] % 2
                    cnt[0] += 1
                    st = stTb[i] if bf else stT[i]
                    fw.op("act", lambda e, b=b, nblk=nblk, st=st: e.activation(out=st[:, 0:nblk], in_=b[:, 0:nblk], func=func), r=[b], w=[st])
                    dd = d0 + (c - c0)
                    fw.dma("sp", dst[tt * 128:(tt + 1) * 128, dd:dd + nblk], st[:, 0:nblk], r=[st], w=[(dst, (tt, dd))])
                c += nblk

        ID, SIG, SILU = AF.Identity, AF.Sigmoid, AF.Silu
        sc = 128.0 ** -0.5
        fm(C_Q, C_Q + 1024, self.QT, 0, ID, sc, True)
        fm(C_KV, C_KV + 768, self.KVT, 0, ID, 1.0, True)
        tm(C_KV + 768, C_KV + 1024, self.VTOK, 0, ID, True)
        fm(C_KV + 1024, C_KV + 1280, self.KVT, 768, ID, 1.0, True)
        tm(C_KV + 1280, C_KV + 1536, self.VTOK, 256, ID, True)
        tm(C_NG, C_NG + 24, self.NG, 0, SIG, False)
        tm(C_Z, C_Z + 2048, self.ZS, 0, SILU, False)
        fm(C_XBC, C_XBC + 3072, self.XBCT, 0, ID, 1.0, False)
        tm(C_DT, C_DT + 32, self.DTR, 0, ID, False)
        fm(C_RQ, C_RQ + 512, self.RQT, 0, ID, sc, True)
        fm(C_RK, C_RK + 512, self.RKT, 0, ID, 1.0, True)
        tm(C_RK, C_RK + 512, self.RKTOK, 0, ID, True)
        tm(C_RV, C_RV + 1024, self.RV, 0, ID, True)
        tm(C_RG, C_RG + 1024, self.RGS, 0, SILU, False)
        fm(C_MG, C_MG + 6144, self.MGT, 0, SIG, 1.0, True)
        fw.release(m)

    def p2_ssdprep(self, l):
        fw = self.fw
        m = fw.mark()
        cw = fw.sb("cw", [128, 24, 5])
        fw.dma("sp", cw[:, :, :], self.d_cw.ap[l].rearrange("p (t k) -> p t k", k=5), r=[self.d_cw], w=[cw])
        xp = [fw.sb("xp%d" % i, [128, 3 + NT]) for i in range(2)]
        acc = [fw.sb("acc%d" % i, [128, NT]) for i in range(2)]
        xo = [fw.sb("xo%d" % i, [128, NT]) for i in range(2)]
        xb = [fw.sb("xob%d" % i, [128, NT], BF16) for i in range(2)]
        stt = [fw.sb("sttok%d" % i, [128, 4, 128]) for i in range(2)]
        sttb = [fw.sb("sttokb%d" % i, [128, 4, 128], BF16) for i in range(2)]
        for i in range(2):
            fw.op("dve", lambda e, i=i: e.memset(xp[i][:, 0:3], 0.0), w=[xp[i]])
        k = 0
        fw.dma("sp", xp[0][:, 3:3 + NT], self.XBCT[0:128, :], r=[self.XBCT], w=[xp[0]])
        for ct in range(24):
            i = ct % 2
            if ct + 1 < 24:
                fw.dma("sp", xp[1 - i][:, 3:3 + NT], self.XBCT[(ct + 1) * 128:(ct + 2) * 128, :], r=[self.XBCT], w=[xp[1 - i]])
            a = acc[i]
            fw.op("dve", lambda e, i=i, ct=ct, a=a: e.tensor_scalar(out=a[:, :], in0=xp[i][:, 0:NT], scalar1=cw[:, ct, 0:1], scalar2=None, op0=ALU.mult),
                  r=[xp[i], cw], w=[a])
            for kk in range(1, 4):
                fw.op("dve", lambda e, i=i, ct=ct, a=a, kk=kk: e.scalar_tensor_tensor(out=a[:, :], in0=xp[i][:, kk:kk + NT], scalar=cw[:, ct, kk:kk + 1],
                                                                                      in1=a[:, :], op0=ALU.mult, op1=ALU.add), r=[xp[i], cw, a], w=[a])
            o = xo[i]
            fw.op("act", lambda e, a=a, o=o, ct=ct: e.activation(out=o[:, :], in_=a[:, :], func=AF.Silu, bias=cw[:, ct, 4:5], scale=1.0), r=[a, cw], w=[o])
            if ct >= 16:
                ob = xb[i]
                fw.op("pool", lambda e, o=o, ob=ob: e.tensor_copy(out=ob[:, :], in_=o[:, :]), r=[o], w=[ob])
                r0 = (ct - 16) * 128
                fw.dma("sp", self.BCT[r0:r0 + 128, :], ob[:, :], r=[ob], w=[(self.BCT, r0)])
            if ct < 20:
                for t4 in range(4):
                    b = self.bank()
                    for j in range(4):
                        tt = t4 * 4 + j
                        fw.op("pe", lambda e, b=b, j=j, tt=tt, o=o: e.transpose(b[:, j * 128:(j + 1) * 128], o[:, tt * 128:(tt + 1) * 128], self.cv("IDENT")),
                              r=[o, self.C], w=[b])
                    if ct < 16:
                        s = stt[k % 2]
                        dst = self.XTOK.ap[t4 * 512:(t4 + 1) * 512, ct * 128:(ct + 1) * 128]
                        dt_ = self.XTOK
                    else:
                        s = sttb[k % 2]
                        cc = (ct - 16) * 128
                        dst = self.BTOK.ap[t4 * 512:(t4 + 1) * 512, cc:cc + 128]
                        dt_ = self.BTOK
                    k += 1
                    fw.op("act", lambda e, b=b, s=s: e.copy(out=s.ap.rearrange("p a b -> p (a b)"), in_=b[:, :]), r=[b], w=[s])
                    fw.dma("sp", dst.rearrange("(a p) c -> p a c", p=128), s[:, :, :], r=[s], w=[(dt_, (t4, ct))])
        fw.release(m)

    def p34_nsa(self, l):
        fw = self.fw
        C = self.C
        m = fw.mark()
        KCT = [fw.sb("KCT%d" % g, [128, 128], BF16) for g in range(2)]
        VCX = [fw.sb("VCX%d" % g, [128, 161]) for g in range(2)]
        m3 = fw.mark()
        w1s = fw.sb("w1s", [128, 32, 128], BF16)
        w2s = fw.sb("w2s", [128, 128], BF16)
        pet = fw.sb("pet", [128, 32], BF16)
        kcT = [fw.sb("kcT%d" % i, [128, NT], BF16) for i in range(2)]
        pb = fw.sb("pb", [128, 1])
        hx = fw.sb("hx", [128, 128])
        h2 = fw.sb("h2", [128, 128])
        G = fw.sb("G", [128, 128], BF16)
        for t in range(2):
            W1 = self.cw1[t]
            w1v = W1.ap[l].rearrange("(l d) o -> d l o", d=128)
            for l0 in range(0, 32, 8):
                fw.dma("pool", w1s[:, l0:l0 + 8, :], w1v[:, l0:l0 + 8, :], r=[W1], w=[(w1s, l0)])
            fw.dma("pool", w2s[:, :], self.cw2[t].ap[l], r=[self.cw2[t]], w=[w2s])
            fw.dma("pool", pet[:, :], self.d_pet.ap[l, t], r=[self.d_pet], w=[pet])
            for g in range(2):
                src = kcT[g]
                r0 = t * 256 + g * 128
                fw.dma("sp", src[:, :], self.KVT[r0:r0 + 128, :], r=[self.KVT], w=[src])
                sv = src.ap.rearrange("p (n s) -> p n s", s=16)
                ps = self.bank()
                for li in range(32):
                    rhs = sv[:, 0:127, li] if li < 16 else sv[:, 1:128, li - 16]
                    fw.op("pe", lambda e, li=li, rhs=rhs, ps=ps: e.matmul(ps[:, 0:127], lhsT=w1s[:, li, :], rhs=rhs, start=(li == 0), stop=(li == 31)),
                          r=[w1s, src], w=[ps])
                ps2 = self.bank()
                for li in range(32):
                    fw.op("pe", lambda e, li=li, ps2=ps2: e.matmul(ps2[:, 0:1], lhsT=w1s[:, li, :], rhs=pet[:, li:li + 1], start=(li == 0), stop=(li == 31)),
                          r=[w1s, pet], w=[ps2])
                fw.op("act", lambda e, ps2=ps2: e.copy(out=pb[:, :], in_=ps2[:, 0:1]), r=[ps2], w=[pb])
                fw.op("dve", lambda e: e.memset(hx[:, 127:128], 0.0), w=[hx])
                fw.op("act", lambda e, ps=ps: e.activation(out=hx[:, 0:127], in_=ps[:, 0:127], func=AF.Identity, bias=pb[:, 0:1], scale=1.0), r=[ps, pb], w=[hx])
                fw.op("dve", lambda e: e.tensor_tensor(out=h2[:, :], in0=hx[:, :], in1=hx[:, :], op=ALU.mult), r=[hx], w=[h2])
                fw.op("dve", lambda e: e.tensor_scalar(out=h2[:, :], in0=h2[:, :], scalar1=0.044715, scalar2=1.0, op0=ALU.mult, op1=ALU.add), r=[h2], w=[h2])
                fw.op("dve", lambda e: e.tensor_tensor(out=h2[:, :], in0=h2[:, :], in1=hx[:, :], op=ALU.mult), r=[h2, hx], w=[h2])
                fw.op("act", lambda e: e.activation(out=h2[:, :], in_=h2[:, :], func=AF.Sigmoid, scale=1.5957691216057308), r=[h2], w=[h2])
                fw.op("dve", lambda e: e.tensor_tensor(out=G[:, :], in0=h2[:, :], in1=hx[:, :], op=ALU.mult), r=[h2, hx], w=[G])
                ps3 = self.bank()
                if t == 0:
                    fw.op("pe", lambda e, ps3=ps3: e.matmul(ps3[:, 0:128], lhsT=w2s[:, :], rhs=G[:, :], start=True, stop=True), r=[w2s, G], w=[ps3])
                    fw.op("act", lambda e, ps3=ps3, g=g: e.copy(out=KCT[g][:, :], in_=ps3[:, 0:128]), r=[ps3], w=[KCT[g]])
                else:
                    fw.op("pe", lambda e, ps3=ps3: e.matmul(ps3[:, 0:128], lhsT=G[:, :], rhs=w2s[:, :], start=True, stop=True), r=[w2s, G], w=[ps3])
                    fw.op("act", lambda e, ps3=ps3, g=g: e.copy(out=VCX[g][:, 0:128], in_=ps3[:, 0:128]), r=[ps3], w=[VCX[g]])
                    fw.op("dve", lambda e, g=g: e.tensor_copy(out=VCX[g][:, 128:161], in_=self.cv("OVL")), r=[C], w=[VCX[g]])
        fw.release(m3)

        ET = fw.sb("ET", [32, NT], BF16)
        fw.dma("pool", ET[:, :], self.d_et[:, :], r=[self.d_et], w=[ET])
        NGs = fw.sb("NGs", [128, 16, 24])
        fw.dma("sp", NGs[:, :, :], self.NG.ap.rearrange("(t p) c -> p t c", p=128), r=[self.NG], w=[NGs])
        ksT = fw.sb("ksT", [128, NT], BF16)
        kwT = fw.sb("kwT", [128, NT], BF16)
        VSX = fw.sb("VSX", [128, 16, 129], BF16)
        VWX = fw.sb("VWX", [128, 16, 129], BF16)
        QTg = [fw.sb("QTg%d" % j, [128, NT], BF16) for j in range(4)]
        TABC = [fw.sb("TABC%d" % i, [128, 2048]) for i in range(2)]
        TABS = [fw.sb("TABS%d" % i, [128, TO_C]) for i in range(2)]
        IMP = fw.sb("IMP", [128, 16, 32])
        NEGT = fw.sb("NEGT", [32, NT], BF16)
        Yh = [fw.sb("Yh%d" % j, [128, 16, 128]) for j in range(4)]
        scb = [fw.sb("scb%d" % i, [128, 512]) for i in range(4)]
        p32 = [fw.sb("p32_%d" % i, [128, 512]) for i in range(3)]
        pbf = [fw.sb("pbf%d" % i, [128, 512], BF16) for i in range(4)]
        sm = [fw.sb("sm%d" % i, [128, 48]) for i in range(4)]
        ytb = [fw.sb("ytb%d" % i, [128, 512], BF16) for i in range(2)]
        MSK = self.cv("MSK").rearrange("p (a q k) -> p a q k", a=3, q=16, k=32)
        cnt = {"s": 0, "p": 0, "sm": 0, "y": 0, "r": 0, "c": 0, "p32": 0}
        PS = fw.PS
        LA = 2

        def sbank():
            b = PS[4 + cnt["c"] % 4]
            cnt["c"] += 1
            return b

        def pipeline(items):
            n = len(items)
            for i in range(n + LA):
                if i < n:
                    items[i][0]()
                if i >= LA:
                    items[i - LA][1]()

        def evac_round(banks, W, Yt, Q0, h, br, first, imp):
            s = sm[cnt["sm"] % 4]
            cnt["sm"] += 1
            for bi in range(2):
                fw.op("dve", lambda e, bi=bi: e.tensor_scalar(out=s[:, 2 * bi:2 * bi + 2], in0=banks[bi][:, 128:128 + W + 1:W], scalar1=1e-30, scalar2=None, op0=ALU.max),
                      r=[banks[bi]], w=[s])
            fw.op("dve", lambda e: e.reciprocal(out=s[:, 0:4], in_=s[:, 0:4]), r=[s], w=[s])
            fw.op("dve", lambda e: e.tensor_tensor(out=s[:, 4:8], in0=s[:, 0:4], in1=NGs[:, Q0:Q0 + 4, h * 3 + br], op=ALU.mult), r=[s, NGs], w=[s])
            for qt in range(4):
                bk = banks[qt // 2]
                c0 = (qt % 2) * W
                Q = Q0 + qt
                if first:
                    fw.op("dve", lambda e, bk=bk, c0=c0, Q=Q, qt=qt: e.tensor_scalar(out=Yt[:, Q, :], in0=bk[:, c0:c0 + 128], scalar1=s[:, 4 + qt:5 + qt], scalar2=None, op0=ALU.mult),
                          r=[bk, s], w=[(Yt, Q)])
                else:
                    fw.op("dve", lambda e, bk=bk, c0=c0, Q=Q, qt=qt: e.scalar_tensor_tensor(out=Yt[:, Q, :], in0=bk[:, c0:c0 + 128], scalar=s[:, 4 + qt:5 + qt], in1=Yt[:, Q, :],
                                                                                           op0=ALU.mult, op1=ALU.add), r=[bk, s, (Yt, Q)], w=[(Yt, Q)])
            if imp:
                for qt in range(4):
                    bk = banks[qt // 2]
                    c0 = (qt % 2) * W
                    Q = Q0 + qt
                    fw.op("dve", lambda e, bk=bk, c0=c0, Q=Q, qt=qt: e.scalar_tensor_tensor(out=IMP[:, Q, :], in0=bk[:, c0 + 129:c0 + 161], scalar=s[:, qt:qt + 1], in1=IMP[:, Q, :],
                                                                                           op0=ALU.mult, op1=ALU.add), r=[bk, s, (IMP, Q)], w=[(IMP, Q)])

        for g in range(2):
            fw.dma("sp", ksT[:, :], self.KVT[512 + g * 128:512 + (g + 1) * 128, :], r=[self.KVT], w=[ksT])
            fw.dma("sp", kwT[:, :], self.KVT[768 + g * 128:768 + (g + 1) * 128, :], r=[self.KVT], w=[kwT])
            fw.dma("sp", VSX[:, :, 0:128], self.VTOK.ap[:, g * 128:(g + 1) * 128].rearrange("(t p) d -> p t d", p=128), r=[self.VTOK], w=[VSX])
            fw.dma("sp", VWX[:, :, 0:128], self.VTOK.ap[:, 256 + g * 128:256 + (g + 1) * 128].rearrange("(t p) d -> p t d", p=128), r=[self.VTOK], w=[VWX])
            fw.op("dve", lambda e: e.memset(VSX[:, :, 128:129], 1.0), w=[VSX])
            fw.op("dve", lambda e: e.memset(VWX[:, :, 128:129], 1.0), w=[VWX])
            fw.op("dve", lambda e: e.memset(IMP[:, :, :], 0.0), w=[IMP])
            for j in range(4):
                h = g * 4 + j
                fw.dma("sp", QTg[j][:, :], self.QT[h * 128:(h + 1) * 128, :], r=[self.QT], w=[QTg[j]])
            items = []
            for j in range(4):
                h = g * 4 + j
                for qg in range(4):
                    st = {}

                    def front(j=j, h=h, qg=qg, st=st):
                        tb = TABC[h % 2]
                        if qg == 0:
                            fw.dma("sp", tb[:, :], self.d_tab.ap[h][:, TO_C:TO_C + 2048], r=[self.d_tab], w=[tb])
                        S = sbank()
                        fw.op("pe", lambda e: e.matmul(S[:, :], lhsT=KCT[g][:, :], rhs=QTg[j][:, qg * 512:(qg + 1) * 512], start=True, stop=True), r=[KCT[g], QTg[j]], w=[S])
                        sc_ = scb[cnt["s"] % 4]
                        cnt["s"] += 1
                        pp = p32[cnt["p32"] % 3]
                        cnt["p32"] += 1
                        fw.op("dve", lambda e: e.tensor_tensor(out=sc_[:, :], in0=S[:, :], in1=tb[:, qg * 512:(qg + 1) * 512], op=ALU.add), r=[S, tb], w=[sc_])
                        fw.op("act", lambda e: e.activation(out=pp[:, :], in_=sc_[:, :], func=AF.Exp), r=[sc_], w=[pp])
                        st["pp"] = pp

                    def back(j=j, h=h, qg=qg, st=st):
                        pp = st["pp"]
                        r_ = cnt["r"] % 2
                        cnt["r"] += 1
                        banks = [PS[2 * r_], PS[2 * r_ + 1]]
                        for qt in range(4):
                            bk = banks[qt // 2]
                            c0 = (qt % 2) * 161
                            fw.op("pe", lambda e, bk=bk, c0=c0, qt=qt: e.matmul(bk[:, c0:c0 + 161], lhsT=pp[:, qt * 128:(qt + 1) * 128], rhs=VCX[g][:, :], start=True, stop=True),
                                  r=[pp, VCX[g]], w=[bk])
                        evac_round(banks, 161, Yh[j], qg * 4, h, 0, True, True)

                    items.append((front, back))
            pipeline(items)
            for Q in range(16):
                s = sm[cnt["sm"] % 4]
                cnt["sm"] += 1
                im = s[:, 8:40]
                fw.op("dve", lambda e, Q=Q, im=im: e.tensor_tensor(out=im, in0=IMP[:, Q, :], in1=MSK[:, 0, Q, :], op=ALU.mult), r=[(IMP, Q), C], w=[s])
                fw.op("dve", lambda e, Q=Q, im=im: e.tensor_tensor(out=im, in0=im, in1=MSK[:, 1, Q, :], op=ALU.add), r=[s, C], w=[s])
                fw.op("dve", lambda e, s=s, im=im: e.max(out=s[:, 0:8], in_=im), r=[s], w=[s])
                fw.op("dve", lambda e, s=s, im=im: e.tensor_scalar(out=im, in0=im, scalar1=s[:, 7:8], scalar2=None, op0=ALU.is_ge), r=[s], w=[s])
                fw.op("dve", lambda e, Q=Q, im=im: e.tensor_tensor(out=im, in0=im, in1=MSK[:, 2, Q, :], op=ALU.mult), r=[s, C], w=[s])
                fw.op("dve", lambda e, im=im: e.tensor_scalar(out=im, in0=im, scalar1=-1.0, scalar2=-NEG, op0=ALU.add, op1=ALU.mult), r=[s], w=[s])
                b = sbank()
                fw.op("pe", lambda e, b=b, im=im: e.transpose(b[0:32, 0:128], im, self.cv("IDENT")), r=[s, C], w=[b])
                fw.op("act", lambda e, b=b, Q=Q: e.copy(out=NEGT[:, Q * 128:(Q + 1) * 128], in_=b[0:32, 0:128]), r=[b], w=[(NEGT, Q)])
            items = []
            for j in range(4):
                h = g * 4 + j
                slope = 2.0 ** (-(h + 1))
                for br in (1, 2):
                    for qg in range(4):
                        kt_lo = 0 if br == 1 else max(0, 4 * qg - 4)
                        kts = list(range(kt_lo, 4 * qg + 4))
                        rst = {"used": [False, False]}
                        for kt in kts:
                            st = {}
                            first_of_head = (br == 1 and qg == 0 and kt == kts[0])
                            last_of_round = (kt == kts[-1])
                            last_of_head = (br == 2 and qg == 3 and last_of_round)

                            def front(j=j, h=h, slope=slope, br=br, qg=qg, kt=kt, st=st, first_of_head=first_of_head):
                                tb = TABS[j % 2]
                                if first_of_head:
                                    fw.dma("sp", tb[:, :], self.d_tab.ap[h][:, 0:TO_C], r=[self.d_tab], w=[tb])
                                KT = ksT if br == 1 else kwT
                                D0 = qg * 512 - kt * 128
                                S = sbank()
                                fw.op("pe", lambda e: e.matmul(S[:, :], lhsT=KT[:, kt * 128:(kt + 1) * 128], rhs=QTg[j][:, qg * 512:(qg + 1) * 512], start=True, stop=(br == 2)),
                                      r=[KT, QTg[j]], w=[S])
                                if br == 1:
                                    fw.op("pe", lambda e: e.matmul(S[:, :], lhsT=ET[:, kt * 128:(kt + 1) * 128], rhs=NEGT[:, qg * 512:(qg + 1) * 512], start=False, stop=True),
                                          r=[ET, NEGT], w=[S])
                                sc_ = scb[cnt["s"] % 4]
                                cnt["s"] += 1
                                pp = pbf[cnt["p"] % 4]
                                cnt["p"] += 1
                                bias = 0.0
                                if br == 1:
                                    if D0 <= 0:
                                        tsl = tb[:, TO_U + D0 + 384:TO_U + D0 + 384 + 512]
                                    else:
                                        tsl = tb[:, TO_W:TO_W + 512]
                                        bias = -slope * D0
                                else:
                                    tsl = tb[:, TO_UW + D0 + 384:TO_UW + D0 + 384 + 512]
                                if bias == 0.0:
                                    fw.op("dve", lambda e: e.tensor_tensor(out=sc_[:, :], in0=S[:, :], in1=tsl, op=ALU.add), r=[S, tb], w=[sc_])
                                else:
                                    fw.op("dve", lambda e: e.scalar_tensor_tensor(out=sc_[:, :], in0=S[:, :], scalar=bias, in1=tsl, op0=ALU.add, op1=ALU.add), r=[S, tb], w=[sc_])
                                fw.op("act", lambda e: e.activation(out=pp[:, :], in_=sc_[:, :], func=AF.Exp), r=[sc_], w=[pp])
                                st["pp"] = pp

                            def back(j=j, h=h, br=br, qg=qg, kt=kt, st=st, rst=rst, first=(kt == kts[0]), last_of_round=last_of_round, last_of_head=last_of_head):
                                pp = st["pp"]
                                VX = VSX if br == 1 else VWX
                                if first:
                                    r_ = cnt["r"] % 2
                                    cnt["r"] += 1
                                    rst["banks"] = [PS[2 * r_], PS[2 * r_ + 1]]
                                banks = rst["banks"]
                                for qt in range(4):
                                    Q = qg * 4 + qt
                                    lo = 0 if br == 1 else max(0, Q - 4)
                                    if kt < lo or kt > Q:
                                        continue
                                    bi = qt // 2
                                    bk = banks[bi]
                                    c0 = (qt % 2) * 129
                                    stt_ = not rst["used"][bi]
                                    rst["used"][bi] = True
                                    fw.op("pe", lambda e, bk=bk, c0=c0, qt=qt, stt_=stt_, Q=Q: e.matmul(bk[:, c0:c0 + 129], lhsT=pp[:, qt * 128:(qt + 1) * 128], rhs=VX[:, kt, :],
                                                                                                    start=stt_, stop=(kt == Q), skip_group_check=True), r=[pp, VX], w=[bk])
                                if last_of_round:
                                    evac_round(banks, 129, Yh[j], qg * 4, h, br, False, False)
                                if last_of_head:
                                    for t4 in range(4):
                                        b = sbank()
                                        for jj in range(4):
                                            Q = t4 * 4 + jj
                                            fw.op("pe", lambda e, b=b, jj=jj, Q=Q: e.transpose(b[:, jj * 128:(jj + 1) * 128], Yh[j][:, Q, :], self.cv("IDENT")), r=[(Yh[j], Q), C], w=[b])
                                        y = ytb[cnt["y"] % 2]
                                        cnt["y"] += 1
                                        fw.op("act", lambda e, b=b, y=y: e.copy(out=y[:, :], in_=b[:, :]), r=[b], w=[y])
                                        fw.dma("sp", self.YT[h * 128:(h + 1) * 128, t4 * 512:(t4 + 1) * 512], y[:, :], r=[y], w=[(self.YT, (h, t4))])

                            items.append((front, back))
            pipeline(items)
        fw.release(m)
        fw.release(m)

    def p5_ssd(self, l):
        fw = self.fw
        C = self.C
        PS = fw.PS
        m = fw.mark()
        sv = fw.sb("ssdv", [128, 96])
        fw.dma("sp", sv[:, :], self.d_ssdv.ap[l], r=[self.d_ssdv], w=[sv])
        snw = fw.sb("snw", [128, 2048])
        fw.dma("sp", snw[:, :], self.d_snw.ap[l], r=[self.d_snw], w=[snw])
        negA = fw.sb("negA", [128, 32])
        fw.op("act", lambda e: e.activation(out=negA[:, :], in_=sv[:, 32:64], func=AF.Exp), r=[sv], w=[negA])
        fw.op("dve", lambda e: e.tensor_scalar(out=negA[:, :], in0=negA[:, :], scalar1=-1.0, scalar2=None, op0=ALU.mult), r=[negA], w=[negA])
        BT = fw.sb("BT", [128, 4, NT], BF16)
        CT = fw.sb("CT", [128, 4, NT], BF16)
        fw.dma("sp", BT[:, :, :], self.BCT.ap[0:512, :].rearrange("(g p) t -> p g t", p=128), r=[self.BCT], w=[BT])
        fw.dma("sp", CT[:, :, :], self.BCT.ap[512:1024, :].rearrange("(g p) t -> p g t", p=128), r=[self.BCT], w=[CT])
        H = fw.sb("H", [128, 4, 512])
        Hb = fw.sb("Hb", [128, 4, 512], BF16)
        fw.op("dve", lambda e: e.memset(H[:, :, :], 0.0), w=[H])
        fw.op("dve", lambda e: e.memset(Hb[:, :, :], 0.0), w=[Hb])
        dtA = fw.sb("dtA", [128, 16, 32])
        tmpA = fw.sb("tmpA", [128, 16, 32])
        aA = fw.sb("aA", [128, 16, 32])
        acA = fw.sb("acA", [128, 16, 32])
        atA = fw.sb("atA", [128, 16, 32])
        eaA = fw.sb("eaA", [128, 16, 32])
        cdA = fw.sb("cdA", [128, 16, 32])
        dteA = fw.sb("dteA", [128, 16, 32])
        fl = lambda t: t.ap.rearrange("p c h -> p (c h)")
        bc = lambda ap: ap.unsqueeze(1).broadcast_to([128, 16, 32])
        fw.dma("sp", dtA[:, :, :], self.DTR.ap.rearrange("(c p) h -> p c h", p=128), r=[self.DTR], w=[dtA])
        fw.op("dve", lambda e: e.tensor_tensor(out=dtA[:, :, :], in0=dtA[:, :, :], in1=bc(sv[:, 0:32]), op=ALU.add), r=[dtA, sv], w=[dtA])
        fw.op("dve", lambda e: e.tensor_scalar(out=fl(tmpA), in0=fl(dtA), scalar1=-1.0, scalar2=None, op0=ALU.mult), r=[dtA], w=[tmpA])
        fw.op("dve", lambda e: e.tensor_tensor(out=fl(tmpA), in0=fl(tmpA), in1=fl(dtA), op=ALU.max), r=[dtA, tmpA], w=[tmpA])
        fw.op("act", lambda e: e.activation(out=fl(tmpA), in_=fl(tmpA), func=AF.Exp, scale=-1.0), r=[tmpA], w=[tmpA])
        fw.op("act", lambda e: e.activation(out=fl(tmpA), in_=fl(tmpA), func=AF.Ln, bias=self.EPSC[:, 1:2], scale=1.0), r=[tmpA, self.EPSC], w=[tmpA])
        fw.op("dve", lambda e: e.scalar_tensor_tensor(out=fl(dtA), in0=fl(dtA), scalar=0.0, in1=fl(tmpA), op0=ALU.max, op1=ALU.add), r=[dtA, tmpA], w=[dtA])
        fw.op("dve", lambda e: e.tensor_tensor(out=aA[:, :, :], in0=dtA[:, :, :], in1=bc(negA[:, :]), op=ALU.mult), r=[dtA, negA], w=[aA])
        fw.op("pe", lambda e: e.matmul(PS[1][:, :], lhsT=self.cv("CAUS"), rhs=fl(aA), start=True, stop=True), r=[aA, C], w=[PS[1]])
        fw.op("pe", lambda e: e.matmul(PS[2][:, :], lhsT=self.cv("ONES1"), rhs=fl(aA), start=True, stop=True), r=[aA, C], w=[PS[2]])
        fw.op("act", lambda e: e.copy(out=fl(acA), in_=PS[1][:, :]), r=[PS[1]], w=[acA])
        fw.op("act", lambda e: e.copy(out=fl(atA), in_=PS[2][:, :]), r=[PS[2]], w=[atA])
        fw.op("act", lambda e: e.activation(out=fl(eaA), in_=fl(acA), func=AF.Exp), r=[acA], w=[eaA])
        fw.op("act", lambda e: e.activation(out=fl(cdA), in_=fl(atA), func=AF.Exp), r=[atA], w=[cdA])
        fw.op("dve", lambda e: e.tensor_tensor(out=fl(dteA), in0=fl(atA), in1=fl(acA), op=ALU.subtract), r=[atA, acA], w=[dteA])
        fw.op("act", lambda e: e.activation(out=fl(dteA), in_=fl(dteA), func=AF.Exp), r=[dteA], w=[dteA])

        xc = [fw.sb("xc%d" % i, [128, 2048]) for i in range(3)]
        zc = [fw.sb("zc%d" % i, [128, 2048]) for i in range(3)]
        bk = [fw.sb("bk%d" % i, [128, 512], BF16) for i in range(3)]
        xdt = fw.sb("xdt", [128, 2048], BF16)
        xdte = fw.sb("xdte", [128, 2048], BF16)
        cbm = fw.sb("cbm", [128, 4, 128])
        seg = [fw.sb("seg%d" % i, [128, 4, 128]) for i in range(2)]
        dec = [fw.sb("dec%d" % i, [128, 4, 128]) for i in range(3)]
        MT = [fw.sb("MT%d" % i, [128, 4, 128], BF16) for i in range(3)]
        ys_ = [fw.sb("y%d" % i, [128, 2048]) for i in range(2)]
        ysq = fw.sb("ysq", [128, 512])
        st4 = fw.sb("st4", [128, 8])
        yT = fw.sb("yT", [128, 16, 128], BF16)
        CAUS = self.cv("CAUS")
        cnt = {"k": 0, "m": 0, "b": 0}
        LA = 2

        def sbank():
            b = PS[1 + cnt["b"] % 3]
            cnt["b"] += 1
            return b

        def loadc(c):
            i = c % 3
            rows = slice(c * 128, (c + 1) * 128)
            fw.dma("sp", xc[i][:, :], self.XTOK[rows, :], r=[self.XTOK], w=[xc[i]])
            fw.dma("sp", bk[i][:, :], self.BTOK[rows, :], r=[self.BTOK], w=[bk[i]])
            fw.dma("sp", zc[i][:, :], self.ZS[rows, :], r=[self.ZS], w=[zc[i]])

        def stageA(c):
            i = c % 3
            x, z, bt = xc[i], zc[i], bk[i]
            y = ys_[c % 2]
            cs = slice(c * 128, (c + 1) * 128)
            x3 = x.ap.rearrange("p (h q) -> p h q", q=64)
            fw.op("dve", lambda e: e.tensor_tensor(out=xdt.ap.rearrange("p (h q) -> p h q", q=64), in0=x3, in1=dtA[:, c, :].unsqueeze(2).broadcast_to([128, 32, 64]), op=ALU.mult),
                  r=[x, dtA], w=[xdt])
            fw.op("dve", lambda e: e.tensor_tensor(out=xdte.ap.rearrange("p (h q) -> p h q", q=64), in0=xdt.ap.rearrange("p (h q) -> p h q", q=64),
                                                   in1=dteA[:, c, :].unsqueeze(2).broadcast_to([128, 32, 64]), op=ALU.mult), r=[xdt, dteA], w=[xdte])
            for g in range(4):
                fw.op("pe", lambda e, g=g: e.matmul(PS[0][:, g * 128:(g + 1) * 128], lhsT=BT[:, g, cs], rhs=CT[:, g, cs], start=True, stop=True), r=[BT, CT], w=[PS[0]])
            fw.op("dve", lambda e: e.tensor_tensor(out=cbm[:, :, :], in0=PS[0].ap.rearrange("p (g l) -> p g l", l=128), in1=CAUS.unsqueeze(1).broadcast_to([128, 4, 128]), op=ALU.mult),
                  r=[PS[0], C], w=[cbm])
            items = []
            for g in range(4):
                for hh in range(2):
                    st = {}

                    def front(g=g, hh=hh, st=st):
                        h0 = g * 8 + hh * 4
                        sg, dc = seg[cnt["k"] % 2], dec[cnt["k"] % 3]
                        cnt["k"] += 1
                        mt = MT[cnt["m"] % 3]
                        cnt["m"] += 1
                        fw.op("dve", lambda e: e.tensor_tensor(out=sg[:, :, :], in0=CAUS.unsqueeze(1).broadcast_to([128, 4, 128]),
                                                               in1=aA[:, c, h0:h0 + 4].unsqueeze(2).broadcast_to([128, 4, 128]), op=ALU.mult), r=[aA, C], w=[sg])
                        pseg = sbank()
                        fw.op("pe", lambda e: e.matmul(pseg[:, :], lhsT=self.cv("TGT"), rhs=sg.ap.rearrange("p a b -> p (a b)"), start=True, stop=True), r=[sg, C], w=[pseg])
                        fw.op("act", lambda e: e.activation(out=dc.ap.rearrange("p a b -> p (a b)"), in_=pseg[:, :], func=AF.Exp), r=[pseg], w=[dc])
                        st["mt"] = mt
                        st["dc"] = dc

                    def mid(g=g, hh=hh, st=st):
                        mt, dc = st["mt"], st["dc"]
                        fw.op("dve", lambda e: e.tensor_tensor(out=mt[:, :, :], in0=dc[:, :, :], in1=cbm[:, g, :].unsqueeze(1).broadcast_to([128, 4, 128]), op=ALU.mult),
                              r=[dc, cbm], w=[mt])

                    def back(g=g, hh=hh, st=st):
                        mt = st["mt"]
                        h0 = g * 8 + hh * 4
                        yd = PS[4 + g % 2]
                        for q in range(4):
                            hd = h0 + q
                            cc = (hh * 4 + q) * 64
                            fw.op("pe", lambda e, q=q, hd=hd, cc=cc: e.matmul(yd[:, cc:cc + 64], lhsT=mt[:, q, :], rhs=xdt[:, hd * 64:(hd + 1) * 64], start=True, stop=True),
                                  r=[mt, xdt], w=[yd])
                        if hh == 1:
                            yo = PS[6]
                            fw.op("pe", lambda e: e.matmul(yo[:, :], lhsT=CT[:, g, cs], rhs=Hb[:, g, :], start=True, stop=True), r=[CT, (Hb, g)], w=[yo])
                            ysl = y.ap[:, g * 512:(g + 1) * 512].rearrange("p (h q) -> p h q", q=64)
                            fw.op("dve", lambda e: e.tensor_tensor(out=ysl, in0=yo.ap.rearrange("p (h q) -> p h q", q=64),
                                                                   in1=eaA[:, c, g * 8:(g + 1) * 8].unsqueeze(2).broadcast_to([128, 8, 64]), op=ALU.mult), r=[yo, eaA], w=[(y, g)])
                            fw.op("dve", lambda e: e.tensor_tensor(out=y[:, g * 512:(g + 1) * 512], in0=yd[:, :], in1=y[:, g * 512:(g + 1) * 512], op=ALU.add), r=[yd, (y, g)], w=[(y, g)])
                            pst = PS[7]
                            fw.op("pe", lambda e: e.matmul(pst[:, :], lhsT=bt[:, g * 128:(g + 1) * 128], rhs=xdte[:, g * 512:(g + 1) * 512], start=True, stop=True),
                                  r=[bt, xdte], w=[pst])
                            Hg = H.ap[:, g, :].rearrange("p (h q) -> p h q", q=64)
                            fw.op("dve", lambda e: e.tensor_tensor(out=Hg, in0=Hg, in1=cdA[:, c, g * 8:(g + 1) * 8].unsqueeze(2).broadcast_to([128, 8, 64]), op=ALU.mult),
                                  r=[(H, g), cdA], w=[(H, g)])
                            fw.op("dve", lambda e: e.tensor_tensor(out=H[:, g, :], in0=pst[:, :], in1=H[:, g, :], op=ALU.add), r=[pst, (H, g)], w=[(H, g)])
                            fw.op("act", lambda e: e.copy(out=Hb[:, g, :], in_=H[:, g, :]), r=[(H, g)], w=[(Hb, g)])

                    items.append((front, mid, back))
            n = len(items)
            for ii in range(n + 2):
                if ii < n:
                    items[ii][0]()
                if 0 <= ii - 1 < n:
                    items[ii - 1][1]()
                if 0 <= ii - 2 < n:
                    items[ii - 2][2]()

        def stageB(c):
            i = c % 3
            x, z = xc[i], zc[i]
            y = ys_[c % 2]
            x3 = x.ap.rearrange("p (h q) -> p h q", q=64)
            fw.op("dve", lambda e: e.tensor_tensor(out=x3, in0=x3, in1=sv[:, 64:96].unsqueeze(2).broadcast_to([128, 32, 64]), op=ALU.mult), r=[x, sv], w=[x])
            fw.op("dve", lambda e: e.tensor_tensor(out=x[:, :], in0=x[:, :], in1=y[:, :], op=ALU.add), r=[y, x], w=[x])
            fw.op("dve", lambda e: e.tensor_tensor(out=y[:, :], in0=x[:, :], in1=z[:, :], op=ALU.mult), r=[x, z], w=[y])
            for g in range(4):
                fw.op("act", lambda e, g=g: e.activation(out=ysq[:, :], in_=y[:, g * 512:(g + 1) * 512], func=AF.Square), r=[y], w=[ysq])
                fw.op("dve", lambda e, g=g: e.tensor_reduce(out=st4[:, g:g + 1], in_=ysq[:, :], axis=AX.X, op=ALU.add), r=[ysq], w=[st4])
            fw.op("act", lambda e: e.activation(out=st4[:, 4:8], in_=st4[:, 0:4], func=AF.Sqrt, bias=self.EPSC[:, 0:1], scale=1.0 / 512), r=[st4, self.EPSC], w=[st4])
            fw.op("dve", lambda e: e.reciprocal(out=st4[:, 4:8], in_=st4[:, 4:8]), r=[st4], w=[st4])
            for g in range(4):
                fw.op("dve", lambda e, g=g: e.scalar_tensor_tensor(out=y[:, g * 512:(g + 1) * 512], in0=y[:, g * 512:(g + 1) * 512], scalar=st4[:, 4 + g:5 + g],
                                                                   in1=snw[:, g * 512:(g + 1) * 512], op0=ALU.mult, op1=ALU.mult), r=[y, st4, snw], w=[y])
            for t4 in range(4):
                b = sbank()
                for jj in range(4):
                    f = t4 * 4 + jj
                    fw.op("pe", lambda e, b=b, jj=jj, f=f: e.transpose(b[:, jj * 128:(jj + 1) * 128], y[:, f * 128:(f + 1) * 128], self.cv("IDENT")), r=[y, C], w=[b])
                fw.op("act", lambda e, b=b, t4=t4: e.copy(out=yT.ap[:, t4 * 4:(t4 + 1) * 4, :].rearrange("p a b -> p (a b)"), in_=b[:, :]), r=[b], w=[yT])
            fw.dma("sp", self.YT.ap[1024:3072, c * 128:(c + 1) * 128].rearrange("(f p) t -> p f t", p=128), yT[:, :, :], r=[yT], w=[(self.YT, ("s", c))])

        loadc(0)
        loadc(1)
        for c in range(16):
            stageA(c)
            if c >= 1:
                stageB(c - 1)
            if c + 2 < 16:
                loadc(c + 2)
        stageB(15)
        fw.release(m)

    def p6_ret(self, l):
        fw = self.fw
        C = self.C
        PS = fw.PS
        m = fw.mark()
        rnw = fw.sb("rnw", [128, 1024])
        fw.dma("sp", rnw[:, :], self.d_rnw.ap[l], r=[self.d_rnw], w=[rnw])
        QT = fw.sb("rQT", [128, 4, NT], BF16)
        KT = fw.sb("rKT", [128, 4, NT], BF16)
        fw.dma("sp", QT[:, :, :], self.RQT.ap.rearrange("(h p) t -> p h t", p=128), r=[self.RQT], w=[QT])
        fw.dma("sp", KT[:, :, :], self.RKT.ap.rearrange("(h p) t -> p h t", p=128), r=[self.RKT], w=[KT])
        QS = fw.sb("rQS", [128, 4, NT], BF16)
        QD = self.cv("QD").rearrange("p (h l) -> p h l", l=128)
        for h in range(4):
            fw.op("dve", lambda e, h=h: e.tensor_tensor(out=QS.ap[:, h, :].rearrange("p (c l) -> p c l", l=128), in0=QT.ap[:, h, :].rearrange("p (c l) -> p c l", l=128),
                                                        in1=QD[:, h, :].unsqueeze(1).broadcast_to([128, 16, 128]), op=ALU.mult), r=[QT, C], w=[QS])
        R = fw.sb("R", [128, 4, 256])
        Rb = fw.sb("Rb", [128, 4, 256], BF16)
        CD = fw.sb("CD", [128, 4])
        fw.op("dve", lambda e: e.memset(R[:, :, :], 0.0), w=[R])
        fw.op("dve", lambda e: e.memset(Rb[:, :, :], 0.0), w=[Rb])
        for h in range(4):
            fw.op("dve", lambda e, h=h: e.memset(CD[:, h:h + 1], self.cdec[h]), w=[CD])
        vc = [fw.sb("vc%d" % i, [128, 1024], BF16) for i in range(3)]
        kc_ = [fw.sb("kc%d" % i, [128, 512], BF16) for i in range(3)]
        gc = [fw.sb("gc%d" % i, [128, 1024]) for i in range(3)]
        kdb = [fw.sb("kd%d" % i, [128, 512], BF16) for i in range(2)]
        sTm = [fw.sb("sTm%d" % i, [128, 4, 128], BF16) for i in range(2)]
        ysb = [fw.sb("ys%d" % i, [128, 4, 256]) for i in range(2)]
        y2 = fw.sb("y2", [128, 4, 256])
        stb = [fw.sb("rst%d" % i, [128, 16]) for i in range(2)]
        yr = fw.sb("yr", [128, 1024])
        yTb = [fw.sb("ryT%d" % i, [128, 8, 128], BF16) for i in range(2)]
        DMT = self.cv("DMT").rearrange("p (h l) -> p h l", l=128)
        KD = self.cv("KD")
        f3 = lambda t: t.ap.rearrange("p a b -> p (a b)")

        def loadc(c):
            i = c % 3
            rows = slice(c * 128, (c + 1) * 128)
            fw.dma("sp", vc[i][:, :], self.RV[rows, :], r=[self.RV], w=[vc[i]])
            fw.dma("sp", kc_[i][:, :], self.RKTOK[rows, :], r=[self.RKTOK], w=[kc_[i]])
            fw.dma("sp", gc[i][:, :], self.RGS[rows, :], r=[self.RGS], w=[gc[i]])

        def stageA(c):
            v, kk = vc[c % 3], kc_[c % 3]
            kd, sm_, ys = kdb[c % 2], sTm[c % 2], ysb[c % 2]
            cs = slice(c * 128, (c + 1) * 128)
            fw.op("dve", lambda e: e.tensor_tensor(out=kd.ap.rearrange("p (h d) -> p h d", d=128), in0=kk.ap.rearrange("p (h d) -> p h d", d=128),
                                                   in1=KD.unsqueeze(2).broadcast_to([128, 4, 128]), op=ALU.mult), r=[kk, C], w=[kd])
            Sb = PS[c % 2]
            for h in range(4):
                fw.op("pe", lambda e, h=h: e.matmul(Sb[:, h * 128:(h + 1) * 128], lhsT=KT[:, h, cs], rhs=QT[:, h, cs], start=True, stop=True), r=[KT, QT], w=[Sb])
            fw.op("dve", lambda e: e.tensor_tensor(out=sm_[:, :, :], in0=Sb.ap.rearrange("p (h l) -> p h l", l=128), in1=DMT, op=ALU.mult), r=[Sb, C], w=[sm_])
            for h in range(4):
                pk = PS[2 + h // 2]
                c0 = (h % 2) * 256
                fw.op("pe", lambda e, h=h, pk=pk, c0=c0: e.matmul(pk[:, c0:c0 + 256], lhsT=kd[:, h * 128:(h + 1) * 128], rhs=v[:, h * 256:(h + 1) * 256], start=True, stop=True),
                      r=[kd, v], w=[pk])
            pyb = [PS[4 + 2 * (c % 2)], PS[5 + 2 * (c % 2)]]
            for h in range(4):
                py = pyb[h // 2]
                c0 = (h % 2) * 256
                fw.op("pe", lambda e, h=h, py=py, c0=c0: e.matmul(py[:, c0:c0 + 256], lhsT=sm_[:, h, :], rhs=v[:, h * 256:(h + 1) * 256], start=True, stop=False, skip_group_check=True),
                      r=[sm_, v], w=[py])
                fw.op("pe", lambda e, h=h, py=py, c0=c0: e.matmul(py[:, c0:c0 + 256], lhsT=QS[:, h, cs], rhs=Rb[:, h, :], start=False, stop=True, skip_group_check=True),
                      r=[QS, Rb], w=[py])
            fw.op("dve", lambda e: e.tensor_tensor(out=R[:, :, :], in0=R[:, :, :], in1=CD[:, :].unsqueeze(2).broadcast_to([128, 4, 256]), op=ALU.mult), r=[R, CD], w=[R])
            for b2 in range(2):
                fw.op("dve", lambda e, b2=b2: e.tensor_tensor(out=f3(R)[:, b2 * 512:(b2 + 1) * 512], in0=PS[2 + b2][:, :], in1=f3(R)[:, b2 * 512:(b2 + 1) * 512], op=ALU.add),
                      r=[PS[2 + b2], R], w=[R])
            fw.op("act", lambda e: e.copy(out=f3(Rb), in_=f3(R)), r=[R], w=[Rb])
            for b2 in range(2):
                fw.op("act", lambda e, b2=b2: e.copy(out=f3(ys)[:, b2 * 512:(b2 + 1) * 512], in_=pyb[b2][:, :]), r=[pyb[b2]], w=[ys])

        def stageB(c):
            ys, s8, gg, yT = ysb[c % 2], stb[c % 2], gc[c % 3], yTb[c % 2]
            bc4 = lambda ap: ap.unsqueeze(2).broadcast_to([128, 4, 256])
            fw.op("dve", lambda e: e.tensor_reduce(out=s8[:, 0:4], in_=ys[:, :, :], axis=AX.X, op=ALU.add), r=[ys], w=[s8])
            fw.op("dve", lambda e: e.tensor_scalar(out=s8[:, 4:8], in0=s8[:, 0:4], scalar1=1.0 / 256, scalar2=None, op0=ALU.mult), r=[s8], w=[s8])
            fw.op("dve", lambda e: e.tensor_tensor(out=ys[:, :, :], in0=ys[:, :, :], in1=bc4(s8[:, 4:8]), op=ALU.subtract), r=[ys, s8], w=[ys])
            fw.op("act", lambda e: e.activation(out=f3(y2), in_=f3(ys), func=AF.Square), r=[ys], w=[y2])
            fw.op("dve", lambda e: e.tensor_reduce(out=s8[:, 8:12], in_=y2[:, :, :], axis=AX.X, op=ALU.add), r=[y2], w=[s8])
            fw.op("act", lambda e: e.activation(out=s8[:, 12:16], in_=s8[:, 8:12], func=AF.Sqrt, bias=self.EPSC[:, 0:1], scale=1.0 / 256), r=[s8, self.EPSC], w=[s8])
            fw.op("dve", lambda e: e.reciprocal(out=s8[:, 12:16], in_=s8[:, 12:16]), r=[s8], w=[s8])
            fw.op("dve", lambda e: e.tensor_tensor(out=ys[:, :, :], in0=ys[:, :, :], in1=bc4(s8[:, 12:16]), op=ALU.mult), r=[ys, s8], w=[ys])
            fw.op("dve", lambda e: e.tensor_tensor(out=f3(ys), in0=f3(ys), in1=rnw[:, :], op=ALU.mult), r=[ys, rnw], w=[ys])
            fw.op("dve", lambda e: e.tensor_tensor(out=yr[:, :], in0=f3(ys), in1=gg[:, :], op=ALU.mult), r=[ys, gg], w=[yr])
            for t4 in range(2):
                b = PS[2 + t4]
                for jj in range(4):
                    f = t4 * 4 + jj
                    fw.op("pe", lambda e, b=b, jj=jj, f=f: e.transpose(b[:, jj * 128:(jj + 1) * 128], yr[:, f * 128:(f + 1) * 128], self.cv("IDENT")), r=[yr, C], w=[b])
                fw.op("act", lambda e, b=b, t4=t4: e.copy(out=yT.ap[:, t4 * 4:(t4 + 1) * 4, :].rearrange("p a b -> p (a b)"), in_=b[:, :]), r=[b], w=[yT])
            fw.dma("sp", self.YT.ap[3072:4096, c * 128:(c + 1) * 128].rearrange("(f p) t -> p f t", p=128), yT[:, :, :], r=[yT], w=[(self.YT, ("r", c))])

        loadc(0)
        loadc(1)
        for c in range(16):
            stageA(c)
            if c >= 1:
                stageB(c - 1)
            if c + 2 < 16:
                loadc(c + 2)
        stageB(15)
        fw.release(m)

    def p7_merge(self, l, src):
        fw = self.fw
        m = fw.mark()
        self.WB = [fw.sb("WB%d" % i, [128, 11264], BF16) for i in range(3)]
        yt = fw.sb("ytall", [128, 32, 512], BF16)
        mg = [fw.sb("mg%d" % i, [128, 512], BF16) for i in range(3)]
        mer = fw.sb("mer", [128, 16, 512])
        merb = fw.sb("merb", [128, 16, 512], BF16)
        tmp = [fw.sb("mtmp%d" % i, [128, 512]) for i in range(2)]
        xt = [fw.sb("mxt%d" % i, [128, 512]) for i in range(2)]
        branches = ((self.p_nsa, 8, 0, 0), (self.p_ssd, 16, 8, 1), (self.p_ret, 8, 24, 2))
        k = 0
        for tg in range(4):
            ts = slice(tg * 512, (tg + 1) * 512)
            for k0 in range(0, 32, 8):
                fw.dma("sp", yt[:, k0:k0 + 8, :], self.YT.ap[k0 * 128:(k0 + 8) * 128, ts].rearrange("(k p) t -> p k t", p=128), r=[self.YT], w=[(yt, k0)])
            for (W, KC, koff, bi) in branches:
                for cb in range(4):
                    buf, view = self.wload(W, W.ap[l][:, cb * 512:(cb + 1) * 512], KC, 512)
                    for j in range(4):
                        ct = cb * 4 + j
                        g = mg[k % 3]
                        t_ = tmp[k % 2]
                        k += 1
                        r0 = bi * 2048 + ct * 128
                        fw.dma("sp", g[:, :], self.MGT[r0:r0 + 128, ts], r=[self.MGT], w=[g])
                        ps = self.bank()
                        for kc in range(KC):
                            fw.op("pe", lambda e, ps=ps, kc=kc, j=j, view=view, koff=koff, KC=KC: e.matmul(ps[:, :], lhsT=view[:, kc, j * 128:(j + 1) * 128], rhs=yt[:, koff + kc, :],
                                                                                                   start=(kc == 0), stop=(kc == KC - 1)), r=[buf, yt], w=[ps])
                        if bi == 0:
                            fw.op("dve", lambda e, ps=ps, g=g, ct=ct: e.tensor_tensor(out=mer[:, ct, :], in0=ps[:, :], in1=g[:, :], op=ALU.mult), r=[ps, g], w=[(mer, ct)])
                        else:
                            fw.op("dve", lambda e, ps=ps, g=g, t_=t_: e.tensor_tensor(out=t_[:, :], in0=ps[:, :], in1=g[:, :], op=ALU.mult), r=[ps, g], w=[t_])
                            fw.op("dve", lambda e, t_=t_, ct=ct: e.tensor_tensor(out=mer[:, ct, :], in0=mer[:, ct, :], in1=t_[:, :], op=ALU.add), r=[t_, (mer, ct)], w=[(mer, ct)])
            for ct in range(16):
                fw.op("act", lambda e, ct=ct: e.copy(out=merb[:, ct, :], in_=mer[:, ct, :]), r=[(mer, ct)], w=[(merb, ct)])
            for cb in range(4):
                buf, view = self.wload(self.w_out, self.w_out.ap[l][:, cb * 512:(cb + 1) * 512], 16, 512)
                for j in range(4):
                    ct = cb * 4 + j
                    x = xt[k % 2]
                    k += 1
                    fw.dma("sp", x[:, :], src[ct * 128:(ct + 1) * 128, ts], r=[(src, (ct, tg))], w=[x])
                    ps = self.bank()
                    for kc in range(16):
                        fw.op("pe", lambda e, ps=ps, kc=kc, j=j, view=view: e.matmul(ps[:, :], lhsT=view[:, kc, j * 128:(j + 1) * 128], rhs=merb[:, kc, :], start=(kc == 0), stop=(kc == 15)),
                              r=[buf, merb], w=[ps])
                    fw.op("dve", lambda e, ps=ps, x=x: e.tensor_tensor(out=x[:, :], in0=ps[:, :], in1=x[:, :], op=ALU.add), r=[ps, x], w=[x])
                    fw.dma("sp", self.XT[ct * 128:(ct + 1) * 128, ts], x[:, :], r=[x], w=[(self.XT, (ct, tg))])
        fw.release(m)

    def p8_ffn(self, l):
        fw = self.fw
        m = fw.mark()
        self.WB = [fw.sb("WB%d" % i, [128, 11264], BF16) for i in range(3)]
        xg = fw.sb("fxg", [128, 16, 512])
        fT = fw.sb("fT", [128, 16, 512], BF16)
        hT = fw.sb("hT", [128, 44, 512], BF16)
        sq = [fw.sb("fsq%d" % i, [128, 512]) for i in range(2)]
        rs = fw.sb("frs", [128, 512])
        sg = [fw.sb("fsg%d" % i, [128, 512]) for i in range(2)]
        xo = [fw.sb("fxo%d" % i, [128, 512]) for i in range(2)]
        wcol = self.NRM[:, (2 * l + 1) * 16:(2 * l + 2) * 16]
        xv = self.XT.ap.rearrange("(k p) t -> p k t", p=128)
        k = 0
        for tg in range(4):
            ts = slice(tg * 512, (tg + 1) * 512)
            for k0 in range(0, 16, 4):
                fw.dma("sp", xg[:, k0:k0 + 4, :], xv[:, k0:k0 + 4, ts], r=[self.XT], w=[(xg, k0)])
            self.rmsnorm(xg, 512, wcol, lambda kc: (fT, fT[:, kc, :]), sq, rs)
            for cb in range(11):
                bg, vg = self.wload(self.w_gate, self.w_gate.ap[l][:, cb * 512:(cb + 1) * 512], 16, 512)
                bu, vu = self.wload(self.w_up, self.w_up.ap[l][:, cb * 512:(cb + 1) * 512], 16, 512)
                for j in range(4):
                    f = cb * 4 + j
                    pg = self.bank()
                    pu = self.bank()
                    for kc in range(16):
                        fw.op("pe", lambda e, pg=pg, kc=kc, j=j, vg=vg: e.matmul(pg[:, :], lhsT=vg[:, kc, j * 128:(j + 1) * 128], rhs=fT[:, kc, :], start=(kc == 0), stop=(kc == 15)),
                              r=[bg, fT], w=[pg])
                    for kc in range(16):
                        fw.op("pe", lambda e, pu=pu, kc=kc, j=j, vu=vu: e.matmul(pu[:, :], lhsT=vu[:, kc, j * 128:(j + 1) * 128], rhs=fT[:, kc, :], start=(kc == 0), stop=(kc == 15)),
                              r=[bu, fT], w=[pu])
                    s = sg[k % 2]
                    k += 1
                    fw.op("act", lambda e, pg=pg, s=s: e.activation(out=s[:, :], in_=pg[:, :], func=AF.Silu), r=[pg], w=[s])
                    fw.op("dve", lambda e, pu=pu, s=s, f=f: e.tensor_tensor(out=hT[:, f, :], in0=pu[:, :], in1=s[:, :], op=ALU.mult), r=[pu, s], w=[(hT, f)])
            for cb in range(8):
                bd, vd = self.wload(self.w_down, self.w_down.ap[l][:, cb * 256:(cb + 1) * 256], 44, 256)
                for j in range(2):
                    ct = cb * 2 + j
                    ps = self.bank()
                    for kc in range(44):
                        fw.op("pe", lambda e, ps=ps, kc=kc, j=j, vd=vd: e.matmul(ps[:, :], lhsT=vd[:, kc, j * 128:(j + 1) * 128], rhs=hT[:, kc, :], start=(kc == 0), stop=(kc == 43)),
                              r=[bd, hT], w=[ps])
                    x = xo[k % 2]
                    k += 1
                    fw.op("dve", lambda e, ps=ps, x=x, ct=ct: e.tensor_tensor(out=x[:, :], in0=ps[:, :], in1=xg[:, ct, :], op=ALU.add), r=[ps, xg], w=[x])
                    fw.dma("sp", self.XT[ct * 128:(ct + 1) * 128, ts], x[:, :], r=[x], w=[(self.XT, (ct, tg))])
        fw.release(m)

    def final_norm(self, src):
        fw = self.fw
        m = fw.mark()
        xg = fw.sb("nxg", [128, 16, 512])
        og = fw.sb("nog", [128, 16, 512])
        sq = [fw.sb("nsq%d" % i, [128, 512]) for i in range(2)]
        rs = fw.sb("nrs", [128, 512])
        wcol = self.NRM[:, (2 * NL) * 16:(2 * NL + 1) * 16]
        xv = src.ap.rearrange("(k p) t -> p k t", p=128)
        ov = self.outT.ap.rearrange("(k p) t -> p k t", p=128)
        for tg in range(4):
            ts = slice(tg * 512, (tg + 1) * 512)
            for k0 in range(0, 16, 4):
                fw.dma("sp", xg[:, k0:k0 + 4, :], xv[:, k0:k0 + 4, ts], r=[src], w=[(xg, k0)])
            self.rmsnorm(xg, 512, wcol, lambda kc: (og, og[:, kc, :]), sq, rs)
            for k0 in range(0, 16, 4):
                fw.dma("sp", ov[:, k0:k0 + 4, ts], og[:, k0:k0 + 4, :], r=[og], w=[(self.outT, (tg, k0))])
        fw.release(m)


def host_inputs(inp, b):
    consts, tab, et, _ = host_consts()
    f = np.float32
    nrm = np.zeros((128, (2 * NL + 1) * 16), f)
    for l in range(NL):
        nrm[:, (2 * l) * 16:(2 * l + 1) * 16] = np.asarray(inp["norm_mix"][l], f).reshape(16, 128).T
        nrm[:, (2 * l + 1) * 16:(2 * l + 2) * 16] = np.asarray(inp["norm_ffn"][l], f).reshape(16, 128).T
    nrm[:, 2 * NL * 16:] = np.asarray(inp["norm_final"], f).reshape(16, 128).T
    cw = np.zeros((NL, 128, 24, 5), f)
    for l in range(NL):
        cw[l, :, :, 0:4] = np.asarray(inp["conv_w"][l], f).T.reshape(24, 128, 4).transpose(1, 0, 2)
        cw[l, :, :, 4] = np.asarray(inp["conv_b"][l], f).reshape(24, 128).T
    ssdv = np.zeros((NL, 128, 96), f)
    ssdv[:, :, 0:32] = np.asarray(inp["dt_bias"], f)[:, None, :]
    ssdv[:, :, 32:64] = np.asarray(inp["a_log"], f)[:, None, :]
    ssdv[:, :, 64:96] = np.asarray(inp["d_skip"], f)[:, None, :]
    snw = np.ascontiguousarray(np.broadcast_to(np.asarray(inp["ssd_norm"], f)[:, None, :], (NL, 128, 2048)))
    rnw = np.ascontiguousarray(np.broadcast_to(np.asarray(inp["ret_norm"], f).reshape(NL, 1, 1024), (NL, 128, 1024)))
    pet = np.stack([np.asarray(inp["cmp_k_pe"], f).transpose(0, 2, 1), np.asarray(inp["cmp_v_pe"], f).transpose(0, 2, 1)], axis=1)
    d = {
        "xT": np.ascontiguousarray(np.asarray(inp["x"][b], f).T),
        "consts": consts, "tab": tab, "et": et, "nrm": nrm, "cw": cw.reshape(NL, 128, 120), "ssdv": ssdv,
        "snw": snw, "rnw": rnw, "pet": np.ascontiguousarray(pet),
    }
    for k in ("w_in", "cmp_k_w1", "cmp_v_w1", "cmp_k_w2", "cmp_v_w2", "p_nsa", "p_ssd", "p_ret", "w_out", "w_gate", "w_up", "w_down"):
        d[k] = np.ascontiguousarray(np.asarray(inp[k], f))
    return d


_PROG = {}


def kernel(**inputs):
    if "p" not in _PROG:
        _PROG["p"] = Prog()
    prog = _PROG["p"]
    base = host_inputs(inputs, 0)
    zero = {k: np.zeros_like(v) for k, v in base.items()}
    work = {0: 0, 1: 1, 4: 2, 5: 3}
    in_maps = []
    for c in range(8):
        if c in work:
            d = dict(base)
            d["xT"] = np.ascontiguousarray(np.asarray(inputs["x"][work[c]], np.float32).T)
        else:
            d = zero
        in_maps.append(d)
    res = run_bass_kernel_spmd(prog.nc, in_maps, core_ids=list(range(8)))
    inv = {b: c for c, b in work.items()}
    out = np.stack([np.asarray(res.results[inv[b]]["outT"], np.float32).T for b in range(4)], axis=0)
    return np.ascontiguousarray(out)
```

```python
import numpy as np
import concourse.bass as bass
import concourse.mybir as mybir
from concourse.bass_utils import run_bass_kernel_spmd
from contextlib import ExitStack

F32 = mybir.dt.float32
BF16 = mybir.dt.bfloat16
ALU = mybir.AluOpType
AF = mybir.ActivationFunctionType
AX = mybir.AxisListType

COMPUTE = ("pe", "dve", "act", "pool")
NSLOT = {"sp": 16, "pool": 16, "act": 8}
ARENA = 53000

NT = 2048
D = 2048
DIN = 16952
DFF = 5632
NL = 4
C_Q, C_KV, C_NG, C_Z, C_XBC, C_DT, C_RQ, C_RK, C_RV, C_RG, C_MG = 0, 1024, 2560, 2584, 4632, 7704, 7736, 8248, 8760, 9784, 10808
NEG = -30000.0
EPS = 1e-6


class St:
    __slots__ = ("lw", "rd")

    def __init__(self, o=None):
        self.lw = o.lw if o else None
        self.rd = list(o.rd) if o else []


class T:
    def __init__(self, ap, name, excl=False):
        self.ap = ap
        self.name = name
        self.excl = excl
        self.st = {None: St()}

    def __getitem__(self, k):
        return self.ap[k]

    def states(self, key):
        if key is None:
            return list(self.st.values())
        if key not in self.st:
            self.st[key] = St(self.st[None])
        return [self.st[key]]


class _Rec:
    def __getattr__(self, name):
        return lambda *a, **k: (name, a, k)


_REC = _Rec()


class FW:
    def __init__(self, nc):
        self.nc = nc
        self.stream = {e: [] for e in ("pe", "dve", "act", "pool", "sp")}
        self.nops = {e: 0 for e in COMPUTE}
        self.sig = {e: set() for e in COMPUTE}
        self.ndma = {q: 0 for q in NSLOT}
        self.known = {e: {} for e in self.stream}
        self.es = ExitStack()
        self.sems = {}
        self.dsems = {}
        self.arena = self.es.enter_context(nc.sbuf_tensor("arena", [128, ARENA], F32))
        self.aoff = 0
        self.PS = [T(self.es.enter_context(nc.psum_tensor("ps%d" % i, [128, 512], F32))[:, :], "ps%d" % i, excl=True)
                   for i in range(8)]

    def sb(self, name, shape, dtype=F32):
        p = shape[0]
        n = int(np.prod(shape[1:]))
        nb = n * (4 if dtype == F32 else 2)
        nf = ((nb + 63) // 64) * 16
        assert self.aoff + nf <= ARENA, ("SBUF arena overflow", name, self.aoff, nf)
        ap = self.arena[0:p, self.aoff:self.aoff + nf]
        self.aoff += nf
        if dtype != F32:
            ap = ap.bitcast(dtype)
        ap = ap[:, 0:n]
        if len(shape) == 3:
            ap = ap.rearrange("p (a b) -> p a b", a=shape[1], b=shape[2])
        elif len(shape) == 4:
            ap = ap.rearrange("p (a b c) -> p a b c", a=shape[1], b=shape[2], c=shape[3])
        return T(ap, name)

    def mark(self):
        return self.aoff

    def release(self, m):
        self.barrier()
        self.aoff = m

    def dram(self, name, shape, dtype, kind="Internal"):
        h = self.nc.dram_tensor(name, list(shape), dtype, kind=kind)
        return T(h.ap(), name)

    def _need(self, eng, ev, waits):
        if ev is None:
            return
        if ev[0] == "c":
            _, f, idx = ev
            if f == eng and eng == "pe":
                return
            k = ("c", f)
            if self.known[eng].get(k, 0) >= idx:
                return
            self.known[eng][k] = idx
            self.sig[f].add(idx)
            waits.append(ev)
        else:
            _, q, j = ev
            k = ("d", q, j % NSLOT[q])
            if self.known[eng].get(k, -1) >= j:
                return
            self.known[eng][k] = j
            waits.append(ev)

    def _deps(self, eng, r, w, is_dma):
        waits = []
        rs, ws = [], []
        for x in r:
            t, key = x if isinstance(x, tuple) else (x, None)
            (ws if t.excl else rs).extend(t.states(key))
        for x in w:
            t, key = x if isinstance(x, tuple) else (x, None)
            ws.extend(t.states(key))
        for s in rs:
            self._need(eng, s.lw, waits)
        for s in ws:
            lw = s.lw
            if lw is not None and not ((not is_dma) and lw[0] == "c" and lw[1] == eng):
                self._need(eng, lw, waits)
            for ev in s.rd:
                if (not is_dma) and ev[0] == "c" and ev[1] == eng:
                    continue
                self._need(eng, ev, waits)
        return waits, rs, ws

    def _commit(self, ev, rs, ws):
        for s in rs:
            if ev[0] == "c":
                s.rd = [e for e in s.rd if not (e[0] == "c" and e[1] == ev[1])]
            s.rd.append(ev)
        for s in ws:
            s.lw = ev
            s.rd = []

    def op(self, eng, fn, r=(), w=()):
        waits, rs, ws = self._deps(eng, r, w, False)
        for ev in waits:
            self.stream[eng].append(("wait", ev))
        self.nops[eng] += 1
        idx = self.nops[eng]
        self.stream[eng].append(("op", fn(_REC), idx))
        self._commit(("c", eng, idx), rs, ws)

    def dma(self, q, out_ap, in_ap, r=(), w=()):
        waits, rs, ws = self._deps(q, r, w, True)
        j = self.ndma[q]
        K = NSLOT[q]
        if j >= K:
            self._need(q, ("d", q, j - K), waits)
        for ev in waits:
            self.stream[q].append(("wait", ev))
        self.ndma[q] += 1
        self.stream[q].append(("dma", out_ap, in_ap, j))
        self._commit(("d", q, j), rs, ws)

    def barrier(self):
        for e in self.stream:
            waits = []
            for f in COMPUTE:
                if self.nops[f] > 0:
                    self._need(e, ("c", f, self.nops[f]), waits)
            for q in NSLOT:
                for j in range(max(0, self.ndma[q] - NSLOT[q]), self.ndma[q]):
                    self._need(e, ("d", q, j), waits)
            for ev in waits:
                self.stream[e].append(("wait", ev))

    def emit(self):
        nc = self.nc
        self.barrier()
        es = self.es
        for e in COMPUTE:
            self.sems[e] = es.enter_context(nc.semaphore("s_" + e))
        for q in NSLOT:
            self.dsems[q] = [es.enter_context(nc.semaphore("d_%s_%d" % (q, i))) for i in range(NSLOT[q])]
        cum = {}
        for e in COMPUTE:
            c = 0
            m = {}
            for i in range(1, self.nops[e] + 1):
                if i in self.sig[e]:
                    c += 1
                    m[i] = c
            cum[e] = m

        def replay(ename, eng):
            for rec in self.stream[ename]:
                if rec[0] == "wait":
                    ev = rec[1]
                    if ev[0] == "c":
                        eng.wait_ge(self.sems[ev[1]], cum[ev[1]][ev[2]])
                    else:
                        _, q, j = ev
                        eng.wait_ge(self.dsems[q][j % NSLOT[q]], 16 * (j // NSLOT[q] + 1))
                elif rec[0] == "op":
                    nm, ar, kw = rec[1]
                    ins = getattr(eng, nm)(*ar, **kw)
                    if rec[2] in self.sig[ename]:
                        ins.then_inc(self.sems[ename], 1)
                elif rec[0] == "cc":
                    _, kind, i, o, rg, j = rec
                    eng.collective_compute(kind, ALU.bypass, replica_groups=rg, ins=[i], outs=[o]).then_inc(self.dsems[ename][j % NSLOT[ename]], 16)
                else:
                    _, o, i, j = rec
                    eng.dma_start(out=o, in_=i).then_inc(self.dsems[ename][j % NSLOT[ename]], 16)

        with nc.Block() as block:
            @block.tensor
            def _(e):
                replay("pe", e)

            @block.vector
            def _(e):
                replay("dve", e)

            @block.scalar
            def _(e):
                replay("act", e)

            @block.gpsimd
            def _(e):
                replay("pool", e)

            @block.sync
            def _(e):
                replay("sp", e)
        es.close()


CO = {}
_off = 0
for _n, _w in (("IDENT", 128), ("CAUS", 128), ("TGT", 128), ("ONESM", 128), ("ONES1", 128), ("MSK", 1536),
               ("OVL", 33), ("DMT", 512), ("QD", 512), ("KD", 4)):
    CO[_n] = (_off, _w)
    _off += _w
NCONST = _off
TABW = 896 + 512 + 1408 + 2048
TO_U, TO_W, TO_UW, TO_C = 0, 896, 1408, 2816


def host_consts():
    c = np.zeros((128, NCONST), np.float64)
    p = np.arange(128)[:, None]
    f = np.arange(128)[None, :]
    c[:, CO["IDENT"][0]:][:, :128] = (p == f)
    c[:, CO["CAUS"][0]:][:, :128] = (f >= p)
    c[:, CO["TGT"][0]:][:, :128] = (p > f)
    c[:, CO["ONESM"][0]:][:, :128] = 1.0 / D
    c[:, CO["ONES1"][0]:][:, :128] = 1.0
    msk = np.zeros((128, 3, 16, 32))
    for Q in range(16):
        tq = Q * 128 + np.arange(128)
        cur = (tq // 64)[:, None]
        blk = np.arange(32)[None, :]
        msk[:, 0, Q, :] = ((blk > 0) & (blk < cur))
        msk[:, 1, Q, :] = np.where((blk == cur) | (blk == 0), 1e9, np.where(blk > cur, -1e30, 0.0))
        msk[:, 2, Q, :] = (blk <= cur)
    c[:, CO["MSK"][0]:][:, :1536] = msk.reshape(128, -1)
    n = np.arange(128)[:, None]
    k = np.arange(32)[None, :]
    ovl = ((16 * n < 64 * k + 64) & (16 * n + 31 >= 64 * k) & (n < 127)).astype(np.float64)
    c[:, CO["OVL"][0]] = 1.0
    c[:, CO["OVL"][0] + 1:][:, :32] = ovl
    h = np.arange(4, dtype=np.float64)
    log_g = np.log1p(-np.exp2(-5.0 - h))
    s_ = np.arange(128)[:, None]
    l_ = np.arange(128)[None, :]
    for hh in range(4):
        dm = np.where(l_ >= s_, np.exp((l_ - s_) * log_g[hh]), 0.0)
        c[:, CO["DMT"][0] + hh * 128:][:, :128] = dm
        c[:, CO["QD"][0] + hh * 128:][:, :128] = np.exp((np.arange(128) + 1.0) * log_g[hh])[None, :]
        c[:, CO["KD"][0] + hh] = np.exp((127.0 - np.arange(128)) * log_g[hh])
    cdec = [float(np.exp(128 * log_g[hh])) for hh in range(4)]
    tab = np.zeros((8, 128, TABW), np.float64)
    ki = np.arange(128)[:, None]
    for hd in range(8):
        s = 2.0 ** (-(hd + 1))
        cc = np.arange(896)[None, :]
        rel = cc - 384 - ki
        tab[hd, :, TO_U:TO_U + 896] = np.where(rel >= 0, -s * rel, NEG)
        qi = np.arange(512)[None, :]
        tab[hd, :, TO_W:TO_W + 512] = -s * (qi - ki)
        cc = np.arange(1408)[None, :]
        rel = cc - 384 - ki
        tab[hd, :, TO_UW:TO_UW + 1408] = np.where((rel >= 0) & (rel < 512), -s * rel, NEG)
        tq = np.arange(2048)[None, :]
        rel = tq - (16 * ki + 31)
        tab[hd, :, TO_C:TO_C + 2048] = np.where((rel >= 0) & (ki < 127), -s * rel, NEG)
    et = (np.arange(2048)[None, :] // 64 == np.arange(32)[:, None]).astype(np.float32)
    return c.astype(np.float32), tab.astype(np.float32), et, cdec


class Prog:
    def __init__(self, n_layers=NL, dbg=False, stop_after=None):
        self.nl = n_layers
        self.dbg = dbg
        self.stop_after = stop_after
        self.nc = bass.Bass("TRN2", target_bir_lowering=False)
        self.fw = FW(self.nc)
        self.cdec = host_consts()[3]
        self.rot = 0
        self.build()

    def bank(self):
        b = self.fw.PS[self.rot % 8]
        self.rot += 1
        return b

    def din(self, name, shape, dtype=F32):
        return self.fw.dram(name, shape, dtype, kind="ExternalInput")

    def scr(self, name, shape, dtype):
        return self.fw.dram(name, shape, dtype, kind="ExternalOutput" if self.dbg else "Internal")

    def wload(self, Wt, w_ap, KC, n):
        fw = self.fw
        buf = self.WB[self.wrot % len(self.WB)]
        self.wrot += 1
        view = buf.ap[:, 0:KC * n].rearrange("p (k c) -> p k c", k=KC, c=n)
        src = w_ap.rearrange("(k p) c -> p k c", p=128)
        step = 4
        for i, k0 in enumerate(range(0, KC, step)):
            k1 = min(KC, k0 + step)
            fw.dma("pool", view[:, k0:k1, :], src[:, k0:k1, :], r=[Wt], w=[(buf, i)])
        return buf, view

    def rmsnorm(self, xg, ntok, wcol, out_fn, sq, rs):
        fw = self.fw
        C = self.C
        ps = self.bank()
        for kc in range(16):
            s = sq[kc % 2]
            fw.op("act", lambda e, s=s, kc=kc: e.activation(out=s[:, 0:ntok], in_=xg[:, kc, :], func=AF.Square), r=[xg], w=[s])
            fw.op("pe", lambda e, s=s, kc=kc: e.matmul(ps[:, 0:ntok], lhsT=self.cv("ONESM"), rhs=s[:, 0:ntok], start=(kc == 0), stop=(kc == 15)),
                  r=[s, C], w=[ps])
        fw.op("act", lambda e: e.activation(out=rs[:, 0:ntok], in_=ps[:, 0:ntok], func=AF.Sqrt, bias=self.EPSC[:, 0:1], scale=1.0), r=[ps, self.EPSC], w=[rs])
        fw.op("dve", lambda e: e.reciprocal(out=rs[:, 0:ntok], in_=rs[:, 0:ntok]), r=[rs], w=[rs])
        for kc in range(16):
            ot, oap = out_fn(kc)
            fw.op("dve", lambda e, kc=kc, oap=oap: e.scalar_tensor_tensor(out=oap, in0=xg[:, kc, :], scalar=wcol[:, kc:kc + 1], in1=rs[:, 0:ntok],
                                                                         op0=ALU.mult, op1=ALU.mult), r=[xg, rs, self.NRM], w=[ot])

    def cv(self, name, width=None):
        o, w = CO[name]
        return self.C[:, o:o + (width or w)]

    def cvt(self, name):
        if name == "EPSC":
            return self.EPSC
        raise KeyError(name)

    def build(self):
        fw = self.fw
        nl = self.nl
        self.xT_in = self.din("xT", [D, NT])
        self.d_consts = self.din("consts", [128, NCONST])
        self.d_tab = self.din("tab", [8, 128, TABW])
        self.d_et = self.din("et", [32, NT])
        self.d_nrm = self.din("nrm", [128, (2 * NL + 1) * 16])
        self.d_cw = self.din("cw", [NL, 128, 24 * 5])
        self.d_ssdv = self.din("ssdv", [NL, 128, 96])
        self.d_snw = self.din("snw", [NL, 128, 2048])
        self.d_rnw = self.din("rnw", [NL, 128, 1024])
        self.d_pet = self.din("pet", [NL, 2, 128, 32])
        self.w_in = self.din("w_in", [NL, D, DIN])
        self.cw1 = [self.din("cmp_k_w1", [NL, 4096, 128]), self.din("cmp_v_w1", [NL, 4096, 128])]
        self.cw2 = [self.din("cmp_k_w2", [NL, 128, 128]), self.din("cmp_v_w2", [NL, 128, 128])]
        self.p_nsa = self.din("p_nsa", [NL, 1024, D])
        self.p_ssd = self.din("p_ssd", [NL, 2048, D])
        self.p_ret = self.din("p_ret", [NL, 1024, D])
        self.w_out = self.din("w_out", [NL, D, D])
        self.w_gate = self.din("w_gate", [NL, D, DFF])
        self.w_up = self.din("w_up", [NL, D, DFF])
        self.w_down = self.din("w_down", [NL, DFF, D])
        self.outT = fw.dram("outT", [D, NT], F32, kind="ExternalOutput")
        self.XT = self.scr("XT", [D, NT], F32)
        self.QT = self.scr("QT", [1024, NT], BF16)
        self.KVT = self.scr("KVT", [1024, NT], BF16)
        self.VTOK = self.scr("VTOK", [NT, 512], BF16)
        self.NG = self.scr("NG", [NT, 24], F32)
        self.ZS = self.scr("ZS", [NT, 2048], F32)
        self.XBCT = self.scr("XBCT", [3072, NT], F32)
        self.DTR = self.scr("DTR", [NT, 32], F32)
        self.RQT = self.scr("RQT", [512, NT], BF16)
        self.RKT = self.scr("RKT", [512, NT], BF16)
        self.RKTOK = self.scr("RKTOK", [NT, 512], BF16)
        self.RV = self.scr("RV", [NT, 1024], BF16)
        self.RGS = self.scr("RGS", [NT, 1024], F32)
        self.MGT = self.scr("MGT", [6144, NT], BF16)
        self.XTOK = self.scr("XTOK", [NT, 2048], F32)
        self.BCT = self.scr("BCT", [1024, NT], BF16)
        self.BTOK = self.scr("BTOK", [NT, 512], BF16)
        self.YT = self.scr("YT", [4096, NT], BF16)

        self.C = fw.sb("C", [128, NCONST])
        self.NRM = fw.sb("NRM", [128, (2 * NL + 1) * 16])
        self.EPSC = fw.sb("EPSC", [128, 2])
        fw.dma("sp", self.C[:, :], self.d_consts[:, :], r=[self.d_consts], w=[self.C])
        fw.dma("sp", self.NRM[:, :], self.d_nrm[:, :], r=[self.d_nrm], w=[self.NRM])
        fw.op("dve", lambda e: e.memset(self.EPSC[:, 0:1], EPS), w=[self.EPSC])
        fw.op("dve", lambda e: e.memset(self.EPSC[:, 1:2], 1.0), w=[self.EPSC])
        self.wrot = 0
        self.base = fw.mark()

        src = self.xT_in
        for l in range(nl):
            self.layer(l, src)
            src = self.XT
            if self.stop_after is not None:
                break
        if self.stop_after is None:
            self.final_norm(src)
        fw.emit()

    def layer(self, l, src):
        sa = self.stop_after
        self.p1_inproj(l, src)
        if sa == "p1":
            return
        self.p2_ssdprep(l)
        if sa == "p2":
            return
        self.p34_nsa(l)
        if sa == "p4":
            return
        self.p5_ssd(l)
        if sa == "p5":
            return
        self.p6_ret(l)
        if sa == "p6":
            return
        self.p7_merge(l, src)
        if sa == "p7":
            return
        self.p8_ffn(l)

    def p1_inproj(self, l, src):
        fw = self.fw
        m = fw.mark()
        self.WB = [fw.sb("WB%d" % i, [128, 11264], BF16) for i in range(3)]
        uT = fw.sb("uT", [128, 16, NT], BF16)
        stF = [fw.sb("stF%d" % i, [128, NT], F32) for i in range(2)]
        stFb = [fw.sb("stFb%d" % i, [128, NT], BF16) for i in range(2)]
        stT = [fw.sb("stT%d" % i, [128, 512], F32) for i in range(2)]
        stTb = [fw.sb("stTb%d" % i, [128, 512], BF16) for i in range(2)]
        m2 = fw.mark()
        xg = [fw.sb("xg%d" % i, [128, 16, 256], F32) for i in range(1)]
        sq = [fw.sb("sq%d" % i, [128, 512], F32) for i in range(2)]
        rs = fw.sb("rs", [128, 512], F32)
        wcol = self.NRM[:, (2 * l) * 16:(2 * l + 1) * 16]
        srcv = src.ap.rearrange("(k p) t -> p k t", p=128)
        for tg in range(8):
            x = xg[0]
            for k0 in range(0, 16, 4):
                fw.dma("sp", x[:, k0:k0 + 4, :], srcv[:, k0:k0 + 4, tg * 256:(tg + 1) * 256], r=[src], w=[(x, k0)])
            self.rmsnorm(x, 256, wcol, lambda kc, tg=tg: (uT, uT[:, kc, tg * 256:(tg + 1) * 256]), sq, rs)
        fw.aoff = m2
        fw.barrier()

        W = self.w_in
        wl = W.ap[l]
        cnt = [0]

        def fm(c0, c1, dst, r0, func, scale, bf):
            c = c0
            while c < c1:
                nblk = min(512, c1 - c)
                buf, view = self.wload(W, wl[:, c:c + nblk], 16, nblk)
                for j0 in range(0, nblk, 128):
                    n = min(128, nblk - j0)
                    banks = [self.bank() for _ in range(4)]
                    for kc in range(16):
                        for tg in range(4):
                            fw.op("pe", lambda e, kc=kc, tg=tg, b=banks[tg], n=n, j0=j0, view=view: e.matmul(
                                b[0:n, :], lhsT=view[:, kc, j0:j0 + n], rhs=uT[:, kc, tg * 512:(tg + 1) * 512],
                                start=(kc == 0), stop=(kc == 15)), r=[buf, uT], w=[banks[tg]])
                    i = cnt[0] % 2
                    cnt[0] += 1
                    st = stFb[i] if bf else stF[i]
                    for tg in range(4):
                        fw.op("act", lambda e, tg=tg, b=banks[tg], n=n, st=st: e.activation(
                            out=st[0:n, tg * 512:(tg + 1) * 512], in_=b[0:n, :], func=func, scale=scale), r=[banks[tg]], w=[(st, tg)])
                    rr = r0 + (c - c0) + j0
                    fw.dma("sp", dst[rr:rr + n, :], st[0:n, :], r=[st], w=[(dst, rr)])
                c += nblk

        def tm(c0, c1, dst, d0, func, bf):
            c = c0
            while c < c1:
                nblk = min(512, c1 - c)
                buf, view = self.wload(W, wl[:, c:c + nblk], 16, nblk)
                for tt in range(16):
                    b = self.bank()
                    for kc in range(16):
                        fw.op("pe", lambda e, kc=kc, tt=tt, b=b, nblk=nblk, view=view: e.matmul(
                            b[:, 0:nblk], lhsT=uT[:, kc, tt * 128:(tt + 1) * 128], rhs=view[:, kc, 0:nblk],
                            start=(kc == 0), stop=(kc == 15)), r=[buf, uT], w=[b])
                    i = cnt[0] % 2
                    cnt[0] += 1
                    st = stTb[i] if bf else stT[i]
                    fw.op("act", lambda e, b=b, nblk=nblk, st=st: e.activation(out=st[:, 0:nblk], in_=b[:, 0:nblk], func=func), r=[b], w=[st])
                    dd = d0 + (c - c0)
                    fw.dma("sp", dst[tt * 128:(tt + 1) * 128, dd:dd + nblk], st[:, 0:nblk], r=[st], w=[(dst, (tt, dd))])
                c += nblk

        ID, SIG, SILU = AF.Identity, AF.Sigmoid, AF.Silu
        sc = 128.0 ** -0.5
        fm(C_Q, C_Q + 1024, self.QT, 0, ID, sc, True)
        fm(C_KV, C_KV + 768, self.KVT, 0, ID, 1.0, True)
        tm(C_KV + 768, C_KV + 1024, self.VTOK, 0, ID, True)
        fm(C_KV + 1024, C_KV + 1280, self.KVT, 768, ID, 1.0, True)
        tm(C_KV + 1280, C_KV + 1536, self.VTOK, 256, ID, True)
        tm(C_NG, C_NG + 24, self.NG, 0, SIG, False)
        tm(C_Z, C_Z + 2048, self.ZS, 0, SILU, False)
        fm(C_XBC, C_XBC + 3072, self.XBCT, 0, ID, 1.0, False)
        tm(C_DT, C_DT + 32, self.DTR, 0, ID, False)
        fm(C_RQ, C_RQ + 512, self.RQT, 0, ID, sc, True)
        fm(C_RK, C_RK + 512, self.RKT, 0, ID, 1.0, True)
        tm(C_RK, C_RK + 512, self.RKTOK, 0, ID, True)
        tm(C_RV, C_RV + 1024, self.RV, 0, ID, True)
        tm(C_RG, C_RG + 1024, self.RGS, 0, SILU, False)
        fm(C_MG, C_MG + 6144, self.MGT, 0, SIG, 1.0, True)
        fw.release(m)

    def p2_ssdprep(self, l):
        fw = self.fw
        m = fw.mark()
        cw = fw.sb("cw", [128, 24, 5])
        fw.dma("sp", cw[:, :, :], self.d_cw.ap[l].rearrange("p (t k) -> p t k", k=5), r=[self.d_cw], w=[cw])
        xp = [fw.sb("xp%d" % i, [128, 3 + NT]) for i in range(2)]
        acc = [fw.sb("acc%d" % i, [128, NT]) for i in range(2)]
        xo = [fw.sb("xo%d" % i, [128, NT]) for i in range(2)]
        xb = [fw.sb("xob%d" % i, [128, NT], BF16) for i in range(2)]
        stt = [fw.sb("sttok%d" % i, [128, 4, 128]) for i in range(2)]
        sttb = [fw.sb("sttokb%d" % i, [128, 4, 128], BF16) for i in range(2)]
        for i in range(2):
            fw.op("dve", lambda e, i=i: e.memset(xp[i][:, 0:3], 0.0), w=[xp[i]])
        k = 0
        fw.dma("sp", xp[0][:, 3:3 + NT], self.XBCT[0:128, :], r=[self.XBCT], w=[xp[0]])
        for ct in range(24):
            i = ct % 2
            if ct + 1 < 24:
                fw.dma("sp", xp[1 - i][:, 3:3 + NT], self.XBCT[(ct + 1) * 128:(ct + 2) * 128, :], r=[self.XBCT], w=[xp[1 - i]])
            a = acc[i]
            fw.op("dve", lambda e, i=i, ct=ct, a=a: e.tensor_scalar(out=a[:, :], in0=xp[i][:, 0:NT], scalar1=cw[:, ct, 0:1], scalar2=None, op0=ALU.mult),
                  r=[xp[i], cw], w=[a])
            for kk in range(1, 4):
                fw.op("dve", lambda e, i=i, ct=ct, a=a, kk=kk: e.scalar_tensor_tensor(out=a[:, :], in0=xp[i][:, kk:kk + NT], scalar=cw[:, ct, kk:kk + 1],
                                                                                      in1=a[:, :], op0=ALU.mult, op1=ALU.add), r=[xp[i], cw, a], w=[a])
            o = xo[i]
            fw.op("act", lambda e, a=a, o=o, ct=ct: e.activation(out=o[:, :], in_=a[:, :], func=AF.Silu, bias=cw[:, ct, 4:5], scale=1.0), r=[a, cw], w=[o])
            if ct >= 16:
                ob = xb[i]
                fw.op("pool", lambda e, o=o, ob=ob: e.tensor_copy(out=ob[:, :], in_=o[:, :]), r=[o], w=[ob])
                r0 = (ct - 16) * 128
                fw.dma("sp", self.BCT[r0:r0 + 128, :], ob[:, :], r=[ob], w=[(self.BCT, r0)])
            if ct < 20:
                for t4 in range(4):
                    b = self.bank()
                    for j in range(4):
                        tt = t4 * 4 + j
                        fw.op("pe", lambda e, b=b, j=j, tt=tt, o=o: e.transpose(b[:, j * 128:(j + 1) * 128], o[:, tt * 128:(tt + 1) * 128], self.cv("IDENT")),
                              r=[o, self.C], w=[b])
                    if ct < 16:
                        s = stt[k % 2]
                        dst = self.XTOK.ap[t4 * 512:(t4 + 1) * 512, ct * 128:(ct + 1) * 128]
                        dt_ = self.XTOK
                    else:
                        s = sttb[k % 2]
                        cc = (ct - 16) * 128
                        dst = self.BTOK.ap[t4 * 512:(t4 + 1) * 512, cc:cc + 128]
                        dt_ = self.BTOK
                    k += 1
                    fw.op("act", lambda e, b=b, s=s: e.copy(out=s.ap.rearrange("p a b -> p (a b)"), in_=b[:, :]), r=[b], w=[s])
                    fw.dma("sp", dst.rearrange("(a p) c -> p a c", p=128), s[:, :, :], r=[s], w=[(dt_, (t4, ct))])
        fw.release(m)

    def p34_nsa(self, l):
        fw = self.fw
        C = self.C
        m = fw.mark()
        KCT = [fw.sb("KCT%d" % g, [128, 128], BF16) for g in range(2)]
        VCX = [fw.sb("VCX%d" % g, [128, 161]) for g in range(2)]
        m3 = fw.mark()
        w1s = fw.sb("w1s", [128, 32, 128], BF16)
        w2s = fw.sb("w2s", [128, 128], BF16)
        pet = fw.sb("pet", [128, 32], BF16)
        kcT = [fw.sb("kcT%d" % i, [128, NT], BF16) for i in range(2)]
        pb = fw.sb("pb", [128, 1])
        hx = fw.sb("hx", [128, 128])
        h2 = fw.sb("h2", [128, 128])
        G = fw.sb("G", [128, 128], BF16)
        for t in range(2):
            W1 = self.cw1[t]
            w1v = W1.ap[l].rearrange("(l d) o -> d l o", d=128)
            for l0 in range(0, 32, 8):
                fw.dma("pool", w1s[:, l0:l0 + 8, :], w1v[:, l0:l0 + 8, :], r=[W1], w=[(w1s, l0)])
            fw.dma("pool", w2s[:, :], self.cw2[t].ap[l], r=[self.cw2[t]], w=[w2s])
            fw.dma("pool", pet[:, :], self.d_pet.ap[l, t], r=[self.d_pet], w=[pet])
            for g in range(2):
                src = kcT[g]
                r0 = t * 256 + g * 128
                fw.dma("sp", src[:, :], self.KVT[r0:r0 + 128, :], r=[self.KVT], w=[src])
                sv = src.ap.rearrange("p (n s) -> p n s", s=16)
                ps = self.bank()
                for li in range(32):
                    rhs = sv[:, 0:127, li] if li < 16 else sv[:, 1:128, li - 16]
                    fw.op("pe", lambda e, li=li, rhs=rhs, ps=ps: e.matmul(ps[:, 0:127], lhsT=w1s[:, li, :], rhs=rhs, start=(li == 0), stop=(li == 31)),
                          r=[w1s, src], w=[ps])
                ps2 = self.bank()
                for li in range(32):
                    fw.op("pe", lambda e, li=li, ps2=ps2: e.matmul(ps2[:, 0:1], lhsT=w1s[:, li, :], rhs=pet[:, li:li + 1], start=(li == 0), stop=(li == 31)),
                          r=[w1s, pet], w=[ps2])
                fw.op("act", lambda e, ps2=ps2: e.copy(out=pb[:, :], in_=ps2[:, 0:1]), r=[ps2], w=[pb])
                fw.op("dve", lambda e: e.memset(hx[:, 127:128], 0.0), w=[hx])
                fw.op("act", lambda e, ps=ps: e.activation(out=hx[:, 0:127], in_=ps[:, 0:127], func=AF.Identity, bias=pb[:, 0:1], scale=1.0), r=[ps, pb], w=[hx])
                fw.op("dve", lambda e: e.tensor_tensor(out=h2[:, :], in0=hx[:, :], in1=hx[:, :], op=ALU.mult), r=[hx], w=[h2])
                fw.op("dve", lambda e: e.tensor_scalar(out=h2[:, :], in0=h2[:, :], scalar1=0.044715, scalar2=1.0, op0=ALU.mult, op1=ALU.add), r=[h2], w=[h2])
                fw.op("dve", lambda e: e.tensor_tensor(out=h2[:, :], in0=h2[:, :], in1=hx[:, :], op=ALU.mult), r=[h2, hx], w=[h2])
                fw.op("act", lambda e: e.activation(out=h2[:, :], in_=h2[:, :], func=AF.Sigmoid, scale=1.5957691216057308), r=[h2], w=[h2])
                fw.op("dve", lambda e: e.tensor_tensor(out=G[:, :], in0=h2[:, :], in1=hx[:, :], op=ALU.mult), r=[h2, hx], w=[G])
                ps3 = self.bank()
                if t == 0:
                    fw.op("pe", lambda e, ps3=ps3: e.matmul(ps3[:, 0:128], lhsT=w2s[:, :], rhs=G[:, :], start=True, stop=True), r=[w2s, G], w=[ps3])
                    fw.op("act", lambda e, ps3=ps3, g=g: e.copy(out=KCT[g][:, :], in_=ps3[:, 0:128]), r=[ps3], w=[KCT[g]])
                else:
                    fw.op("pe", lambda e, ps3=ps3: e.matmul(ps3[:, 0:128], lhsT=G[:, :], rhs=w2s[:, :], start=True, stop=True), r=[w2s, G], w=[ps3])
                    fw.op("act", lambda e, ps3=ps3, g=g: e.copy(out=VCX[g][:, 0:128], in_=ps3[:, 0:128]), r=[ps3], w=[VCX[g]])
                    fw.op("dve", lambda e, g=g: e.tensor_copy(out=VCX[g][:, 128:161], in_=self.cv("OVL")), r=[C], w=[VCX[g]])
        fw.release(m3)

        ET = fw.sb("ET", [32, NT], BF16)
        fw.dma("pool", ET[:, :], self.d_et[:, :], r=[self.d_et], w=[ET])
        NGs = fw.sb("NGs", [128, 16, 24])
        fw.dma("sp", NGs[:, :, :], self.NG.ap.rearrange("(t p) c -> p t c", p=128), r=[self.NG], w=[NGs])
        ksT = fw.sb("ksT", [128, NT], BF16)
        kwT = fw.sb("kwT", [128, NT], BF16)
        VSX = fw.sb("VSX", [128, 16, 129], BF16)
        VWX = fw.sb("VWX", [128, 16, 129], BF16)
        QTg = [fw.sb("QTg%d" % j, [128, NT], BF16) for j in range(4)]
        TABC = [fw.sb("TABC%d" % i, [128, 2048]) for i in range(2)]
        TABS = [fw.sb("TABS%d" % i, [128, TO_C]) for i in range(2)]
        IMP = fw.sb("IMP", [128, 16, 32])
        NEGT = fw.sb("NEGT", [32, NT], BF16)
        Yh = [fw.sb("Yh%d" % j, [128, 16, 128]) for j in range(4)]
        scb = [fw.sb("scb%d" % i, [128, 512]) for i in range(4)]
        p32 = [fw.sb("p32_%d" % i, [128, 512]) for i in range(3)]
        pbf = [fw.sb("pbf%d" % i, [128, 512], BF16) for i in range(4)]
        sm = [fw.sb("sm%d" % i, [128, 48]) for i in range(4)]
        ytb = [fw.sb("ytb%d" % i, [128, 512], BF16) for i in range(2)]
        MSK = self.cv("MSK").rearrange("p (a q k) -> p a q k", a=3, q=16, k=32)
        cnt = {"s": 0, "p": 0, "sm": 0, "y": 0, "r": 0, "c": 0, "p32": 0}
        PS = fw.PS
        LA = 2

        def sbank():
            b = PS[4 + cnt["c"] % 4]
            cnt["c"] += 1
            return b

        def pipeline(items):
            n = len(items)
            for i in range(n + LA):
                if i < n:
                    items[i][0]()
                if i >= LA:
                    items[i - LA][1]()

        def evac_round(banks, W, Yt, Q0, h, br, first, imp):
            s = sm[cnt["sm"] % 4]
            cnt["sm"] += 1
            for bi in range(2):
                fw.op("dve", lambda e, bi=bi: e.tensor_scalar(out=s[:, 2 * bi:2 * bi + 2], in0=banks[bi][:, 128:128 + W + 1:W], scalar1=1e-30, scalar2=None, op0=ALU.max),
                      r=[banks[bi]], w=[s])
            fw.op("dve", lambda e: e.reciprocal(out=s[:, 0:4], in_=s[:, 0:4]), r=[s], w=[s])
            fw.op("dve", lambda e: e.tensor_tensor(out=s[:, 4:8], in0=s[:, 0:4], in1=NGs[:, Q0:Q0 + 4, h * 3 + br], op=ALU.mult), r=[s, NGs], w=[s])
            for qt in range(4):
                bk = banks[qt // 2]
                c0 = (qt % 2) * W
                Q = Q0 + qt
                if first:
                    fw.op("dve", lambda e, bk=bk, c0=c0, Q=Q, qt=qt: e.tensor_scalar(out=Yt[:, Q, :], in0=bk[:, c0:c0 + 128], scalar1=s[:, 4 + qt:5 + qt], scalar2=None, op0=ALU.mult),
                          r=[bk, s], w=[(Yt, Q)])
                else:
                    fw.op("dve", lambda e, bk=bk, c0=c0, Q=Q, qt=qt: e.scalar_tensor_tensor(out=Yt[:, Q, :], in0=bk[:, c0:c0 + 128], scalar=s[:, 4 + qt:5 + qt], in1=Yt[:, Q, :],
                                                                                           op0=ALU.mult, op1=ALU.add), r=[bk, s, (Yt, Q)], w=[(Yt, Q)])
            if imp:
                for qt in range(4):
                    bk = banks[qt // 2]
                    c0 = (qt % 2) * W
                    Q = Q0 + qt
                    fw.op("dve", lambda e, bk=bk, c0=c0, Q=Q, qt=qt: e.scalar_tensor_tensor(out=IMP[:, Q, :], in0=bk[:, c0 + 129:c0 + 161], scalar=s[:, qt:qt + 1], in1=IMP[:, Q, :],
                                                                                           op0=ALU.mult, op1=ALU.add), r=[bk, s, (IMP, Q)], w=[(IMP, Q)])

        for g in range(2):
            fw.dma("sp", ksT[:, :], self.KVT[512 + g * 128:512 + (g + 1) * 128, :], r=[self.KVT], w=[ksT])
            fw.dma("sp", kwT[:, :], self.KVT[768 + g * 128:768 + (g + 1) * 128, :], r=[self.KVT], w=[kwT])
            fw.dma("sp", VSX[:, :, 0:128], self.VTOK.ap[:, g * 128:(g + 1) * 128].rearrange("(t p) d -> p t d", p=128), r=[self.VTOK], w=[VSX])
            fw.dma("sp", VWX[:, :, 0:128], self.VTOK.ap[:, 256 + g * 128:256 + (g + 1) * 128].rearrange("(t p) d -> p t d", p=128), r=[self.VTOK], w=[VWX])
            fw.op("dve", lambda e: e.memset(VSX[:, :, 128:129], 1.0), w=[VSX])
            fw.op("dve", lambda e: e.memset(VWX[:, :, 128:129], 1.0), w=[VWX])
            fw.op("dve", lambda e: e.memset(IMP[:, :, :], 0.0), w=[IMP])
            for j in range(4):
                h = g * 4 + j
                fw.dma("sp", QTg[j][:, :], self.QT[h * 128:(h + 1) * 128, :], r=[self.QT], w=[QTg[j]])
            items = []
            for j in range(4):
                h = g * 4 + j
                for qg in range(4):
                    st = {}

                    def front(j=j, h=h, qg=qg, st=st):
                        tb = TABC[h % 2]
                        if qg == 0:
                            fw.dma("sp", tb[:, :], self.d_tab.ap[h][:, TO_C:TO_C + 2048], r=[self.d_tab], w=[tb])
                        S = sbank()
                        fw.op("pe", lambda e: e.matmul(S[:, :], lhsT=KCT[g][:, :], rhs=QTg[j][:, qg * 512:(qg + 1) * 512], start=True, stop=True), r=[KCT[g], QTg[j]], w=[S])
                        sc_ = scb[cnt["s"] % 4]
                        cnt["s"] += 1
                        pp = p32[cnt["p32"] % 3]
                        cnt["p32"] += 1
                        fw.op("dve", lambda e: e.tensor_tensor(out=sc_[:, :], in0=S[:, :], in1=tb[:, qg * 512:(qg + 1) * 512], op=ALU.add), r=[S, tb], w=[sc_])
                        fw.op("act", lambda e: e.activation(out=pp[:, :], in_=sc_[:, :], func=AF.Exp), r=[sc_], w=[pp])
                        st["pp"] = pp

                    def back(j=j, h=h, qg=qg, st=st):
                        pp = st["pp"]
                        r_ = cnt["r"] % 2
                        cnt["r"] += 1
                        banks = [PS[2 * r_], PS[2 * r_ + 1]]
                        for qt in range(4):
                            bk = banks[qt // 2]
                            c0 = (qt % 2) * 161
                            fw.op("pe", lambda e, bk=bk, c0=c0, qt=qt: e.matmul(bk[:, c0:c0 + 161], lhsT=pp[:, qt * 128:(qt + 1) * 128], rhs=VCX[g][:, :], start=True, stop=True),
                                  r=[pp, VCX[g]], w=[bk])
                        evac_round(banks, 161, Yh[j], qg * 4, h, 0, True, True)

                    items.append((front, back))
            pipeline(items)
            for Q in range(16):
                s = sm[cnt["sm"] % 4]
                cnt["sm"] += 1
                im = s[:, 8:40]
                fw.op("dve", lambda e, Q=Q, im=im: e.tensor_tensor(out=im, in0=IMP[:, Q, :], in1=MSK[:, 0, Q, :], op=ALU.mult), r=[(IMP, Q), C], w=[s])
                fw.op("dve", lambda e, Q=Q, im=im: e.tensor_tensor(out=im, in0=im, in1=MSK[:, 1, Q, :], op=ALU.add), r=[s, C], w=[s])
                fw.op("dve", lambda e, s=s, im=im: e.max(out=s[:, 0:8], in_=im), r=[s], w=[s])
                fw.op("dve", lambda e, s=s, im=im: e.tensor_scalar(out=im, in0=im, scalar1=s[:, 7:8], scalar2=None, op0=ALU.is_ge), r=[s], w=[s])
                fw.op("dve", lambda e, Q=Q, im=im: e.tensor_tensor(out=im, in0=im, in1=MSK[:, 2, Q, :], op=ALU.mult), r=[s, C], w=[s])
                fw.op("dve", lambda e, im=im: e.tensor_scalar(out=im, in0=im, scalar1=-1.0, scalar2=-NEG, op0=ALU.add, op1=ALU.mult), r=[s], w=[s])
                b = sbank()
                fw.op("pe", lambda e, b=b, im=im: e.transpose(b[0:32, 0:128], im, self.cv("IDENT")), r=[s, C], w=[b])
                fw.op("act", lambda e, b=b, Q=Q: e.copy(out=NEGT[:, Q * 128:(Q + 1) * 128], in_=b[0:32, 0:128]), r=[b], w=[(NEGT, Q)])
            items = []
            for j in range(4):
                h = g * 4 + j
                slope = 2.0 ** (-(h + 1))
                for br in (1, 2):
                    for qg in range(4):
                        kt_lo = 0 if br == 1 else max(0, 4 * qg - 4)
                        kts = list(range(kt_lo, 4 * qg + 4))
                        rst = {"used": [False, False]}
                        for kt in kts:
                            st = {}
                            first_of_head = (br == 1 and qg == 0 and kt == kts[0])
                            last_of_round = (kt == kts[-1])
                            last_of_head = (br == 2 and qg == 3 and last_of_round)

                            def front(j=j, h=h, slope=slope, br=br, qg=qg, kt=kt, st=st, first_of_head=first_of_head):
                                tb = TABS[j % 2]
                                if first_of_head:
                                    fw.dma("sp", tb[:, :], self.d_tab.ap[h][:, 0:TO_C], r=[self.d_tab], w=[tb])
                                KT = ksT if br == 1 else kwT
                                D0 = qg * 512 - kt * 128
                                S = sbank()
                                fw.op("pe", lambda e: e.matmul(S[:, :], lhsT=KT[:, kt * 128:(kt + 1) * 128], rhs=QTg[j][:, qg * 512:(qg + 1) * 512], start=True, stop=(br == 2)),
                                      r=[KT, QTg[j]], w=[S])
                                if br == 1:
                                    fw.op("pe", lambda e: e.matmul(S[:, :], lhsT=ET[:, kt * 128:(kt + 1) * 128], rhs=NEGT[:, qg * 512:(qg + 1) * 512], start=False, stop=True),
                                          r=[ET, NEGT], w=[S])
                                sc_ = scb[cnt["s"] % 4]
                                cnt["s"] += 1
                                pp = pbf[cnt["p"] % 4]
                                cnt["p"] += 1
                                bias = 0.0
                                if br == 1:
                                    if D0 <= 0:
                                        tsl = tb[:, TO_U + D0 + 384:TO_U + D0 + 384 + 512]
                                    else:
                                        tsl = tb[:, TO_W:TO_W + 512]
                                        bias = -slope * D0
                                else:
                                    tsl = tb[:, TO_UW + D0 + 384:TO_UW + D0 + 384 + 512]
                                if bias == 0.0:
                                    fw.op("dve", lambda e: e.tensor_tensor(out=sc_[:, :], in0=S[:, :], in1=tsl, op=ALU.add), r=[S, tb], w=[sc_])
                                else:
                                    fw.op("dve", lambda e: e.scalar_tensor_tensor(out=sc_[:, :], in0=S[:, :], scalar=bias, in1=tsl, op0=ALU.add, op1=ALU.add), r=[S, tb], w=[sc_])
                                fw.op("act", lambda e: e.activation(out=pp[:, :], in_=sc_[:, :], func=AF.Exp), r=[sc_], w=[pp])
                                st["pp"] = pp

                            def back(j=j, h=h, br=br, qg=qg, kt=kt, st=st, rst=rst, first=(kt == kts[0]), last_of_round=last_of_round, last_of_head=last_of_head):
                                pp = st["pp"]
                                VX = VSX if br == 1 else VWX
                                if first:
                                    r_ = cnt["r"] % 2
                                    cnt["r"] += 1
                                    rst["banks"] = [PS[2 * r_], PS[2 * r_ + 1]]
                                banks = rst["banks"]
                                for qt in range(4):
                                    Q = qg * 4 + qt
                                    lo = 0 if br == 1 else max(0, Q - 4)
                                    if kt < lo or kt > Q:
                                        continue
                                    bi = qt // 2
                                    bk = banks[bi]
                                    c0 = (qt % 2) * 129
                                    stt_ = not rst["used"][bi]
                                    rst["used"][bi] = True
                                    fw.op("pe", lambda e, bk=bk, c0=c0, qt=qt, stt_=stt_, Q=Q: e.matmul(bk[:, c0:c0 + 129], lhsT=pp[:, qt * 128:(qt + 1) * 128], rhs=VX[:, kt, :],
                                                                                                    start=stt_, stop=(kt == Q), skip_group_check=True), r=[pp, VX], w=[bk])
                                if last_of_round:
                                    evac_round(banks, 129, Yh[j], qg * 4, h, br, False, False)
                                if last_of_head:
                                    for t4 in range(4):
                                        b = sbank()
                                        for jj in range(4):
                                            Q = t4 * 4 + jj
                                            fw.op("pe", lambda e, b=b, jj=jj, Q=Q: e.transpose(b[:, jj * 128:(jj + 1) * 128], Yh[j][:, Q, :], self.cv("IDENT")), r=[(Yh[j], Q), C], w=[b])
                                        y = ytb[cnt["y"] % 2]
                                        cnt["y"] += 1
                                        fw.op("act", lambda e, b=b, y=y: e.copy(out=y[:, :], in_=b[:, :]), r=[b], w=[y])
                                        fw.dma("sp", self.YT[h * 128:(h + 1) * 128, t4 * 512:(t4 + 1) * 512], y[:, :], r=[y], w=[(self.YT, (h, t4))])

                            items.append((front, back))
            pipeline(items)
        fw.release(m)
        fw.release(m)

    def p5_ssd(self, l):
        fw = self.fw
        C = self.C
        PS = fw.PS
        m = fw.mark()
        sv = fw.sb("ssdv", [128, 96])
        fw.dma("sp", sv[:, :], self.d_ssdv.ap[l], r=[self.d_ssdv], w=[sv])
        snw = fw.sb("snw", [128, 2048])
        fw.dma("sp", snw[:, :], self.d_snw.ap[l], r=[self.d_snw], w=[snw])
        negA = fw.sb("negA", [128, 32])
        fw.op("act", lambda e: e.activation(out=negA[:, :], in_=sv[:, 32:64], func=AF.Exp), r=[sv], w=[negA])
        fw.op("dve", lambda e: e.tensor_scalar(out=negA[:, :], in0=negA[:, :], scalar1=-1.0, scalar2=None, op0=ALU.mult), r=[negA], w=[negA])
        BT = fw.sb("BT", [128, 4, NT], BF16)
        CT = fw.sb("CT", [128, 4, NT], BF16)
        fw.dma("sp", BT[:, :, :], self.BCT.ap[0:512, :].rearrange("(g p) t -> p g t", p=128), r=[self.BCT], w=[BT])
        fw.dma("sp", CT[:, :, :], self.BCT.ap[512:1024, :].rearrange("(g p) t -> p g t", p=128), r=[self.BCT], w=[CT])
        H = fw.sb("H", [128, 4, 512])
        Hb = fw.sb("Hb", [128, 4, 512], BF16)
        fw.op("dve", lambda e: e.memset(H[:, :, :], 0.0), w=[H])
        fw.op("dve", lambda e: e.memset(Hb[:, :, :], 0.0), w=[Hb])
        dtA = fw.sb("dtA", [128, 16, 32])
        tmpA = fw.sb("tmpA", [128, 16, 32])
        aA = fw.sb("aA", [128, 16, 32])
        acA = fw.sb("acA", [128, 16, 32])
        atA = fw.sb("atA", [128, 16, 32])
        eaA = fw.sb("eaA", [128, 16, 32])
        cdA = fw.sb("cdA", [128, 16, 32])
        dteA = fw.sb("dteA", [128, 16, 32])
        fl = lambda t: t.ap.rearrange("p c h -> p (c h)")
        bc = lambda ap: ap.unsqueeze(1).broadcast_to([128, 16, 32])
        fw.dma("sp", dtA[:, :, :], self.DTR.ap.rearrange("(c p) h -> p c h", p=128), r=[self.DTR], w=[dtA])
        fw.op("dve", lambda e: e.tensor_tensor(out=dtA[:, :, :], in0=dtA[:, :, :], in1=bc(sv[:, 0:32]), op=ALU.add), r=[dtA, sv], w=[dtA])
        fw.op("dve", lambda e: e.tensor_scalar(out=fl(tmpA), in0=fl(dtA), scalar1=-1.0, scalar2=None, op0=ALU.mult), r=[dtA], w=[tmpA])
        fw.op("dve", lambda e: e.tensor_tensor(out=fl(tmpA), in0=fl(tmpA), in1=fl(dtA), op=ALU.max), r=[dtA, tmpA], w=[tmpA])
        fw.op("act", lambda e: e.activation(out=fl(tmpA), in_=fl(tmpA), func=AF.Exp, scale=-1.0), r=[tmpA], w=[tmpA])
        fw.op("act", lambda e: e.activation(out=fl(tmpA), in_=fl(tmpA), func=AF.Ln, bias=self.EPSC[:, 1:2], scale=1.0), r=[tmpA, self.EPSC], w=[tmpA])
        fw.op("dve", lambda e: e.scalar_tensor_tensor(out=fl(dtA), in0=fl(dtA), scalar=0.0, in1=fl(tmpA), op0=ALU.max, op1=ALU.add), r=[dtA, tmpA], w=[dtA])
        fw.op("dve", lambda e: e.tensor_tensor(out=aA[:, :, :], in0=dtA[:, :, :], in1=bc(negA[:, :]), op=ALU.mult), r=[dtA, negA], w=[aA])
        fw.op("pe", lambda e: e.matmul(PS[1][:, :], lhsT=self.cv("CAUS"), rhs=fl(aA), start=True, stop=True), r=[aA, C], w=[PS[1]])
        fw.op("pe", lambda e: e.matmul(PS[2][:, :], lhsT=self.cv("ONES1"), rhs=fl(aA), start=True, stop=True), r=[aA, C], w=[PS[2]])
        fw.op("act", lambda e: e.copy(out=fl(acA), in_=PS[1][:, :]), r=[PS[1]], w=[acA])
        fw.op("act", lambda e: e.copy(out=fl(atA), in_=PS[2][:, :]), r=[PS[2]], w=[atA])
        fw.op("act", lambda e: e.activation(out=fl(eaA), in_=fl(acA), func=AF.Exp), r=[acA], w=[eaA])
        fw.op("act", lambda e: e.activation(out=fl(cdA), in_=fl(atA), func=AF.Exp), r=[atA], w=[cdA])
        fw.op("dve", lambda e: e.tensor_tensor(out=fl(dteA), in0=fl(atA), in1=fl(acA), op=ALU.subtract), r=[atA, acA], w=[dteA])
        fw.op("act", lambda e: e.activation(out=fl(dteA), in_=fl(dteA), func=AF.Exp), r=[dteA], w=[dteA])

        xc = [fw.sb("xc%d" % i, [128, 2048]) for i in range(3)]
        zc = [fw.sb("zc%d" % i, [128, 2048]) for i in range(3)]
        bk = [fw.sb("bk%d" % i, [128, 512], BF16) for i in range(3)]
        xdt = fw.sb("xdt", [128, 2048], BF16)
        xdte = fw.sb("xdte", [128, 2048], BF16)
        cbm = fw.sb("cbm", [128, 4, 128])
        seg = [fw.sb("seg%d" % i, [128, 4, 128]) for i in range(2)]
        dec = [fw.sb("dec%d" % i, [128, 4, 128]) for i in range(3)]
        MT = [fw.sb("MT%d" % i, [128, 4, 128], BF16) for i in range(3)]
        ys_ = [fw.sb("y%d" % i, [128, 2048]) for i in range(2)]
        ysq = fw.sb("ysq", [128, 512])
        st4 = fw.sb("st4", [128, 8])
        yT = fw.sb("yT", [128, 16, 128], BF16)
        CAUS = self.cv("CAUS")
        cnt = {"k": 0, "m": 0, "b": 0}
        LA = 2

        def sbank():
            b = PS[1 + cnt["b"] % 3]
            cnt["b"] += 1
            return b

        def loadc(c):
            i = c % 3
            rows = slice(c * 128, (c + 1) * 128)
            fw.dma("sp", xc[i][:, :], self.XTOK[rows, :], r=[self.XTOK], w=[xc[i]])
            fw.dma("sp", bk[i][:, :], self.BTOK[rows, :], r=[self.BTOK], w=[bk[i]])
            fw.dma("sp", zc[i][:, :], self.ZS[rows, :], r=[self.ZS], w=[zc[i]])

        def stageA(c):
            i = c % 3
            x, z, bt = xc[i], zc[i], bk[i]
            y = ys_[c % 2]
            cs = slice(c * 128, (c + 1) * 128)
            x3 = x.ap.rearrange("p (h q) -> p h q", q=64)
            fw.op("dve", lambda e: e.tensor_tensor(out=xdt.ap.rearrange("p (h q) -> p h q", q=64), in0=x3, in1=dtA[:, c, :].unsqueeze(2).broadcast_to([128, 32, 64]), op=ALU.mult),
                  r=[x, dtA], w=[xdt])
            fw.op("dve", lambda e: e.tensor_tensor(out=xdte.ap.rearrange("p (h q) -> p h q", q=64), in0=xdt.ap.rearrange("p (h q) -> p h q", q=64),
                                                   in1=dteA[:, c, :].unsqueeze(2).broadcast_to([128, 32, 64]), op=ALU.mult), r=[xdt, dteA], w=[xdte])
            for g in range(4):
                fw.op("pe", lambda e, g=g: e.matmul(PS[0][:, g * 128:(g + 1) * 128], lhsT=BT[:, g, cs], rhs=CT[:, g, cs], start=True, stop=True), r=[BT, CT], w=[PS[0]])
            fw.op("dve", lambda e: e.tensor_tensor(out=cbm[:, :, :], in0=PS[0].ap.rearrange("p (g l) -> p g l", l=128), in1=CAUS.unsqueeze(1).broadcast_to([128, 4, 128]), op=ALU.mult),
                  r=[PS[0], C], w=[cbm])
            items = []
            for g in range(4):
                for hh in range(2):
                    st = {}

                    def front(g=g, hh=hh, st=st):
                        h0 = g * 8 + hh * 4
                        sg, dc = seg[cnt["k"] % 2], dec[cnt["k"] % 3]
                        cnt["k"] += 1
                        mt = MT[cnt["m"] % 3]
                        cnt["m"] += 1
                        fw.op("dve", lambda e: e.tensor_tensor(out=sg[:, :, :], in0=CAUS.unsqueeze(1).broadcast_to([128, 4, 128]),
                                                               in1=aA[:, c, h0:h0 + 4].unsqueeze(2).broadcast_to([128, 4, 128]), op=ALU.mult), r=[aA, C], w=[sg])
                        pseg = sbank()
                        fw.op("pe", lambda e: e.matmul(pseg[:, :], lhsT=self.cv("TGT"), rhs=sg.ap.rearrange("p a b -> p (a b)"), start=True, stop=True), r=[sg, C], w=[pseg])
                        fw.op("act", lambda e: e.activation(out=dc.ap.rearrange("p a b -> p (a b)"), in_=pseg[:, :], func=AF.Exp), r=[pseg], w=[dc])
                        st["mt"] = mt
                        st["dc"] = dc

                    def mid(g=g, hh=hh, st=st):
                        mt, dc = st["mt"], st["dc"]
                        fw.op("dve", lambda e: e.tensor_tensor(out=mt[:, :, :], in0=dc[:, :, :], in1=cbm[:, g, :].unsqueeze(1).broadcast_to([128, 4, 128]), op=ALU.mult),
                              r=[dc, cbm], w=[mt])

                    def back(g=g, hh=hh, st=st):
                        mt = st["mt"]
                        h0 = g * 8 + hh * 4
                        yd = PS[4 + g % 2]
                        for q in range(4):
                            hd = h0 + q
                            cc = (hh * 4 + q) * 64
                            fw.op("pe", lambda e, q=q, hd=hd, cc=cc: e.matmul(yd[:, cc:cc + 64], lhsT=mt[:, q, :], rhs=xdt[:, hd * 64:(hd + 1) * 64], start=True, stop=True),
                                  r=[mt, xdt], w=[yd])
                        if hh == 1:
                            yo = PS[6]
                            fw.op("pe", lambda e: e.matmul(yo[:, :], lhsT=CT[:, g, cs], rhs=Hb[:, g, :], start=True, stop=True), r=[CT, (Hb, g)], w=[yo])
                            ysl = y.ap[:, g * 512:(g + 1) * 512].rearrange("p (h q) -> p h q", q=64)
                            fw.op("dve", lambda e: e.tensor_tensor(out=ysl, in0=yo.ap.rearrange("p (h q) -> p h q", q=64),
                                                                   in1=eaA[:, c, g * 8:(g + 1) * 8].unsqueeze(2).broadcast_to([128, 8, 64]), op=ALU.mult), r=[yo, eaA], w=[(y, g)])
                            fw.op("dve", lambda e: e.tensor_tensor(out=y[:, g * 512:(g + 1) * 512], in0=yd[:, :], in1=y[:, g * 512:(g + 1) * 512], op=ALU.add), r=[yd, (y, g)], w=[(y, g)])
                            pst = PS[7]
                            fw.op("pe", lambda e: e.matmul(pst[:, :], lhsT=bt[:, g * 128:(g + 1) * 128], rhs=xdte[:, g * 512:(g + 1) * 512], start=True, stop=True),
                                  r=[bt, xdte], w=[pst])
                            Hg = H.ap[:, g, :].rearrange("p (h q) -> p h q", q=64)
                            fw.op("dve", lambda e: e.tensor_tensor(out=Hg, in0=Hg, in1=cdA[:, c, g * 8:(g + 1) * 8].unsqueeze(2).broadcast_to([128, 8, 64]), op=ALU.mult),
                                  r=[(H, g), cdA], w=[(H, g)])
                            fw.op("dve", lambda e: e.tensor_tensor(out=H[:, g, :], in0=pst[:, :], in1=H[:, g, :], op=ALU.add), r=[pst, (H, g)], w=[(H, g)])
                            fw.op("act", lambda e: e.copy(out=Hb[:, g, :], in_=H[:, g, :]), r=[(H, g)], w=[(Hb, g)])

                    items.append((front, mid, back))
            n = len(items)
            for ii in range(n + 2):
                if ii < n:
                    items[ii][0]()
                if 0 <= ii - 1 < n:
                    items[ii - 1][1]()
                if 0 <= ii - 2 < n:
                    items[ii - 2][2]()

        def stageB(c):
            i = c % 3
            x, z = xc[i], zc[i]
            y = ys_[c % 2]
            x3 = x.ap.rearrange("p (h q) -> p h q", q=64)
            fw.op("dve", lambda e: e.tensor_tensor(out=x3, in0=x3, in1=sv[:, 64:96].unsqueeze(2).broadcast_to([128, 32, 64]), op=ALU.mult), r=[x, sv], w=[x])
            fw.op("dve", lambda e: e.tensor_tensor(out=x[:, :], in0=x[:, :], in1=y[:, :], op=ALU.add), r=[y, x], w=[x])
            fw.op("dve", lambda e: e.tensor_tensor(out=y[:, :], in0=x[:, :], in1=z[:, :], op=ALU.mult), r=[x, z], w=[y])
            for g in range(4):
                fw.op("act", lambda e, g=g: e.activation(out=ysq[:, :], in_=y[:, g * 512:(g + 1) * 512], func=AF.Square), r=[y], w=[ysq])
                fw.op("dve", lambda e, g=g: e.tensor_reduce(out=st4[:, g:g + 1], in_=ysq[:, :], axis=AX.X, op=ALU.add), r=[ysq], w=[st4])
            fw.op("act", lambda e: e.activation(out=st4[:, 4:8], in_=st4[:, 0:4], func=AF.Sqrt, bias=self.EPSC[:, 0:1], scale=1.0 / 512), r=[st4, self.EPSC], w=[st4])
            fw.op("dve", lambda e: e.reciprocal(out=st4[:, 4:8], in_=st4[:, 4:8]), r=[st4], w=[st4])
            for g in range(4):
                fw.op("dve", lambda e, g=g: e.scalar_tensor_tensor(out=y[:, g * 512:(g + 1) * 512], in0=y[:, g * 512:(g + 1) * 512], scalar=st4[:, 4 + g:5 + g],
                                                                   in1=snw[:, g * 512:(g + 1) * 512], op0=ALU.mult, op1=ALU.mult), r=[y, st4, snw], w=[y])
            for t4 in range(4):
                b = sbank()
                for jj in range(4):
                    f = t4 * 4 + jj
                    fw.op("pe", lambda e, b=b, jj=jj, f=f: e.transpose(b[:, jj * 128:(jj + 1) * 128], y[:, f * 128:(f + 1) * 128], self.cv("IDENT")), r=[y, C], w=[b])
                fw.op("act", lambda e, b=b, t4=t4: e.copy(out=yT.ap[:, t4 * 4:(t4 + 1) * 4, :].rearrange("p a b -> p (a b)"), in_=b[:, :]), r=[b], w=[yT])
            fw.dma("sp", self.YT.ap[1024:3072, c * 128:(c + 1) * 128].rearrange("(f p) t -> p f t", p=128), yT[:, :, :], r=[yT], w=[(self.YT, ("s", c))])

        loadc(0)
        loadc(1)
        for c in range(16):
            stageA(c)
            if c >= 1:
                stageB(c - 1)
            if c + 2 < 16:
                loadc(c + 2)
        stageB(15)
        fw.release(m)

    def p6_ret(self, l):
        fw = self.fw
        C = self.C
        PS = fw.PS
        m = fw.mark()
        rnw = fw.sb("rnw", [128, 1024])
        fw.dma("sp", rnw[:, :], self.d_rnw.ap[l], r=[self.d_rnw], w=[rnw])
        QT = fw.sb("rQT", [128, 4, NT], BF16)
        KT = fw.sb("rKT", [128, 4, NT], BF16)
        fw.dma("sp", QT[:, :, :], self.RQT.ap.rearrange("(h p) t -> p h t", p=128), r=[self.RQT], w=[QT])
        fw.dma("sp", KT[:, :, :], self.RKT.ap.rearrange("(h p) t -> p h t", p=128), r=[self.RKT], w=[KT])
        QS = fw.sb("rQS", [128, 4, NT], BF16)
        QD = self.cv("QD").rearrange("p (h l) -> p h l", l=128)
        for h in range(4):
            fw.op("dve", lambda e, h=h: e.tensor_tensor(out=QS.ap[:, h, :].rearrange("p (c l) -> p c l", l=128), in0=QT.ap[:, h, :].rearrange("p (c l) -> p c l", l=128),
                                                        in1=QD[:, h, :].unsqueeze(1).broadcast_to([128, 16, 128]), op=ALU.mult), r=[QT, C], w=[QS])
        R = fw.sb("R", [128, 4, 256])
        Rb = fw.sb("Rb", [128, 4, 256], BF16)
        CD = fw.sb("CD", [128, 4])
        fw.op("dve", lambda e: e.memset(R[:, :, :], 0.0), w=[R])
        fw.op("dve", lambda e: e.memset(Rb[:, :, :], 0.0), w=[Rb])
        for h in range(4):
            fw.op("dve", lambda e, h=h: e.memset(CD[:, h:h + 1], self.cdec[h]), w=[CD])
        vc = [fw.sb("vc%d" % i, [128, 1024], BF16) for i in range(3)]
        kc_ = [fw.sb("kc%d" % i, [128, 512], BF16) for i in range(3)]
        gc = [fw.sb("gc%d" % i, [128, 1024]) for i in range(3)]
        kdb = [fw.sb("kd%d" % i, [128, 512], BF16) for i in range(2)]
        sTm = [fw.sb("sTm%d" % i, [128, 4, 128], BF16) for i in range(2)]
        ysb = [fw.sb("ys%d" % i, [128, 4, 256]) for i in range(2)]
        y2 = fw.sb("y2", [128, 4, 256])
        stb = [fw.sb("rst%d" % i, [128, 16]) for i in range(2)]
        yr = fw.sb("yr", [128, 1024])
        yTb = [fw.sb("ryT%d" % i, [128, 8, 128], BF16) for i in range(2)]
        DMT = self.cv("DMT").rearrange("p (h l) -> p h l", l=128)
        KD = self.cv("KD")
        f3 = lambda t: t.ap.rearrange("p a b -> p (a b)")

        def loadc(c):
            i = c % 3
            rows = slice(c * 128, (c + 1) * 128)
            fw.dma("sp", vc[i][:, :], self.RV[rows, :], r=[self.RV], w=[vc[i]])
            fw.dma("sp", kc_[i][:, :], self.RKTOK[rows, :], r=[self.RKTOK], w=[kc_[i]])
            fw.dma("sp", gc[i][:, :], self.RGS[rows, :], r=[self.RGS], w=[gc[i]])

        def stageA(c):
            v, kk = vc[c % 3], kc_[c % 3]
            kd, sm_, ys = kdb[c % 2], sTm[c % 2], ysb[c % 2]
            cs = slice(c * 128, (c + 1) * 128)
            fw.op("dve", lambda e: e.tensor_tensor(out=kd.ap.rearrange("p (h d) -> p h d", d=128), in0=kk.ap.rearrange("p (h d) -> p h d", d=128),
                                                   in1=KD.unsqueeze(2).broadcast_to([128, 4, 128]), op=ALU.mult), r=[kk, C], w=[kd])
            Sb = PS[c % 2]
            for h in range(4):
                fw.op("pe", lambda e, h=h: e.matmul(Sb[:, h * 128:(h + 1) * 128], lhsT=KT[:, h, cs], rhs=QT[:, h, cs], start=True, stop=True), r=[KT, QT], w=[Sb])
            fw.op("dve", lambda e: e.tensor_tensor(out=sm_[:, :, :], in0=Sb.ap.rearrange("p (h l) -> p h l", l=128), in1=DMT, op=ALU.mult), r=[Sb, C], w=[sm_])
            for h in range(4):
                pk = PS[2 + h // 2]
                c0 = (h % 2) * 256
                fw.op("pe", lambda e, h=h, pk=pk, c0=c0: e.matmul(pk[:, c0:c0 + 256], lhsT=kd[:, h * 128:(h + 1) * 128], rhs=v[:, h * 256:(h + 1) * 256], start=True, stop=True),
                      r=[kd, v], w=[pk])
            pyb = [PS[4 + 2 * (c % 2)], PS[5 + 2 * (c % 2)]]
            for h in range(4):
                py = pyb[h // 2]
                c0 = (h % 2) * 256
                fw.op("pe", lambda e, h=h, py=py, c0=c0: e.matmul(py[:, c0:c0 + 256], lhsT=sm_[:, h, :], rhs=v[:, h * 256:(h + 1) * 256], start=True, stop=False, skip_group_check=True),
                      r=[sm_, v], w=[py])
                fw.op("pe", lambda e, h=h, py=py, c0=c0: e.matmul(py[:, c0:c0 + 256], lhsT=QS[:, h, cs], rhs=Rb[:, h, :], start=False, stop=True, skip_group_check=True),
                      r=[QS, Rb], w=[py])
            fw.op("dve", lambda e: e.tensor_tensor(out=R[:, :, :], in0=R[:, :, :], in1=CD[:, :].unsqueeze(2).broadcast_to([128, 4, 256]), op=ALU.mult), r=[R, CD], w=[R])
            for b2 in range(2):
                fw.op("dve", lambda e, b2=b2: e.tensor_tensor(out=f3(R)[:, b2 * 512:(b2 + 1) * 512], in0=PS[2 + b2][:, :], in1=f3(R)[:, b2 * 512:(b2 + 1) * 512], op=ALU.add),
                      r=[PS[2 + b2], R], w=[R])
            fw.op("act", lambda e: e.copy(out=f3(Rb), in_=f3(R)), r=[R], w=[Rb])
            for b2 in range(2):
                fw.op("act", lambda e, b2=b2: e.copy(out=f3(ys)[:, b2 * 512:(b2 + 1) * 512], in_=pyb[b2][:, :]), r=[pyb[b2]], w=[ys])

        def stageB(c):
            ys, s8, gg, yT = ysb[c % 2], stb[c % 2], gc[c % 3], yTb[c % 2]
            bc4 = lambda ap: ap.unsqueeze(2).broadcast_to([128, 4, 256])
            fw.op("dve", lambda e: e.tensor_reduce(out=s8[:, 0:4], in_=ys[:, :, :], axis=AX.X, op=ALU.add), r=[ys], w=[s8])
            fw.op("dve", lambda e: e.tensor_scalar(out=s8[:, 4:8], in0=s8[:, 0:4], scalar1=1.0 / 256, scalar2=None, op0=ALU.mult), r=[s8], w=[s8])
            fw.op("dve", lambda e: e.tensor_tensor(out=ys[:, :, :], in0=ys[:, :, :], in1=bc4(s8[:, 4:8]), op=ALU.subtract), r=[ys, s8], w=[ys])
            fw.op("act", lambda e: e.activation(out=f3(y2), in_=f3(ys), func=AF.Square), r=[ys], w=[y2])
            fw.op("dve", lambda e: e.tensor_reduce(out=s8[:, 8:12], in_=y2[:, :, :], axis=AX.X, op=ALU.add), r=[y2], w=[s8])
            fw.op("act", lambda e: e.activation(out=s8[:, 12:16], in_=s8[:, 8:12], func=AF.Sqrt, bias=self.EPSC[:, 0:1], scale=1.0 / 256), r=[s8, self.EPSC], w=[s8])
            fw.op("dve", lambda e: e.reciprocal(out=s8[:, 12:16], in_=s8[:, 12:16]), r=[s8], w=[s8])
            fw.op("dve", lambda e: e.tensor_tensor(out=ys[:, :, :], in0=ys[:, :, :], in1=bc4(s8[:, 12:16]), op=ALU.mult), r=[ys, s8], w=[ys])
            fw.op("dve", lambda e: e.tensor_tensor(out=f3(ys), in0=f3(ys), in1=rnw[:, :], op=ALU.mult), r=[ys, rnw], w=[ys])
            fw.op("dve", lambda e: e.tensor_tensor(out=yr[:, :], in0=f3(ys), in1=gg[:, :], op=ALU.mult), r=[ys, gg], w=[yr])
            for t4 in range(2):
                b = PS[2 + t4]
                for jj in range(4):
                    f = t4 * 4 + jj
                    fw.op("pe", lambda e, b=b, jj=jj, f=f: e.transpose(b[:, jj * 128:(jj + 1) * 128], yr[:, f * 128:(f + 1) * 128], self.cv("IDENT")), r=[yr, C], w=[b])
                fw.op("act", lambda e, b=b, t4=t4: e.copy(out=yT.ap[:, t4 * 4:(t4 + 1) * 4, :].rearrange("p a b -> p (a b)"), in_=b[:, :]), r=[b], w=[yT])
            fw.dma("sp", self.YT.ap[3072:4096, c * 128:(c + 1) * 128].rearrange("(f p) t -> p f t", p=128), yT[:, :, :], r=[yT], w=[(self.YT, ("r", c))])

        loadc(0)
        loadc(1)
        for c in range(16):
            stageA(c)
            if c >= 1:
                stageB(c - 1)
            if c + 2 < 16:
                loadc(c + 2)
        stageB(15)
        fw.release(m)

    def p7_merge(self, l, src):
        fw = self.fw
        m = fw.mark()
        self.WB = [fw.sb("WB%d" % i, [128, 8192], BF16) for i in range(3)]
        yt = fw.sb("ytall", [128, 32, 1024], BF16)
        mg = [fw.sb("mg%d" % i, [128, 1024], BF16) for i in range(3)]
        mer = fw.sb("mer", [128, 4, 1024])
        merb = fw.sb("merb", [128, 16, 1024], BF16)
        tmp = [fw.sb("mtmp%d" % i, [128, 512]) for i in range(2)]
        xt = [fw.sb("mxt%d" % i, [128, 512]) for i in range(2)]
        branches = ((self.p_nsa, 8, 0, 0), (self.p_ssd, 16, 8, 1), (self.p_ret, 8, 24, 2))
        k = 0
        def load_yt(th):
            t0 = th * 1024
            for k0 in range(0, 32, 8):
                fw.dma("sp", yt[:, k0:k0 + 8, :], self.YT.ap[k0 * 128:(k0 + 8) * 128, t0:t0 + 1024].rearrange("(k p) t -> p k t", p=128), r=[self.YT], w=[(yt, k0)])

        load_yt(0)
        for th in range(2):
            t0 = th * 1024
            for cb in range(4):
                for (W, KC, koff, bi) in branches:
                    buf, view = self.wload(W, W.ap[l][:, cb * 512:(cb + 1) * 512], KC, 512)
                    for j in range(4):
                        ct = cb * 4 + j
                        g = mg[k % 3]
                        k += 1
                        r0 = bi * 2048 + ct * 128
                        fw.dma("sp", g[:, :], self.MGT[r0:r0 + 128, t0:t0 + 1024], r=[self.MGT], w=[g])
                        pss = [self.bank(), self.bank()]
                        for kc in range(KC):
                            for t2 in range(2):
                                fw.op("pe", lambda e, kc=kc, t2=t2: e.matmul(pss[t2][:, :], lhsT=view[:, kc, j * 128:(j + 1) * 128], rhs=yt[:, koff + kc, t2 * 512:(t2 + 1) * 512],
                                                                         start=(kc == 0), stop=(kc == KC - 1)), r=[buf, yt], w=[pss[t2]])
                        for t2 in range(2):
                            ts2 = slice(t2 * 512, (t2 + 1) * 512)
                            if bi == 0:
                                fw.op("dve", lambda e, t2=t2, ts2=ts2: e.tensor_tensor(out=mer[:, j, ts2], in0=pss[t2][:, :], in1=g[:, ts2], op=ALU.mult), r=[pss[t2], g], w=[(mer, (j, t2))])
                            else:
                                t_ = tmp[k % 2]
                                k += 1
                                fw.op("dve", lambda e, t2=t2, ts2=ts2, t_=t_: e.tensor_tensor(out=t_[:, :], in0=pss[t2][:, :], in1=g[:, ts2], op=ALU.mult), r=[pss[t2], g], w=[t_])
                                fw.op("dve", lambda e, t2=t2, ts2=ts2, t_=t_: e.tensor_tensor(out=mer[:, j, ts2], in0=mer[:, j, ts2], in1=t_[:, :], op=ALU.add), r=[t_, (mer, (j, t2))], w=[(mer, (j, t2))])
                for j in range(4):
                    fw.op("act", lambda e, j=j: e.copy(out=merb[:, cb * 4 + j, :], in_=mer[:, j, :]), r=[mer], w=[(merb, cb * 4 + j)])
            if th == 0:
                load_yt(1)
            for cb in range(4):
                buf, view = self.wload(self.w_out, self.w_out.ap[l][:, cb * 512:(cb + 1) * 512], 16, 512)
                for j in range(4):
                    ct = cb * 4 + j
                    pss = [self.bank(), self.bank()]
                    for kc in range(16):
                        for t2 in range(2):
                            fw.op("pe", lambda e, kc=kc, t2=t2: e.matmul(pss[t2][:, :], lhsT=view[:, kc, j * 128:(j + 1) * 128], rhs=merb[:, kc, t2 * 512:(t2 + 1) * 512],
                                                                     start=(kc == 0), stop=(kc == 15)), r=[buf, merb], w=[pss[t2]])
                    for t2 in range(2):
                        ts = slice(t0 + t2 * 512, t0 + (t2 + 1) * 512)
                        x = xt[k % 2]
                        k += 1
                        fw.dma("sp", x[:, :], src[ct * 128:(ct + 1) * 128, ts], r=[(src, (ct, th * 2 + t2))], w=[x])
                        fw.op("dve", lambda e, t2=t2, x=x: e.tensor_tensor(out=x[:, :], in0=pss[t2][:, :], in1=x[:, :], op=ALU.add), r=[pss[t2], x], w=[x])
                        fw.dma("sp", self.XT[ct * 128:(ct + 1) * 128, ts], x[:, :], r=[x], w=[(self.XT, (ct, th * 2 + t2))])
        fw.release(m)

    def p8_ffn(self, l):
        fw = self.fw
        m = fw.mark()
        self.WB = [fw.sb("WB%d" % i, [128, 5632], BF16) for i in range(3)]
        xg = fw.sb("fxg", [128, 16, 256])
        fT = fw.sb("fT", [128, 16, 1024], BF16)
        hT = fw.sb("hT", [128, 44, 1024], BF16)
        sq = [fw.sb("fsq%d" % i, [128, 256]) for i in range(2)]
        rs = fw.sb("frs", [128, 256])
        sg = [fw.sb("fsg%d" % i, [128, 512]) for i in range(2)]
        xo = [fw.sb("fxo%d" % i, [128, 512]) for i in range(2)]
        wcol = self.NRM[:, (2 * l + 1) * 16:(2 * l + 2) * 16]
        xv = self.XT.ap.rearrange("(k p) t -> p k t", p=128)
        k = 0
        for th in range(2):
            t0 = th * 1024
            for sub in range(4):
                ts = slice(t0 + sub * 256, t0 + (sub + 1) * 256)
                for k0 in range(0, 16, 4):
                    fw.dma("sp", xg[:, k0:k0 + 4, :], xv[:, k0:k0 + 4, ts], r=[self.XT], w=[(xg, k0)])
                self.rmsnorm(xg, 256, wcol, lambda kc, sub=sub: (fT, fT[:, kc, sub * 256:(sub + 1) * 256]), sq, rs)
            for cb in range(22):
                bg, vg = self.wload(self.w_gate, self.w_gate.ap[l][:, cb * 256:(cb + 1) * 256], 16, 256)
                bu, vu = self.wload(self.w_up, self.w_up.ap[l][:, cb * 256:(cb + 1) * 256], 16, 256)
                for j in range(2):
                    f = cb * 2 + j
                    pg = [self.bank(), self.bank()]
                    pu = [self.bank(), self.bank()]
                    for kc in range(16):
                        for t2 in range(2):
                            fw.op("pe", lambda e, kc=kc, t2=t2: e.matmul(pg[t2][:, :], lhsT=vg[:, kc, j * 128:(j + 1) * 128], rhs=fT[:, kc, t2 * 512:(t2 + 1) * 512],
                                                                     start=(kc == 0), stop=(kc == 15)), r=[bg, fT], w=[pg[t2]])
                    for kc in range(16):
                        for t2 in range(2):
                            fw.op("pe", lambda e, kc=kc, t2=t2: e.matmul(pu[t2][:, :], lhsT=vu[:, kc, j * 128:(j + 1) * 128], rhs=fT[:, kc, t2 * 512:(t2 + 1) * 512],
                                                                     start=(kc == 0), stop=(kc == 15)), r=[bu, fT], w=[pu[t2]])
                    for t2 in range(2):
                        s_ = sg[k % 2]
                        k += 1
                        fw.op("act", lambda e, t2=t2, s_=s_: e.activation(out=s_[:, :], in_=pg[t2][:, :], func=AF.Silu), r=[pg[t2]], w=[s_])
                        fw.op("dve", lambda e, t2=t2, s_=s_: e.tensor_tensor(out=hT[:, f, t2 * 512:(t2 + 1) * 512], in0=pu[t2][:, :], in1=s_[:, :], op=ALU.mult), r=[pu[t2], s_], w=[(hT, f)])
            for ct in range(16):
                bd, vd = self.wload(self.w_down, self.w_down.ap[l][:, ct * 128:(ct + 1) * 128], 44, 128)
                pss = [self.bank(), self.bank()]
                for kc in range(44):
                    for t2 in range(2):
                        fw.op("pe", lambda e, kc=kc, t2=t2: e.matmul(pss[t2][:, :], lhsT=vd[:, kc, :], rhs=hT[:, kc, t2 * 512:(t2 + 1) * 512], start=(kc == 0), stop=(kc == 43)),
                              r=[bd, hT], w=[pss[t2]])
                for t2 in range(2):
                    ts = slice(t0 + t2 * 512, t0 + (t2 + 1) * 512)
                    x = xo[k % 2]
                    k += 1
                    fw.dma("sp", x[:, :], self.XT[ct * 128:(ct + 1) * 128, ts], r=[(self.XT, (ct, th * 2 + t2))], w=[x])
                    fw.op("dve", lambda e, t2=t2, x=x: e.tensor_tensor(out=x[:, :], in0=pss[t2][:, :], in1=x[:, :], op=ALU.add), r=[pss[t2], x], w=[x])
                    fw.dma("sp", self.XT[ct * 128:(ct + 1) * 128, ts], x[:, :], r=[x], w=[(self.XT, (ct, th * 2 + t2))])
        fw.release(m)

    def final_norm(self, src):
        fw = self.fw
        m = fw.mark()
        xg = fw.sb("nxg", [128, 16, 512])
        og = fw.sb("nog", [128, 16, 512])
        sq = [fw.sb("nsq%d" % i, [128, 512]) for i in range(2)]
        rs = fw.sb("nrs", [128, 512])
        wcol = self.NRM[:, (2 * NL) * 16:(2 * NL + 1) * 16]
        xv = src.ap.rearrange("(k p) t -> p k t", p=128)
        ov = self.outT.ap.rearrange("(k p) t -> p k t", p=128)
        for tg in range(4):
            ts = slice(tg * 512, (tg + 1) * 512)
            for k0 in range(0, 16, 4):
                fw.dma("sp", xg[:, k0:k0 + 4, :], xv[:, k0:k0 + 4, ts], r=[src], w=[(xg, k0)])
            self.rmsnorm(xg, 512, wcol, lambda kc: (og, og[:, kc, :]), sq, rs)
            for k0 in range(0, 16, 4):
                fw.dma("sp", ov[:, k0:k0 + 4, ts], og[:, k0:k0 + 4, :], r=[og], w=[(self.outT, (tg, k0))])
        fw.release(m)


def host_inputs(inp, b):
    consts, tab, et, _ = host_consts()
    f = np.float32
    nrm = np.zeros((128, (2 * NL + 1) * 16), f)
    for l in range(NL):
        nrm[:, (2 * l) * 16:(2 * l + 1) * 16] = np.asarray(inp["norm_mix"][l], f).reshape(16, 128).T
        nrm[:, (2 * l + 1) * 16:(2 * l + 2) * 16] = np.asarray(inp["norm_ffn"][l], f).reshape(16, 128).T
    nrm[:, 2 * NL * 16:] = np.asarray(inp["norm_final"], f).reshape(16, 128).T
    cw = np.zeros((NL, 128, 24, 5), f)
    for l in range(NL):
        cw[l, :, :, 0:4] = np.asarray(inp["conv_w"][l], f).T.reshape(24, 128, 4).transpose(1, 0, 2)
        cw[l, :, :, 4] = np.asarray(inp["conv_b"][l], f).reshape(24, 128).T
    ssdv = np.zeros((NL, 128, 96), f)
    ssdv[:, :, 0:32] = np.asarray(inp["dt_bias"], f)[:, None, :]
    ssdv[:, :, 32:64] = np.asarray(inp["a_log"], f)[:, None, :]
    ssdv[:, :, 64:96] = np.asarray(inp["d_skip"], f)[:, None, :]
    snw = np.ascontiguousarray(np.broadcast_to(np.asarray(inp["ssd_norm"], f)[:, None, :], (NL, 128, 2048)))
    rnw = np.ascontiguousarray(np.broadcast_to(np.asarray(inp["ret_norm"], f).reshape(NL, 1, 1024), (NL, 128, 1024)))
    pet = np.stack([np.asarray(inp["cmp_k_pe"], f).transpose(0, 2, 1), np.asarray(inp["cmp_v_pe"], f).transpose(0, 2, 1)], axis=1)
    d = {
        "xT": np.ascontiguousarray(np.asarray(inp["x"][b], f).T),
        "consts": consts, "tab": tab, "et": et, "nrm": nrm, "cw": cw.reshape(NL, 128, 120), "ssdv": ssdv,
        "snw": snw, "rnw": rnw, "pet": np.ascontiguousarray(pet),
    }
    for k in ("w_in", "cmp_k_w1", "cmp_v_w1", "cmp_k_w2", "cmp_v_w2", "p_nsa", "p_ssd", "p_ret", "w_out", "w_gate", "w_up", "w_down"):
        d[k] = np.ascontiguousarray(np.asarray(inp[k], f))
    return d


_PROG = {}


def kernel(**inputs):
    if "p" not in _PROG:
        _PROG["p"] = Prog()
    prog = _PROG["p"]
    base = host_inputs(inputs, 0)
    zero = {k: np.zeros_like(v) for k, v in base.items()}
    work = {0: 0, 1: 1, 4: 2, 5: 3}
    in_maps = []
    for c in range(8):
        if c in work:
            d = dict(base)
            d["xT"] = np.ascontiguousarray(np.asarray(inputs["x"][work[c]], np.float32).T)
        else:
            d = zero
        in_maps.append(d)
    res = run_bass_kernel_spmd(prog.nc, in_maps, core_ids=list(range(8)))
    inv = {b: c for c, b in work.items()}
    out = np.stack([np.asarray(res.results[inv[b]]["outT"], np.float32).T for b in range(4)], axis=0)
    return np.ascontiguousarray(out)
```

```python
import numpy as np
import concourse.bass as bass
import concourse.mybir as mybir
from concourse.bass_utils import run_bass_kernel_spmd
from contextlib import ExitStack

F32 = mybir.dt.float32
BF16 = mybir.dt.bfloat16
ALU = mybir.AluOpType
AF = mybir.ActivationFunctionType
AX = mybir.AxisListType

COMPUTE = ("pe", "dve", "act", "pool")
NSLOT = {"sp": 16, "pool": 16, "act": 8}
ARENA = 53000

NT = 2048
D = 2048
DIN = 16952
DFF = 5632
NL = 4
C_Q, C_KV, C_NG, C_Z, C_XBC, C_DT, C_RQ, C_RK, C_RV, C_RG, C_MG = 0, 1024, 2560, 2584, 4632, 7704, 7736, 8248, 8760, 9784, 10808
NEG = -30000.0
EPS = 1e-6


class St:
    __slots__ = ("lw", "rd")

    def __init__(self, o=None):
        self.lw = o.lw if o else None
        self.rd = list(o.rd) if o else []


class T:
    def __init__(self, ap, name, excl=False):
        self.ap = ap
        self.name = name
        self.excl = excl
        self.st = {None: St()}

    def __getitem__(self, k):
        return self.ap[k]

    def states(self, key):
        if key is None:
            return list(self.st.values())
        if key not in self.st:
            self.st[key] = St(self.st[None])
        return [self.st[key]]


class _Rec:
    def __getattr__(self, name):
        return lambda *a, **k: (name, a, k)


_REC = _Rec()


class FW:
    def __init__(self, nc):
        self.nc = nc
        self.stream = {e: [] for e in ("pe", "dve", "act", "pool", "sp")}
        self.nops = {e: 0 for e in COMPUTE}
        self.sig = {e: set() for e in COMPUTE}
        self.ndma = {q: 0 for q in NSLOT}
        self.known = {e: {} for e in self.stream}
        self.es = ExitStack()
        self.sems = {}
        self.dsems = {}
        self.arena = self.es.enter_context(nc.sbuf_tensor("arena", [128, ARENA], F32))
        self.aoff = 0
        self.PS = [T(self.es.enter_context(nc.psum_tensor("ps%d" % i, [128, 512], F32))[:, :], "ps%d" % i, excl=True)
                   for i in range(8)]

    def sb(self, name, shape, dtype=F32):
        p = shape[0]
        n = int(np.prod(shape[1:]))
        nb = n * (4 if dtype == F32 else 2)
        nf = ((nb + 63) // 64) * 16
        assert self.aoff + nf <= ARENA, ("SBUF arena overflow", name, self.aoff, nf)
        ap = self.arena[0:p, self.aoff:self.aoff + nf]
        self.aoff += nf
        if dtype != F32:
            ap = ap.bitcast(dtype)
        ap = ap[:, 0:n]
        if len(shape) == 3:
            ap = ap.rearrange("p (a b) -> p a b", a=shape[1], b=shape[2])
        elif len(shape) == 4:
            ap = ap.rearrange("p (a b c) -> p a b c", a=shape[1], b=shape[2], c=shape[3])
        return T(ap, name)

    def mark(self):
        return self.aoff

    def release(self, m):
        self.barrier()
        self.aoff = m

    def dram(self, name, shape, dtype, kind="Internal"):
        h = self.nc.dram_tensor(name, list(shape), dtype, kind=kind)
        return T(h.ap(), name)

    def _need(self, eng, ev, waits):
        if ev is None:
            return
        if ev[0] == "c":
            _, f, idx = ev
            if f == eng and eng == "pe":
                return
            k = ("c", f)
            if self.known[eng].get(k, 0) >= idx:
                return
            self.known[eng][k] = idx
            self.sig[f].add(idx)
            waits.append(ev)
        else:
            _, q, j = ev
            k = ("d", q, j % NSLOT[q])
            if self.known[eng].get(k, -1) >= j:
                return
            self.known[eng][k] = j
            waits.append(ev)

    def _deps(self, eng, r, w, is_dma):
        waits = []
        rs, ws = [], []
        for x in r:
            t, key = x if isinstance(x, tuple) else (x, None)
            (ws if t.excl else rs).extend(t.states(key))
        for x in w:
            t, key = x if isinstance(x, tuple) else (x, None)
            ws.extend(t.states(key))
        for s in rs:
            self._need(eng, s.lw, waits)
        for s in ws:
            lw = s.lw
            if lw is not None and not ((not is_dma) and lw[0] == "c" and lw[1] == eng):
                self._need(eng, lw, waits)
            for ev in s.rd:
                if (not is_dma) and ev[0] == "c" and ev[1] == eng:
                    continue
                self._need(eng, ev, waits)
        return waits, rs, ws

    def _commit(self, ev, rs, ws):
        for s in rs:
            if ev[0] == "c":
                s.rd = [e for e in s.rd if not (e[0] == "c" and e[1] == ev[1])]
            s.rd.append(ev)
        for s in ws:
            s.lw = ev
            s.rd = []

    def op(self, eng, fn, r=(), w=()):
        waits, rs, ws = self._deps(eng, r, w, False)
        for ev in waits:
            self.stream[eng].append(("wait", ev))
        self.nops[eng] += 1
        idx = self.nops[eng]
        self.stream[eng].append(("op", fn(_REC), idx))
        self._commit(("c", eng, idx), rs, ws)

    def dma(self, q, out_ap, in_ap, r=(), w=()):
        waits, rs, ws = self._deps(q, r, w, True)
        j = self.ndma[q]
        K = NSLOT[q]
        if j >= K:
            self._need(q, ("d", q, j - K), waits)
        for ev in waits:
            self.stream[q].append(("wait", ev))
        self.ndma[q] += 1
        self.stream[q].append(("dma", out_ap, in_ap, j))
        self._commit(("d", q, j), rs, ws)

    def barrier(self):
        for e in self.stream:
            waits = []
            for f in COMPUTE:
                if self.nops[f] > 0:
                    self._need(e, ("c", f, self.nops[f]), waits)
            for q in NSLOT:
                for j in range(max(0, self.ndma[q] - NSLOT[q]), self.ndma[q]):
                    self._need(e, ("d", q, j), waits)
            for ev in waits:
                self.stream[e].append(("wait", ev))

    def emit(self):
        nc = self.nc
        self.barrier()
        es = self.es
        for e in COMPUTE:
            self.sems[e] = es.enter_context(nc.semaphore("s_" + e))
        for q in NSLOT:
            self.dsems[q] = [es.enter_context(nc.semaphore("d_%s_%d" % (q, i))) for i in range(NSLOT[q])]
        cum = {}
        for e in COMPUTE:
            c = 0
            m = {}
            for i in range(1, self.nops[e] + 1):
                if i in self.sig[e]:
                    c += 1
                    m[i] = c
            cum[e] = m

        def replay(ename, eng):
            for rec in self.stream[ename]:
                if rec[0] == "wait":
                    ev = rec[1]
                    if ev[0] == "c":
                        eng.wait_ge(self.sems[ev[1]], cum[ev[1]][ev[2]])
                    else:
                        _, q, j = ev
                        eng.wait_ge(self.dsems[q][j % NSLOT[q]], 16 * (j // NSLOT[q] + 1))
                elif rec[0] == "op":
                    nm, ar, kw = rec[1]
                    ins = getattr(eng, nm)(*ar, **kw)
                    if rec[2] in self.sig[ename]:
                        ins.then_inc(self.sems[ename], 1)
                elif rec[0] == "cc":
                    _, kind, i, o, rg, j = rec
                    eng.collective_compute(kind, ALU.bypass, replica_groups=rg, ins=[i], outs=[o]).then_inc(self.dsems[ename][j % NSLOT[ename]], 16)
                else:
                    _, o, i, j = rec
                    eng.dma_start(out=o, in_=i).then_inc(self.dsems[ename][j % NSLOT[ename]], 16)

        with nc.Block() as block:
            @block.tensor
            def _(e):
                replay("pe", e)

            @block.vector
            def _(e):
                replay("dve", e)

            @block.scalar
            def _(e):
                replay("act", e)

            @block.gpsimd
            def _(e):
                replay("pool", e)

            @block.sync
            def _(e):
                replay("sp", e)
        es.close()


CO = {}
_off = 0
for _n, _w in (("IDENT", 128), ("CAUS", 128), ("TGT", 128), ("ONESM", 128), ("ONES1", 128), ("MSK", 1536),
               ("OVL", 33), ("DMT", 512), ("QD", 512), ("KD", 4)):
    CO[_n] = (_off, _w)
    _off += _w
NCONST = _off
TABW = 896 + 512 + 1408 + 2048
TO_U, TO_W, TO_UW, TO_C = 0, 896, 1408, 2816


def host_consts():
    c = np.zeros((128, NCONST), np.float64)
    p = np.arange(128)[:, None]
    f = np.arange(128)[None, :]
    c[:, CO["IDENT"][0]:][:, :128] = (p == f)
    c[:, CO["CAUS"][0]:][:, :128] = (f >= p)
    c[:, CO["TGT"][0]:][:, :128] = (p > f)
    c[:, CO["ONESM"][0]:][:, :128] = 1.0 / D
    c[:, CO["ONES1"][0]:][:, :128] = 1.0
    msk = np.zeros((128, 3, 16, 32))
    for Q in range(16):
        tq = Q * 128 + np.arange(128)
        cur = (tq // 64)[:, None]
        blk = np.arange(32)[None, :]
        msk[:, 0, Q, :] = ((blk > 0) & (blk < cur))
        msk[:, 1, Q, :] = np.where((blk == cur) | (blk == 0), 1e9, np.where(blk > cur, -1e30, 0.0))
        msk[:, 2, Q, :] = (blk <= cur)
    c[:, CO["MSK"][0]:][:, :1536] = msk.reshape(128, -1)
    n = np.arange(128)[:, None]
    k = np.arange(32)[None, :]
    ovl = ((16 * n < 64 * k + 64) & (16 * n + 31 >= 64 * k) & (n < 127)).astype(np.float64)
    c[:, CO["OVL"][0]] = 1.0
    c[:, CO["OVL"][0] + 1:][:, :32] = ovl
    h = np.arange(4, dtype=np.float64)
    log_g = np.log1p(-np.exp2(-5.0 - h))
    s_ = np.arange(128)[:, None]
    l_ = np.arange(128)[None, :]
    for hh in range(4):
        dm = np.where(l_ >= s_, np.exp((l_ - s_) * log_g[hh]), 0.0)
        c[:, CO["DMT"][0] + hh * 128:][:, :128] = dm
        c[:, CO["QD"][0] + hh * 128:][:, :128] = np.exp((np.arange(128) + 1.0) * log_g[hh])[None, :]
        c[:, CO["KD"][0] + hh] = np.exp((127.0 - np.arange(128)) * log_g[hh])
    cdec = [float(np.exp(128 * log_g[hh])) for hh in range(4)]
    tab = np.zeros((8, 128, TABW), np.float64)
    ki = np.arange(128)[:, None]
    for hd in range(8):
        s = 2.0 ** (-(hd + 1))
        cc = np.arange(896)[None, :]
        rel = cc - 384 - ki
        tab[hd, :, TO_U:TO_U + 896] = np.where(rel >= 0, -s * rel, NEG)
        qi = np.arange(512)[None, :]
        tab[hd, :, TO_W:TO_W + 512] = -s * (qi - ki)
        cc = np.arange(1408)[None, :]
        rel = cc - 384 - ki
        tab[hd, :, TO_UW:TO_UW + 1408] = np.where((rel >= 0) & (rel < 512), -s * rel, NEG)
        tq = np.arange(2048)[None, :]
        rel = tq - (16 * ki + 31)
        tab[hd, :, TO_C:TO_C + 2048] = np.where((rel >= 0) & (ki < 127), -s * rel, NEG)
    et = (np.arange(2048)[None, :] // 64 == np.arange(32)[:, None]).astype(np.float32)
    return c.astype(np.float32), tab.astype(np.float32), et, cdec


class Prog:
    def __init__(self, n_layers=NL, dbg=False, stop_after=None):
        self.nl = n_layers
        self.dbg = dbg
        self.stop_after = stop_after
        self.nc = bass.Bass("TRN2", target_bir_lowering=False)
        self.fw = FW(self.nc)
        self.cdec = host_consts()[3]
        self.rot = 0
        self.build()

    def bank(self):
        b = self.fw.PS[self.rot % 8]
        self.rot += 1
        return b

    def din(self, name, shape, dtype=F32):
        return self.fw.dram(name, shape, dtype, kind="ExternalInput")

    def scr(self, name, shape, dtype):
        return self.fw.dram(name, shape, dtype, kind="ExternalOutput" if self.dbg else "Internal")

    def wload(self, Wt, w_ap, KC, n):
        fw = self.fw
        buf = self.WB[self.wrot % len(self.WB)]
        self.wrot += 1
        view = buf.ap[:, 0:KC * n].rearrange("p (k c) -> p k c", k=KC, c=n)
        src = w_ap.rearrange("(k p) c -> p k c", p=128)
        step = 4
        for i, k0 in enumerate(range(0, KC, step)):
            k1 = min(KC, k0 + step)
            fw.dma("pool", view[:, k0:k1, :], src[:, k0:k1, :], r=[Wt], w=[(buf, i)])
        return buf, view

    def rmsnorm(self, xg, ntok, wcol, out_fn, sq, rs):
        fw = self.fw
        C = self.C
        ps = self.bank()
        for kc in range(16):
            s = sq[kc % 2]
            fw.op("act", lambda e, s=s, kc=kc: e.activation(out=s[:, 0:ntok], in_=xg[:, kc, :], func=AF.Square), r=[xg], w=[s])
            fw.op("pe", lambda e, s=s, kc=kc: e.matmul(ps[:, 0:ntok], lhsT=self.cv("ONESM"), rhs=s[:, 0:ntok], start=(kc == 0), stop=(kc == 15)),
                  r=[s, C], w=[ps])
        fw.op("act", lambda e: e.activation(out=rs[:, 0:ntok], in_=ps[:, 0:ntok], func=AF.Sqrt, bias=self.EPSC[:, 0:1], scale=1.0), r=[ps, self.EPSC], w=[rs])
        fw.op("dve", lambda e: e.reciprocal(out=rs[:, 0:ntok], in_=rs[:, 0:ntok]), r=[rs], w=[rs])
        for kc in range(16):
            ot, oap = out_fn(kc)
            fw.op("dve", lambda e, kc=kc, oap=oap: e.scalar_tensor_tensor(out=oap, in0=xg[:, kc, :], scalar=wcol[:, kc:kc + 1], in1=rs[:, 0:ntok],
                                                                         op0=ALU.mult, op1=ALU.mult), r=[xg, rs, self.NRM], w=[ot])

    def cv(self, name, width=None):
        o, w = CO[name]
        return self.C[:, o:o + (width or w)]

    def cvt(self, name):
        if name == "EPSC":
            return self.EPSC
        raise KeyError(name)

    def build(self):
        fw = self.fw
        nl = self.nl
        self.xT_in = self.din("xT", [D, NT])
        self.d_consts = self.din("consts", [128, NCONST])
        self.d_tab = self.din("tab", [8, 128, TABW])
        self.d_et = self.din("et", [32, NT])
        self.d_nrm = self.din("nrm", [128, (2 * NL + 1) * 16])
        self.d_cw = self.din("cw", [NL, 128, 24 * 5])
        self.d_ssdv = self.din("ssdv", [NL, 128, 96])
        self.d_snw = self.din("snw", [NL, 128, 2048])
        self.d_rnw = self.din("rnw", [NL, 128, 1024])
        self.d_pet = self.din("pet", [NL, 2, 128, 32])
        self.w_in = self.din("w_in", [NL, D, DIN])
        self.cw1 = [self.din("cmp_k_w1", [NL, 4096, 128]), self.din("cmp_v_w1", [NL, 4096, 128])]
        self.cw2 = [self.din("cmp_k_w2", [NL, 128, 128]), self.din("cmp_v_w2", [NL, 128, 128])]
        self.p_nsa = self.din("p_nsa", [NL, 1024, D])
        self.p_ssd = self.din("p_ssd", [NL, 2048, D])
        self.p_ret = self.din("p_ret", [NL, 1024, D])
        self.w_out = self.din("w_out", [NL, D, D])
        self.w_gate = self.din("w_gate", [NL, D, DFF])
        self.w_up = self.din("w_up", [NL, D, DFF])
        self.w_down = self.din("w_down", [NL, DFF, D])
        self.outT = fw.dram("outT", [D, NT], F32, kind="ExternalOutput")
        self.XT = self.scr("XT", [D, NT], F32)
        self.QT = self.scr("QT", [1024, NT], BF16)
        self.KVT = self.scr("KVT", [1024, NT], BF16)
        self.VTOK = self.scr("VTOK", [NT, 512], BF16)
        self.NG = self.scr("NG", [NT, 24], F32)
        self.ZS = self.scr("ZS", [NT, 2048], F32)
        self.XBCT = self.scr("XBCT", [3072, NT], F32)
        self.DTR = self.scr("DTR", [NT, 32], F32)
        self.RQT = self.scr("RQT", [512, NT], BF16)
        self.RKT = self.scr("RKT", [512, NT], BF16)
        self.RKTOK = self.scr("RKTOK", [NT, 512], BF16)
        self.RV = self.scr("RV", [NT, 1024], BF16)
        self.RGS = self.scr("RGS", [NT, 1024], F32)
        self.MGT = self.scr("MGT", [6144, NT], BF16)
        self.XTOK = self.scr("XTOK", [NT, 2048], F32)
        self.BCT = self.scr("BCT", [1024, NT], BF16)
        self.BTOK = self.scr("BTOK", [NT, 512], BF16)
        self.YT = self.scr("YT", [4096, NT], BF16)

        self.C = fw.sb("C", [128, NCONST])
        self.NRM = fw.sb("NRM", [128, (2 * NL + 1) * 16])
        self.EPSC = fw.sb("EPSC", [128, 2])
        fw.dma("sp", self.C[:, :], self.d_consts[:, :], r=[self.d_consts], w=[self.C])
        fw.dma("sp", self.NRM[:, :], self.d_nrm[:, :], r=[self.d_nrm], w=[self.NRM])
        fw.op("dve", lambda e: e.memset(self.EPSC[:, 0:1], EPS), w=[self.EPSC])
        fw.op("dve", lambda e: e.memset(self.EPSC[:, 1:2], 1.0), w=[self.EPSC])
        self.wrot = 0
        self.base = fw.mark()

        src = self.xT_in
        for l in range(nl):
            self.layer(l, src)
            src = self.XT
            if self.stop_after is not None:
                break
        if self.stop_after is None:
            self.final_norm(src)
        fw.emit()

    def layer(self, l, src):
        sa = self.stop_after
        self.p1_inproj(l, src)
        if sa == "p1":
            return
        self.p2_ssdprep(l)
        if sa == "p2":
            return
        self.p34_nsa(l)
        if sa == "p4":
            return
        self.p5_ssd(l)
        if sa == "p5":
            return
        self.p6_ret(l)
        if sa == "p6":
            return
        self.p7_merge(l, src)
        if sa == "p7":
            return
        self.p8_ffn(l)

    def p1_inproj(self, l, src):
        fw = self.fw
        m = fw.mark()
        self.WB = [fw.sb("WB%d" % i, [128, 8192], BF16) for i in range(3)]
        uT = fw.sb("uT", [128, 16, NT], BF16)
        stF = [fw.sb("stF%d" % i, [128, NT], F32) for i in range(2)]
        stFb = [fw.sb("stFb%d" % i, [128, NT], BF16) for i in range(2)]
        stT = [fw.sb("stT%d" % i, [128, 512], F32) for i in range(2)]
        stTb = [fw.sb("stTb%d" % i, [128, 512], BF16) for i in range(2)]
        m2 = fw.mark()
        xg = [fw.sb("xg%d" % i, [128, 16, 256], F32) for i in range(2)]
        sq = [fw.sb("sq%d" % i, [128, 512], F32) for i in range(2)]
        rs = fw.sb("rs", [128, 512], F32)
        wcol = self.NRM[:, (2 * l) * 16:(2 * l + 1) * 16]
        srcv = src.ap.rearrange("(k p) t -> p k t", p=128)
        for tg in range(8):
            x = xg[tg % 2]
            for k0 in range(0, 16, 4):
                fw.dma("sp", x[:, k0:k0 + 4, :], srcv[:, k0:k0 + 4, tg * 256:(tg + 1) * 256], r=[src], w=[(x, k0)])
            self.rmsnorm(x, 256, wcol, lambda kc, tg=tg: (uT, uT[:, kc, tg * 256:(tg + 1) * 256]), sq, rs)
        fw.aoff = m2
        fw.barrier()

        W = self.w_in
        wl = W.ap[l]
        cnt = [0]

        def fm(c0, c1, dst, r0, func, scale, bf):
            c = c0
            while c < c1:
                nblk = min(512, c1 - c)
                buf, view = self.wload(W, wl[:, c:c + nblk], 16, nblk)
                for j0 in range(0, nblk, 128):
                    n = min(128, nblk - j0)
                    banks = [self.bank() for _ in range(4)]
                    for kc in range(16):
                        for tg in range(4):
                            fw.op("pe", lambda e, kc=kc, tg=tg, b=banks[tg], n=n, j0=j0, view=view: e.matmul(
                                b[0:n, :], lhsT=view[:, kc, j0:j0 + n], rhs=uT[:, kc, tg * 512:(tg + 1) * 512],
                                start=(kc == 0), stop=(kc == 15)), r=[buf, uT], w=[banks[tg]])
                    i = cnt[0] % 2
                    cnt[0] += 1
                    st = stFb[i] if bf else stF[i]
                    for tg in range(4):
                        fw.op("act", lambda e, tg=tg, b=banks[tg], n=n, st=st: e.activation(
                            out=st[0:n, tg * 512:(tg + 1) * 512], in_=b[0:n, :], func=func, scale=scale), r=[banks[tg]], w=[(st, tg)])
                    rr = r0 + (c - c0) + j0
                    fw.dma("sp", dst[rr:rr + n, :], st[0:n, :], r=[st], w=[(dst, rr)])
                c += nblk

        def tm(c0, c1, dst, d0, func, bf):
            c = c0
            while c < c1:
                nblk = min(512, c1 - c)
                buf, view = self.wload(W, wl[:, c:c + nblk], 16, nblk)
                for tt in range(16):
                    b = self.bank()
                    for kc in range(16):
                        fw.op("pe", lambda e, kc=kc, tt=tt, b=b, nblk=nblk, view=view: e.matmul(
                            b[:, 0:nblk], lhsT=uT[:, kc, tt * 128:(tt + 1) * 128], rhs=view[:, kc, 0:nblk],
                            start=(kc == 0), stop=(kc == 15)), r=[buf, uT], w=[b])
                    i = cnt[0] % 2
                    cnt[0] += 1
                    st = stTb[i] if bf else stT[i]
                    fw.op("act", lambda e, b=b, nblk=nblk, st=st: e.activation(out=st[:, 0:nblk], in_=b[:, 0:nblk], func=func), r=[b], w=[st])
                    dd = d0 + (c - c0)
                    fw.dma("sp", dst[tt * 128:(tt + 1) * 128, dd:dd + nblk], st[:, 0:nblk], r=[st], w=[(dst, (tt, dd))])
                c += nblk

        ID, SIG, SILU = AF.Identity, AF.Sigmoid, AF.Silu
        sc = 128.0 ** -0.5
        fm(C_Q, C_Q + 1024, self.QT, 0, ID, sc, True)
        fm(C_KV, C_KV + 768, self.KVT, 0, ID, 1.0, True)
        tm(C_KV + 768, C_KV + 1024, self.VTOK, 0, ID, True)
        fm(C_KV + 1024, C_KV + 1280, self.KVT, 768, ID, 1.0, True)
        tm(C_KV + 1280, C_KV + 1536, self.VTOK, 256, ID, True)
        tm(C_NG, C_NG + 24, self.NG, 0, SIG, False)
        tm(C_Z, C_Z + 2048, self.ZS, 0, SILU, False)
        fm(C_XBC, C_XBC + 3072, self.XBCT, 0, ID, 1.0, False)
        tm(C_DT, C_DT + 32, self.DTR, 0, ID, False)
        fm(C_RQ, C_RQ + 512, self.RQT, 0, ID, sc, True)
        fm(C_RK, C_RK + 512, self.RKT, 0, ID, 1.0, True)
        tm(C_RK, C_RK + 512, self.RKTOK, 0, ID, True)
        tm(C_RV, C_RV + 1024, self.RV, 0, ID, True)
        tm(C_RG, C_RG + 1024, self.RGS, 0, SILU, False)
        fm(C_MG, C_MG + 6144, self.MGT, 0, SIG, 1.0, True)
        fw.release(m)

    def p2_ssdprep(self, l):
        fw = self.fw
        m = fw.mark()
        cw = fw.sb("cw", [128, 24, 5])
        fw.dma("sp", cw[:, :, :], self.d_cw.ap[l].rearrange("p (t k) -> p t k", k=5), r=[self.d_cw], w=[cw])
        xp = [fw.sb("xp%d" % i, [128, 3 + NT]) for i in range(2)]
        acc = [fw.sb("acc%d" % i, [128, NT]) for i in range(2)]
        xo = [fw.sb("xo%d" % i, [128, NT]) for i in range(2)]
        xb = [fw.sb("xob%d" % i, [128, NT], BF16) for i in range(2)]
        stt = [fw.sb("sttok%d" % i, [128, 4, 128]) for i in range(2)]
        sttb = [fw.sb("sttokb%d" % i, [128, 4, 128], BF16) for i in range(2)]
        for i in range(2):
            fw.op("dve", lambda e, i=i: e.memset(xp[i][:, 0:3], 0.0), w=[xp[i]])
        k = 0
        fw.dma("sp", xp[0][:, 3:3 + NT], self.XBCT[0:128, :], r=[self.XBCT], w=[xp[0]])
        for ct in range(24):
            i = ct % 2
            if ct + 1 < 24:
                fw.dma("sp", xp[1 - i][:, 3:3 + NT], self.XBCT[(ct + 1) * 128:(ct + 2) * 128, :], r=[self.XBCT], w=[xp[1 - i]])
            a = acc[i]
            fw.op("dve", lambda e, i=i, ct=ct, a=a: e.tensor_scalar(out=a[:, :], in0=xp[i][:, 0:NT], scalar1=cw[:, ct, 0:1], scalar2=None, op0=ALU.mult),
                  r=[xp[i], cw], w=[a])
            for kk in range(1, 4):
                fw.op("dve", lambda e, i=i, ct=ct, a=a, kk=kk: e.scalar_tensor_tensor(out=a[:, :], in0=xp[i][:, kk:kk + NT], scalar=cw[:, ct, kk:kk + 1],
                                                                                      in1=a[:, :], op0=ALU.mult, op1=ALU.add), r=[xp[i], cw, a], w=[a])
            o = xo[i]
            fw.op("act", lambda e, a=a, o=o, ct=ct: e.activation(out=o[:, :], in_=a[:, :], func=AF.Silu, bias=cw[:, ct, 4:5], scale=1.0), r=[a, cw], w=[o])
            if ct >= 16:
                ob = xb[i]
                fw.op("pool", lambda e, o=o, ob=ob: e.tensor_copy(out=ob[:, :], in_=o[:, :]), r=[o], w=[ob])
                r0 = (ct - 16) * 128
                fw.dma("sp", self.BCT[r0:r0 + 128, :], ob[:, :], r=[ob], w=[(self.BCT, r0)])
            if ct < 20:
                for t4 in range(4):
                    b = self.bank()
                    for j in range(4):
                        tt = t4 * 4 + j
                        fw.op("pe", lambda e, b=b, j=j, tt=tt, o=o: e.transpose(b[:, j * 128:(j + 1) * 128], o[:, tt * 128:(tt + 1) * 128], self.cv("IDENT")),
                              r=[o, self.C], w=[b])
                    if ct < 16:
                        s = stt[k % 2]
                        dst = self.XTOK.ap[t4 * 512:(t4 + 1) * 512, ct * 128:(ct + 1) * 128]
                        dt_ = self.XTOK
                    else:
                        s = sttb[k % 2]
                        cc = (ct - 16) * 128
                        dst = self.BTOK.ap[t4 * 512:(t4 + 1) * 512, cc:cc + 128]
                        dt_ = self.BTOK
                    k += 1
                    fw.op("act", lambda e, b=b, s=s: e.copy(out=s.ap.rearrange("p a b -> p (a b)"), in_=b[:, :]), r=[b], w=[s])
                    fw.dma("sp", dst.rearrange("(a p) c -> p a c", p=128), s[:, :, :], r=[s], w=[(dt_, (t4, ct))])
        fw.release(m)

    def p34_nsa(self, l):
        fw = self.fw
        C = self.C
        m = fw.mark()
        KCT = [fw.sb("KCT%d" % g, [128, 128], BF16) for g in range(2)]
        VCX = [fw.sb("VCX%d" % g, [128, 161]) for g in range(2)]
        m3 = fw.mark()
        w1s = fw.sb("w1s", [128, 32, 128], BF16)
        w2s = fw.sb("w2s", [128, 128], BF16)
        pet = fw.sb("pet", [128, 32], BF16)
        kcT = [fw.sb("kcT%d" % i, [128, NT], BF16) for i in range(2)]
        pb = fw.sb("pb", [128, 1])
        hx = fw.sb("hx", [128, 128])
        h2 = fw.sb("h2", [128, 128])
        G = fw.sb("G", [128, 128], BF16)
        for t in range(2):
            W1 = self.cw1[t]
            w1v = W1.ap[l].rearrange("(l d) o -> d l o", d=128)
            for l0 in range(0, 32, 8):
                fw.dma("pool", w1s[:, l0:l0 + 8, :], w1v[:, l0:l0 + 8, :], r=[W1], w=[(w1s, l0)])
            fw.dma("pool", w2s[:, :], self.cw2[t].ap[l], r=[self.cw2[t]], w=[w2s])
            fw.dma("pool", pet[:, :], self.d_pet.ap[l, t], r=[self.d_pet], w=[pet])
            for g in range(2):
                src = kcT[g]
                r0 = t * 256 + g * 128
                fw.dma("sp", src[:, :], self.KVT[r0:r0 + 128, :], r=[self.KVT], w=[src])
                sv = src.ap.rearrange("p (n s) -> p n s", s=16)
                ps = self.bank()
                for li in range(32):
                    rhs = sv[:, 0:127, li] if li < 16 else sv[:, 1:128, li - 16]
                    fw.op("pe", lambda e, li=li, rhs=rhs, ps=ps: e.matmul(ps[:, 0:127], lhsT=w1s[:, li, :], rhs=rhs, start=(li == 0), stop=(li == 31)),
                          r=[w1s, src], w=[ps])
                ps2 = self.bank()
                for li in range(32):
                    fw.op("pe", lambda e, li=li, ps2=ps2: e.matmul(ps2[:, 0:1], lhsT=w1s[:, li, :], rhs=pet[:, li:li + 1], start=(li == 0), stop=(li == 31)),
                          r=[w1s, pet], w=[ps2])
                fw.op("act", lambda e, ps2=ps2: e.copy(out=pb[:, :], in_=ps2[:, 0:1]), r=[ps2], w=[pb])
                fw.op("dve", lambda e: e.memset(hx[:, 127:128], 0.0), w=[hx])
                fw.op("act", lambda e, ps=ps: e.activation(out=hx[:, 0:127], in_=ps[:, 0:127], func=AF.Identity, bias=pb[:, 0:1], scale=1.0), r=[ps, pb], w=[hx])
                fw.op("dve", lambda e: e.tensor_tensor(out=h2[:, :], in0=hx[:, :], in1=hx[:, :], op=ALU.mult), r=[hx], w=[h2])
                fw.op("dve", lambda e: e.tensor_scalar(out=h2[:, :], in0=h2[:, :], scalar1=0.044715, scalar2=1.0, op0=ALU.mult, op1=ALU.add), r=[h2], w=[h2])
                fw.op("dve", lambda e: e.tensor_tensor(out=h2[:, :], in0=h2[:, :], in1=hx[:, :], op=ALU.mult), r=[h2, hx], w=[h2])
                fw.op("act", lambda e: e.activation(out=h2[:, :], in_=h2[:, :], func=AF.Sigmoid, scale=1.5957691216057308), r=[h2], w=[h2])
                fw.op("dve", lambda e: e.tensor_tensor(out=G[:, :], in0=h2[:, :], in1=hx[:, :], op=ALU.mult), r=[h2, hx], w=[G])
                ps3 = self.bank()
                if t == 0:
                    fw.op("pe", lambda e, ps3=ps3: e.matmul(ps3[:, 0:128], lhsT=w2s[:, :], rhs=G[:, :], start=True, stop=True), r=[w2s, G], w=[ps3])
                    fw.op("act", lambda e, ps3=ps3, g=g: e.copy(out=KCT[g][:, :], in_=ps3[:, 0:128]), r=[ps3], w=[KCT[g]])
                else:
                    fw.op("pe", lambda e, ps3=ps3: e.matmul(ps3[:, 0:128], lhsT=G[:, :], rhs=w2s[:, :], start=True, stop=True), r=[w2s, G], w=[ps3])
                    fw.op("act", lambda e, ps3=ps3, g=g: e.copy(out=VCX[g][:, 0:128], in_=ps3[:, 0:128]), r=[ps3], w=[VCX[g]])
                    fw.op("dve", lambda e, g=g: e.tensor_copy(out=VCX[g][:, 128:161], in_=self.cv("OVL")), r=[C], w=[VCX[g]])
        fw.release(m3)

        ET = fw.sb("ET", [32, NT], BF16)
        fw.dma("pool", ET[:, :], self.d_et[:, :], r=[self.d_et], w=[ET])
        NGs = fw.sb("NGs", [128, 16, 24])
        fw.dma("sp", NGs[:, :, :], self.NG.ap.rearrange("(t p) c -> p t c", p=128), r=[self.NG], w=[NGs])
        ksT = fw.sb("ksT", [128, NT], BF16)
        kwT = fw.sb("kwT", [128, NT], BF16)
        VSX = fw.sb("VSX", [128, 16, 129], BF16)
        VWX = fw.sb("VWX", [128, 16, 129], BF16)
        QTg = [fw.sb("QTg%d" % j, [128, NT], BF16) for j in range(4)]
        TABC = [fw.sb("TABC%d" % i, [128, 2048]) for i in range(2)]
        TABS = [fw.sb("TABS%d" % i, [128, TO_C]) for i in range(2)]
        IMP = fw.sb("IMP", [128, 16, 32])
        NEGT = fw.sb("NEGT", [32, NT], BF16)
        Yh = [fw.sb("Yh%d" % j, [128, 16, 128]) for j in range(4)]
        scb = [fw.sb("scb%d" % i, [128, 512]) for i in range(4)]
        p32 = [fw.sb("p32_%d" % i, [128, 512]) for i in range(4)]
        pbf = [fw.sb("pbf%d" % i, [128, 512], BF16) for i in range(4)]
        sm = [fw.sb("sm%d" % i, [128, 48]) for i in range(4)]
        ytb = [fw.sb("ytb%d" % i, [128, 512], BF16) for i in range(2)]
        MSK = self.cv("MSK").rearrange("p (a q k) -> p a q k", a=3, q=16, k=32)
        cnt = {"s": 0, "p": 0, "sm": 0, "y": 0, "r": 0, "c": 0, "p32": 0}
        PS = fw.PS
        LA = 3

        def sbank():
            b = PS[4 + cnt["c"] % 4]
            cnt["c"] += 1
            return b

        def pipeline(items):
            n = len(items)
            for i in range(n + LA):
                if i < n:
                    items[i][0]()
                if i >= LA:
                    items[i - LA][1]()

        def evac_round(banks, W, Yt, Q0, h, br, first, imp):
            s = sm[cnt["sm"] % 4]
            cnt["sm"] += 1
            for bi in range(2):
                fw.op("dve", lambda e, bi=bi: e.tensor_scalar(out=s[:, 2 * bi:2 * bi + 2], in0=banks[bi][:, 128:128 + W + 1:W], scalar1=1e-30, scalar2=None, op0=ALU.max),
                      r=[banks[bi]], w=[s])
            fw.op("dve", lambda e: e.reciprocal(out=s[:, 0:4], in_=s[:, 0:4]), r=[s], w=[s])
            fw.op("dve", lambda e: e.tensor_tensor(out=s[:, 4:8], in0=s[:, 0:4], in1=NGs[:, Q0:Q0 + 4, h * 3 + br], op=ALU.mult), r=[s, NGs], w=[s])
            for qt in range(4):
                bk = banks[qt // 2]
                c0 = (qt % 2) * W
                Q = Q0 + qt
                if first:
                    fw.op("dve", lambda e, bk=bk, c0=c0, Q=Q, qt=qt: e.tensor_scalar(out=Yt[:, Q, :], in0=bk[:, c0:c0 + 128], scalar1=s[:, 4 + qt:5 + qt], scalar2=None, op0=ALU.mult),
                          r=[bk, s], w=[(Yt, Q)])
                else:
                    fw.op("dve", lambda e, bk=bk, c0=c0, Q=Q, qt=qt: e.scalar_tensor_tensor(out=Yt[:, Q, :], in0=bk[:, c0:c0 + 128], scalar=s[:, 4 + qt:5 + qt], in1=Yt[:, Q, :],
                                                                                           op0=ALU.mult, op1=ALU.add), r=[bk, s, (Yt, Q)], w=[(Yt, Q)])
            if imp:
                for qt in range(4):
                    bk = banks[qt // 2]
                    c0 = (qt % 2) * W
                    Q = Q0 + qt
                    fw.op("dve", lambda e, bk=bk, c0=c0, Q=Q, qt=qt: e.scalar_tensor_tensor(out=IMP[:, Q, :], in0=bk[:, c0 + 129:c0 + 161], scalar=s[:, qt:qt + 1], in1=IMP[:, Q, :],
                                                                                           op0=ALU.mult, op1=ALU.add), r=[bk, s, (IMP, Q)], w=[(IMP, Q)])

        for g in range(2):
            fw.dma("sp", ksT[:, :], self.KVT[512 + g * 128:512 + (g + 1) * 128, :], r=[self.KVT], w=[ksT])
            fw.dma("sp", kwT[:, :], self.KVT[768 + g * 128:768 + (g + 1) * 128, :], r=[self.KVT], w=[kwT])
            fw.dma("sp", VSX[:, :, 0:128], self.VTOK.ap[:, g * 128:(g + 1) * 128].rearrange("(t p) d -> p t d", p=128), r=[self.VTOK], w=[VSX])
            fw.dma("sp", VWX[:, :, 0:128], self.VTOK.ap[:, 256 + g * 128:256 + (g + 1) * 128].rearrange("(t p) d -> p t d", p=128), r=[self.VTOK], w=[VWX])
            fw.op("dve", lambda e: e.memset(VSX[:, :, 128:129], 1.0), w=[VSX])
            fw.op("dve", lambda e: e.memset(VWX[:, :, 128:129], 1.0), w=[VWX])
            fw.op("dve", lambda e: e.memset(IMP[:, :, :], 0.0), w=[IMP])
            for j in range(4):
                h = g * 4 + j
                fw.dma("sp", QTg[j][:, :], self.QT[h * 128:(h + 1) * 128, :], r=[self.QT], w=[QTg[j]])
            items = []
            for j in range(4):
                h = g * 4 + j
                for qg in range(4):
                    st = {}

                    def front(j=j, h=h, qg=qg, st=st):
                        tb = TABC[h % 2]
                        if qg == 0:
                            fw.dma("sp", tb[:, :], self.d_tab.ap[h][:, TO_C:TO_C + 2048], r=[self.d_tab], w=[tb])
                        S = sbank()
                        fw.op("pe", lambda e: e.matmul(S[:, :], lhsT=KCT[g][:, :], rhs=QTg[j][:, qg * 512:(qg + 1) * 512], start=True, stop=True), r=[KCT[g], QTg[j]], w=[S])
                        sc_ = scb[cnt["s"] % 4]
                        cnt["s"] += 1
                        pp = p32[cnt["p32"] % 4]
                        cnt["p32"] += 1
                        fw.op("dve", lambda e: e.tensor_tensor(out=sc_[:, :], in0=S[:, :], in1=tb[:, qg * 512:(qg + 1) * 512], op=ALU.add), r=[S, tb], w=[sc_])
                        fw.op("act", lambda e: e.activation(out=pp[:, :], in_=sc_[:, :], func=AF.Exp), r=[sc_], w=[pp])
                        st["pp"] = pp

                    def back(j=j, h=h, qg=qg, st=st):
                        pp = st["pp"]
                        r_ = cnt["r"] % 2
                        cnt["r"] += 1
                        banks = [PS[2 * r_], PS[2 * r_ + 1]]
                        for qt in range(4):
                            bk = banks[qt // 2]
                            c0 = (qt % 2) * 161
                            fw.op("pe", lambda e, bk=bk, c0=c0, qt=qt: e.matmul(bk[:, c0:c0 + 161], lhsT=pp[:, qt * 128:(qt + 1) * 128], rhs=VCX[g][:, :], start=True, stop=True),
                                  r=[pp, VCX[g]], w=[bk])
                        evac_round(banks, 161, Yh[j], qg * 4, h, 0, True, True)

                    items.append((front, back))
            pipeline(items)
            for Q in range(16):
                s = sm[cnt["sm"] % 4]
                cnt["sm"] += 1
                im = s[:, 8:40]
                fw.op("dve", lambda e, Q=Q, im=im: e.tensor_tensor(out=im, in0=IMP[:, Q, :], in1=MSK[:, 0, Q, :], op=ALU.mult), r=[(IMP, Q), C], w=[s])
                fw.op("dve", lambda e, Q=Q, im=im: e.tensor_tensor(out=im, in0=im, in1=MSK[:, 1, Q, :], op=ALU.add), r=[s, C], w=[s])
                fw.op("dve", lambda e, s=s, im=im: e.max(out=s[:, 0:8], in_=im), r=[s], w=[s])
                fw.op("dve", lambda e, s=s, im=im: e.tensor_scalar(out=im, in0=im, scalar1=s[:, 7:8], scalar2=None, op0=ALU.is_ge), r=[s], w=[s])
                fw.op("dve", lambda e, Q=Q, im=im: e.tensor_tensor(out=im, in0=im, in1=MSK[:, 2, Q, :], op=ALU.mult), r=[s, C], w=[s])
                fw.op("dve", lambda e, im=im: e.tensor_scalar(out=im, in0=im, scalar1=-1.0, scalar2=-NEG, op0=ALU.add, op1=ALU.mult), r=[s], w=[s])
                b = sbank()
                fw.op("pe", lambda e, b=b, im=im: e.transpose(b[0:32, 0:128], im, self.cv("IDENT")), r=[s, C], w=[b])
                fw.op("act", lambda e, b=b, Q=Q: e.copy(out=NEGT[:, Q * 128:(Q + 1) * 128], in_=b[0:32, 0:128]), r=[b], w=[(NEGT, Q)])
            items = []
            for j in range(4):
                h = g * 4 + j
                slope = 2.0 ** (-(h + 1))
                for br in (1, 2):
                    for qg in range(4):
                        kt_lo = 0 if br == 1 else max(0, 4 * qg - 4)
                        kts = list(range(kt_lo, 4 * qg + 4))
                        rst = {"used": [False, False]}
                        for kt in kts:
                            st = {}
                            first_of_head = (br == 1 and qg == 0 and kt == kts[0])
                            last_of_round = (kt == kts[-1])
                            last_of_head = (br == 2 and qg == 3 and last_of_round)

                            def front(j=j, h=h, slope=slope, br=br, qg=qg, kt=kt, st=st, first_of_head=first_of_head):
                                tb = TABS[j % 2]
                                if first_of_head:
                                    fw.dma("sp", tb[:, :], self.d_tab.ap[h][:, 0:TO_C], r=[self.d_tab], w=[tb])
                                KT = ksT if br == 1 else kwT
                                D0 = qg * 512 - kt * 128
                                S = sbank()
                                fw.op("pe", lambda e: e.matmul(S[:, :], lhsT=KT[:, kt * 128:(kt + 1) * 128], rhs=QTg[j][:, qg * 512:(qg + 1) * 512], start=True, stop=(br == 2)),
                                      r=[KT, QTg[j]], w=[S])
                                if br == 1:
                                    fw.op("pe", lambda e: e.matmul(S[:, :], lhsT=ET[:, kt * 128:(kt + 1) * 128], rhs=NEGT[:, qg * 512:(qg + 1) * 512], start=False, stop=True),
                                          r=[ET, NEGT], w=[S])
                                sc_ = scb[cnt["s"] % 4]
                                cnt["s"] += 1
                                pp = pbf[cnt["p"] % 4]
                                cnt["p"] += 1
                                bias = 0.0
                                if br == 1:
                                    if D0 <= 0:
                                        tsl = tb[:, TO_U + D0 + 384:TO_U + D0 + 384 + 512]
                                    else:
                                        tsl = tb[:, TO_W:TO_W + 512]
                                        bias = -slope * D0
                                else:
                                    tsl = tb[:, TO_UW + D0 + 384:TO_UW + D0 + 384 + 512]
                                if bias == 0.0:
                                    fw.op("dve", lambda e: e.tensor_tensor(out=sc_[:, :], in0=S[:, :], in1=tsl, op=ALU.add), r=[S, tb], w=[sc_])
                                else:
                                    fw.op("dve", lambda e: e.scalar_tensor_tensor(out=sc_[:, :], in0=S[:, :], scalar=bias, in1=tsl, op0=ALU.add, op1=ALU.add), r=[S, tb], w=[sc_])
                                fw.op("act", lambda e: e.activation(out=pp[:, :], in_=sc_[:, :], func=AF.Exp), r=[sc_], w=[pp])
                                st["pp"] = pp

                            def back(j=j, h=h, br=br, qg=qg, kt=kt, st=st, rst=rst, first=(kt == kts[0]), last_of_round=last_of_round, last_of_head=last_of_head):
                                pp = st["pp"]
                                VX = VSX if br == 1 else VWX
                                if first:
                                    r_ = cnt["r"] % 2
                                    cnt["r"] += 1
                                    rst["banks"] = [PS[2 * r_], PS[2 * r_ + 1]]
                                banks = rst["banks"]
                                for qt in range(4):
                                    Q = qg * 4 + qt
                                    lo = 0 if br == 1 else max(0, Q - 4)
                                    if kt < lo or kt > Q:
                                        continue
                                    bi = qt // 2
                                    bk = banks[bi]
                                    c0 = (qt % 2) * 129
                                    stt_ = not rst["used"][bi]
                                    rst["used"][bi] = True
                                    fw.op("pe", lambda e, bk=bk, c0=c0, qt=qt, stt_=stt_, Q=Q: e.matmul(bk[:, c0:c0 + 129], lhsT=pp[:, qt * 128:(qt + 1) * 128], rhs=VX[:, kt, :],
                                                                                                    start=stt_, stop=(kt == Q), skip_group_check=True), r=[pp, VX], w=[bk])
                                if last_of_round:
                                    evac_round(banks, 129, Yh[j], qg * 4, h, br, False, False)
                                if last_of_head:
                                    for t4 in range(4):
                                        b = sbank()
                                        for jj in range(4):
                                            Q = t4 * 4 + jj
                                            fw.op("pe", lambda e, b=b, jj=jj, Q=Q: e.transpose(b[:, jj * 128:(jj + 1) * 128], Yh[j][:, Q, :], self.cv("IDENT")), r=[(Yh[j], Q), C], w=[b])
                                        y = ytb[cnt["y"] % 2]
                                        cnt["y"] += 1
                                        fw.op("act", lambda e, b=b, y=y: e.copy(out=y[:, :], in_=b[:, :]), r=[b], w=[y])
                                        fw.dma("sp", self.YT[h * 128:(h + 1) * 128, t4 * 512:(t4 + 1) * 512], y[:, :], r=[y], w=[(self.YT, (h, t4))])

                            items.append((front, back))
            pipeline(items)
        fw.release(m)
        fw.release(m)

    def p5_ssd(self, l):
        fw = self.fw
        C = self.C
        PS = fw.PS
        m = fw.mark()
        sv = fw.sb("ssdv", [128, 96])
        fw.dma("sp", sv[:, :], self.d_ssdv.ap[l], r=[self.d_ssdv], w=[sv])
        snw = fw.sb("snw", [128, 2048])
        fw.dma("sp", snw[:, :], self.d_snw.ap[l], r=[self.d_snw], w=[snw])
        negA = fw.sb("negA", [128, 32])
        fw.op("act", lambda e: e.activation(out=negA[:, :], in_=sv[:, 32:64], func=AF.Exp), r=[sv], w=[negA])
        fw.op("dve", lambda e: e.tensor_scalar(out=negA[:, :], in0=negA[:, :], scalar1=-1.0, scalar2=None, op0=ALU.mult), r=[negA], w=[negA])
        BT = fw.sb("BT", [128, 4, NT], BF16)
        CT = fw.sb("CT", [128, 4, NT], BF16)
        fw.dma("sp", BT[:, :, :], self.BCT.ap[0:512, :].rearrange("(g p) t -> p g t", p=128), r=[self.BCT], w=[BT])
        fw.dma("sp", CT[:, :, :], self.BCT.ap[512:1024, :].rearrange("(g p) t -> p g t", p=128), r=[self.BCT], w=[CT])
        H = fw.sb("H", [128, 4, 512])
        Hb = fw.sb("Hb", [128, 4, 512], BF16)
        fw.op("dve", lambda e: e.memset(H[:, :, :], 0.0), w=[H])
        fw.op("dve", lambda e: e.memset(Hb[:, :, :], 0.0), w=[Hb])
        dtA = fw.sb("dtA", [128, 16, 32])
        tmpA = fw.sb("tmpA", [128, 16, 32])
        aA = fw.sb("aA", [128, 16, 32])
        acA = fw.sb("acA", [128, 16, 32])
        atA = fw.sb("atA", [128, 16, 32])
        eaA = fw.sb("eaA", [128, 16, 32])
        cdA = fw.sb("cdA", [128, 16, 32])
        dteA = fw.sb("dteA", [128, 16, 32])
        fl = lambda t: t.ap.rearrange("p c h -> p (c h)")
        bc = lambda ap: ap.unsqueeze(1).broadcast_to([128, 16, 32])
        fw.dma("sp", dtA[:, :, :], self.DTR.ap.rearrange("(c p) h -> p c h", p=128), r=[self.DTR], w=[dtA])
        fw.op("dve", lambda e: e.tensor_tensor(out=dtA[:, :, :], in0=dtA[:, :, :], in1=bc(sv[:, 0:32]), op=ALU.add), r=[dtA, sv], w=[dtA])
        fw.op("dve", lambda e: e.tensor_scalar(out=fl(tmpA), in0=fl(dtA), scalar1=-1.0, scalar2=None, op0=ALU.mult), r=[dtA], w=[tmpA])
        fw.op("dve", lambda e: e.tensor_tensor(out=fl(tmpA), in0=fl(tmpA), in1=fl(dtA), op=ALU.max), r=[dtA, tmpA], w=[tmpA])
        fw.op("act", lambda e: e.activation(out=fl(tmpA), in_=fl(tmpA), func=AF.Exp, scale=-1.0), r=[tmpA], w=[tmpA])
        fw.op("act", lambda e: e.activation(out=fl(tmpA), in_=fl(tmpA), func=AF.Ln, bias=self.EPSC[:, 1:2], scale=1.0), r=[tmpA, self.EPSC], w=[tmpA])
        fw.op("dve", lambda e: e.scalar_tensor_tensor(out=fl(dtA), in0=fl(dtA), scalar=0.0, in1=fl(tmpA), op0=ALU.max, op1=ALU.add), r=[dtA, tmpA], w=[dtA])
        fw.op("dve", lambda e: e.tensor_tensor(out=aA[:, :, :], in0=dtA[:, :, :], in1=bc(negA[:, :]), op=ALU.mult), r=[dtA, negA], w=[aA])
        fw.op("pe", lambda e: e.matmul(PS[1][:, :], lhsT=self.cv("CAUS"), rhs=fl(aA), start=True, stop=True), r=[aA, C], w=[PS[1]])
        fw.op("pe", lambda e: e.matmul(PS[2][:, :], lhsT=self.cv("ONES1"), rhs=fl(aA), start=True, stop=True), r=[aA, C], w=[PS[2]])
        fw.op("act", lambda e: e.copy(out=fl(acA), in_=PS[1][:, :]), r=[PS[1]], w=[acA])
        fw.op("act", lambda e: e.copy(out=fl(atA), in_=PS[2][:, :]), r=[PS[2]], w=[atA])
        fw.op("act", lambda e: e.activation(out=fl(eaA), in_=fl(acA), func=AF.Exp), r=[acA], w=[eaA])
        fw.op("act", lambda e: e.activation(out=fl(cdA), in_=fl(atA), func=AF.Exp), r=[atA], w=[cdA])
        fw.op("dve", lambda e: e.tensor_tensor(out=fl(dteA), in0=fl(atA), in1=fl(acA), op=ALU.subtract), r=[atA, acA], w=[dteA])
        fw.op("act", lambda e: e.activation(out=fl(dteA), in_=fl(dteA), func=AF.Exp), r=[dteA], w=[dteA])

        xc = [fw.sb("xc%d" % i, [128, 2048]) for i in range(3)]
        zc = [fw.sb("zc%d" % i, [128, 2048]) for i in range(3)]
        bk = [fw.sb("bk%d" % i, [128, 512], BF16) for i in range(3)]
        xdt = fw.sb("xdt", [128, 2048], BF16)
        xdte = fw.sb("xdte", [128, 2048], BF16)
        cbm = fw.sb("cbm", [128, 4, 128])
        seg = [fw.sb("seg%d" % i, [128, 4, 128]) for i in range(2)]
        dec = [fw.sb("dec%d" % i, [128, 4, 128]) for i in range(3)]
        MT = [fw.sb("MT%d" % i, [128, 4, 128], BF16) for i in range(3)]
        ys_ = [fw.sb("y%d" % i, [128, 2048]) for i in range(2)]
        ysq = fw.sb("ysq", [128, 512])
        st4 = fw.sb("st4", [128, 8])
        yT = fw.sb("yT", [128, 16, 128], BF16)
        CAUS = self.cv("CAUS")
        cnt = {"k": 0, "m": 0, "b": 0}
        LA = 2

        def sbank():
            b = PS[1 + cnt["b"] % 3]
            cnt["b"] += 1
            return b

        def loadc(c):
            i = c % 3
            rows = slice(c * 128, (c + 1) * 128)
            fw.dma("sp", xc[i][:, :], self.XTOK[rows, :], r=[self.XTOK], w=[xc[i]])
            fw.dma("sp", bk[i][:, :], self.BTOK[rows, :], r=[self.BTOK], w=[bk[i]])
            fw.dma("sp", zc[i][:, :], self.ZS[rows, :], r=[self.ZS], w=[zc[i]])

        def stageA(c):
            i = c % 3
            x, z, bt = xc[i], zc[i], bk[i]
            y = ys_[c % 2]
            cs = slice(c * 128, (c + 1) * 128)
            x3 = x.ap.rearrange("p (h q) -> p h q", q=64)
            fw.op("dve", lambda e: e.tensor_tensor(out=xdt.ap.rearrange("p (h q) -> p h q", q=64), in0=x3, in1=dtA[:, c, :].unsqueeze(2).broadcast_to([128, 32, 64]), op=ALU.mult),
                  r=[x, dtA], w=[xdt])
            fw.op("dve", lambda e: e.tensor_tensor(out=xdte.ap.rearrange("p (h q) -> p h q", q=64), in0=xdt.ap.rearrange("p (h q) -> p h q", q=64),
                                                   in1=dteA[:, c, :].unsqueeze(2).broadcast_to([128, 32, 64]), op=ALU.mult), r=[xdt, dteA], w=[xdte])
            for g in range(4):
                fw.op("pe", lambda e, g=g: e.matmul(PS[0][:, g * 128:(g + 1) * 128], lhsT=BT[:, g, cs], rhs=CT[:, g, cs], start=True, stop=True), r=[BT, CT], w=[PS[0]])
            fw.op("dve", lambda e: e.tensor_tensor(out=cbm[:, :, :], in0=PS[0].ap.rearrange("p (g l) -> p g l", l=128), in1=CAUS.unsqueeze(1).broadcast_to([128, 4, 128]), op=ALU.mult),
                  r=[PS[0], C], w=[cbm])
            items = []
            for g in range(4):
                for hh in range(2):
                    st = {}

                    def front(g=g, hh=hh, st=st):
                        h0 = g * 8 + hh * 4
                        sg, dc = seg[cnt["k"] % 2], dec[cnt["k"] % 3]
                        cnt["k"] += 1
                        mt = MT[cnt["m"] % 3]
                        cnt["m"] += 1
                        fw.op("dve", lambda e: e.tensor_tensor(out=sg[:, :, :], in0=CAUS.unsqueeze(1).broadcast_to([128, 4, 128]),
                                                               in1=aA[:, c, h0:h0 + 4].unsqueeze(2).broadcast_to([128, 4, 128]), op=ALU.mult), r=[aA, C], w=[sg])
                        pseg = sbank()
                        fw.op("pe", lambda e: e.matmul(pseg[:, :], lhsT=self.cv("TGT"), rhs=sg.ap.rearrange("p a b -> p (a b)"), start=True, stop=True), r=[sg, C], w=[pseg])
                        fw.op("act", lambda e: e.activation(out=dc.ap.rearrange("p a b -> p (a b)"), in_=pseg[:, :], func=AF.Exp), r=[pseg], w=[dc])
                        st["mt"] = mt
                        st["dc"] = dc

                    def mid(g=g, hh=hh, st=st):
                        mt, dc = st["mt"], st["dc"]
                        fw.op("dve", lambda e: e.tensor_tensor(out=mt[:, :, :], in0=dc[:, :, :], in1=cbm[:, g, :].unsqueeze(1).broadcast_to([128, 4, 128]), op=ALU.mult),
                              r=[dc, cbm], w=[mt])

                    def back(g=g, hh=hh, st=st):
                        mt = st["mt"]
                        h0 = g * 8 + hh * 4
                        yd = PS[4 + g % 2]
                        for q in range(4):
                            hd = h0 + q
                            cc = (hh * 4 + q) * 64
                            fw.op("pe", lambda e, q=q, hd=hd, cc=cc: e.matmul(yd[:, cc:cc + 64], lhsT=mt[:, q, :], rhs=xdt[:, hd * 64:(hd + 1) * 64], start=True, stop=True),
                                  r=[mt, xdt], w=[yd])
                        if hh == 1:
                            yo = PS[6]
                            fw.op("pe", lambda e: e.matmul(yo[:, :], lhsT=CT[:, g, cs], rhs=Hb[:, g, :], start=True, stop=True), r=[CT, (Hb, g)], w=[yo])
                            ysl = y.ap[:, g * 512:(g + 1) * 512].rearrange("p (h q) -> p h q", q=64)
                            fw.op("dve", lambda e: e.tensor_tensor(out=ysl, in0=yo.ap.rearrange("p (h q) -> p h q", q=64),
                                                                   in1=eaA[:, c, g * 8:(g + 1) * 8].unsqueeze(2).broadcast_to([128, 8, 64]), op=ALU.mult), r=[yo, eaA], w=[(y, g)])
                            fw.op("dve", lambda e: e.tensor_tensor(out=y[:, g * 512:(g + 1) * 512], in0=yd[:, :], in1=y[:, g * 512:(g + 1) * 512], op=ALU.add), r=[yd, (y, g)], w=[(y, g)])
                            pst = PS[7]
                            fw.op("pe", lambda e: e.matmul(pst[:, :], lhsT=bt[:, g * 128:(g + 1) * 128], rhs=xdte[:, g * 512:(g + 1) * 512], start=True, stop=True),
                                  r=[bt, xdte], w=[pst])
                            Hg = H.ap[:, g, :].rearrange("p (h q) -> p h q", q=64)
                            fw.op("dve", lambda e: e.tensor_tensor(out=Hg, in0=Hg, in1=cdA[:, c, g * 8:(g + 1) * 8].unsqueeze(2).broadcast_to([128, 8, 64]), op=ALU.mult),
                                  r=[(H, g), cdA], w=[(H, g)])
                            fw.op("dve", lambda e: e.tensor_tensor(out=H[:, g, :], in0=pst[:, :], in1=H[:, g, :], op=ALU.add), r=[pst, (H, g)], w=[(H, g)])
                            fw.op("act", lambda e: e.copy(out=Hb[:, g, :], in_=H[:, g, :]), r=[(H, g)], w=[(Hb, g)])

                    items.append((front, mid, back))
            n = len(items)
            for ii in range(n + 2):
                if ii < n:
                    items[ii][0]()
                if 0 <= ii - 1 < n:
                    items[ii - 1][1]()
                if 0 <= ii - 2 < n:
                    items[ii - 2][2]()

        def stageB(c):
            i = c % 3
            x, z = xc[i], zc[i]
            y = ys_[c % 2]
            x3 = x.ap.rearrange("p (h q) -> p h q", q=64)
            fw.op("dve", lambda e: e.tensor_tensor(out=x3, in0=x3, in1=sv[:, 64:96].unsqueeze(2).broadcast_to([128, 32, 64]), op=ALU.mult), r=[x, sv], w=[x])
            fw.op("dve", lambda e: e.tensor_tensor(out=x[:, :], in0=x[:, :], in1=y[:, :], op=ALU.add), r=[y, x], w=[x])
            fw.op("dve", lambda e: e.tensor_tensor(out=y[:, :], in0=x[:, :], in1=z[:, :], op=ALU.mult), r=[x, z], w=[y])
            for g in range(4):
                fw.op("act", lambda e, g=g: e.activation(out=ysq[:, :], in_=y[:, g * 512:(g + 1) * 512], func=AF.Square), r=[y], w=[ysq])
                fw.op("dve", lambda e, g=g: e.tensor_reduce(out=st4[:, g:g + 1], in_=ysq[:, :], axis=AX.X, op=ALU.add), r=[ysq], w=[st4])
            fw.op("act", lambda e: e.activation(out=st4[:, 4:8], in_=st4[:, 0:4], func=AF.Sqrt, bias=self.EPSC[:, 0:1], scale=1.0 / 512), r=[st4, self.EPSC], w=[st4])
            fw.op("dve", lambda e: e.reciprocal(out=st4[:, 4:8], in_=st4[:, 4:8]), r=[st4], w=[st4])
            for g in range(4):
                fw.op("dve", lambda e, g=g: e.scalar_tensor_tensor(out=y[:, g * 512:(g + 1) * 512], in0=y[:, g * 512:(g + 1) * 512], scalar=st4[:, 4 + g:5 + g],
                                                                   in1=snw[:, g * 512:(g + 1) * 512], op0=ALU.mult, op1=ALU.mult), r=[y, st4, snw], w=[y])
            for t4 in range(4):
                b = sbank()
                for jj in range(4):
                    f = t4 * 4 + jj
                    fw.op("pe", lambda e, b=b, jj=jj, f=f: e.transpose(b[:, jj * 128:(jj + 1) * 128], y[:, f * 128:(f + 1) * 128], self.cv("IDENT")), r=[y, C], w=[b])
                fw.op("act", lambda e, b=b, t4=t4: e.copy(out=yT.ap[:, t4 * 4:(t4 + 1) * 4, :].rearrange("p a b -> p (a b)"), in_=b[:, :]), r=[b], w=[yT])
            fw.dma("sp", self.YT.ap[1024:3072, c * 128:(c + 1) * 128].rearrange("(f p) t -> p f t", p=128), yT[:, :, :], r=[yT], w=[(self.YT, ("s", c))])

        loadc(0)
        loadc(1)
        for c in range(16):
            stageA(c)
            if c >= 1:
                stageB(c - 1)
            if c + 2 < 16:
                loadc(c + 2)
        stageB(15)
        fw.release(m)

    def p6_ret(self, l):
        fw = self.fw
        C = self.C
        PS = fw.PS
        m = fw.mark()
        rnw = fw.sb("rnw", [128, 1024])
        fw.dma("sp", rnw[:, :], self.d_rnw.ap[l], r=[self.d_rnw], w=[rnw])
        QT = fw.sb("rQT", [128, 4, NT], BF16)
        KT = fw.sb("rKT", [128, 4, NT], BF16)
        fw.dma("sp", QT[:, :, :], self.RQT.ap.rearrange("(h p) t -> p h t", p=128), r=[self.RQT], w=[QT])
        fw.dma("sp", KT[:, :, :], self.RKT.ap.rearrange("(h p) t -> p h t", p=128), r=[self.RKT], w=[KT])
        QS = fw.sb("rQS", [128, 4, NT], BF16)
        QD = self.cv("QD").rearrange("p (h l) -> p h l", l=128)
        for h in range(4):
            fw.op("dve", lambda e, h=h: e.tensor_tensor(out=QS.ap[:, h, :].rearrange("p (c l) -> p c l", l=128), in0=QT.ap[:, h, :].rearrange("p (c l) -> p c l", l=128),
                                                        in1=QD[:, h, :].unsqueeze(1).broadcast_to([128, 16, 128]), op=ALU.mult), r=[QT, C], w=[QS])
        R = fw.sb("R", [128, 4, 256])
        Rb = fw.sb("Rb", [128, 4, 256], BF16)
        CD = fw.sb("CD", [128, 4])
        fw.op("dve", lambda e: e.memset(R[:, :, :], 0.0), w=[R])
        fw.op("dve", lambda e: e.memset(Rb[:, :, :], 0.0), w=[Rb])
        for h in range(4):
            fw.op("dve", lambda e, h=h: e.memset(CD[:, h:h + 1], self.cdec[h]), w=[CD])
        vc = [fw.sb("vc%d" % i, [128, 1024], BF16) for i in range(3)]
        kc_ = [fw.sb("kc%d" % i, [128, 512], BF16) for i in range(3)]
        gc = [fw.sb("gc%d" % i, [128, 1024]) for i in range(3)]
        kdb = [fw.sb("kd%d" % i, [128, 512], BF16) for i in range(2)]
        sTm = [fw.sb("sTm%d" % i, [128, 4, 128], BF16) for i in range(2)]
        ysb = [fw.sb("ys%d" % i, [128, 4, 256]) for i in range(2)]
        y2 = fw.sb("y2", [128, 4, 256])
        stb = [fw.sb("rst%d" % i, [128, 16]) for i in range(2)]
        yr = fw.sb("yr", [128, 1024])
        yTb = [fw.sb("ryT%d" % i, [128, 8, 128], BF16) for i in range(2)]
        DMT = self.cv("DMT").rearrange("p (h l) -> p h l", l=128)
        KD = self.cv("KD")
        f3 = lambda t: t.ap.rearrange("p a b -> p (a b)")

        def loadc(c):
            i = c % 3
            rows = slice(c * 128, (c + 1) * 128)
            fw.dma("sp", vc[i][:, :], self.RV[rows, :], r=[self.RV], w=[vc[i]])
            fw.dma("sp", kc_[i][:, :], self.RKTOK[rows, :], r=[self.RKTOK], w=[kc_[i]])
            fw.dma("sp", gc[i][:, :], self.RGS[rows, :], r=[self.RGS], w=[gc[i]])

        def stageA(c):
            v, kk = vc[c % 3], kc_[c % 3]
            kd, sm_, ys = kdb[c % 2], sTm[c % 2], ysb[c % 2]
            cs = slice(c * 128, (c + 1) * 128)
            fw.op("dve", lambda e: e.tensor_tensor(out=kd.ap.rearrange("p (h d) -> p h d", d=128), in0=kk.ap.rearrange("p (h d) -> p h d", d=128),
                                                   in1=KD.unsqueeze(2).broadcast_to([128, 4, 128]), op=ALU.mult), r=[kk, C], w=[kd])
            Sb = PS[c % 2]
            for h in range(4):
                fw.op("pe", lambda e, h=h: e.matmul(Sb[:, h * 128:(h + 1) * 128], lhsT=KT[:, h, cs], rhs=QT[:, h, cs], start=True, stop=True), r=[KT, QT], w=[Sb])
            fw.op("dve", lambda e: e.tensor_tensor(out=sm_[:, :, :], in0=Sb.ap.rearrange("p (h l) -> p h l", l=128), in1=DMT, op=ALU.mult), r=[Sb, C], w=[sm_])
            for h in range(4):
                pk = PS[2 + h // 2]
                c0 = (h % 2) * 256
                fw.op("pe", lambda e, h=h, pk=pk, c0=c0: e.matmul(pk[:, c0:c0 + 256], lhsT=kd[:, h * 128:(h + 1) * 128], rhs=v[:, h * 256:(h + 1) * 256], start=True, stop=True),
                      r=[kd, v], w=[pk])
            pyb = [PS[4 + 2 * (c % 2)], PS[5 + 2 * (c % 2)]]
            for h in range(4):
                py = pyb[h // 2]
                c0 = (h % 2) * 256
                fw.op("pe", lambda e, h=h, py=py, c0=c0: e.matmul(py[:, c0:c0 + 256], lhsT=sm_[:, h, :], rhs=v[:, h * 256:(h + 1) * 256], start=True, stop=False, skip_group_check=True),
                      r=[sm_, v], w=[py])
                fw.op("pe", lambda e, h=h, py=py, c0=c0: e.matmul(py[:, c0:c0 + 256], lhsT=QS[:, h, cs], rhs=Rb[:, h, :], start=False, stop=True, skip_group_check=True),
                      r=[QS, Rb], w=[py])
            fw.op("dve", lambda e: e.tensor_tensor(out=R[:, :, :], in0=R[:, :, :], in1=CD[:, :].unsqueeze(2).broadcast_to([128, 4, 256]), op=ALU.mult), r=[R, CD], w=[R])
            for b2 in range(2):
                fw.op("dve", lambda e, b2=b2: e.tensor_tensor(out=f3(R)[:, b2 * 512:(b2 + 1) * 512], in0=PS[2 + b2][:, :], in1=f3(R)[:, b2 * 512:(b2 + 1) * 512], op=ALU.add),
                      r=[PS[2 + b2], R], w=[R])
            fw.op("act", lambda e: e.copy(out=f3(Rb), in_=f3(R)), r=[R], w=[Rb])
            for b2 in range(2):
                fw.op("act", lambda e, b2=b2: e.copy(out=f3(ys)[:, b2 * 512:(b2 + 1) * 512], in_=pyb[b2][:, :]), r=[pyb[b2]], w=[ys])

        def stageB(c):
            ys, s8, gg, yT = ysb[c % 2], stb[c % 2], gc[c % 3], yTb[c % 2]
            bc4 = lambda ap: ap.unsqueeze(2).broadcast_to([128, 4, 256])
            fw.op("dve", lambda e: e.tensor_reduce(out=s8[:, 0:4], in_=ys[:, :, :], axis=AX.X, op=ALU.add), r=[ys], w=[s8])
            fw.op("dve", lambda e: e.tensor_scalar(out=s8[:, 4:8], in0=s8[:, 0:4], scalar1=1.0 / 256, scalar2=None, op0=ALU.mult), r=[s8], w=[s8])
            fw.op("dve", lambda e: e.tensor_tensor(out=ys[:, :, :], in0=ys[:, :, :], in1=bc4(s8[:, 4:8]), op=ALU.subtract), r=[ys, s8], w=[ys])
            fw.op("act", lambda e: e.activation(out=f3(y2), in_=f3(ys), func=AF.Square), r=[ys], w=[y2])
            fw.op("dve", lambda e: e.tensor_reduce(out=s8[:, 8:12], in_=y2[:, :, :], axis=AX.X, op=ALU.add), r=[y2], w=[s8])
            fw.op("act", lambda e: e.activation(out=s8[:, 12:16], in_=s8[:, 8:12], func=AF.Sqrt, bias=self.EPSC[:, 0:1], scale=1.0 / 256), r=[s8, self.EPSC], w=[s8])
            fw.op("dve", lambda e: e.reciprocal(out=s8[:, 12:16], in_=s8[:, 12:16]), r=[s8], w=[s8])
            fw.op("dve", lambda e: e.tensor_tensor(out=ys[:, :, :], in0=ys[:, :, :], in1=bc4(s8[:, 12:16]), op=ALU.mult), r=[ys, s8], w=[ys])
            fw.op("dve", lambda e: e.tensor_tensor(out=f3(ys), in0=f3(ys), in1=rnw[:, :], op=ALU.mult), r=[ys, rnw], w=[ys])
            fw.op("dve", lambda e: e.tensor_tensor(out=yr[:, :], in0=f3(ys), in1=gg[:, :], op=ALU.mult), r=[ys, gg], w=[yr])
            for t4 in range(2):
                b = PS[2 + t4]
                for jj in range(4):
                    f = t4 * 4 + jj
                    fw.op("pe", lambda e, b=b, jj=jj, f=f: e.transpose(b[:, jj * 128:(jj + 1) * 128], yr[:, f * 128:(f + 1) * 128], self.cv("IDENT")), r=[yr, C], w=[b])
                fw.op("act", lambda e, b=b, t4=t4: e.copy(out=yT.ap[:, t4 * 4:(t4 + 1) * 4, :].rearrange("p a b -> p (a b)"), in_=b[:, :]), r=[b], w=[yT])
            fw.dma("sp", self.YT.ap[3072:4096, c * 128:(c + 1) * 128].rearrange("(f p) t -> p f t", p=128), yT[:, :, :], r=[yT], w=[(self.YT, ("r", c))])

        loadc(0)
        loadc(1)
        for c in range(16):
            stageA(c)
            if c >= 1:
                stageB(c - 1)
            if c + 2 < 16:
                loadc(c + 2)
        stageB(15)
        fw.release(m)

    def p7_merge(self, l, src):
        fw = self.fw
        m = fw.mark()
        self.WB = [fw.sb("WB%d" % i, [128, 8192], BF16) for i in range(3)]
        yt = fw.sb("ytall", [128, 32, 1024], BF16)
        mg = [fw.sb("mg%d" % i, [128, 1024], BF16) for i in range(3)]
        mer = fw.sb("mer", [128, 4, 1024])
        merb = fw.sb("merb", [128, 16, 1024], BF16)
        tmp = [fw.sb("mtmp%d" % i, [128, 512]) for i in range(2)]
        xt = [fw.sb("mxt%d" % i, [128, 512]) for i in range(2)]
        branches = ((self.p_nsa, 8, 0, 0), (self.p_ssd, 16, 8, 1), (self.p_ret, 8, 24, 2))
        k = 0
        def load_yt(th):
            t0 = th * 1024
            for k0 in range(0, 32, 8):
                fw.dma("sp", yt[:, k0:k0 + 8, :], self.YT.ap[k0 * 128:(k0 + 8) * 128, t0:t0 + 1024].rearrange("(k p) t -> p k t", p=128), r=[self.YT], w=[(yt, k0)])

        load_yt(0)
        for th in range(2):
            t0 = th * 1024
            for cb in range(4):
                for (W, KC, koff, bi) in branches:
                    buf, view = self.wload(W, W.ap[l][:, cb * 512:(cb + 1) * 512], KC, 512)
                    for j in range(4):
                        ct = cb * 4 + j
                        g = mg[k % 3]
                        k += 1
                        r0 = bi * 2048 + ct * 128
                        fw.dma("sp", g[:, :], self.MGT[r0:r0 + 128, t0:t0 + 1024], r=[self.MGT], w=[g])
                        pss = [self.bank(), self.bank()]
                        for kc in range(KC):
                            for t2 in range(2):
                                fw.op("pe", lambda e, kc=kc, t2=t2: e.matmul(pss[t2][:, :], lhsT=view[:, kc, j * 128:(j + 1) * 128], rhs=yt[:, koff + kc, t2 * 512:(t2 + 1) * 512],
                                                                         start=(kc == 0), stop=(kc == KC - 1)), r=[buf, yt], w=[pss[t2]])
                        for t2 in range(2):
                            ts2 = slice(t2 * 512, (t2 + 1) * 512)
                            if bi == 0:
                                fw.op("dve", lambda e, t2=t2, ts2=ts2: e.tensor_tensor(out=mer[:, j, ts2], in0=pss[t2][:, :], in1=g[:, ts2], op=ALU.mult), r=[pss[t2], g], w=[(mer, (j, t2))])
                            else:
                                t_ = tmp[k % 2]
                                k += 1
                                fw.op("dve", lambda e, t2=t2, ts2=ts2, t_=t_: e.tensor_tensor(out=t_[:, :], in0=pss[t2][:, :], in1=g[:, ts2], op=ALU.mult), r=[pss[t2], g], w=[t_])
                                fw.op("dve", lambda e, t2=t2, ts2=ts2, t_=t_: e.tensor_tensor(out=mer[:, j, ts2], in0=mer[:, j, ts2], in1=t_[:, :], op=ALU.add), r=[t_, (mer, (j, t2))], w=[(mer, (j, t2))])
                for j in range(4):
                    fw.op("act", lambda e, j=j: e.copy(out=merb[:, cb * 4 + j, :], in_=mer[:, j, :]), r=[mer], w=[(merb, cb * 4 + j)])
            if th == 0:
                load_yt(1)
            for cb in range(4):
                buf, view = self.wload(self.w_out, self.w_out.ap[l][:, cb * 512:(cb + 1) * 512], 16, 512)
                for j in range(4):
                    ct = cb * 4 + j
                    pss = [self.bank(), self.bank()]
                    for kc in range(16):
                        for t2 in range(2):
                            fw.op("pe", lambda e, kc=kc, t2=t2: e.matmul(pss[t2][:, :], lhsT=view[:, kc, j * 128:(j + 1) * 128], rhs=merb[:, kc, t2 * 512:(t2 + 1) * 512],
                                                                     start=(kc == 0), stop=(kc == 15)), r=[buf, merb], w=[pss[t2]])
                    for t2 in range(2):
                        ts = slice(t0 + t2 * 512, t0 + (t2 + 1) * 512)
                        x = xt[k % 2]
                        k += 1
                        fw.dma("sp", x[:, :], src[ct * 128:(ct + 1) * 128, ts], r=[(src, (ct, th * 2 + t2))], w=[x])
                        fw.op("dve", lambda e, t2=t2, x=x: e.tensor_tensor(out=x[:, :], in0=pss[t2][:, :], in1=x[:, :], op=ALU.add), r=[pss[t2], x], w=[x])
                        fw.dma("sp", self.XT[ct * 128:(ct + 1) * 128, ts], x[:, :], r=[x], w=[(self.XT, (ct, th * 2 + t2))])
        fw.release(m)

    def p8_ffn(self, l):
        fw = self.fw
        m = fw.mark()
        self.WB = [fw.sb("WB%d" % i, [128, 5632], BF16) for i in range(3)]
        xg = fw.sb("fxg", [128, 16, 256])
        fT = fw.sb("fT", [128, 16, 1024], BF16)
        hT = fw.sb("hT", [128, 44, 1024], BF16)
        sq = [fw.sb("fsq%d" % i, [128, 256]) for i in range(2)]
        rs = fw.sb("frs", [128, 256])
        sg = [fw.sb("fsg%d" % i, [128, 512]) for i in range(2)]
        xo = [fw.sb("fxo%d" % i, [128, 512]) for i in range(2)]
        wcol = self.NRM[:, (2 * l + 1) * 16:(2 * l + 2) * 16]
        xv = self.XT.ap.rearrange("(k p) t -> p k t", p=128)
        k = 0
        for th in range(2):
            t0 = th * 1024
            for sub in range(4):
                ts = slice(t0 + sub * 256, t0 + (sub + 1) * 256)
                for k0 in range(0, 16, 4):
                    fw.dma("sp", xg[:, k0:k0 + 4, :], xv[:, k0:k0 + 4, ts], r=[self.XT], w=[(xg, k0)])
                self.rmsnorm(xg, 256, wcol, lambda kc, sub=sub: (fT, fT[:, kc, sub * 256:(sub + 1) * 256]), sq, rs)
            for cb in range(22):
                bg, vg = self.wload(self.w_gate, self.w_gate.ap[l][:, cb * 256:(cb + 1) * 256], 16, 256)
                bu, vu = self.wload(self.w_up, self.w_up.ap[l][:, cb * 256:(cb + 1) * 256], 16, 256)
                for j in range(2):
                    f = cb * 2 + j
                    pg = [self.bank(), self.bank()]
                    pu = [self.bank(), self.bank()]
                    for kc in range(16):
                        for t2 in range(2):
                            fw.op("pe", lambda e, kc=kc, t2=t2: e.matmul(pg[t2][:, :], lhsT=vg[:, kc, j * 128:(j + 1) * 128], rhs=fT[:, kc, t2 * 512:(t2 + 1) * 512],
                                                                     start=(kc == 0), stop=(kc == 15)), r=[bg, fT], w=[pg[t2]])
                    for kc in range(16):
                        for t2 in range(2):
                            fw.op("pe", lambda e, kc=kc, t2=t2: e.matmul(pu[t2][:, :], lhsT=vu[:, kc, j * 128:(j + 1) * 128], rhs=fT[:, kc, t2 * 512:(t2 + 1) * 512],
                                                                     start=(kc == 0), stop=(kc == 15)), r=[bu, fT], w=[pu[t2]])
                    for t2 in range(2):
                        s_ = sg[k % 2]
                        k += 1
                        fw.op("act", lambda e, t2=t2, s_=s_: e.activation(out=s_[:, :], in_=pg[t2][:, :], func=AF.Silu), r=[pg[t2]], w=[s_])
                        fw.op("dve", lambda e, t2=t2, s_=s_: e.tensor_tensor(out=hT[:, f, t2 * 512:(t2 + 1) * 512], in0=pu[t2][:, :], in1=s_[:, :], op=ALU.mult), r=[pu[t2], s_], w=[(hT, f)])
            for ct in range(16):
                bd, vd = self.wload(self.w_down, self.w_down.ap[l][:, ct * 128:(ct + 1) * 128], 44, 128)
                pss = [self.bank(), self.bank()]
                for kc in range(44):
                    for t2 in range(2):
                        fw.op("pe", lambda e, kc=kc, t2=t2: e.matmul(pss[t2][:, :], lhsT=vd[:, kc, :], rhs=hT[:, kc, t2 * 512:(t2 + 1) * 512], start=(kc == 0), stop=(kc == 43)),
                              r=[bd, hT], w=[pss[t2]])
                for t2 in range(2):
                    ts = slice(t0 + t2 * 512, t0 + (t2 + 1) * 512)
                    x = xo[k % 2]
                    k += 1
                    fw.dma("sp", x[:, :], self.XT[ct * 128:(ct + 1) * 128, ts], r=[(self.XT, (ct, th * 2 + t2))], w=[x])
                    fw.op("dve", lambda e, t2=t2, x=x: e.tensor_tensor(out=x[:, :], in0=pss[t2][:, :], in1=x[:, :], op=ALU.add), r=[pss[t2], x], w=[x])
                    fw.dma("sp", self.XT[ct * 128:(ct + 1) * 128, ts], x[:, :], r=[x], w=[(self.XT, (ct, th * 2 + t2))])
        fw.release(m)

    def final_norm(self, src):
        fw = self.fw
        m = fw.mark()
        xg = fw.sb("nxg", [128, 16, 512])
        og = fw.sb("nog", [128, 16, 512])
        sq = [fw.sb("nsq%d" % i, [128, 512]) for i in range(2)]
        rs = fw.sb("nrs", [128, 512])
        wcol = self.NRM[:, (2 * NL) * 16:(2 * NL + 1) * 16]
        xv = src.ap.rearrange("(k p) t -> p k t", p=128)
        ov = self.outT.ap.rearrange("(k p) t -> p k t", p=128)
        for tg in range(4):
            ts = slice(tg * 512, (tg + 1) * 512)
            for k0 in range(0, 16, 4):
                fw.dma("sp", xg[:, k0:k0 + 4, :], xv[:, k0:k0 + 4, ts], r=[src], w=[(xg, k0)])
            self.rmsnorm(xg, 512, wcol, lambda kc: (og, og[:, kc, :]), sq, rs)
            for k0 in range(0, 16, 4):
                fw.dma("sp", ov[:, k0:k0 + 4, ts], og[:, k0:k0 + 4, :], r=[og], w=[(self.outT, (tg, k0))])
        fw.release(m)


def host_inputs(inp, b):
    consts, tab, et, _ = host_consts()
    f = np.float32
    nrm = np.zeros((128, (2 * NL + 1) * 16), f)
    for l in range(NL):
        nrm[:, (2 * l) * 16:(2 * l + 1) * 16] = np.asarray(inp["norm_mix"][l], f).reshape(16, 128).T
        nrm[:, (2 * l + 1) * 16:(2 * l + 2) * 16] = np.asarray(inp["norm_ffn"][l], f).reshape(16, 128).T
    nrm[:, 2 * NL * 16:] = np.asarray(inp["norm_final"], f).reshape(16, 128).T
    cw = np.zeros((NL, 128, 24, 5), f)
    for l in range(NL):
        cw[l, :, :, 0:4] = np.asarray(inp["conv_w"][l], f).T.reshape(24, 128, 4).transpose(1, 0, 2)
        cw[l, :, :, 4] = np.asarray(inp["conv_b"][l], f).reshape(24, 128).T
    ssdv = np.zeros((NL, 128, 96), f)
    ssdv[:, :, 0:32] = np.asarray(inp["dt_bias"], f)[:, None, :]
    ssdv[:, :, 32:64] = np.asarray(inp["a_log"], f)[:, None, :]
    ssdv[:, :, 64:96] = np.asarray(inp["d_skip"], f)[:, None, :]
    snw = np.ascontiguousarray(np.broadcast_to(np.asarray(inp["ssd_norm"], f)[:, None, :], (NL, 128, 2048)))
    rnw = np.ascontiguousarray(np.broadcast_to(np.asarray(inp["ret_norm"], f).reshape(NL, 1, 1024), (NL, 128, 1024)))
    pet = np.stack([np.asarray(inp["cmp_k_pe"], f).transpose(0, 2, 1), np.asarray(inp["cmp_v_pe"], f).transpose(0, 2, 1)], axis=1)
    d = {
        "xT": np.ascontiguousarray(np.asarray(inp["x"][b], f).T),
        "consts": consts, "tab": tab, "et": et, "nrm": nrm, "cw": cw.reshape(NL, 128, 120), "ssdv": ssdv,
        "snw": snw, "rnw": rnw, "pet": np.ascontiguousarray(pet),
    }
    for k in ("w_in", "cmp_k_w1", "cmp_v_w1", "cmp_k_w2", "cmp_v_w2", "p_nsa", "p_ssd", "p_ret", "w_out", "w_gate", "w_up", "w_down"):
        d[k] = np.ascontiguousarray(np.asarray(inp[k], f))
    return d


_PROG = {}


def kernel(**inputs):
    if "p" not in _PROG:
        _PROG["p"] = Prog()
    prog = _PROG["p"]
    base = host_inputs(inputs, 0)
    zero = {k: np.zeros_like(v) for k, v in base.items()}
    work = {0: 0, 1: 1, 4: 2, 5: 3}
    in_maps = []
    for c in range(8):
        if c in work:
            d = dict(base)
            d["xT"] = np.ascontiguousarray(np.asarray(inputs["x"][work[c]], np.float32).T)
        else:
            d = zero
        in_maps.append(d)
    res = run_bass_kernel_spmd(prog.nc, in_maps, core_ids=list(range(8)))
    inv = {b: c for c, b in work.items()}
    out = np.stack([np.asarray(res.results[inv[b]]["outT"], np.float32).T for b in range(4)], axis=0)
    return np.ascontiguousarray(out)
```
